# Optimizing a Trainium2 kernel written in Bass

```python
import jax, jax.numpy as jnp
from jax import lax
import numpy as np

D_MODEL = 1024
BATCH = 8
SEQ = 2048
DEPTH = 1
DEC_BATCH = 128
DEC_SEQ = 4
PAST_LEN = 16384
PAGE_SIZE = 128

A_HEAD = 64
A_HEADS = 8
A_WIDTH = A_HEADS * A_HEAD
A_RANK_W = 64
A_RANK_A = 64
A_RANK_G = 128
A_PROJ = 3 * A_WIDTH + A_RANK_W + A_RANK_A + A_RANK_G
A_LNX_EPS = 64e-5
B_HEADS = 4
B_HEAD = 128
B_WIDTH = B_HEADS * B_HEAD
CONV_K = 4
CONV_CH = 3 * B_WIDTH
B_PROJ = CONV_CH + 2 * B_HEADS + B_WIDTH
DELTA_CHUNK = 64
GATE_COLS = 2 * D_MODEL
IN_COLS = A_PROJ + B_PROJ + GATE_COLS
D_FF = 2816
RMS_EPS = 1e-6

kernel_name = 'hybrid_rwkv7_gdn_macaron_step'

F32 = jnp.float32


def _rms(x, w):
    x32 = x.astype(F32)
    y = x32 * lax.rsqrt(jnp.mean(x32 * x32, axis=-1, keepdims=True) + RMS_EPS) * w.astype(F32)
    return y.astype(x.dtype)


def _swiglu(x, wg, wu, wd):
    return (jax.nn.silu(x @ wg) * (x @ wu)) @ wd


def _l2norm(t):
    return t * lax.rsqrt(jnp.sum(t * t, axis=-1, keepdims=True) + 1e-6)


def _rwkv7(pa, s_shift, s_state, mu, w0, w2, a0, a2, g2, k_k, k_a, r_k, lnx_w, lnx_b):
    Bn, T, _ = pa.shape
    pa32 = pa.astype(F32)
    prev = jnp.concatenate([s_shift.astype(F32)[:, None], pa32[:, :-1]], axis=1)
    pm = pa32 + (prev - pa32) * mu.astype(F32)
    splits = (A_WIDTH, A_WIDTH + A_RANK_W, 2 * A_WIDTH + A_RANK_W,
              3 * A_WIDTH + A_RANK_W, 3 * A_WIDTH + A_RANK_W + A_RANK_A)
    r, wd, k, v, ad, gd = jnp.split(pm, splits, axis=-1)
    w_log = -jax.nn.softplus(-(w0.astype(F32) + jnp.tanh(wd) @ w2.astype(F32))) - 0.5
    decay = jnp.exp(-jnp.exp(w_log))
    a = jax.nn.sigmoid(a0.astype(F32) + ad @ a2.astype(F32))
    g = jax.nn.sigmoid(gd) @ g2.astype(F32)
    hs = lambda t: t.reshape(Bn, T, A_HEADS, A_HEAD)
    kk = hs(k * k_k.astype(F32))
    kk = kk / jnp.maximum(jnp.sqrt(jnp.sum(kk * kk, axis=-1, keepdims=True)), 1e-12)
    k = k * (1.0 + (a - 1.0) * k_a.astype(F32))
    r_h, k_h, v_h, a_h, d_h = hs(r), hs(k), hs(v), hs(a), hs(decay)

    def step(S, inp):
        r_t, d_t, k_t, v_t, kk_t, a_t = inp
        sa = jnp.einsum('bhvk,bhk->bhv', S, -kk_t)
        S = (S * d_t[:, :, None, :] + sa[..., None] * (kk_t * a_t)[:, :, None, :]
             + v_t[..., None] * k_t[:, :, None, :])
        return S, jnp.einsum('bhvk,bhk->bhv', S, r_t)

    xs = tuple(jnp.moveaxis(t, 1, 0) for t in (r_h, d_h, k_h, v_h, kk, a_h))
    S_fin, o = lax.scan(step, s_state.astype(F32), xs)
    o = jnp.moveaxis(o, 0, 1)
    mean = jnp.mean(o, axis=-1, keepdims=True)
    var = jnp.mean(jnp.square(o - mean), axis=-1, keepdims=True)
    o = ((o - mean) * lax.rsqrt(var + A_LNX_EPS)).reshape(Bn, T, A_WIDTH) * lnx_w.astype(F32) + lnx_b.astype(F32)
    bonus = jnp.sum(r_h * k_h * r_k.astype(F32), axis=-1, keepdims=True) * v_h
    o = (o + bonus.reshape(Bn, T, A_WIDTH)) * g
    return o, S_fin, pa32[:, -1]


def _chunked_gated_delta(q, k, v, beta, g, S0):
    Bn, T, H, DK = q.shape
    DV = v.shape[-1]
    C = min(DELTA_CHUNK, T)
    n = -(-T // C)
    pad = n * C - T

    def blk(t):
        t = jnp.pad(t, [(0, 0), (0, pad)] + [(0, 0)] * (t.ndim - 2))
        t = t.reshape((Bn, n, C) + t.shape[2:])
        return jnp.moveaxis(t, 3, 1)

    q, k, v, beta, g = blk(q), blk(k), blk(v), blk(beta), blk(g)
    gc = jnp.cumsum(g, axis=-1)
    idx = jnp.arange(C)
    incl = idx[:, None] >= idx[None, :]
    strict = idx[:, None] > idx[None, :]
    diff = gc[..., :, None] - gc[..., None, :]
    decay_mat = jnp.where(incl, jnp.exp(jnp.where(incl, diff, 0.0)), 0.0)
    kb = k * beta[..., None]
    A = jnp.where(strict, jnp.einsum('bhnik,bhnjk->bhnij', kb, k) * decay_mat, 0.0)
    M = A + jnp.eye(C, dtype=A.dtype)
    rhs = jnp.concatenate([v * beta[..., None], kb * jnp.exp(gc)[..., None]], axis=-1)
    sol = lax.linalg.triangular_solve(M, rhs, left_side=True, lower=True, unit_diagonal=True)
    u_c, w_c = sol[..., :DV], sol[..., DV:]
    qk = jnp.einsum('bhnik,bhnjk->bhnij', q, k) * decay_mat
    q_dec = q * jnp.exp(gc)[..., None]
    k_dec = k * jnp.exp(gc[..., -1:] - gc)[..., None]
    g_last = jnp.exp(gc[..., -1])

    def step(S, inp):
        u_t, w_t, qk_t, qd_t, kd_t, gl_t = inp
        v_new = u_t - jnp.einsum('bhck,bhkv->bhcv', w_t, S)
        o = jnp.einsum('bhck,bhkv->bhcv', qd_t, S) + jnp.einsum('bhij,bhjv->bhiv', qk_t, v_new)
        S = S * gl_t[..., None, None] + jnp.einsum('bhck,bhcv->bhkv', kd_t, v_new)
        return S, o

    xs = tuple(jnp.moveaxis(t, 2, 0) for t in (u_c, w_c, qk, q_dec, k_dec, g_last))
    S_fin, o = lax.scan(step, S0, xs)
    o = jnp.transpose(o, (1, 0, 3, 2, 4)).reshape(Bn, n * C, H, DV)[:, :T]
    return o, S_fin


def _gated_delta_branch(pb, s_conv, s_state, conv_w, A_log, dt_bias, norm_w):
    Bn, T, _ = pb.shape
    pb = pb.astype(F32)
    qkv = pb[..., :CONV_CH]
    a_in = pb[..., CONV_CH:CONV_CH + B_HEADS]
    b_in = pb[..., CONV_CH + B_HEADS:CONV_CH + 2 * B_HEADS]
    z = pb[..., CONV_CH + 2 * B_HEADS:]
    xp = jnp.concatenate([s_conv.astype(F32), qkv], axis=1)
    cw = conv_w.astype(F32)
    conv = sum(xp[:, i:i + T] * cw[i] for i in range(CONV_K))
    conv = jax.nn.silu(conv)
    new_conv = xp[:, T:]
    q, k, v = jnp.split(conv, 3, axis=-1)
    hs = lambda t: t.reshape(Bn, T, B_HEADS, B_HEAD)
    q = _l2norm(hs(q)) * (B_HEAD ** -0.5)
    k = _l2norm(hs(k))
    v = hs(v)
    beta = jax.nn.sigmoid(b_in)
    g = -jnp.exp(A_log.astype(F32)) * jax.nn.softplus(a_in + dt_bias.astype(F32))
    o, S_fin = _chunked_gated_delta(q, k, v, beta, g, s_state.astype(F32))
    o = o * lax.rsqrt(jnp.mean(o * o, axis=-1, keepdims=True) + RMS_EPS) * norm_w.astype(F32)
    o = o * jax.nn.silu(hs(z))
    return o.reshape(Bn, T, B_WIDTH), S_fin, new_conv


def _layer(x, s_rwkv, s_shift, s_delta, s_conv,
           ffn1_norm, ffn1_w_gate, ffn1_w_up, ffn1_w_down, mix_norm, w_in,
           rwkv_mu, rwkv_w0, rwkv_w2, rwkv_a0, rwkv_a2, rwkv_g2, rwkv_k_k, rwkv_k_a, rwkv_r_k,
           rwkv_lnx_w, rwkv_lnx_b, gdn_conv_w, gdn_A_log, gdn_dt_bias, gdn_norm_w,
           proj_a, proj_b, w_out, ffn2_norm, ffn2_w_gate, ffn2_w_up, ffn2_w_down):
    dt = x.dtype
    h = x + 0.5 * _swiglu(_rms(x, ffn1_norm), ffn1_w_gate, ffn1_w_up, ffn1_w_down)
    u = _rms(h, mix_norm)
    P = u @ w_in
    pa = P[..., :A_PROJ]
    pb = P[..., A_PROJ:A_PROJ + B_PROJ]
    gate_a = P[..., A_PROJ + B_PROJ:A_PROJ + B_PROJ + D_MODEL]
    gate_b = P[..., A_PROJ + B_PROJ + D_MODEL:]
    oa, rwkv_new, shift_new = _rwkv7(pa, s_shift, s_rwkv, rwkv_mu, rwkv_w0, rwkv_w2, rwkv_a0, rwkv_a2,
                                     rwkv_g2, rwkv_k_k, rwkv_k_a, rwkv_r_k, rwkv_lnx_w, rwkv_lnx_b)
    ob, delta_new, conv_new = _gated_delta_branch(pb, s_conv, s_delta, gdn_conv_w, gdn_A_log,
                                                  gdn_dt_bias, gdn_norm_w)
    merged = (jax.nn.sigmoid(gate_a) * (oa.astype(dt) @ proj_a)
              + jax.nn.sigmoid(gate_b) * (ob.astype(dt) @ proj_b))
    h = h + merged @ w_out
    h = h + 0.5 * _swiglu(_rms(h, ffn2_norm), ffn2_w_gate, ffn2_w_up, ffn2_w_down)
    return h, rwkv_new, shift_new, delta_new, conv_new


def setup_inputs(seed: int = 0) -> dict:
    key = jax.random.key(seed)
    ks = iter(jax.random.split(key, 48))
    L, D = DEPTH, D_MODEL

    def nrm(shape, scale):
        return jax.random.normal(next(ks), shape, F32) * scale

    inp = {}
    inp['x_prompt'] = nrm((BATCH, SEQ, D), 1.0)
    inp['x_sample'] = nrm((DEC_BATCH, DEC_SEQ, D), 1.0)
    inp['state_rwkv'] = nrm((L, DEC_BATCH, A_HEADS, A_HEAD, A_HEAD), 0.5)
    inp['state_rwkv_shift'] = nrm((L, DEC_BATCH, A_PROJ), 1.0)
    inp['state_delta'] = nrm((L, DEC_BATCH, B_HEADS, B_HEAD, B_HEAD), 0.1)
    inp['state_conv'] = nrm((L, DEC_BATCH, CONV_K - 1, CONV_CH), 1.0)
    inp['ffn1_norm'] = 1.0 + nrm((L, D), 0.05)
    inp['ffn1_w_gate'] = nrm((L, D, D_FF), D ** -0.5)
    inp['ffn1_w_up'] = nrm((L, D, D_FF), D ** -0.5)
    inp['ffn1_w_down'] = nrm((L, D_FF, D), D_FF ** -0.5)
    inp['mix_norm'] = 1.0 + nrm((L, D), 0.05)
    inp['w_in'] = nrm((L, D, IN_COLS), D ** -0.5)
    inp['rwkv_mu'] = jax.random.uniform(next(ks), (L, A_PROJ), F32)
    inp['rwkv_w0'] = -0.5 + nrm((L, A_WIDTH), 0.5)
    inp['rwkv_w2'] = nrm((L, A_RANK_W, A_WIDTH), A_RANK_W ** -0.5)
    inp['rwkv_a0'] = nrm((L, A_WIDTH), 0.1)
    inp['rwkv_a2'] = nrm((L, A_RANK_A, A_WIDTH), A_RANK_A ** -0.5)
    inp['rwkv_g2'] = nrm((L, A_RANK_G, A_WIDTH), A_RANK_G ** -0.5)
    inp['rwkv_k_k'] = 0.85 + nrm((L, A_WIDTH), 0.05)
    inp['rwkv_k_a'] = 1.0 + nrm((L, A_WIDTH), 0.05)
    inp['rwkv_r_k'] = nrm((L, A_HEADS, A_HEAD), 0.1)
    inp['rwkv_lnx_w'] = 1.0 + nrm((L, A_WIDTH), 0.05)
    inp['rwkv_lnx_b'] = nrm((L, A_WIDTH), 0.01)
    inp['gdn_conv_w'] = nrm((L, CONV_K, CONV_CH), CONV_K ** -0.5)
    inp['gdn_A_log'] = jnp.log(jax.random.uniform(next(ks), (L, B_HEADS), F32, 1.0, 16.0))
    inp['gdn_dt_bias'] = nrm((L, B_HEADS), 0.1)
    inp['gdn_norm_w'] = 1.0 + nrm((L, B_HEAD), 0.05)
    inp['proj_a'] = nrm((L, A_WIDTH, D), A_WIDTH ** -0.5)
    inp['proj_b'] = nrm((L, B_WIDTH, D), B_WIDTH ** -0.5)
    inp['w_out'] = nrm((L, D, D), D ** -0.5)
    inp['ffn2_norm'] = 1.0 + nrm((L, D), 0.05)
    inp['ffn2_w_gate'] = nrm((L, D, D_FF), D ** -0.5)
    inp['ffn2_w_up'] = nrm((L, D, D_FF), D ** -0.5)
    inp['ffn2_w_down'] = nrm((L, D_FF, D), D_FF ** -0.5)
    inp['final_norm'] = 1.0 + nrm((D,), 0.05)
    return inp


def reference(x_prompt, x_sample, state_rwkv, state_rwkv_shift, state_delta, state_conv,
              ffn1_norm, ffn1_w_gate, ffn1_w_up, ffn1_w_down, mix_norm, w_in,
              rwkv_mu, rwkv_w0, rwkv_w2, rwkv_a0, rwkv_a2, rwkv_g2, rwkv_k_k, rwkv_k_a, rwkv_r_k,
              rwkv_lnx_w, rwkv_lnx_b, gdn_conv_w, gdn_A_log, gdn_dt_bias, gdn_norm_w,
              proj_a, proj_b, w_out, ffn2_norm, ffn2_w_gate, ffn2_w_up, ffn2_w_down, final_norm):
    Bp = x_prompt.shape[0]
    dt = x_prompt.dtype
    zr = jnp.zeros((Bp, A_HEADS, A_HEAD, A_HEAD), dt)
    zs = jnp.zeros((Bp, A_PROJ), dt)
    zd = jnp.zeros((Bp, B_HEADS, B_HEAD, B_HEAD), dt)
    zc = jnp.zeros((Bp, CONV_K - 1, CONV_CH), dt)
    hp, hs = x_prompt, x_sample
    rp, sp, dp, cp = [], [], [], []
    rs, ss, ds, cs = [], [], [], []
    for l in range(DEPTH):
        lp = (ffn1_norm[l], ffn1_w_gate[l], ffn1_w_up[l], ffn1_w_down[l], mix_norm[l], w_in[l],
              rwkv_mu[l], rwkv_w0[l], rwkv_w2[l], rwkv_a0[l], rwkv_a2[l], rwkv_g2[l], rwkv_k_k[l],
              rwkv_k_a[l], rwkv_r_k[l], rwkv_lnx_w[l], rwkv_lnx_b[l], gdn_conv_w[l], gdn_A_log[l],
              gdn_dt_bias[l], gdn_norm_w[l], proj_a[l], proj_b[l], w_out[l],
              ffn2_norm[l], ffn2_w_gate[l], ffn2_w_up[l], ffn2_w_down[l])
        hp, r1, s1, d1, c1 = _layer(hp, zr, zs, zd, zc, *lp)
        hs, r2, s2, d2, c2 = _layer(hs, state_rwkv[l], state_rwkv_shift[l], state_delta[l], state_conv[l], *lp)
        rp.append(r1); sp.append(s1); dp.append(d1); cp.append(c1)
        rs.append(r2); ss.append(s2); ds.append(d2); cs.append(c2)
    y_prompt = _rms(hp, final_norm)
    y_sample = _rms(hs, final_norm)
    return (y_prompt, y_sample,
            jnp.stack(rp), jnp.stack(sp), jnp.stack(dp), jnp.stack(cp),
            jnp.stack(rs), jnp.stack(ss), jnp.stack(ds), jnp.stack(cs))
```

```python
from contextlib import ExitStack
import numpy as np
import concourse.bass as bass
import concourse.mybir as mybir
from concourse.bass_utils import run_bass_kernel_spmd

F32 = mybir.dt.float32
BF16 = mybir.dt.bfloat16
AF = mybir.ActivationFunctionType
ALU = mybir.AluOpType

NCORES = 8
DBG = {}
D = 1024
DFF = 2816
SEQ = 2048
NSQ = 16
LS = 4
NT = SEQ + NSQ * LS
A_PROJ = 1792
C0 = float(np.exp(-0.5))

COLS = {}
_nc = 0
for _name, _n in [("ffn1_norm", 8), ("mix_norm", 8), ("ffn2_norm", 8), ("final_norm", 8), ("mu", 14),
                  ("w0", 4), ("a0", 4), ("k_k", 4), ("k_a", 4), ("r_k", 4), ("lnx_w", 4), ("lnx_b", 4),
                  ("conv_w", 48), ("norm_w", 1), ("A_log", 4), ("dt_bias", 4)]:
    COLS[_name] = _nc
    _nc += _n
NCOL = _nc

CONST = {}
_k = 0
for _name, _n in [("ident", 128), ("incl", 128), ("strict", 128), ("tail", 128), ("incl_s", 128), ("strict_s", 128),
                  ("tail_s", 128), ("bones", 128), ("ones", 128), ("hmask", 256), ("rowseg", 16),
                  ("reset_s", 64)]:
    CONST[_name] = _k
    _k += _n
NCONST = _k


def make_consts2():
    sm = np.zeros((128, 16, 64), np.float32)
    for s in range(16):
        sm[:, s, s * LS:(s + 1) * LS] = 1
    return sm.reshape(128, 1024)


def make_consts():
    c = np.zeros((128, NCONST), np.float32)
    i = np.arange(128)
    seg = i // LS
    same = (seg[:, None] == seg[None, :])
    def put(name, a):
        c[:a.shape[0], CONST[name]:CONST[name] + a.shape[1]] = a
    put("ident", np.eye(128, dtype=np.float32))
    put("incl", (i[None, :] >= i[:, None]).astype(np.float32))
    put("strict", (i[None, :] > i[:, None]).astype(np.float32))
    put("tail", (i[:, None] > i[None, :]).astype(np.float32))
    put("incl_s", ((i[None, :] >= i[:, None]) & same).astype(np.float32))
    put("strict_s", ((i[None, :] > i[:, None]) & same).astype(np.float32))
    put("tail_s", ((i[:, None] > i[None, :]) & same).astype(np.float32))
    bo = np.zeros((128, 128), np.float32); bo[:64, :64] = 1; bo[64:, 64:] = 1
    put("bones", bo)
    put("ones", np.ones((128, 128), np.float32))
    hm = np.zeros((128, 256), np.float32); hm[:, 0:64] = 1; hm[:, 128 + 64:256] = 1
    put("hmask", hm)
    rs = np.zeros((128, 16), np.float32)
    for s in range(16):
        rs[s * LS:(s + 1) * LS, s] = 1
    put("rowseg", rs)
    r = np.ones((128, 64), np.float32); r[:, ::LS] = 0
    put("reset_s", r)
    return c


class K:
    def __init__(self, nc, es):
        self.nc = nc
        self.eng = {"pe": nc.tensor, "act": nc.scalar, "dve": nc.vector, "pool": nc.gpsimd, "sp": nc.sync}
        self.sem = {e: es.enter_context(nc.semaphore("sem_" + e)) for e in ("pe", "act", "dve", "pool")}
        self.cnt = {e: 0 for e in self.sem}
        self.ndma = 24
        self.dsem = [es.enter_context(nc.semaphore("dsem%d" % i)) for i in range(self.ndma)]
        self.dval = [0] * self.ndma
        self.di = {"sp": 0, "pool": 12}
        self.waited = {e: {} for e in self.eng}
        self.lastw = {}
        self.readers = {}
        self.nops = 0
        self._cap = None

    def begin(self):
        assert self._cap is None
        self._cap = []

    def end(self):
        c = self._cap
        self._cap = None
        return c

    def emit(self, items):
        for it in items:
            if it[0] == "op":
                self.op(*it[1:])
            else:
                self.dma(*it[1:])

    def emit_interleaved(self, a, b):
        na, nb = len(a), len(b)
        ia = ib = 0
        while ia < na or ib < nb:
            if ib >= nb or (ia < na and ia * nb <= ib * na):
                self.emit([a[ia]]); ia += 1
            else:
                self.emit([b[ib]]); ib += 1

    def _semh(self, key):
        return self.sem[key] if isinstance(key, str) else self.dsem[key[1]]

    def _deps(self, reads, writes):
        toks = []
        for k in reads:
            if k in self.lastw:
                toks.append(self.lastw[k])
        for k in writes:
            if k in self.lastw:
                toks.append(self.lastw[k])
            toks.extend(self.readers.get(k, ()))
        return toks

    def _wait(self, en, toks):
        need = {}
        for (sk, v) in toks:
            if sk == "pe" and en == "pe":
                continue
            if v > need.get(sk, 0):
                need[sk] = v
        w = self.waited[en]
        for sk, v in need.items():
            if w.get(sk, 0) < v:
                self.eng[en].wait_ge(self._semh(sk), v)
                w[sk] = v

    def _record(self, tok, reads, writes):
        for k in reads:
            self.readers.setdefault(k, []).append(tok)
        for k in writes:
            self.lastw[k] = tok
            self.readers[k] = []

    @staticmethod
    def _excl(reads, writes):
        pr = [x for x in reads if isinstance(x, tuple) and x[0] == "ps"]
        if pr:
            reads = [x for x in reads if x not in pr]
            writes = list(writes) + [x for x in pr if x not in writes]
        return reads, writes

    def op(self, en, fn, reads=(), writes=()):
        if self._cap is not None:
            self._cap.append(("op", en, fn, list(reads), list(writes)))
            return
        reads, writes = self._excl(reads, writes)
        self._wait(en, self._deps(reads, writes))
        ins = fn(self.eng[en])
        self.cnt[en] += 1
        ins.then_inc(self.sem[en], 1)
        self._record((en, self.cnt[en]), reads, writes)
        self.nops += 1

    def dma(self, q, out, in_, reads=(), writes=()):
        if self._cap is not None:
            self._cap.append(("dma", q, out, in_, list(reads), list(writes)))
            return
        toks = self._deps(reads, writes)
        i = self.di[q]
        base = 0 if q == "sp" else 12
        self.di[q] = base + (i - base + 1) % 12
        if self.dval[i] > 0:
            toks.append((("dma", i), self.dval[i]))
        self._wait(q, toks)
        ins = self.eng[q].dma_start(out=out, in_=in_)
        self.dval[i] += 16
        ins.then_inc(self.dsem[i], 16)
        self._record((("dma", i), self.dval[i]), reads, writes)
        self.nops += 1

    def barrier(self):
        toks = [(e, self.cnt[e]) for e in self.cnt if self.cnt[e] > 0]
        toks += [(("dma", i), self.dval[i]) for i in range(self.ndma) if self.dval[i] > 0]
        for en in self.eng:
            self._wait(en, toks)

    def finish(self):
        toks = [(("dma", i), self.dval[i]) for i in range(self.ndma) if self.dval[i] > 0]
        toks += [(e, self.cnt[e]) for e in self.cnt if self.cnt[e] > 0]
        self._wait("sp", toks)


def build(stage=99):
    nc = bass.Bass("TRN2", target_bir_lowering=False)
    es = ExitStack()

    def din(name, shape):
        return nc.dram_tensor(name, list(shape), F32, kind="ExternalInput").ap()

    def dout(name, shape):
        return nc.dram_tensor(name, list(shape), F32, kind="ExternalOutput").ap()

    x_d = din("x", [NT, D])
    srw_d = din("s_rwkv", [NSQ * 512, 64])
    ssh_d = din("s_shift", [NSQ, A_PROJ])
    sdl_d = din("s_delta", [NSQ * 512, 128])
    scv_d = din("s_conv", [NSQ * 3, 1536])
    w1g_d = din("w1g", [D, DFF]); w1u_d = din("w1u", [D, DFF]); w1d_d = din("w1d", [DFF, D])
    w2g_d = din("w2g", [D, DFF]); w2u_d = din("w2u", [D, DFF]); w2d_d = din("w2d", [DFF, D])
    win_d = din("w_in", [D, 5896])
    pja_d = din("proj_a", [512, D]); pjb_d = din("proj_b", [512, D]); wo_d = din("w_out", [D, D])
    lw2_d = din("lw2", [64, 512]); la2_d = din("la2", [64, 512]); lg2_d = din("lg2", [128, 512])
    cols_d = din("cols", [128, NCOL]); consts_d = din("consts", [128, NCONST]); consts2_d = din("consts2", [128, 1024])

    winb_d = nc.dram_tensor("w_in_bf", [D, 5896], BF16, kind="Internal").ap()
    pjab_d = nc.dram_tensor("proj_a_bf", [512, D], BF16, kind="Internal").ap()
    pjbb_d = nc.dram_tensor("proj_b_bf", [512, D], BF16, kind="Internal").ap()
    wob_d = nc.dram_tensor("w_out_bf", [D, D], BF16, kind="Internal").ap()
    y_d = dout("y", [NT, D])
    rwp_d = dout("rwkv_p", [512, 64]); dlp_d = dout("delta_p", [512, 128])
    rws_d = dout("rwkv_s", [NSQ * 512, 64]); dls_d = dout("delta_s", [NSQ * 512, 128])
    trp_d = dout("tokrows_p", [4, 3840]); trs_d = dout("tokrows_s", [NSQ * LS, 3840])

    with es:
        k = K(nc, es)

        def sb(name, shape, dt=F32, stack=es):
            return stack.enter_context(nc.sbuf_tensor("sb_" + name, list(shape), dt))

        ps = [es.enter_context(nc.psum_tensor("ps%d" % i, [128, 512], F32)) for i in range(8)]
        psi = [0, 0]
        psmode = [0]

        def nps():
            if psmode[0] == 0:
                i = psi[0]
                psi[0] = (i + 1) % 8
                return i
            if psmode[0] == 1:
                i = psi[0] % 4
                psi[0] = (i + 1) % 4
                return i
            i = psi[1]
            psi[1] = (i + 1) % 4
            return 4 + i

        h = sb("h", [128, 8, NT])
        cols = sb("cols", [128, NCOL])
        cst = sb("cst", [128, NCONST])
        cb = sb("cb", [128, 5 * 128], BF16)
        pending_casts = []
        for r8 in range(8):
            pending_casts.append(lambda r8=r8: k.dma("pool", winb_d[r8 * 128:(r8 + 1) * 128, :], win_d[r8 * 128:(r8 + 1) * 128, :], writes=[("winb", r8)]))
        pending_casts.append(lambda: k.dma("pool", pjab_d[:, :], pja_d[:, :], writes=["pjab"]))
        pending_casts.append(lambda: k.dma("pool", pjbb_d[:, :], pjb_d[:, :], writes=["pjbb"]))
        for r2 in range(2):
            pending_casts.append(lambda r2=r2: k.dma("pool", wob_d[r2 * 512:(r2 + 1) * 512, :], wo_d[r2 * 512:(r2 + 1) * 512, :], writes=[("wob", r2)]))

        def one_cast():
            if pending_casts:
                pending_casts.pop(0)()
        k.dma("sp", cols[:], cols_d[:, :], writes=["cols"])
        k.dma("sp", cst[:], consts_d[:, :], writes=["cst"])

        def cc(name, n=128, rows=128):
            o = CONST[name]
            return cst[:rows, o:o + n]

        def col(name, j=0):
            o = COLS[name] + j
            return cols[:, o:o + 1]

        for j, nm in enumerate(["ident", "ones", "bones"]):
            k.op("dve", lambda e, j=j, nm=nm: e.tensor_copy(out=cb[:, j * 128:(j + 1) * 128], in_=cc(nm)),
                 reads=["cst"], writes=["cb"])
        ident_f = cc("ident")
        ones_b = cb[:, 128:256]
        bones_b = cb[:, 256:384]

        TILES = [(t * 512, 512, 1, 512) for t in range(4)] + [(SEQ, NSQ * LS, NSQ, LS)]

        with ExitStack() as st:
            xin = [sb("xin%d" % i, [128, D], F32, st) for i in range(2)]
            for blk in range(17):
                rows = 128 if blk < 16 else 64
                xb = xin[blk % 2]
                k.dma("sp", xb[:rows, :], x_d[blk * 128: blk * 128 + rows, :], writes=[("xin", blk % 2)])
                for half in range(2):
                    b = nps()
                    for m4 in range(4):
                        m = half * 4 + m4
                        k.op("pe", lambda e, b=b, m4=m4, m=m, xb=xb, rows=rows: e.transpose(
                            ps[b][:, m4 * 128: m4 * 128 + rows], xb[:rows, m * 128:(m + 1) * 128], ident_f[:rows, :rows]),
                            reads=[("xin", blk % 2), "cst"], writes=[("ps", b)])
                    k.op("act", lambda e, b=b, half=half, blk=blk, rows=rows: e.activation(
                        out=h[:, half * 4:(half + 1) * 4, blk * 128: blk * 128 + rows],
                        in_=ps[b][:].rearrange("p (a c) -> p a c", a=4)[:, :, :rows], func=AF.Copy),
                        reads=[("ps", b)], writes=[("h", blk)])
            k.barrier()

        def rms_stats(c0, N, rstd, tmp_sq, hkeys, rkey="rstd", sqkeys=(("sq", 0), ("sq", 1))):
            b = nps()
            for m in range(8):
                sq = tmp_sq[m % 2]
                k.op("act", lambda e, sq=sq, m=m: e.activation(out=sq[:, :N], in_=h[:, m, c0:c0 + N], func=AF.Square),
                     reads=hkeys, writes=[sqkeys[m % 2]])
                k.op("pe", lambda e, sq=sq, m=m, b=b: e.matmul(ps[b][:, :N], ones_b, sq[:, :N], start=(m == 0), stop=(m == 7)),
                     reads=[sqkeys[m % 2], "cb"], writes=[("ps", b)])
            k.op("act", lambda e: e.activation(out=rstd[:, :N], in_=ps[b][:, :N], func=AF.Sqrt, bias=1e-6, scale=1.0 / D),
                 reads=[("ps", b)], writes=[rkey])
            k.op("dve", lambda e: e.reciprocal(out=rstd[:, :N], in_=rstd[:, :N]), reads=[rkey], writes=[rkey])

        def hkeys_of(c0, N):
            return [("h", bk) for bk in range(c0 // 128, (c0 + N + 127) // 128)]

        def ffn(wg_d, wu_d, wd_d, normname, tag):
            with ExitStack() as st:
                u = sb("u" + tag, [128, 8, NT], BF16, st)
                hid = sb("hid" + tag, [128, 11, NT], BF16, st)
                WG = [sb("wg%d" % i + tag, [128, 8, 512], BF16, st) for i in range(2)]
                WU = [sb("wu%d" % i + tag, [128, 8, 512], BF16, st) for i in range(2)]
                WD = [sb("wd%d" % i + tag, [128, 11, 128], BF16, st) for i in range(2)]
                rstd = sb("rstd" + tag, [128, 512], F32, st)
                sq = [sb("sq%d" % i + tag, [128, 512], BF16, st) for i in range(2)]
                sg = [sb("sg%d" % i + tag, [128, 512], F32, st) for i in range(2)]
                def norm_tile(ti):
                    c0, N = TILES[ti][0], TILES[ti][1]
                    rms_stats(c0, N, rstd, sq, hkeys_of(c0, N))
                    for m in range(8):
                        k.op("dve", lambda e, m=m: e.scalar_tensor_tensor(
                            out=u[:, m, c0:c0 + N], in0=h[:, m, c0:c0 + N], scalar=col(normname, m), in1=rstd[:, :N],
                            op0=ALU.mult, op1=ALU.mult), reads=hkeys_of(c0, N) + ["rstd", "cols"], writes=[("u", ti)])
                norm_tile(0)
                gi = 0
                di = 0
                sgi = 0
                for half in range(2):
                    groups = [(0, 4), (4, 4), (8, 3)]
                    for (j0, nj) in groups:
                        ff0 = (half * 11 + j0) * 128
                        wgt, wut = WG[gi % 2], WU[gi % 2]
                        k.dma("pool", wgt[:, :, :nj * 128], wg_d[:, ff0:ff0 + nj * 128].rearrange("(kc p) c -> p kc c", p=128),
                              writes=[("wg", gi % 2)])
                        k.dma("pool", wut[:, :, :nj * 128], wu_d[:, ff0:ff0 + nj * 128].rearrange("(kc p) c -> p kc c", p=128),
                              writes=[("wu", gi % 2)])
                        one_cast()
                        for ti, (c0, N, _, _) in enumerate(TILES):
                            if gi == 0 and ti + 1 < len(TILES):
                                norm_tile(ti + 1)
                            for j in range(nj):
                                b1 = nps(); b2 = nps()
                                def mm(e, wt, b, j=j):
                                    ins = None
                                    for kc in range(8):
                                        ins = e.matmul(ps[b][:, :N], wt[:, kc, j * 128:(j + 1) * 128], u[:, kc, c0:c0 + N],
                                                       start=(kc == 0), stop=(kc == 7))
                                    return ins
                                k.op("pe", lambda e, mm=mm, wgt=wgt, b1=b1: mm(e, wgt, b1),
                                     reads=[("wg", gi % 2), ("u", ti)], writes=[("ps", b1)])
                                k.op("pe", lambda e, mm=mm, wut=wut, b2=b2: mm(e, wut, b2),
                                     reads=[("wu", gi % 2), ("u", ti)], writes=[("ps", b2)])
                                sgt = sg[sgi % 2]
                                k.op("act", lambda e, sgt=sgt, b1=b1: e.activation(out=sgt[:, :N], in_=ps[b1][:, :N], func=AF.Silu),
                                     reads=[("ps", b1)], writes=[("sg", sgi % 2)])
                                k.op("dve", lambda e, sgt=sgt, b2=b2, jj=j0 + j: e.tensor_tensor(
                                    out=hid[:, jj, c0:c0 + N], in0=sgt[:, :N], in1=ps[b2][:, :N], op=ALU.mult),
                                    reads=[("sg", sgi % 2), ("ps", b2)], writes=[("hid", ti, j0 + j)])
                                sgi += 1
                        gi += 1
                    for piece in range(8):
                        wdt = WD[di % 2]
                        r0 = half * 11 * 128
                        k.dma("pool", wdt[:, :, :], wd_d[r0:r0 + 11 * 128, piece * 128:(piece + 1) * 128].rearrange("(kc p) c -> p kc c", p=128),
                              writes=[("wd", di % 2)])
                        one_cast()
                        for ti, (c0, N, _, _) in enumerate(TILES):
                            for mm_ in range(1):
                                m = piece
                                b = nps()
                                def mmd(e, wdt=wdt, b=b, mm_=mm_, c0=c0, N=N):
                                    ins = None
                                    for kc in range(11):
                                        ins = e.matmul(ps[b][:, :N], wdt[:, kc, mm_ * 128:(mm_ + 1) * 128], hid[:, kc, c0:c0 + N],
                                                       start=(kc == 0), stop=(kc == 10))
                                    return ins
                                k.op("pe", mmd, reads=[("wd", di % 2)] + [("hid", ti, jj) for jj in range(11)], writes=[("ps", b)])
                                k.op("dve", lambda e, b=b, m=m, c0=c0, N=N: e.scalar_tensor_tensor(
                                    out=h[:, m, c0:c0 + N], in0=ps[b][:, :N], scalar=0.5, in1=h[:, m, c0:c0 + N],
                                    op0=ALU.mult, op1=ALU.add), reads=[("ps", b)] + hkeys_of(c0, N), writes=hkeys_of(c0, N))
                        di += 1
            k.barrier()

        if stage >= 1:
            ffn(w1g_d, w1u_d, w1d_d, "ffn1_norm", "a")
        def act(out, in_, func, r, w, **kw):
            k.op("act", lambda e: e.activation(out=out, in_=in_, func=func, **kw), reads=r, writes=w)

        def tt(en, out, a, b, op, r, w):
            k.op(en, lambda e: e.tensor_tensor(out=out, in0=a, in1=b, op=op), reads=r, writes=w)

        def ts(en, out, a, s1, s2, op0, op1, r, w):
            if s2 is None:
                k.op(en, lambda e: e.tensor_scalar(out, a, s1, None, op0), reads=r, writes=w)
            else:
                k.op(en, lambda e: e.tensor_scalar(out, a, s1, s2, op0, op1), reads=r, writes=w)

        def stt(en, out, in0, sc, in1, op0, op1, r, w):
            k.op(en, lambda e: e.scalar_tensor_tensor(out=out, in0=in0, scalar=sc, in1=in1, op0=op0, op1=op1),
                 reads=r, writes=w)

        def mm(out, lhsT, rhs, r, w, start=True, stop=True):
            k.op("pe", lambda e: e.matmul(out, lhsT, rhs, start=start, stop=stop), reads=r, writes=w)

        def mmg(items, r, w):
            def f(e):
                ins = None
                n = len(items)
                for i, (o, l, rh) in enumerate(items):
                    ins = e.matmul(o, l, rh, start=(i == 0), stop=(i == n - 1))
                return ins
            k.op("pe", f, reads=r, writes=w)

        def tr(out, in_, rows, r, w):
            k.op("pe", lambda e: e.transpose(out, in_, ident_f[:rows, :rows]), reads=r + ["cst"], writes=w)

        def copy(en, out, in_, r, w):
            if en == "act":
                act(out, in_, AF.Copy, r, w)
            else:
                k.op(en, lambda e: e.tensor_copy(out=out, in_=in_), reads=r, writes=w)

        def mixer():
            with ExitStack() as st:
                def T(name, shape, dt=F32, stack=st):
                    return sb(name, shape, dt, stack)
                lora_wa = T("lora_wa", [128, 512], BF16)
                lg2 = T("lg2s", [128, 512], BF16)
                wab = T("wab", [128, 8, 8], BF16)
                pjbuf = [T("pjbuf%d" % i, [128, 2, 4, 128], BF16) for i in range(2)]
                WI = [T("wi%d" % i, [128, 8, 256], BF16) for i in range(2)]
                u_t = T("u_t", [128, 8, 512], BF16)
                oab = T("oab", [128, 8, 512], BF16)
                rowst = T("rowst", [128, 256])
                halo_a = T("halo_a", [128, 14]); halo_c = T("halo_c", [128, 12, 3])
                sshT = T("sshT", [128, 14, 16]); scvT = T("scvT", [128, 12, 48])
                Zr = T("Zr", [128, 4, 128]); Zrb = T("Zrb", [128, 4, 2, 128], BF16)
                Zg = T("Zg", [128, 4, 128]); Zgb = T("Zgb", [128, 4, 2, 128], BF16)
                omka = T("omka", [128, 4]); eA = T("eA", [128, 4]); dtb = T("dtb", [128, 4]); omu = T("omu", [128, 14])
                dc = T("dc", [128, 16])
                k.dma("pool", lora_wa[0:64, :], lw2_d[:, :], writes=["lora_wa"])
                k.dma("pool", lora_wa[64:128, :], la2_d[:, :], writes=["lora_wa"])
                k.dma("pool", lg2[:], lg2_d[:, :], writes=["lg2"])
                k.dma("sp", wab[:], winb_d[:, 5888:5896].rearrange("(kc p) c -> p kc c", p=128), reads=[("winb", r_) for r_ in range(8)], writes=["wab"])
                for t_, nm in [(halo_a, "halo_a"), (halo_c, "halo_c")]:
                    k.op("pool", lambda e, t_=t_: e.memset(t_[:], 0.0), writes=[nm])
                for t_, nm in [(Zr, "Zr"), (Zg, "Zg")]:
                    k.op("pool", lambda e, t_=t_: e.memset(t_[:], 0.0), writes=[(nm, j) for j in range(4)])
                for t_, nm in [(Zrb, "Zrb"), (Zgb, "Zgb")]:
                    k.op("pool", lambda e, t_=t_: e.memset(t_[:], 0.0), writes=[(nm, j, p_) for j in range(4) for p_ in range(2)])
                ts("dve", omka[:], cols[:, COLS["k_a"]:COLS["k_a"] + 4], -1.0, 1.0, ALU.mult, ALU.add, ["cols"], ["omka"])
                act(eA[:], cols[:, COLS["A_log"]:COLS["A_log"] + 4], AF.Exp, ["cols"], ["eA"])
                ts("dve", omu[:], cols[:, COLS["mu"]:COLS["mu"] + 14], -1.0, 1.0, ALU.mult, ALU.add, ["cols"], ["omu"])
                copy("dve", dtb[:], cols[:, COLS["dt_bias"]:COLS["dt_bias"] + 4], ["cols"], ["dtb"])
                with ExitStack() as s0:
                    ld1 = T("ld1", [16, A_PROJ], F32, s0); ld2 = T("ld2", [48, 1536], F32, s0)
                    k.dma("sp", ld1[:], ssh_d[:, :], writes=["ld1"])
                    k.dma("sp", ld2[:], scv_d[:, :], writes=["ld2"])
                    b = nps()
                    for c in range(14):
                        tr(ps[b][:, c * 16:(c + 1) * 16], ld1[:, c * 128:(c + 1) * 128], 16, ["ld1"], [("ps", b)])
                    copy("act", sshT[:].rearrange("p a s -> p (a s)"), ps[b][:, :224], [("ps", b)], ["sshT"])
                    for hf in range(2):
                        b = nps()
                        for c6 in range(6):
                            c = hf * 6 + c6
                            tr(ps[b][:, c6 * 48:(c6 + 1) * 48], ld2[:, c * 128:(c + 1) * 128], 48, ["ld2"], [("ps", b)])
                        copy("act", scvT[:, hf * 6:(hf + 1) * 6, :].rearrange("p a s -> p (a s)"), ps[b][:, :288],
                             [("ps", b)], ["scvT"])
                    k.barrier()

                bones_f = cc("bones"); ones_f = cc("ones"); ident_b = cb[:, 0:128]

                for ti, (c0, N, nseq, L) in enumerate(TILES):
                    if ti not in DBG.get('tiles', range(5)):
                        continue
                    samp = nseq > 1
                    C = 64 if samp else 128
                    nch = 1 if samp else 4
                    nS = NSQ if samp else 1
                    nsg, lsg = (NSQ, LS) if samp else (4, 128)
                    nlev = 1 if samp else 6
                    mI = cc("incl_s" if samp else "incl")[:C, :C]
                    mS = cc("strict_s" if samp else "strict")[:C, :C]
                    mT = cc("tail_s" if samp else "tail")[:C, :C]
                    hk = hkeys_of(c0, N)
                    with ExitStack() as ts_:
                        def TT(name, shape, dt=F32):
                            return sb(name + "_%d" % ti, shape, dt, ts_)
                        tmp = [TT("tmp%d" % i, [128, N]) for i in range(12)]
                        tk = ["tmp%d" % i for i in range(12)]
                        pa_c = TT("pa_c", [128, N + 3 * nseq]); pa_c2 = TT("pa_c2", [128, N + 3 * nseq])
                        r_t = TT("r_t", [128, N]); k_t = TT("k_t", [128, N]); v_t = TT("v_t", [128, N]); o_t = TT("o_t", [128, N])
                        z_t = TT("z_t", [128, N])
                        lora_in = TT("lora_in", [128, N], BF16); sig_gd = TT("sig_gd", [128, N], BF16)
                        sqb = TT("sqb", [128, N], BF16); bnb = TT("bnb", [128, N], BF16); sqp = TT("sqp", [128, N], BF16)
                        atB = TT("atB", [128, N], BF16); btB = TT("btB", [128, N], BF16); ktB = TT("ktB", [128, N], BF16); rtB = TT("rtB", [128, N], BF16)
                        gtB = TT("gtB", [128, N]); bnB = TT("bnB", [128, N]); dcB = TT("dcB", [128, 16])
                        at = TT("at", [128, N], BF16); bt = TT("bt", [128, N], BF16); kt = TT("kt", [128, N], BF16)
                        rt = TT("rt", [128, N], BF16)
                        atz = TT("atz", [128, 2, N], BF16); btz = TT("btz", [128, 2, N], BF16); rtz = TT("rtz", [128, 2, N], BF16)
                        tok = TT("tok", [128, nch, 5, 128], BF16); Vz = TT("Vz", [128, nch, 2, 128], BF16)
                        cm1 = [TT("cm1_%d" % i, [128, 4, 128], BF16) for i in range(nch)]
                        nt2 = [TT("nt2_%d" % i, [128, 2, 128], BF16) for i in range(nch)]
                        cm3 = [TT("cm3_%d" % i, [128, 4, 128], BF16) for i in range(nch)]
                        Tt = [TT("Tt%d" % i, [128, 2, 128], BF16) for i in range(nch)]
                        PQ = [TT("PQ%d" % i, [128, 4, 128], BF16) for i in range(nch)]
                        Xb = TT("Xb", [128, 128], BF16); Wz = TT("Wz", [128, 2, 128], BF16); Wb = TT("Wb", [128, 128], BF16)
                        gtok = TT("gtok", [128, nch, 4]); btok = TT("btok", [128, nch, 4]); sptmp = TT("sptmp", [128, nch, 4])
                        if samp:
                            g_rep, b_rep, Gs, eg, ek, dmi, dms = [[TT(nm_, [128, 128])] for nm_ in ("g_rep", "b_rep", "Gs", "eg", "ek", "dmi", "dms")]
                            segm_t = TT("segm", [128, 1024])
                            k.dma("sp", segm_t[:], consts2_d[:, :], writes=["segm"])
                            segm = segm_t[:, :].rearrange("p (s i) -> p s i", s=NSQ)
                        else:
                            g_rep, b_rep, Gs, eg, ek, dmi, dms = [[tmp[5 + j_][:, q_ * 128:(q_ + 1) * 128] for q_ in range(4)] for j_ in range(7)]
                        dmt_t = TT("dmt", [128, nch, 128]); kbf_t = TT("kbf", [128, nch, 128]); kbb_t = TT("kbb", [128, nch, 128], BF16)
                        dmt = [dmt_t[:, q_, :] for q_ in range(nch)]; kbf = [kbf_t[:, q_, :] for q_ in range(nch)]
                        kbb = [kbb_t[:, q_, :] for q_ in range(nch)]
                        merged = None
                        if samp:
                            Zs = TT("Zs", [128, NSQ, 128]); Zsb = TT("Zsb", [128, NSQ, 128], BF16)
                            bigf = TT("bigf", [128, NSQ, 128]); stg = TT("stg", [128, NSQ, 64])
                            sbd = [TT("sbd%d" % i, [128, NSQ, 128]) for i in range(2)]
                            atm = TT("atm", [128, NSQ, 64], BF16); rtm = TT("rtm", [128, NSQ, 64], BF16)
                            Wm = TT("Wm", [128, NSQ, 128], BF16); Vm = TT("Vm", [128, NSQ, 128], BF16)
                        else:
                            bigf = TT("bigf", [128, 1, 128])

                        ATs = [(at, bt, kt, rt), (atB, btB, ktB, rtB)]
                        GTs = [(tmp[2], tk[2]), (gtB, "gtB")]
                        BNs = [(tmp[11], tk[11]), (bnB, "bnB")]
                        DCs = [dc, dcB]

                        def v3(ap):
                            return ap.rearrange("p (s l) -> p s l", s=nseq)

                        def sg3(ap):
                            return ap.rearrange("p (s l) -> p s l", s=nsg)

                        rms_stats(c0, N, tmp[0], [at, bt], hk, rkey=tk[0], sqkeys=("at", "bt"))
                        for m in range(8):
                            stt("dve", u_t[:, m, :N], h[:, m, c0:c0 + N], col("mix_norm", m), tmp[0][:, :N], ALU.mult, ALU.mult,
                                hk + [tk[0], "cols"], ["u_t"])

                        need_rows = (ti >= 3)
                        M_rows = 64 if samp else 4
                        rows_lo = 0 if samp else 508
                        state = {"g": -1, "issued": set()}

                        def issue_load(g):
                            if g > 22 or g in state["issued"]:
                                return
                            state["issued"].add(g)
                            wt = WI[g % 2]
                            ncols = 256
                            k.dma("sp", wt[:, :, :ncols], winb_d[:, g * 256: g * 256 + ncols].rearrange("(kc p) c -> p kc c", p=128),
                                  reads=[("winb", r_) for r_ in range(8)], writes=[("wi", g % 2)])

                        def load_group(g):
                            wt = WI[g % 2]
                            issue_load(g)
                            if need_rows and g < 15:
                                b = nps()
                                mmg([(ps[b][:M_rows, :256], u_t[:, kc, rows_lo:rows_lo + M_rows], wt[:, kc, :]) for kc in range(8)],
                                    ["u_t", ("wi", g % 2)], [("ps", b)])
                                copy("act", rowst[:M_rows, :], ps[b][:M_rows, :256], [("ps", b)], ["rowst"])
                                if samp:
                                    k.dma("sp", trs_d[:, g * 256:(g + 1) * 256], rowst[:64, :], reads=["rowst"])
                                else:
                                    k.dma("sp", trp_d[:, g * 256:(g + 1) * 256], rowst[:4, :], reads=["rowst"])

                        def pchunk(c):
                            g = c // 2
                            if g != state["g"]:
                                load_group(g)
                                state["g"] = g
                                issue_load(g + 1)
                            wt = WI[g % 2]
                            off = (c % 2) * 128
                            b = nps()
                            mmg([(ps[b][:, :N], wt[:, kc, off:off + 128], u_t[:, kc, :N]) for kc in range(8)],
                                ["u_t", ("wi", g % 2)], [("ps", b)])
                            return b

                        def a_chunk(c, dest, dkey):
                            b = pchunk(c)
                            pab, pak = ((pa_c, "pa_c"), (pa_c2, "pa_c2"))[c % 2]
                            pv = pab[:, :N + nseq].rearrange("p (s l) -> p s l", s=nseq)
                            copy("act", pv[:, :, 1:], v3(ps[b][:, :N]), [("ps", b)], [pak])
                            act(v3(dest), v3(ps[b][:, :N]), AF.Copy, [("ps", b), "omu"], [dkey], scale=omu[:, c:c + 1])
                            if samp:
                                copy("pool", pv[:, :, 0], sshT[:, c, :], ["sshT"], [pak])
                            else:
                                copy("pool", pv[:, :, 0], halo_a[:, c:c + 1], ["halo_a"], [pak])
                                copy("pool", halo_a[:, c:c + 1], pv[:, :, L], [pak], ["halo_a"])
                            stt("dve", v3(dest), pv[:, :, 0:L], col("mu", c), v3(dest), ALU.mult, ALU.add,
                                [pak, "cols", dkey], [dkey])

                        def g_load(hh, role, dest, dk):
                            cq = role * 4 + hh
                            b = pchunk(14 + hh * 4 + role)
                            pab, pak = ((pa_c, "pa_c"), (pa_c2, "pa_c2"))[role % 2]
                            xv3 = pab[:, :N + 3 * nseq].rearrange("p (s l) -> p s l", s=nseq)
                            copy("act", xv3[:, :, 3:], v3(ps[b][:, :N]), [("ps", b)], [pak])
                            if samp:
                                copy("pool", xv3[:, :, 0:3], scvT[:, cq, :].rearrange("p (s i) -> p s i", s=NSQ), ["scvT"], [pak])
                            else:
                                copy("pool", xv3[:, :, 0:3], halo_c[:, cq:cq + 1, :], ["halo_c"], [pak])
                                copy("pool", halo_c[:, cq:cq + 1, :], xv3[:, :, L:L + 3], [pak], ["halo_c"])

                        def g_conv(hh, role, dest, dk):
                            cq = role * 4 + hh
                            pab, pak = ((pa_c, "pa_c"), (pa_c2, "pa_c2"))[role % 2]
                            xv3 = pab[:, :N + 3 * nseq].rearrange("p (s l) -> p s l", s=nseq)
                            ts("dve", v3(dest[:]), xv3[:, :, 0:L], col("conv_w", 0 * 12 + cq), None, ALU.mult, None, [pak, "cols"], [dk])
                            for i in range(1, 4):
                                stt("dve", v3(dest[:]), xv3[:, :, i:i + L], col("conv_w", i * 12 + cq), v3(dest[:]),
                                    ALU.mult, ALU.add, [pak, "cols", dk], [dk])

                        def g_silu(dest, dk):
                            act(dest[:], dest[:], AF.Silu, [dk], [dk])


                        def gdn_prefetch(hh):
                            g_load(hh, 0, r_t, "r_t"); g_load(hh, 1, k_t, "k_t")
                            g_conv(hh, 0, r_t, "r_t")
                            g_load(hh, 2, v_t, "v_t")
                            g_conv(hh, 1, k_t, "k_t")
                            g_silu(r_t, "r_t")
                            g_conv(hh, 2, v_t, "v_t")
                            g_silu(k_t, "k_t"); g_silu(v_t, "v_t")

                        issue_load(0)
                        if samp:
                            srv = srw_d.rearrange("(s h v) k -> h v s k", s=NSQ, h=8)
                            sdv = sdl_d.rearrange("(s h k) v -> h k s v", s=NSQ, h=4)

                            def load_rwkv_state(u_):
                                t_ = sbd[u_ % 2]; tkey = ("sbd", u_ % 2)
                                k.dma("sp", t_[0:64, :, 0:64], srv[2 * u_], writes=[tkey])
                                k.dma("sp", t_[64:128, :, 64:128], srv[2 * u_ + 1], writes=[tkey])

                            def load_gdn_state(h_):
                                k.dma("sp", sbd[h_ % 2][:], sdv[h_], writes=[("sbd", h_ % 2)])

                            for i_ in range(2):
                                k.op("pool", lambda e, i_=i_: e.memset(sbd[i_][:], 0.0), writes=[("sbd", i_)])
                            load_rwkv_state(0)
                        a_chunk(0, tmp[1][:], tk[1])
                        act(lora_in[0:64, :], tmp[1][0:64, :], AF.Tanh, [tk[1]], ["lora_in"])
                        copy("pool", lora_in[64:128, :], tmp[1][64:128, :], [tk[1]], ["lora_in"])
                        a_chunk(1, tmp[1][:], tk[1])
                        act(sig_gd[:], tmp[1][:], AF.Sigmoid, [tk[1]], ["sig_gd"])
                        pre_ops, fin_ops, chunk_ops, gpf_ops = [], [], [], []
                        for u in range(DBG.get('nu', 4)):
                            p_ = u % 2
                            at_, bt_, kt_, rt_ = ATs[p_]
                            kat, kbt, kkt, krt = ["%s%d" % (n_, p_) for n_ in ("at", "bt", "kt", "rt")] if p_ else ["at", "bt", "kt", "rt"]
                            dc_, kdc = DCs[p_], "dc%d" % p_
                            psmode[0] = 2 if u > 0 else 0
                            k.begin()
                            a_chunk(2 + 3 * u, r_t[:], "r_t"); a_chunk(3 + 3 * u, k_t[:], "k_t"); a_chunk(4 + 3 * u, v_t[:], "v_t")
                            if samp and u > 0:
                                load_rwkv_state(u)
                            ucol = slice(u * 128, (u + 1) * 128)
                            sigw, a_t, g_t, kk, kkn, kmod, b_t, cl, e_t, Bh, Kh, bonus = tmp
                            ksw, ka, kg, kkk, kkkn, kkmod, kb_, kcl, ke, kBh, kKh, kbon = tk
                            g_t, kg = GTs[p_]
                            bonus, kbon = BNs[p_]
                            e1, e2, e3 = tmp[3], Bh, Kh
                            ke1, ke2, ke3 = tk[3], kBh, kKh
                            act(sqb[:], k_t[:], AF.Square, ["k_t", "cols"], ["sqb"], scale=col("k_k", u))
                            bw = nps()
                            mm(ps[bw][:, :N], lora_wa[0:64, ucol], lora_in[0:64, :], ["lora_wa", "lora_in"], [("ps", bw)])
                            ba = nps()
                            mm(ps[ba][:, :N], lora_wa[64:128, ucol], lora_in[64:128, :], ["lora_wa", "lora_in"], [("ps", ba)])
                            bg = nps()
                            mm(ps[bg][:, :N], lg2[:, ucol], sig_gd[:], ["lg2", "sig_gd"], [("ps", bg)])
                            bn = nps()
                            mm(ps[bn][:, :N], bones_b, sqb[:], ["sqb", "cb"], [("ps", bn)])
                            act(sigw[:], ps[bw][:, :N], AF.Sigmoid, [("ps", bw), "cols"], [ksw], bias=col("w0", u))
                            act(a_t[:], ps[ba][:, :N], AF.Sigmoid, [("ps", ba), "cols"], [ka], bias=col("a0", u))
                            copy("act", g_t[:], ps[bg][:, :N], [("ps", bg)], [kg])
                            ts("dve", e_t[:], ps[bn][:, :N], 1e-24, None, ALU.max, None, [("ps", bn)], [ke])
                            if samp:
                                k.op("dve", lambda e: e.tensor_tensor_scan(out=cl[:], data0=cc("reset_s", 64), data1=sigw[:], initial=0.0,
                                                                           op0=ALU.mult, op1=ALU.add), reads=[ksw, "cst"], writes=[kcl])
                            else:
                                for q in range(4):
                                    k.op("dve", lambda e, q=q: e.tensor_tensor_scan(
                                        out=cl[:, q * 128:(q + 1) * 128], data0=ones_f, data1=sigw[:, q * 128:(q + 1) * 128], initial=0.0,
                                        op0=ALU.mult, op1=ALU.add), reads=[ksw, "cst"], writes=[kcl])
                            act(e_t[:], e_t[:], AF.Ln, [ke], [ke])
                            act(e_t[:], e_t[:], AF.Exp, [ke], [ke], scale=-0.5)
                            ts("dve", kmod[:], a_t[:], col("k_a", u), omka[:, u:u + 1], ALU.mult, ALU.add, [ka, "cols", "omka"], [kkmod])
                            tt("pool", e1[:], cl[:], sigw[:], ALU.subtract, [kcl, ksw], [ke1])
                            tt("pool", kmod[:], kmod[:], k_t[:], ALU.mult, [kkmod, "k_t"], [kkmod])
                            stt("dve", kkn[:], k_t[:], col("k_k", u), e_t[:], ALU.mult, ALU.mult, ["k_t", "cols", ke], [kkkn])
                            act(e1[:], e1[:], AF.Exp, [ke1], [ke1], scale=-C0)
                            act(e2[:], cl[:], AF.Exp, [kcl], [ke2], scale=C0)
                            act(e3[:], cl[:], AF.Exp, [kcl], [ke3], scale=-C0)
                            tt("pool", b_t[:], kkn[:], a_t[:], ALU.mult, [kkkn, ka], [kb_])
                            stt("dve", bnb[:], r_t[:], col("r_k", u), kmod[:], ALU.mult, ALU.mult, ["r_t", kkmod, "cols"], ["bnb"])
                            bb = nps()
                            mm(ps[bb][:, :N], bones_b, bnb[:], ["bnb", "cb"], [("ps", bb)])
                            stt("dve", at_[:], kkn[:], -1.0, e1[:], ALU.mult, ALU.mult, [kkkn, ke1], [kat])
                            cl3 = sg3(cl[:])
                            tt("pool", sg3(e_t[:]), cl3[:, :, lsg - 1:lsg].to_broadcast([128, nsg, lsg]), cl3, ALU.subtract, [kcl, kkkn], [ke])
                            tt("dve", kt_[:], kmod[:], e2[:], ALU.mult, [kkmod, ke2], [kkt])
                            tt("pool", bt_[:], b_t[:], e2[:], ALU.mult, [kb_, ke2], [kbt])
                            tt("dve", rt_[:], r_t[:], e3[:], ALU.mult, ["r_t", ke3], [krt])
                            act(e_t[:], e_t[:], AF.Exp, [ke], [ke], scale=-C0)
                            act(dc_[:, :nsg], cl3[:, :, lsg - 1], AF.Exp, [kcl], [kdc], scale=-C0)
                            tt("dve", bonus[:], ps[bb][:, :N], v_t[:], ALU.mult, [("ps", bb), "v_t"], [kbon])
                            tt("pool", Bh[:], b_t[:], e_t[:], ALU.mult, [kb_, ke, kbt, kkt], [kBh])
                            tt("dve", Kh[:], kmod[:], e_t[:], ALU.mult, [kkmod, ke, krt], [kKh])
                            pre_ops.append(k.end())
                            psmode[0] = 1
                            k.begin()
                            for hh_ in range(2):
                                lo = hh_ * 64
                                for src, skey_, dst, kn in [(at_, kat, atz, "atz"), (bt_, kbt, btz, "btz"), (rt_, krt, rtz, "rtz")]:
                                    act(dst[:, hh_, :], src[:, :], AF.Copy, [skey_, "cst"], [kn], scale=bones_f[:, lo:lo + 1])
                            for ci in range(nch):
                                cs = slice(ci * C, (ci + 1) * C)
                                b = nps()
                                tr(ps[b][:C, 0:128], Bh[:, cs], 128, [kBh], [("ps", b)])
                                tr(ps[b][:C, 128:256], Kh[:, cs], 128, [kKh], [("ps", b)])
                                tr(ps[b][:C, 256:384], v_t[:, cs], 128, ["v_t"], [("ps", b)])
                                hm2 = cc("hmask", 256)[:C, :].rearrange("p (a c) -> p a c", a=2)
                                tt("dve", tok[:C, ci, 0:2, :], ps[b][:C, 0:128].unsqueeze(1).to_broadcast([C, 2, 128]), hm2, ALU.mult,
                                   [("ps", b), "cst"], ["tok"])
                                tt("dve", tok[:C, ci, 2:4, :], ps[b][:C, 128:256].unsqueeze(1).to_broadcast([C, 2, 128]), hm2, ALU.mult,
                                   [("ps", b), "cst"], ["tok"])
                                copy("act", tok[:C, ci, 4, :], ps[b][:C, 256:384], [("ps", b)], ["tok"])
                                tt("pool", Vz[:C, ci, :, :], tok[:C, ci, 4:5, :].to_broadcast([C, 2, 128]), hm2, ALU.mult, ["tok", "cst"], ["Vz"])
                            fin_ops.append(k.end())
                            if u == 3 and DBG.get('nh', 4) > 0:
                                psmode[0] = 2
                                k.begin(); gdn_prefetch(0); gpf_ops = k.end()
                                psmode[0] = 1
                            k.begin()
                            if samp:
                                for q in range(4):
                                    b = nps()
                                    for s4 in range(4):
                                        s = q * 4 + s4
                                        tr(ps[b][:, s4 * 128:(s4 + 1) * 128], sbd[u % 2][:, s, :], 128, [("sbd", u % 2)], [("ps", b)])
                                    copy("act", Zs[:, q * 4:(q + 1) * 4, :].rearrange("p a c -> p (a c)"), ps[b][:, :], [("ps", b)], ["Zs"])
                                copy("act", Zsb[:], Zs[:], ["Zs"], ["Zsb"])
                                Zf, zk = Zs, "Zs"
                                Zrd = lambda ci: [Zsb[:, s_, :] for s_ in range(NSQ)]
                                zrk = lambda ci: "Zsb"
                                tt("pool", atm[:], at_[:, :].unsqueeze(1).to_broadcast([128, NSQ, 64]),
                                   segm, ALU.mult, [kat, "segm"], ["atm"])
                                tt("pool", rtm[:], rt_[:, :].unsqueeze(1).to_broadcast([128, NSQ, 64]),
                                   segm, ALU.mult, [krt, "segm"], ["rtm"])
                            else:
                                Zf, zk = Zr[:, u:u + 1, :], ("Zr", u)
                                Zrd = lambda ci: [Zrb[:, u, ci % 2, :]]
                                zrk = lambda ci: ("Zrb", u, ci % 2)
                            pv4 = lambda b_: ps[b_][:C, :].rearrange("p (a c) -> p a c", a=4)[:, :, :C]
                            for ci in range(nch):
                                cs = slice(ci * C, (ci + 1) * C)
                                b1 = nps()
                                for hh_ in range(2):
                                    mm(ps[b1][:C, hh_ * 128: hh_ * 128 + C], bt_[:, cs], atz[:, hh_, cs], [kbt, "atz"], [("ps", b1)])
                                    mm(ps[b1][:C, (2 + hh_) * 128: (2 + hh_) * 128 + C], kt_[:, cs], atz[:, hh_, cs], [kkt, "atz"], [("ps", b1)])
                                b2 = nps()
                                for hh_ in range(2):
                                    mm(ps[b2][:C, hh_ * 128: hh_ * 128 + C], at_[:, cs], btz[:, hh_, cs], [kat, "btz"], [("ps", b2)])
                                b3 = nps()
                                for hh_ in range(2):
                                    mm(ps[b3][:C, hh_ * 128: hh_ * 128 + C], bt_[:, cs], rtz[:, hh_, cs], [kbt, "rtz"], [("ps", b3)])
                                    mm(ps[b3][:C, (2 + hh_) * 128: (2 + hh_) * 128 + C], kt_[:, cs], rtz[:, hh_, cs], [kkt, "rtz"], [("ps", b3)])
                                tt("dve", cm1[ci][:C, :, :C], pv4(b1), mS.unsqueeze(1).to_broadcast([C, 4, C]), ALU.mult, [("ps", b1), "cst"], [("cm1", ci)])
                                tt("dve", nt2[ci][:C, :, :C], pv4(b2)[:, 0:2, :], mT.unsqueeze(1).to_broadcast([C, 2, C]), ALU.mult, [("ps", b2), "cst"], [("nt2", ci)])
                                tt("dve", cm3[ci][:C, :, :C], pv4(b3), mI.unsqueeze(1).to_broadcast([C, 4, C]), ALU.mult, [("ps", b3), "cst"], [("cm3", ci)])
                                tt("pool", Tt[ci][:C, :, :C], cm1[ci][:C, 0:2, :C], ident_b[:C, :C].unsqueeze(1).to_broadcast([C, 2, C]), ALU.add,
                                   [("cm1", ci), "cb"], [("Tt", ci)])
                            Pm = [[cm1[ci][:C, 0, :C], cm1[ci][:C, 1, :C]] for ci in range(nch)]
                            Qm = [[nt2[ci][:C, 0, :C], nt2[ci][:C, 1, :C]] for ci in range(nch)]
                            pqk = [[("cm1", ci), ("nt2", ci)] for ci in range(nch)]
                            for lev in range(nlev):
                                bs = []
                                for ci in range(nch):
                                    b = nps(); bs.append(b)
                                    for hh_ in range(2):
                                        if lev < nlev - 1:
                                            mm(ps[b][:C, hh_ * 128: hh_ * 128 + C], Qm[ci][hh_], Pm[ci][hh_], pqk[ci], [("ps", b)])
                                        mm(ps[b][:C, (2 + hh_) * 128: (2 + hh_) * 128 + C], Pm[ci][hh_], Qm[ci][hh_], pqk[ci], [("ps", b)])
                                for ci in range(nch):
                                    b = bs[ci]; pq = PQ[ci]
                                    if lev < nlev - 1:
                                        copy("act" if ci != 3 else "dve", pq[:C, :, :C], pv4(b), [("ps", b)], [("PQ", ci)])
                                    else:
                                        copy("act" if ci != 3 else "dve", pq[:C, 2:4, :C], pv4(b)[:, 2:4, :], [("ps", b)], [("PQ", ci)])
                                    Pm[ci] = [pq[:C, 0, :C], pq[:C, 1, :C]]; Qm[ci] = [pq[:C, 2, :C], pq[:C, 3, :C]]
                                    pqk[ci] = [("PQ", ci)]
                                bs = []
                                for ci in range(nch):
                                    b = nps(); bs.append(b)
                                    for hh_ in range(2):
                                        mm(ps[b][:C, hh_ * 128: hh_ * 128 + C], Qm[ci][hh_], Tt[ci][:C, hh_, :C], [("PQ", ci), ("Tt", ci)], [("ps", b)])
                                for ci in range(nch):
                                    b = bs[ci]
                                    tt("dve", Tt[ci][:C, :, :C], pv4(b)[:, 0:2, :], Tt[ci][:C, :, :C], ALU.add,
                                       [("ps", b), ("Tt", ci)], [("Tt", ci)])
                            for ci in range(nch):
                                cs = slice(ci * C, (ci + 1) * C)
                                b = nps()
                                items = []
                                for s in range(nS):
                                    items.append((ps[b][:C, 0:128], (atm[:, s, :] if samp else at_[:, cs]), Zrd(ci)[s]))
                                for hh_ in range(2):
                                    items.append((ps[b][:C, 0:128], cm1[ci][:C, 2 + hh_, :C], Vz[:C, ci, hh_, :]))
                                mmg(items, [kat, "atm", zrk(ci), ("cm1", ci), "Vz"] if samp else [kat, zrk(ci), ("cm1", ci), "Vz"], [("ps", b)])
                                copy("act", Xb[:C, :], ps[b][:C, 0:128], [("ps", b)], ["Xb"])
                                b = nps()
                                for hh_ in range(2):
                                    mm(ps[b][:C, hh_ * 128:(hh_ + 1) * 128], Tt[ci][:C, hh_, :C], Xb[:C, :], [("Tt", ci), "Xb"], [("ps", b)])
                                tt("dve", Wz[:C, :, :], ps[b][:C, 0:256].rearrange("p (a c) -> p a c", a=2),
                                   cc("hmask", 256)[:C, :].rearrange("p (a c) -> p a c", a=2), ALU.mult, [("ps", b), "cst"], ["Wz"])
                                if samp:
                                    tt("pool", Wb[:C, :], Wz[:C, 0, :], Wz[:C, 1, :], ALU.add, ["Wz"], ["Wb"])
                                    tt("dve", Wm[:C, :, :], Wb[:C, :].unsqueeze(1).to_broadcast([C, NSQ, 128]),
                                       cc("rowseg", 16)[:C, :].unsqueeze(2).to_broadcast([C, NSQ, 128]), ALU.mult, ["Wb", "cst"], ["Wm"])
                                    tt("dve", Vm[:C, :, :], tok[:C, ci, 4:5, :].to_broadcast([C, NSQ, 128]),
                                       cc("rowseg", 16)[:C, :].unsqueeze(2).to_broadcast([C, NSQ, 128]), ALU.mult, ["tok", "cst"], ["Vm"])
                                    for q in range(4):
                                        b = nps()
                                        mmg([(ps[b][:, :], tok[:C, ci, 0, :], Wm[:C, q * 4:(q + 1) * 4, :].rearrange("p a c -> p (a c)")),
                                             (ps[b][:, :], tok[:C, ci, 1, :], Wm[:C, q * 4:(q + 1) * 4, :].rearrange("p a c -> p (a c)")),
                                             (ps[b][:, :], tok[:C, ci, 2, :], Vm[:C, q * 4:(q + 1) * 4, :].rearrange("p a c -> p (a c)")),
                                             (ps[b][:, :], tok[:C, ci, 3, :], Vm[:C, q * 4:(q + 1) * 4, :].rearrange("p a c -> p (a c)"))],
                                            ["tok", "Wm", "Vm"], [("ps", b)])
                                        tt("dve", bigf[:, q * 4:(q + 1) * 4, :], ps[b][:, :].rearrange("p (a c) -> p a c", a=4),
                                           bones_f.unsqueeze(1).to_broadcast([128, 4, 128]), ALU.mult, [("ps", b), "cst"], ["bigf"])
                                    tt("dve", Zs[:], Zs[:], dc_[:, :NSQ].unsqueeze(2).to_broadcast([128, NSQ, 128]), ALU.mult, ["Zs", kdc], ["Zs"])
                                    tt("dve", Zs[:], Zs[:], bigf[:], ALU.add, ["Zs", "bigf"], ["Zs"])
                                else:
                                    bz = nps()
                                    mmg([(ps[bz][:, 0:128], tok[:, ci, 0, :], Wz[:, 0, :]), (ps[bz][:, 0:128], tok[:, ci, 1, :], Wz[:, 1, :]),
                                         (ps[bz][:, 0:128], tok[:, ci, 2, :], Vz[:, ci, 0, :]), (ps[bz][:, 0:128], tok[:, ci, 3, :], Vz[:, ci, 1, :])],
                                        ["tok", "Wz", "Vz"], [("ps", bz)])
                                    stt("dve", Zrb[:, u, (ci + 1) % 2, :], Zf[:, 0, :], dc_[:, ci:ci + 1], ps[bz][:, 0:128], ALU.mult, ALU.add,
                                        [zk, kdc, ("ps", bz)], [("Zrb", u, (ci + 1) % 2)])
                                    stt("dve", Zf[:, 0, :], Zf[:, 0, :], dc_[:, ci:ci + 1], ps[bz][:, 0:128], ALU.mult, ALU.add, [zk, kdc, ("ps", bz)], [zk])
                                b = nps()
                                items = []
                                for s in range(nS):
                                    items.append((ps[b][:, :C], Zrd(ci)[s], (rtm[:, s, :] if samp else rt_[:, cs])))
                                for hh_ in range(2):
                                    items.append((ps[b][:, :C], Wz[:C, hh_, :], cm3[ci][:C, hh_, :C]))
                                    items.append((ps[b][:, :C], Vz[:C, ci, hh_, :], cm3[ci][:C, 2 + hh_, :C]))
                                mmg(items, [zrk(ci), krt, "rtm", "Wz", ("cm3", ci), "Vz"] if samp else [zrk(ci), krt, "Wz", ("cm3", ci), "Vz"], [("ps", b)])
                                copy("act", o_t[:, cs], ps[b][:, :C], [("ps", b)], ["o_t"])

                            if samp:
                                for q in range(4):
                                    b = nps()
                                    for s4 in range(4):
                                        tr(ps[b][:, s4 * 128:(s4 + 1) * 128], Zs[:, q * 4 + s4, :], 128, ["Zs"], [("ps", b)])
                                    pvq = ps[b][:, :].rearrange("p (a c) -> p a c", a=4)
                                    copy("act", stg[0:64, q * 4:(q + 1) * 4, :], pvq[0:64, :, 0:64], [("ps", b)], ["stg"])
                                    copy("act", stg[64:128, q * 4:(q + 1) * 4, :], pvq[64:128, :, 64:128], [("ps", b)], ["stg"])
                                rwv = rws_d.rearrange("(s h v) k -> h v s k", s=NSQ, h=8)
                                k.dma("sp", rwv[2 * u], stg[0:64, :, :], reads=["stg"])
                                k.dma("sp", rwv[2 * u + 1], stg[64:128, :, :], reads=["stg"])
                            elif ti == 3:
                                b = nps()
                                tr(ps[b][:, 0:128], Zr[:, u, :], 128, [("Zr", u)], [("ps", b)])
                                copy("act", bigf[0:64, 0, 0:64], ps[b][0:64, 0:64], [("ps", b)], ["bigf"])
                                copy("act", bigf[64:128, 0, 0:64], ps[b][64:128, 64:128], [("ps", b)], ["bigf"])
                                k.dma("sp", rwp_d[u * 128:(u + 1) * 128, :], bigf[:, 0, 0:64], reads=["bigf"])
                            m1, var_, o2 = [t_[:].rearrange("p a n -> p (a n)").bitcast(F32) for t_ in (atz, btz, rtz)]
                            km1, kvar, ko2 = "atz", "btz", "rtz"
                            b = nps()
                            mm(ps[b][:, :N], bones_f, o_t[:, :N], ["o_t", "cst"], [("ps", b)])
                            stt("dve", o2, ps[b][:, :N], -1.0 / 64, o_t[:, :N], ALU.mult, ALU.add, [("ps", b), "o_t"], [ko2])
                            act(sqp[:], o2, AF.Square, [ko2], ["sqp"])
                            b = nps()
                            mm(ps[b][:, :N], bones_b, sqp[:], ["sqp", "cb"], [("ps", b)])
                            act(var_, ps[b][:, :N], AF.Ln, [("ps", b)], [kvar], bias=64e-5, scale=1.0 / 64)
                            act(var_, var_, AF.Exp, [kvar], [kvar], scale=-0.5)
                            tt("pool", o2, o2, var_, ALU.mult, [ko2, kvar], [ko2])
                            ts("dve", o2, o2, col("lnx_w", u), col("lnx_b", u), ALU.mult, ALU.add, [ko2, "cols"], [ko2])
                            tt("pool", o2, o2, bonus[:], ALU.add, [ko2, kbon], [ko2])
                            tt("dve", oab[:, u, :N], o2, g_t[:], ALU.mult, [ko2, kg], [("oab", u)])
                            chunk_ops.append(k.end())
                        psmode[0] = 0
                        nu_ = len(pre_ops)
                        if nu_:
                            k.emit(pre_ops[0]); k.emit(fin_ops[0])
                            for u in range(nu_):
                                nxt = pre_ops[u + 1] if u + 1 < nu_ else gpf_ops
                                k.emit_interleaved(chunk_ops[u], nxt)
                                if u + 1 < nu_:
                                    k.emit(fin_ops[u + 1])

                        k.barrier()
                        for ci in range(nch):
                            b = nps()
                            mmg([(ps[b][:C, 0:8], u_t[:, kc, ci * C:(ci + 1) * C], wab[:, kc, :]) for kc in range(8)], ["u_t", "wab"], [("ps", b)])
                            act(btok[:C, ci, :], ps[b][:C, 4:8], AF.Sigmoid, [("ps", b)], ["btok"])
                            tt("dve", sptmp[:C, ci, :], ps[b][:C, 0:4], dtb[:C, :], ALU.add, [("ps", b), "dtb"], ["sptmp"])
                        act(sptmp[:C, :, :], sptmp[:C, :, :], AF.Exp, ["sptmp"], ["sptmp"])
                        act(sptmp[:C, :, :], sptmp[:C, :, :], AF.Ln, ["sptmp"], ["sptmp"], bias=1.0, scale=1.0)
                        stt("dve", gtok[:C, :, :], sptmp[:C, :, :], -1.0, eA[:C, :].unsqueeze(1).to_broadcast([C, nch, 4]), ALU.mult, ALU.mult,
                            ["sptmp", "eA"], ["gtok"])
                        for hh in range(DBG.get('nh', 4)):
                            xq, xk, xv = r_t, k_t, v_t
                            if hh == 0 and DBG.get('nu', 4) < 4:
                                gdn_prefetch(0)
                            bz_ = pchunk(14 + hh * 4 + 3)
                            act(z_t[:], ps[bz_][:, :N], AF.Silu, [("ps", bz_)], ["z_t"])
                            if DBG.get("gstop", 99) <= 1:
                                continue
                            qn, kn, qnb, knb_ = tmp[1], tmp[2], at, bt
                            for src, skey, dst, dkey, scl, sc_, sck, sb_, sbk in [(xq, "r_t", qn, tk[1], 128.0 ** -0.5, tmp[3], tk[3], sqb, "sqb"),
                                                                                    (xk, "k_t", kn, tk[2], 1.0, tmp[0], tk[0], bnb, "bnb")]:
                                act(sb_[:], src[:], AF.Square, [skey], [sbk])
                                b = nps()
                                mm(ps[b][:, :N], ones_b, sb_[:], [sbk, "cb"], [("ps", b)])
                                act(sc_[:], ps[b][:, :N], AF.Ln, [("ps", b)], [sck], bias=1e-6, scale=1.0)
                                act(sc_[:], sc_[:], AF.Exp, [sck], [sck], scale=-0.5)
                                stt("dve", dst[:], src[:], scl, sc_[:], ALU.mult, ALU.mult, [skey, sck], [dkey])
                            if DBG.get("gstop", 99) <= 2:
                                continue
                            copy("act", qnb[:], qn[:], [tk[1]], ["at"])
                            copy("dve", knb_[:], kn[:], [tk[2]], ["bt"])
                            qt_, Khf = kt, tmp[4]
                            atg = rt
                            if samp:
                                if hh == 0:
                                    load_gdn_state(0)
                                ZS, ZSK = sbd[hh % 2], ("sbd", hh % 2)
                                copy("act", Zsb[:], ZS[:], [ZSK], ["Zsb"])
                                Zf, zk = ZS, ZSK
                                Zrd = lambda ci: [Zsb[:, s_, :] for s_ in range(NSQ)]
                                zrk = lambda ci: "Zsb"
                            else:
                                Zf, zk = Zg[:, hh:hh + 1, :], ("Zg", hh)
                                Zrd = lambda ci, hh=hh: [Zgb[:, hh, ci % 2, :]]
                                zrk = lambda ci, hh=hh: ("Zgb", hh, ci % 2)
                            bAs, bBs = [], []
                            for ci in range(nch):
                                copy("pool", g_rep[ci][:C, :], gtok[:C, ci, hh:hh + 1].to_broadcast([C, 128]), ["gtok"], [("g_rep", ci)])
                                copy("pool", b_rep[ci][:C, :], btok[:C, ci, hh:hh + 1].to_broadcast([C, 128]), ["btok"], [("b_rep", ci)])
                                ts("dve", Gs[ci][:C, :C], mT, gtok[:C, ci, hh:hh + 1], None, ALU.mult, None, ["cst", "gtok"], [("Gs", ci)])
                            for ci in range(nch):
                                bA = nps(); bAs.append(bA)
                                mm(ps[bA][:, 0:C], g_rep[ci][:C, :], mI, [("g_rep", ci), "cst"], [("ps", bA)])
                                mm(ps[bA][:, 128:128 + C], g_rep[ci][:C, :], mT, [("g_rep", ci), "cst"], [("ps", bA)])
                                mm(ps[bA][:, 256:256 + C], b_rep[ci][:C, :], ident_f[:C, :C], [("b_rep", ci), "cst"], [("ps", bA)])
                                mm(ps[bA][:C, 384:384 + C], Gs[ci][:C, :C], mI, [("Gs", ci), "cst"], [("ps", bA)])
                                bB = nps(); bBs.append(bB)
                                mm(ps[bB][:C, 0:C], mI, Gs[ci][:C, :C], [("Gs", ci), "cst"], [("ps", bB)])
                            for ci in range(nch):
                                cs = slice(ci * C, (ci + 1) * C)
                                bA, bB = bAs[ci], bBs[ci]
                                act(eg[ci][:, :C], ps[bA][:, 0:C], AF.Exp, [("ps", bA)], [("eg", ci)])
                                act(ek[ci][:, :C], ps[bA][:, 128:128 + C], AF.Exp, [("ps", bA)], [("ek", ci)])
                                act(dmi[ci][:C, :C], ps[bA][:C, 384:384 + C], AF.Exp, [("ps", bA)], [("dmi", ci)])
                                tt("dve", kbf[ci][:, :C], kn[:, cs], ps[bA][:, 256:256 + C], ALU.mult, [tk[2], ("ps", bA)], [("kbf", ci)])
                                act(dmt[ci][:C, :C], ps[bB][:C, 0:C], AF.Exp, [("ps", bB)], [("dmt", ci)])
                                tt("pool", dms[ci][:C, :C], dmi[ci][:C, :C], mS, ALU.mult, [("dmi", ci), "cst"], [("dms", ci)])
                                tt("pool", dmi[ci][:C, :C], dmi[ci][:C, :C], mI, ALU.mult, [("dmi", ci), "cst"], [("dmi", ci)])
                                tt("pool", dmt[ci][:C, :C], dmt[ci][:C, :C], mT, ALU.mult, [("dmt", ci), "cst"], [("dmt", ci)])
                                copy("pool", kbb[ci][:, :C], kbf[ci][:, :C], [("kbf", ci)], [("kbb", ci)])
                                stt("dve", atg[:, cs], kbf[ci][:, :C], -1.0, eg[ci][:, :C], ALU.mult, ALU.mult, [("kbf", ci), ("eg", ci)], [("rt", ci)])
                                tt("pool", qt_[:, cs], qn[:, cs], eg[ci][:, :C], ALU.mult, [tk[1], ("eg", ci)], [("kt", ci)])
                                tt("pool", Khf[:, cs], kn[:, cs], ek[ci][:, :C], ALU.mult, [tk[2], ("ek", ci)], [(tk[4], ci)])
                            for ci in range(nch):
                                cs = slice(ci * C, (ci + 1) * C)
                                b = nps()
                                tr(ps[b][:C, 0:128], Khf[:, cs], 128, [(tk[4], ci)], [("ps", b)])
                                tr(ps[b][:C, 128:256], xv[:, cs], 128, ["v_t"], [("ps", b)])
                                copy("act", tok[:C, ci, 3:5, :].rearrange("p a c -> p (a c)"), ps[b][:C, 0:256], [("ps", b)], [("tok", ci)])
                            if hh + 1 < DBG.get('nh', 4):
                                gdn_prefetch(hh + 1)
                                if samp:
                                    load_gdn_state(hh + 1)
                            for ci in range(nch):
                                cs = slice(ci * C, (ci + 1) * C)
                                b1 = nps()
                                mm(ps[b1][:C, 0:C], knb_[:, cs], kbb[ci][:, :C], ["bt", ("kbb", ci)], [("ps", b1)])
                                mm(ps[b1][:C, 128:128 + C], kbb[ci][:, :C], knb_[:, cs], ["bt", ("kbb", ci)], [("ps", b1)])
                                mm(ps[b1][:C, 256:256 + C], knb_[:, cs], qnb[:, cs], ["bt", "at"], [("ps", b1)])
                                stt("dve", cm1[ci][:C, 0, :C], ps[b1][:C, 0:C], -1.0, dms[ci][:C, :C], ALU.mult, ALU.mult, [("ps", b1), ("dms", ci)], [("cm1", ci)])
                                stt("dve", nt2[ci][:C, 0, :C], ps[b1][:C, 128:128 + C], -1.0, dmt[ci][:C, :C], ALU.mult, ALU.mult, [("ps", b1), ("dmt", ci)], [("nt2", ci)])
                                tt("dve", cm3[ci][:C, 0, :C], ps[b1][:C, 256:256 + C], dmi[ci][:C, :C], ALU.mult, [("ps", b1), ("dmi", ci)], [("cm3", ci)])
                                tt("pool", Tt[ci][:C, 0, :C], cm1[ci][:C, 0, :C], ident_b[:C, :C], ALU.add, [("cm1", ci), "cb"], [("Tt", ci)])
                            Pm = [cm1[ci][:C, 0, :C] for ci in range(nch)]
                            Qm = [nt2[ci][:C, 0, :C] for ci in range(nch)]
                            pqk = [[("cm1", ci), ("nt2", ci)] for ci in range(nch)]
                            for lev in range(nlev):
                                bs = []
                                for ci in range(nch):
                                    b = nps(); bs.append(b)
                                    if lev < nlev - 1:
                                        mm(ps[b][:C, 0:C], Qm[ci], Pm[ci], pqk[ci], [("ps", b)])
                                    mm(ps[b][:C, 128:128 + C], Pm[ci], Qm[ci], pqk[ci], [("ps", b)])
                                for ci in range(nch):
                                    b = bs[ci]; pq = PQ[ci]
                                    pvv = ps[b][:C, 0:256].rearrange("p (a c) -> p a c", a=2)[:, :, :C]
                                    if lev < nlev - 1:
                                        copy("act", pq[:C, 0:2, :C], pvv, [("ps", b)], [("PQ", ci)])
                                    else:
                                        copy("act", pq[:C, 1, :C], ps[b][:C, 128:128 + C], [("ps", b)], [("PQ", ci)])
                                    Pm[ci], Qm[ci] = pq[:C, 0, :C], pq[:C, 1, :C]
                                    pqk[ci] = [("PQ", ci)]
                                bs = []
                                for ci in range(nch):
                                    b = nps(); bs.append(b)
                                    mm(ps[b][:C, 0:C], Qm[ci], Tt[ci][:C, 0, :C], [("PQ", ci), ("Tt", ci)], [("ps", b)])
                                for ci in range(nch):
                                    b = bs[ci]
                                    tt("dve", Tt[ci][:C, 0, :C], ps[b][:C, 0:C], Tt[ci][:C, 0, :C], ALU.add, [("ps", b), ("Tt", ci)], [("Tt", ci)])
                            if samp:
                                tt("pool", atm[:], atg[:, :].unsqueeze(1).to_broadcast([128, NSQ, 64]),
                                   segm, ALU.mult, [("rt", 0), "segm"], ["atm"])
                                tt("pool", rtm[:], qt_[:, :].unsqueeze(1).to_broadcast([128, NSQ, 64]),
                                   segm, ALU.mult, [("kt", 0), "segm"], ["rtm"])
                            for ci in range(nch):
                                cs = slice(ci * C, (ci + 1) * C)
                                zr = Zrd(ci); zrkey = zrk(ci)
                                b = nps()
                                mmg([(ps[b][:C, 0:128], (atm[:, s, :] if samp else atg[:, cs]), zr[s]) for s in range(nS)],
                                    [("rt", ci), "atm", zrkey] if samp else [("rt", ci), zrkey], [("ps", b)])
                                stt("dve", Xb[:C, :], tok[:C, ci, 4, :], btok[:C, ci, hh:hh + 1], ps[b][:C, 0:128], ALU.mult, ALU.add,
                                    [("tok", ci), "btok", ("ps", b)], ["Xb"])
                                b = nps()
                                mm(ps[b][:C, 0:128], Tt[ci][:C, 0, :C], Xb[:C, :], [("Tt", ci), "Xb"], [("ps", b)])
                                copy("act", Wb[:C, :], ps[b][:C, 0:128], [("ps", b)], ["Wb"])
                                if samp:
                                    tt("dve", Wm[:C, :, :], Wb[:C, :].unsqueeze(1).to_broadcast([C, NSQ, 128]),
                                       cc("rowseg", 16)[:C, :].unsqueeze(2).to_broadcast([C, NSQ, 128]), ALU.mult, ["Wb", "cst"], ["Wm"])
                                    egl = eg[ci][:, :C].rearrange("p (s l) -> p s l", s=NSQ)[:, :, LS - 1:LS]
                                    tt("dve", ZS[:], ZS[:], egl.to_broadcast([128, NSQ, 128]), ALU.mult, [ZSK, ("eg", ci)], [ZSK])
                                    for q in range(4):
                                        b = nps()
                                        mm(ps[b][:, :], tok[:C, ci, 3, :], Wm[:C, q * 4:(q + 1) * 4, :].rearrange("p a c -> p (a c)"), [("tok", ci), "Wm"], [("ps", b)])
                                        tt("dve", ZS[:, q * 4:(q + 1) * 4, :], ZS[:, q * 4:(q + 1) * 4, :], ps[b][:, :].rearrange("p (a c) -> p a c", a=4),
                                           ALU.add, [ZSK, ("ps", b)], [ZSK])
                                else:
                                    bz = nps()
                                    mm(ps[bz][:, 0:128], tok[:, ci, 3, :], Wb[:, :], [("tok", ci), "Wb"], [("ps", bz)])
                                    stt("dve", Zgb[:, hh, (ci + 1) % 2, :], Zf[:, 0, :], eg[ci][:, C - 1:C], ps[bz][:, 0:128], ALU.mult, ALU.add,
                                        [zk, ("eg", ci), ("ps", bz)], [("Zgb", hh, (ci + 1) % 2)])
                                    stt("dve", Zf[:, 0, :], Zf[:, 0, :], eg[ci][:, C - 1:C], ps[bz][:, 0:128], ALU.mult, ALU.add,
                                        [zk, ("eg", ci), ("ps", bz)], [zk])
                                b = nps()
                                items = [(ps[b][:, :C], zr[s], (rtm[:, s, :] if samp else qt_[:, cs])) for s in range(nS)]
                                items.append((ps[b][:, :C], Wb[:C, :], cm3[ci][:C, 0, :C]))
                                mmg(items, [zrkey, ("kt", ci), "rtm", "Wb", ("cm3", ci)] if samp else [zrkey, ("kt", ci), "Wb", ("cm3", ci)], [("ps", b)])
                                copy("act", o_t[:, cs], ps[b][:, :C], [("ps", b)], ["o_t"])

                            if samp:
                                k.dma("sp", dls_d.rearrange("(s h k) v -> h k s v", s=NSQ, h=4)[hh], ZS[:], reads=[ZSK])
                            elif ti == 3:
                                k.dma("sp", dlp_d[hh * 128:(hh + 1) * 128, :], Zg[:, hh, :], reads=[("Zg", hh)])
                            o2 = tmp[3]
                            tt("pool", sqb[:], o_t[:, :N], o_t[:, :N], ALU.mult, ["o_t"], ["sqb"])
                            b = nps()
                            mm(ps[b][:, :N], ones_b, sqb[:], ["sqb", "cb"], [("ps", b)])
                            act(o2[:], ps[b][:, :N], AF.Ln, [("ps", b)], [tk[3]], bias=1e-6, scale=1.0 / 128)
                            act(o2[:], o2[:], AF.Exp, [tk[3]], [tk[3]], scale=-0.5)
                            stt("dve", o2[:], o_t[:, :N], col("norm_w", 0), o2[:], ALU.mult, ALU.mult, ["o_t", "cols", tk[3]], [tk[3]])
                            tt("dve", oab[:, 4 + hh, :N], o2[:], z_t[:], ALU.mult, [tk[3], "z_t"], [("oab", 4 + hh)])

                        mrg = [at, bt, kt, rt, atz[:, 0, :], atz[:, 1, :], btz[:, 0, :], btz[:, 1, :]]
                        mrk = ["at", "bt", "kt", "rt", "atz", "atz", "btz", "btz"]
                        okeys = [("oab", j) for j in range(8)]
                        k.barrier()
                        for m in range(DBG.get('ng', 8)):
                            pj = pjbuf[m % 2]
                            k.dma("sp", pj[:, 0, :, :], pjab_d[:, m * 128:(m + 1) * 128].rearrange("(kc p) c -> p kc c", p=128), reads=["pjab"], writes=[("pj", m % 2)])
                            k.dma("sp", pj[:, 1, :, :], pjbb_d[:, m * 128:(m + 1) * 128].rearrange("(kc p) c -> p kc c", p=128), reads=["pjbb"], writes=[("pj", m % 2)])
                            bga = pchunk(30 + 2 * m)
                            act(tmp[0][:], ps[bga][:, :N], AF.Sigmoid, [("ps", bga)], [tk[0]])
                            b = nps()
                            mmg([(ps[b][:, :N], pj[:, 0, kc, :], oab[:, kc, :N]) for kc in range(4)], [("pj", m % 2)] + okeys, [("ps", b)])
                            tt("dve", tmp[1][:], tmp[0][:], ps[b][:, :N], ALU.mult, [tk[0], ("ps", b)], [tk[1]])
                            bgb = pchunk(31 + 2 * m)
                            act(tmp[0][:], ps[bgb][:, :N], AF.Sigmoid, [("ps", bgb)], [tk[0]])
                            b = nps()
                            mmg([(ps[b][:, :N], pj[:, 1, kc, :], oab[:, 4 + kc, :N]) for kc in range(4)], [("pj", m % 2)] + okeys, [("ps", b)])
                            tt("dve", tmp[2][:], tmp[0][:], ps[b][:, :N], ALU.mult, [tk[0], ("ps", b)], [tk[2]])
                            tt("pool", mrg[m][:, :N] if m < 4 else mrg[m], tmp[1][:], tmp[2][:], ALU.add, [tk[1], tk[2]], [mrk[m], ("mrg", m)])
                        for half in range(4):
                            wt = WI[half % 2]
                            k.dma("sp", wt[:, :, :], wob_d[:, half * 256:(half + 1) * 256].rearrange("(kc p) c -> p kc c", p=128),
                                  reads=[("wob", r_) for r_ in range(2)], writes=[("wi", half % 2)])
                            for m4 in range(2):
                                m = half * 2 + m4
                                b = nps()
                                mmg([(ps[b][:, :N], wt[:, kc, m4 * 128:(m4 + 1) * 128], (mrg[kc][:, :N] if kc < 4 else mrg[kc])) for kc in range(8)],
                                    [("wi", half % 2)] + [("mrg", j) for j in range(8)], [("ps", b)])
                                tt("dve", h[:, m, c0:c0 + N], h[:, m, c0:c0 + N], ps[b][:, :N], ALU.add, hk + [("ps", b)], hk)
                        k.barrier()
            k.barrier()

        while pending_casts:
            one_cast()
        if stage >= 2:
            mixer()
        if stage >= 3:
            ffn(w2g_d, w2u_d, w2d_d, "ffn2_norm", "b")

        with ExitStack() as st:
            rstd = sb("rstdf", [128, 512], F32, st)
            sq = [sb("sqf%d" % i, [128, 512], BF16, st) for i in range(2)]
            yf = [sb("yf%d" % i, [128, 8, 512], F32, st) for i in range(2)]
            yo = [sb("yo%d" % i, [128, D], F32, st) for i in range(3)]
            bi = 0
            for ti, (c0, N, _, _) in enumerate(TILES):
                hk = hkeys_of(c0, N)
                rms_stats(c0, N, rstd, sq, hk)
                yft = yf[ti % 2]
                for m in range(8):
                    k.op("dve", lambda e, m=m, yft=yft, c0=c0, N=N: e.scalar_tensor_tensor(
                        out=yft[:, m, :N], in0=h[:, m, c0:c0 + N], scalar=col("final_norm", m), in1=rstd[:, :N],
                        op0=ALU.mult, op1=ALU.mult), reads=hk + ["rstd", "cols"], writes=[("yf", ti % 2, m)])
                for blk in range((N + 127) // 128):
                    rows = min(128, N - blk * 128)
                    yot = yo[bi % 3]
                    for half in range(2):
                        b = nps()
                        for m4 in range(4):
                            m = half * 4 + m4
                            k.op("pe", lambda e, b=b, m4=m4, m=m, yft=yft, rows=rows, blk=blk: e.transpose(
                                ps[b][:rows, m4 * 128:(m4 + 1) * 128], yft[:, m, blk * 128:blk * 128 + rows], ident_f),
                                reads=[("yf", ti % 2, m), "cst"], writes=[("ps", b)])
                        if half == 0:
                            k.op("act", lambda e, b=b, yot=yot, rows=rows: e.activation(
                                out=yot[:rows, 0:512], in_=ps[b][:rows, :], func=AF.Copy), reads=[("ps", b)], writes=[("yo", bi % 3)])
                        else:
                            k.op("dve", lambda e, b=b, yot=yot, rows=rows: e.tensor_copy(
                                out=yot[:rows, 512:1024], in_=ps[b][:rows, :]), reads=[("ps", b)], writes=[("yo", bi % 3)])
                    k.dma("sp", y_d[c0 + blk * 128:c0 + blk * 128 + rows, :], yot[:rows, :], reads=[("yo", bi % 3)])
                    bi += 1
        k.finish()
    return nc


def _perm_a():
    parts = [np.arange(512, 576), np.arange(1600, 1664), np.arange(1664, 1792)]
    for u in range(4):
        parts += [np.arange(u * 128, (u + 1) * 128), np.arange(576 + u * 128, 576 + (u + 1) * 128),
                  np.arange(1088 + u * 128, 1088 + (u + 1) * 128)]
    return np.concatenate(parts)


def _perm_b():
    parts = []
    for hh in range(4):
        parts += [np.arange(hh * 128, (hh + 1) * 128), np.arange(512 + hh * 128, 512 + (hh + 1) * 128),
                  np.arange(1024 + hh * 128, 1024 + (hh + 1) * 128), np.arange(1544 + hh * 128, 1544 + (hh + 1) * 128)]
    return np.concatenate(parts)


def _qkv_from_b():
    j = np.arange(1536)
    role = j // 512; hh = (j % 512) // 128; off = j % 128
    return hh * 512 + role * 128 + off


def _win_perm():
    pa = _perm_a()
    pb = A_PROJ + _perm_b()
    g0 = A_PROJ + 2056
    pg = np.concatenate([np.concatenate([np.arange(g0 + m * 128, g0 + (m + 1) * 128),
                                         np.arange(g0 + 1024 + m * 128, g0 + 1024 + (m + 1) * 128)]) for m in range(8)])
    ab = np.arange(A_PROJ + 1536, A_PROJ + 1544)
    return np.concatenate([pa, pb, pg, ab])


def _colvec(v):
    v = np.asarray(v, np.float32).reshape(-1)
    return np.ascontiguousarray(v.reshape(-1, 128).T)


def kernel(**inp):
    stage = int(inp.pop("_stage", 99))
    f = lambda a: np.ascontiguousarray(np.asarray(a, dtype=np.float32))
    pa = _perm_a()
    cols = np.zeros((128, NCOL), np.float32)
    def putc(name, arr):
        cols[:, COLS[name]:COLS[name] + arr.shape[1]] = arr
    putc("ffn1_norm", _colvec(inp["ffn1_norm"][0])); putc("mix_norm", _colvec(inp["mix_norm"][0]))
    putc("ffn2_norm", _colvec(inp["ffn2_norm"][0])); putc("final_norm", _colvec(inp["final_norm"]))
    putc("mu", _colvec(f(inp["rwkv_mu"])[0][pa]))
    for nm, key in [("w0", "rwkv_w0"), ("a0", "rwkv_a0"), ("k_k", "rwkv_k_k"), ("k_a", "rwkv_k_a"),
                    ("r_k", "rwkv_r_k"), ("lnx_w", "rwkv_lnx_w"), ("lnx_b", "rwkv_lnx_b")]:
        putc(nm, _colvec(f(inp[key])[0]))
    cw = f(inp["gdn_conv_w"])[0]
    putc("conv_w", np.concatenate([_colvec(cw[i]) for i in range(4)], axis=1))
    putc("norm_w", _colvec(f(inp["gdn_norm_w"])[0]))
    putc("A_log", np.tile(f(inp["gdn_A_log"])[0][None, :], (128, 1)))
    putc("dt_bias", np.tile(f(inp["gdn_dt_bias"])[0][None, :], (128, 1)))
    consts = make_consts()
    w_in = np.ascontiguousarray(f(inp["w_in"])[0][:, _win_perm()])
    shared = {
        "w1g": f(inp["ffn1_w_gate"])[0], "w1u": f(inp["ffn1_w_up"])[0], "w1d": f(inp["ffn1_w_down"])[0],
        "w2g": f(inp["ffn2_w_gate"])[0], "w2u": f(inp["ffn2_w_up"])[0], "w2d": f(inp["ffn2_w_down"])[0],
        "w_in": w_in, "proj_a": f(inp["proj_a"])[0], "proj_b": f(inp["proj_b"])[0], "w_out": f(inp["w_out"])[0],
        "lw2": f(inp["rwkv_w2"])[0], "la2": f(inp["rwkv_a2"])[0], "lg2": f(inp["rwkv_g2"])[0],
        "cols": cols, "consts": consts, "consts2": make_consts2(),
    }
    xp = f(inp["x_prompt"]); xs = f(inp["x_sample"])
    srw = f(inp["state_rwkv"])[0]; ssh = f(inp["state_rwkv_shift"])[0]
    sdl = f(inp["state_delta"])[0]; scv = f(inp["state_conv"])[0]
    in_maps = []
    for c in range(NCORES):
        sl = slice(c * NSQ, (c + 1) * NSQ)
        m = dict(shared)
        m["x"] = np.ascontiguousarray(np.concatenate([xp[c], xs[sl].reshape(NSQ * LS, D)], axis=0))
        m["s_rwkv"] = np.ascontiguousarray(srw[sl].reshape(NSQ * 512, 64))
        m["s_shift"] = np.ascontiguousarray(ssh[sl][:, pa])
        m["s_delta"] = np.ascontiguousarray(sdl[sl].reshape(NSQ * 512, 128))
        m["s_conv"] = np.ascontiguousarray(scv[sl].reshape(NSQ * 3, 1536))
        in_maps.append(m)
    nc = build(stage)
    res = run_bass_kernel_spmd(nc, in_maps, core_ids=list(range(NCORES)))
    R = res.results
    inv = np.argsort(pa)
    y = np.stack([r["y"] for r in R])
    y_prompt = np.ascontiguousarray(y[:, :SEQ, :])
    y_sample = np.ascontiguousarray(y[:, SEQ:, :].reshape(NCORES * NSQ, LS, D))
    qb = _qkv_from_b()
    rwkv_p = np.stack([r["rwkv_p"].reshape(8, 64, 64) for r in R])[None]
    shift_p = np.stack([r["tokrows_p"][3, :A_PROJ][inv] for r in R])[None]
    delta_p = np.stack([r["delta_p"].reshape(4, 128, 128) for r in R])[None]
    conv_p = np.stack([r["tokrows_p"][1:4, A_PROJ:][:, qb] for r in R])[None]
    rwkv_s = np.concatenate([r["rwkv_s"].reshape(NSQ, 8, 64, 64) for r in R])[None]
    shift_s = np.concatenate([r["tokrows_s"].reshape(NSQ, LS, 3840)[:, 3, :A_PROJ][:, inv] for r in R])[None]
    delta_s = np.concatenate([r["delta_s"].reshape(NSQ, 4, 128, 128) for r in R])[None]
    conv_s = np.concatenate([r["tokrows_s"].reshape(NSQ, LS, 3840)[:, 1:4, A_PROJ:][:, :, qb] for r in R])[None]
    outs = (y_prompt, y_sample, rwkv_p, shift_p, delta_p, conv_p, rwkv_s, shift_s, delta_s, conv_s)
    return tuple(np.ascontiguousarray(o.astype(np.float32)) for o in outs)
```

```python
from contextlib import ExitStack
import numpy as np
import concourse.bass as bass
import concourse.mybir as mybir
from concourse.bass_utils import run_bass_kernel_spmd

F32 = mybir.dt.float32
BF16 = mybir.dt.bfloat16
AF = mybir.ActivationFunctionType
ALU = mybir.AluOpType

NCORES = 8
DBG = {}
D = 1024
DFF = 2816
SEQ = 2048
NSQ = 16
LS = 4
NT = SEQ + NSQ * LS
A_PROJ = 1792
C0 = float(np.exp(-0.5))

COLS = {}
_nc = 0
for _name, _n in [("ffn1_norm", 8), ("mix_norm", 8), ("ffn2_norm", 8), ("final_norm", 8), ("mu", 14),
                  ("w0", 4), ("a0", 4), ("k_k", 4), ("k_a", 4), ("r_k", 4), ("lnx_w", 4), ("lnx_b", 4),
                  ("conv_w", 48), ("norm_w", 1), ("A_log", 4), ("dt_bias", 4)]:
    COLS[_name] = _nc
    _nc += _n
NCOL = _nc

CONST = {}
_k = 0
for _name, _n in [("ident", 128), ("incl", 128), ("strict", 128), ("tail", 128), ("incl_s", 128), ("strict_s", 128),
                  ("tail_s", 128), ("bones", 128), ("ones", 128), ("hmask", 256), ("rowseg", 16),
                  ("reset_s", 64)]:
    CONST[_name] = _k
    _k += _n
NCONST = _k


def make_consts2():
    sm = np.zeros((128, 16, 64), np.float32)
    for s in range(16):
        sm[:, s, s * LS:(s + 1) * LS] = 1
    return sm.reshape(128, 1024)


def make_consts():
    c = np.zeros((128, NCONST), np.float32)
    i = np.arange(128)
    seg = i // LS
    same = (seg[:, None] == seg[None, :])
    def put(name, a):
        c[:a.shape[0], CONST[name]:CONST[name] + a.shape[1]] = a
    put("ident", np.eye(128, dtype=np.float32))
    put("incl", (i[None, :] >= i[:, None]).astype(np.float32))
    put("strict", (i[None, :] > i[:, None]).astype(np.float32))
    put("tail", (i[:, None] > i[None, :]).astype(np.float32))
    put("incl_s", ((i[None, :] >= i[:, None]) & same).astype(np.float32))
    put("strict_s", ((i[None, :] > i[:, None]) & same).astype(np.float32))
    put("tail_s", ((i[:, None] > i[None, :]) & same).astype(np.float32))
    bo = np.zeros((128, 128), np.float32); bo[:64, :64] = 1; bo[64:, 64:] = 1
    put("bones", bo)
    put("ones", np.ones((128, 128), np.float32))
    hm = np.zeros((128, 256), np.float32); hm[:, 0:64] = 1; hm[:, 128 + 64:256] = 1
    put("hmask", hm)
    rs = np.zeros((128, 16), np.float32)
    for s in range(16):
        rs[s * LS:(s + 1) * LS, s] = 1
    put("rowseg", rs)
    r = np.ones((128, 64), np.float32); r[:, ::LS] = 0
    put("reset_s", r)
    return c


class K:
    def __init__(self, nc, es):
        self.nc = nc
        self.eng = {"pe": nc.tensor, "act": nc.scalar, "dve": nc.vector, "pool": nc.gpsimd, "sp": nc.sync}
        self.sem = {e: es.enter_context(nc.semaphore("sem_" + e)) for e in ("pe", "act", "dve", "pool")}
        self.cnt = {e: 0 for e in self.sem}
        self.ndma = 24
        self.dsem = [es.enter_context(nc.semaphore("dsem%d" % i)) for i in range(self.ndma)]
        self.dval = [0] * self.ndma
        self.di = {"sp": 0, "pool": 12}
        self.waited = {e: {} for e in self.eng}
        self.lastw = {}
        self.readers = {}
        self.nops = 0
        self._cap = None

    def begin(self):
        assert self._cap is None
        self._cap = []

    def end(self):
        c = self._cap
        self._cap = None
        return c

    def emit(self, items):
        for it in items:
            if it[0] == "op":
                self.op(*it[1:])
            else:
                self.dma(*it[1:])

    def emit_interleaved(self, a, b):
        na, nb = len(a), len(b)
        ia = ib = 0
        while ia < na or ib < nb:
            if ib >= nb or (ia < na and ia * nb <= ib * na):
                self.emit([a[ia]]); ia += 1
            else:
                self.emit([b[ib]]); ib += 1

    def _semh(self, key):
        return self.sem[key] if isinstance(key, str) else self.dsem[key[1]]

    def _deps(self, reads, writes):
        toks = []
        for k in reads:
            if k in self.lastw:
                toks.append(self.lastw[k])
        for k in writes:
            if k in self.lastw:
                toks.append(self.lastw[k])
            toks.extend(self.readers.get(k, ()))
        return toks

    def _wait(self, en, toks):
        need = {}
        for (sk, v) in toks:
            if sk == "pe" and en == "pe":
                continue
            if v > need.get(sk, 0):
                need[sk] = v
        w = self.waited[en]
        for sk, v in need.items():
            if w.get(sk, 0) < v:
                self.eng[en].wait_ge(self._semh(sk), v)
                w[sk] = v

    def _record(self, tok, reads, writes):
        for k in reads:
            self.readers.setdefault(k, []).append(tok)
        for k in writes:
            self.lastw[k] = tok
            self.readers[k] = []

    @staticmethod
    def _excl(reads, writes):
        pr = [x for x in reads if isinstance(x, tuple) and x[0] == "ps"]
        if pr:
            reads = [x for x in reads if x not in pr]
            writes = list(writes) + [x for x in pr if x not in writes]
        return reads, writes

    def op(self, en, fn, reads=(), writes=()):
        if self._cap is not None:
            self._cap.append(("op", en, fn, list(reads), list(writes)))
            return
        reads, writes = self._excl(reads, writes)
        self._wait(en, self._deps(reads, writes))
        ins = fn(self.eng[en])
        self.cnt[en] += 1
        ins.then_inc(self.sem[en], 1)
        self._record((en, self.cnt[en]), reads, writes)
        self.nops += 1

    def dma(self, q, out, in_, reads=(), writes=()):
        if self._cap is not None:
            self._cap.append(("dma", q, out, in_, list(reads), list(writes)))
            return
        toks = self._deps(reads, writes)
        i = self.di[q]
        base = 0 if q == "sp" else 12
        self.di[q] = base + (i - base + 1) % 12
        if self.dval[i] > 0:
            toks.append((("dma", i), self.dval[i]))
        self._wait(q, toks)
        ins = self.eng[q].dma_start(out=out, in_=in_)
        self.dval[i] += 16
        ins.then_inc(self.dsem[i], 16)
        self._record((("dma", i), self.dval[i]), reads, writes)
        self.nops += 1

    def barrier(self):
        toks = [(e, self.cnt[e]) for e in self.cnt if self.cnt[e] > 0]
        toks += [(("dma", i), self.dval[i]) for i in range(self.ndma) if self.dval[i] > 0]
        for en in self.eng:
            self._wait(en, toks)

    def finish(self):
        toks = [(("dma", i), self.dval[i]) for i in range(self.ndma) if self.dval[i] > 0]
        toks += [(e, self.cnt[e]) for e in self.cnt if self.cnt[e] > 0]
        self._wait("sp", toks)


def build(stage=99):
    nc = bass.Bass("TRN2", target_bir_lowering=False)
    es = ExitStack()

    def din(name, shape):
        return nc.dram_tensor(name, list(shape), F32, kind="ExternalInput").ap()

    def dout(name, shape):
        return nc.dram_tensor(name, list(shape), F32, kind="ExternalOutput").ap()

    x_d = din("x", [NT, D])
    srw_d = din("s_rwkv", [NSQ * 512, 64])
    ssh_d = din("s_shift", [NSQ, A_PROJ])
    sdl_d = din("s_delta", [NSQ * 512, 128])
    scv_d = din("s_conv", [NSQ * 3, 1536])
    w1g_d = din("w1g", [D, DFF]); w1u_d = din("w1u", [D, DFF]); w1d_d = din("w1d", [DFF, D])
    w2g_d = din("w2g", [D, DFF]); w2u_d = din("w2u", [D, DFF]); w2d_d = din("w2d", [DFF, D])
    win_d = din("w_in", [D, 5896])
    pja_d = din("proj_a", [512, D]); pjb_d = din("proj_b", [512, D]); wo_d = din("w_out", [D, D])
    lw2_d = din("lw2", [64, 512]); la2_d = din("la2", [64, 512]); lg2_d = din("lg2", [128, 512])
    cols_d = din("cols", [128, NCOL]); consts_d = din("consts", [128, NCONST]); consts2_d = din("consts2", [128, 1024])

    winb_d = nc.dram_tensor("w_in_bf", [D, 5896], BF16, kind="Internal").ap()
    pjab_d = nc.dram_tensor("proj_a_bf", [512, D], BF16, kind="Internal").ap()
    pjbb_d = nc.dram_tensor("proj_b_bf", [512, D], BF16, kind="Internal").ap()
    wob_d = nc.dram_tensor("w_out_bf", [D, D], BF16, kind="Internal").ap()
    y_d = dout("y", [NT, D])
    rwp_d = dout("rwkv_p", [512, 64]); dlp_d = dout("delta_p", [512, 128])
    rws_d = dout("rwkv_s", [NSQ * 512, 64]); dls_d = dout("delta_s", [NSQ * 512, 128])
    trp_d = dout("tokrows_p", [4, 3840]); trs_d = dout("tokrows_s", [NSQ * LS, 3840])

    with es:
        k = K(nc, es)

        def sb(name, shape, dt=F32, stack=es):
            return stack.enter_context(nc.sbuf_tensor("sb_" + name, list(shape), dt))

        ps = [es.enter_context(nc.psum_tensor("ps%d" % i, [128, 512], F32)) for i in range(8)]
        psi = [0, 0]
        psmode = [0]

        def nps():
            if psmode[0] == 0:
                i = psi[0]
                psi[0] = (i + 1) % 8
                return i
            if psmode[0] == 1:
                i = psi[0] % 4
                psi[0] = (i + 1) % 4
                return i
            i = psi[1]
            psi[1] = (i + 1) % 4
            return 4 + i

        h = sb("h", [128, 8, NT])
        cols = sb("cols", [128, NCOL])
        cst = sb("cst", [128, NCONST])
        cb = sb("cb", [128, 5 * 128], BF16)
        pending_casts = []
        for r8 in range(8):
            pending_casts.append(lambda r8=r8: k.dma("pool", winb_d[r8 * 128:(r8 + 1) * 128, :], win_d[r8 * 128:(r8 + 1) * 128, :], writes=[("winb", r8)]))
        pending_casts.append(lambda: k.dma("pool", pjab_d[:, :], pja_d[:, :], writes=["pjab"]))
        pending_casts.append(lambda: k.dma("pool", pjbb_d[:, :], pjb_d[:, :], writes=["pjbb"]))
        for r2 in range(2):
            pending_casts.append(lambda r2=r2: k.dma("pool", wob_d[r2 * 512:(r2 + 1) * 512, :], wo_d[r2 * 512:(r2 + 1) * 512, :], writes=[("wob", r2)]))

        def one_cast():
            if pending_casts:
                pending_casts.pop(0)()
        k.dma("sp", cols[:], cols_d[:, :], writes=["cols"])
        k.dma("sp", cst[:], consts_d[:, :], writes=["cst"])

        def cc(name, n=128, rows=128):
            o = CONST[name]
            return cst[:rows, o:o + n]

        def col(name, j=0):
            o = COLS[name] + j
            return cols[:, o:o + 1]

        for j, nm in enumerate(["ident", "ones", "bones"]):
            k.op("dve", lambda e, j=j, nm=nm: e.tensor_copy(out=cb[:, j * 128:(j + 1) * 128], in_=cc(nm)),
                 reads=["cst"], writes=["cb"])
        ident_f = cc("ident")
        ones_b = cb[:, 128:256]
        bones_b = cb[:, 256:384]

        TILES = [(t * 512, 512, 1, 512) for t in range(4)] + [(SEQ, NSQ * LS, NSQ, LS)]

        with ExitStack() as st:
            xin = [sb("xin%d" % i, [128, D], F32, st) for i in range(2)]
            for blk in range(17):
                rows = 128 if blk < 16 else 64
                xb = xin[blk % 2]
                k.dma("sp", xb[:rows, :], x_d[blk * 128: blk * 128 + rows, :], writes=[("xin", blk % 2)])
                for half in range(2):
                    b = nps()
                    for m4 in range(4):
                        m = half * 4 + m4
                        k.op("pe", lambda e, b=b, m4=m4, m=m, xb=xb, rows=rows: e.transpose(
                            ps[b][:, m4 * 128: m4 * 128 + rows], xb[:rows, m * 128:(m + 1) * 128], ident_f[:rows, :rows]),
                            reads=[("xin", blk % 2), "cst"], writes=[("ps", b)])
                    k.op("act", lambda e, b=b, half=half, blk=blk, rows=rows: e.activation(
                        out=h[:, half * 4:(half + 1) * 4, blk * 128: blk * 128 + rows],
                        in_=ps[b][:].rearrange("p (a c) -> p a c", a=4)[:, :, :rows], func=AF.Copy),
                        reads=[("ps", b)], writes=[("h", blk)])
            k.barrier()

        def rms_stats(c0, N, rstd, tmp_sq, hkeys, rkey="rstd", sqkeys=(("sq", 0), ("sq", 1))):
            b = nps()
            for m in range(8):
                sq = tmp_sq[m % 2]
                k.op("act", lambda e, sq=sq, m=m: e.activation(out=sq[:, :N], in_=h[:, m, c0:c0 + N], func=AF.Square),
                     reads=hkeys, writes=[sqkeys[m % 2]])
                k.op("pe", lambda e, sq=sq, m=m, b=b: e.matmul(ps[b][:, :N], ones_b, sq[:, :N], start=(m == 0), stop=(m == 7)),
                     reads=[sqkeys[m % 2], "cb"], writes=[("ps", b)])
            k.op("act", lambda e: e.activation(out=rstd[:, :N], in_=ps[b][:, :N], func=AF.Sqrt, bias=1e-6, scale=1.0 / D),
                 reads=[("ps", b)], writes=[rkey])
            k.op("dve", lambda e: e.reciprocal(out=rstd[:, :N], in_=rstd[:, :N]), reads=[rkey], writes=[rkey])

        def hkeys_of(c0, N):
            return [("h", bk) for bk in range(c0 // 128, (c0 + N + 127) // 128)]

        def ffn(wg_d, wu_d, wd_d, normname, tag):
            with ExitStack() as st:
                u = sb("u" + tag, [128, 8, NT], BF16, st)
                hid = sb("hid" + tag, [128, 11, NT], BF16, st)
                WG = [sb("wg%d" % i + tag, [128, 8, 512], BF16, st) for i in range(2)]
                WU = [sb("wu%d" % i + tag, [128, 8, 512], BF16, st) for i in range(2)]
                WD = [sb("wd%d" % i + tag, [128, 11, 128], BF16, st) for i in range(2)]
                rstd = sb("rstd" + tag, [128, 512], F32, st)
                sq = [sb("sq%d" % i + tag, [128, 512], BF16, st) for i in range(2)]
                sg = [sb("sg%d" % i + tag, [128, 512], F32, st) for i in range(2)]
                def norm_tile(ti):
                    c0, N = TILES[ti][0], TILES[ti][1]
                    rms_stats(c0, N, rstd, sq, hkeys_of(c0, N))
                    for m in range(8):
                        k.op("dve", lambda e, m=m: e.scalar_tensor_tensor(
                            out=u[:, m, c0:c0 + N], in0=h[:, m, c0:c0 + N], scalar=col(normname, m), in1=rstd[:, :N],
                            op0=ALU.mult, op1=ALU.mult), reads=hkeys_of(c0, N) + ["rstd", "cols"], writes=[("u", ti)])
                norm_tile(0)
                gi = 0
                di = 0
                sgi = 0
                for half in range(2):
                    groups = [(0, 4), (4, 4), (8, 3)]
                    for (j0, nj) in groups:
                        ff0 = (half * 11 + j0) * 128
                        wgt, wut = WG[gi % 2], WU[gi % 2]
                        k.dma("pool", wgt[:, :, :nj * 128], wg_d[:, ff0:ff0 + nj * 128].rearrange("(kc p) c -> p kc c", p=128),
                              writes=[("wg", gi % 2)])
                        k.dma("pool", wut[:, :, :nj * 128], wu_d[:, ff0:ff0 + nj * 128].rearrange("(kc p) c -> p kc c", p=128),
                              writes=[("wu", gi % 2)])
                        one_cast()
                        for ti, (c0, N, _, _) in enumerate(TILES):
                            if gi == 0 and ti + 1 < len(TILES):
                                norm_tile(ti + 1)
                            for j in range(nj):
                                b1 = nps(); b2 = nps()
                                def mm(e, wt, b, j=j):
                                    ins = None
                                    for kc in range(8):
                                        ins = e.matmul(ps[b][:, :N], wt[:, kc, j * 128:(j + 1) * 128], u[:, kc, c0:c0 + N],
                                                       start=(kc == 0), stop=(kc == 7))
                                    return ins
                                k.op("pe", lambda e, mm=mm, wgt=wgt, b1=b1: mm(e, wgt, b1),
                                     reads=[("wg", gi % 2), ("u", ti)], writes=[("ps", b1)])
                                k.op("pe", lambda e, mm=mm, wut=wut, b2=b2: mm(e, wut, b2),
                                     reads=[("wu", gi % 2), ("u", ti)], writes=[("ps", b2)])
                                sgt = sg[sgi % 2]
                                k.op("act", lambda e, sgt=sgt, b1=b1: e.activation(out=sgt[:, :N], in_=ps[b1][:, :N], func=AF.Silu),
                                     reads=[("ps", b1)], writes=[("sg", sgi % 2)])
                                k.op("dve", lambda e, sgt=sgt, b2=b2, jj=j0 + j: e.tensor_tensor(
                                    out=hid[:, jj, c0:c0 + N], in0=sgt[:, :N], in1=ps[b2][:, :N], op=ALU.mult),
                                    reads=[("sg", sgi % 2), ("ps", b2)], writes=[("hid", ti, j0 + j)])
                                sgi += 1
                        gi += 1
                    for piece in range(8):
                        wdt = WD[di % 2]
                        r0 = half * 11 * 128
                        k.dma("pool", wdt[:, :, :], wd_d[r0:r0 + 11 * 128, piece * 128:(piece + 1) * 128].rearrange("(kc p) c -> p kc c", p=128),
                              writes=[("wd", di % 2)])
                        one_cast()
                        for ti, (c0, N, _, _) in enumerate(TILES):
                            for mm_ in range(1):
                                m = piece
                                b = nps()
                                def mmd(e, wdt=wdt, b=b, mm_=mm_, c0=c0, N=N):
                                    ins = None
                                    for kc in range(11):
                                        ins = e.matmul(ps[b][:, :N], wdt[:, kc, mm_ * 128:(mm_ + 1) * 128], hid[:, kc, c0:c0 + N],
                                                       start=(kc == 0), stop=(kc == 10))
                                    return ins
                                k.op("pe", mmd, reads=[("wd", di % 2)] + [("hid", ti, jj) for jj in range(11)], writes=[("ps", b)])
                                k.op("dve", lambda e, b=b, m=m, c0=c0, N=N: e.scalar_tensor_tensor(
                                    out=h[:, m, c0:c0 + N], in0=ps[b][:, :N], scalar=0.5, in1=h[:, m, c0:c0 + N],
                                    op0=ALU.mult, op1=ALU.add), reads=[("ps", b)] + hkeys_of(c0, N), writes=hkeys_of(c0, N))
                        di += 1
            k.barrier()

        if stage >= 1:
            ffn(w1g_d, w1u_d, w1d_d, "ffn1_norm", "a")
        def act(out, in_, func, r, w, **kw):
            k.op("act", lambda e: e.activation(out=out, in_=in_, func=func, **kw), reads=r, writes=w)

        def tt(en, out, a, b, op, r, w):
            k.op(en, lambda e: e.tensor_tensor(out=out, in0=a, in1=b, op=op), reads=r, writes=w)

        def ts(en, out, a, s1, s2, op0, op1, r, w):
            if s2 is None:
                k.op(en, lambda e: e.tensor_scalar(out, a, s1, None, op0), reads=r, writes=w)
            else:
                k.op(en, lambda e: e.tensor_scalar(out, a, s1, s2, op0, op1), reads=r, writes=w)

        def stt(en, out, in0, sc, in1, op0, op1, r, w):
            k.op(en, lambda e: e.scalar_tensor_tensor(out=out, in0=in0, scalar=sc, in1=in1, op0=op0, op1=op1),
                 reads=r, writes=w)

        def mm(out, lhsT, rhs, r, w, start=True, stop=True):
            k.op("pe", lambda e: e.matmul(out, lhsT, rhs, start=start, stop=stop), reads=r, writes=w)

        def mmg(items, r, w):
            def f(e):
                ins = None
                n = len(items)
                for i, (o, l, rh) in enumerate(items):
                    ins = e.matmul(o, l, rh, start=(i == 0), stop=(i == n - 1))
                return ins
            k.op("pe", f, reads=r, writes=w)

        def tr(out, in_, rows, r, w):
            k.op("pe", lambda e: e.transpose(out, in_, ident_f[:rows, :rows]), reads=r + ["cst"], writes=w)

        def copy(en, out, in_, r, w):
            if en == "act":
                act(out, in_, AF.Copy, r, w)
            else:
                k.op(en, lambda e: e.tensor_copy(out=out, in_=in_), reads=r, writes=w)

        def mixer():
            with ExitStack() as st:
                def T(name, shape, dt=F32, stack=st):
                    return sb(name, shape, dt, stack)
                lora_wa = T("lora_wa", [128, 512], BF16)
                lg2 = T("lg2s", [128, 512], BF16)
                wab = T("wab", [128, 8, 8], BF16)
                pjbuf = [T("pjbuf%d" % i, [128, 2, 4, 128], BF16) for i in range(2)]
                WI = [T("wi%d" % i, [128, 8, 256], BF16) for i in range(2)]
                u_t = T("u_t", [128, 8, 512], BF16)
                oab = T("oab", [128, 8, 512], BF16)
                rowst = T("rowst", [128, 256])
                halo_a = T("halo_a", [128, 14]); halo_c = T("halo_c", [128, 12, 3])
                sshT = T("sshT", [128, 14, 16]); scvT = T("scvT", [128, 12, 48])
                Zr = T("Zr", [128, 4, 128]); Zrb = T("Zrb", [128, 4, 2, 128], BF16)
                Zg = T("Zg", [128, 4, 128]); Zgb = T("Zgb", [128, 4, 2, 128], BF16)
                omka = T("omka", [128, 4]); eA = T("eA", [128, 4]); dtb = T("dtb", [128, 4]); omu = T("omu", [128, 14])
                dc = T("dc", [128, 16])
                k.dma("pool", lora_wa[0:64, :], lw2_d[:, :], writes=["lora_wa"])
                k.dma("pool", lora_wa[64:128, :], la2_d[:, :], writes=["lora_wa"])
                k.dma("pool", lg2[:], lg2_d[:, :], writes=["lg2"])
                k.dma("sp", wab[:], winb_d[:, 5888:5896].rearrange("(kc p) c -> p kc c", p=128), reads=[("winb", r_) for r_ in range(8)], writes=["wab"])
                for t_, nm in [(halo_a, "halo_a"), (halo_c, "halo_c")]:
                    k.op("pool", lambda e, t_=t_: e.memset(t_[:], 0.0), writes=[nm])
                for t_, nm in [(Zr, "Zr"), (Zg, "Zg")]:
                    k.op("pool", lambda e, t_=t_: e.memset(t_[:], 0.0), writes=[(nm, j) for j in range(4)])
                for t_, nm in [(Zrb, "Zrb"), (Zgb, "Zgb")]:
                    k.op("pool", lambda e, t_=t_: e.memset(t_[:], 0.0), writes=[(nm, j, p_) for j in range(4) for p_ in range(2)])
                ts("dve", omka[:], cols[:, COLS["k_a"]:COLS["k_a"] + 4], -1.0, 1.0, ALU.mult, ALU.add, ["cols"], ["omka"])
                act(eA[:], cols[:, COLS["A_log"]:COLS["A_log"] + 4], AF.Exp, ["cols"], ["eA"])
                ts("dve", omu[:], cols[:, COLS["mu"]:COLS["mu"] + 14], -1.0, 1.0, ALU.mult, ALU.add, ["cols"], ["omu"])
                copy("dve", dtb[:], cols[:, COLS["dt_bias"]:COLS["dt_bias"] + 4], ["cols"], ["dtb"])
                with ExitStack() as s0:
                    ld1 = T("ld1", [16, A_PROJ], F32, s0); ld2 = T("ld2", [48, 1536], F32, s0)
                    k.dma("sp", ld1[:], ssh_d[:, :], writes=["ld1"])
                    k.dma("sp", ld2[:], scv_d[:, :], writes=["ld2"])
                    b = nps()
                    for c in range(14):
                        tr(ps[b][:, c * 16:(c + 1) * 16], ld1[:, c * 128:(c + 1) * 128], 16, ["ld1"], [("ps", b)])
                    copy("act", sshT[:].rearrange("p a s -> p (a s)"), ps[b][:, :224], [("ps", b)], ["sshT"])
                    for hf in range(2):
                        b = nps()
                        for c6 in range(6):
                            c = hf * 6 + c6
                            tr(ps[b][:, c6 * 48:(c6 + 1) * 48], ld2[:, c * 128:(c + 1) * 128], 48, ["ld2"], [("ps", b)])
                        copy("act", scvT[:, hf * 6:(hf + 1) * 6, :].rearrange("p a s -> p (a s)"), ps[b][:, :288],
                             [("ps", b)], ["scvT"])
                    k.barrier()

                bones_f = cc("bones"); ones_f = cc("ones"); ident_b = cb[:, 0:128]

                for ti, (c0, N, nseq, L) in enumerate(TILES):
                    if ti not in DBG.get('tiles', range(5)):
                        continue
                    samp = nseq > 1
                    C = 64 if samp else 128
                    nch = 1 if samp else 4
                    nS = NSQ if samp else 1
                    nsg, lsg = (NSQ, LS) if samp else (4, 128)
                    nlev = 1 if samp else 6
                    mI = cc("incl_s" if samp else "incl")[:C, :C]
                    mS = cc("strict_s" if samp else "strict")[:C, :C]
                    mT = cc("tail_s" if samp else "tail")[:C, :C]
                    hk = hkeys_of(c0, N)
                    with ExitStack() as ts_:
                        def TT(name, shape, dt=F32):
                            return sb(name + "_%d" % ti, shape, dt, ts_)
                        tmp = [TT("tmp%d" % i, [128, N]) for i in range(12)]
                        tk = ["tmp%d" % i for i in range(12)]
                        pa_c = TT("pa_c", [128, N + 3 * nseq]); pa_c2 = TT("pa_c2", [128, N + 3 * nseq])
                        r_t = TT("r_t", [128, N]); k_t = TT("k_t", [128, N]); v_t = TT("v_t", [128, N]); o_t = TT("o_t", [128, N])
                        z_t = TT("z_t", [128, N])
                        lora_in = TT("lora_in", [128, N], BF16); sig_gd = TT("sig_gd", [128, N], BF16)
                        sqb = TT("sqb", [128, N], BF16); bnb = TT("bnb", [128, N], BF16); sqp = TT("sqp", [128, N], BF16)
                        atB = TT("atB", [128, N], BF16); btB = TT("btB", [128, N], BF16); ktB = TT("ktB", [128, N], BF16); rtB = TT("rtB", [128, N], BF16)
                        gtB = TT("gtB", [128, N]); bnB = TT("bnB", [128, N]); dcB = TT("dcB", [128, 16])
                        at = TT("at", [128, N], BF16); bt = TT("bt", [128, N], BF16); kt = TT("kt", [128, N], BF16)
                        rt = TT("rt", [128, N], BF16)
                        atz = TT("atz", [128, 2, N], BF16); btz = TT("btz", [128, 2, N], BF16); rtz = TT("rtz", [128, 2, N], BF16)
                        tok = TT("tok", [128, nch, 5, 128], BF16); Vz = TT("Vz", [128, nch, 2, 128], BF16)
                        cm1 = [TT("cm1_%d" % i, [128, 4, 128], BF16) for i in range(nch)]
                        nt2 = [TT("nt2_%d" % i, [128, 2, 128], BF16) for i in range(nch)]
                        cm3 = [TT("cm3_%d" % i, [128, 4, 128], BF16) for i in range(nch)]
                        Tt = [TT("Tt%d" % i, [128, 2, 128], BF16) for i in range(nch)]
                        PQ = [TT("PQ%d" % i, [128, 4, 128], BF16) for i in range(nch)]
                        Xb = TT("Xb", [128, 128], BF16); Wz = TT("Wz", [128, 2, 128], BF16); Wb = TT("Wb", [128, 128], BF16)
                        gtok = TT("gtok", [128, nch, 4]); btok = TT("btok", [128, nch, 4]); sptmp = TT("sptmp", [128, nch, 4])
                        if samp:
                            g_rep, b_rep, Gs, eg, ek, dmi, dms = [[TT(nm_, [128, 128])] for nm_ in ("g_rep", "b_rep", "Gs", "eg", "ek", "dmi", "dms")]
                            segm_t = TT("segm", [128, 1024])
                            k.dma("sp", segm_t[:], consts2_d[:, :], writes=["segm"])
                            segm = segm_t[:, :].rearrange("p (s i) -> p s i", s=NSQ)
                        else:
                            g_rep, b_rep, Gs, eg, ek, dmi, dms = [[tmp[5 + j_][:, q_ * 128:(q_ + 1) * 128] for q_ in range(4)] for j_ in range(7)]
                        dmt_t = TT("dmt", [128, nch, 128]); kbf_t = TT("kbf", [128, nch, 128]); kbb_t = TT("kbb", [128, nch, 128], BF16)
                        dmt = [dmt_t[:, q_, :] for q_ in range(nch)]; kbf = [kbf_t[:, q_, :] for q_ in range(nch)]
                        kbb = [kbb_t[:, q_, :] for q_ in range(nch)]
                        merged = None
                        if samp:
                            Zs = TT("Zs", [128, NSQ, 128]); Zsb = TT("Zsb", [128, NSQ, 128], BF16)
                            bigf = TT("bigf", [128, NSQ, 128]); stg = TT("stg", [128, NSQ, 64])
                            sbd = [TT("sbd%d" % i, [128, NSQ, 128]) for i in range(2)]
                            atm = TT("atm", [128, NSQ, 64], BF16); rtm = TT("rtm", [128, NSQ, 64], BF16)
                            Wm = TT("Wm", [128, NSQ, 128], BF16); Vm = TT("Vm", [128, NSQ, 128], BF16)
                        else:
                            bigf = TT("bigf", [128, 1, 128])

                        ATs = [(at, bt, kt, rt), (atB, btB, ktB, rtB)]
                        GTs = [(tmp[2], tk[2]), (gtB, "gtB")]
                        BNs = [(tmp[11], tk[11]), (bnB, "bnB")]
                        DCs = [dc, dcB]

                        def v3(ap):
                            return ap.rearrange("p (s l) -> p s l", s=nseq)

                        def sg3(ap):
                            return ap.rearrange("p (s l) -> p s l", s=nsg)

                        rms_stats(c0, N, tmp[0], [at, bt], hk, rkey=tk[0], sqkeys=("at", "bt"))
                        for m in range(8):
                            stt("dve", u_t[:, m, :N], h[:, m, c0:c0 + N], col("mix_norm", m), tmp[0][:, :N], ALU.mult, ALU.mult,
                                hk + [tk[0], "cols"], ["u_t"])

                        need_rows = (ti >= 3)
                        M_rows = 64 if samp else 4
                        rows_lo = 0 if samp else 508
                        state = {"g": -1, "issued": set()}

                        def issue_load(g):
                            if g > 22 or g in state["issued"]:
                                return
                            state["issued"].add(g)
                            wt = WI[g % 2]
                            ncols = 256
                            k.dma("sp", wt[:, :, :ncols], winb_d[:, g * 256: g * 256 + ncols].rearrange("(kc p) c -> p kc c", p=128),
                                  reads=[("winb", r_) for r_ in range(8)], writes=[("wi", g % 2)])

                        def load_group(g):
                            wt = WI[g % 2]
                            issue_load(g)
                            if need_rows and g < 15:
                                b = nps()
                                mmg([(ps[b][:M_rows, :256], u_t[:, kc, rows_lo:rows_lo + M_rows], wt[:, kc, :]) for kc in range(8)],
                                    ["u_t", ("wi", g % 2)], [("ps", b)])
                                copy("act", rowst[:M_rows, :], ps[b][:M_rows, :256], [("ps", b)], ["rowst"])
                                if samp:
                                    k.dma("sp", trs_d[:, g * 256:(g + 1) * 256], rowst[:64, :], reads=["rowst"])
                                else:
                                    k.dma("sp", trp_d[:, g * 256:(g + 1) * 256], rowst[:4, :], reads=["rowst"])

                        def pchunk(c):
                            g = c // 2
                            if g != state["g"]:
                                load_group(g)
                                state["g"] = g
                                issue_load(g + 1)
                            wt = WI[g % 2]
                            off = (c % 2) * 128
                            b = nps()
                            mmg([(ps[b][:, :N], wt[:, kc, off:off + 128], u_t[:, kc, :N]) for kc in range(8)],
                                ["u_t", ("wi", g % 2)], [("ps", b)])
                            return b

                        def a_chunk(c, dest, dkey):
                            b = pchunk(c)
                            pab, pak = ((pa_c, "pa_c"), (pa_c2, "pa_c2"))[c % 2]
                            pv = pab[:, :N + nseq].rearrange("p (s l) -> p s l", s=nseq)
                            copy("act", pv[:, :, 1:], v3(ps[b][:, :N]), [("ps", b)], [pak])
                            act(v3(dest), v3(ps[b][:, :N]), AF.Copy, [("ps", b), "omu"], [dkey], scale=omu[:, c:c + 1])
                            if samp:
                                copy("pool", pv[:, :, 0], sshT[:, c, :], ["sshT"], [pak])
                            else:
                                copy("pool", pv[:, :, 0], halo_a[:, c:c + 1], ["halo_a"], [pak])
                                copy("pool", halo_a[:, c:c + 1], pv[:, :, L], [pak], ["halo_a"])
                            stt("dve", v3(dest), pv[:, :, 0:L], col("mu", c), v3(dest), ALU.mult, ALU.add,
                                [pak, "cols", dkey], [dkey])

                        def g_load(hh, role, dest, dk):
                            cq = role * 4 + hh
                            b = pchunk(14 + hh * 4 + role)
                            pab, pak = ((pa_c, "pa_c"), (pa_c2, "pa_c2"))[role % 2]
                            xv3 = pab[:, :N + 3 * nseq].rearrange("p (s l) -> p s l", s=nseq)
                            copy("act", xv3[:, :, 3:], v3(ps[b][:, :N]), [("ps", b)], [pak])
                            if samp:
                                copy("pool", xv3[:, :, 0:3], scvT[:, cq, :].rearrange("p (s i) -> p s i", s=NSQ), ["scvT"], [pak])
                            else:
                                copy("pool", xv3[:, :, 0:3], halo_c[:, cq:cq + 1, :], ["halo_c"], [pak])
                                copy("pool", halo_c[:, cq:cq + 1, :], xv3[:, :, L:L + 3], [pak], ["halo_c"])

                        def g_conv(hh, role, dest, dk):
                            cq = role * 4 + hh
                            pab, pak = ((pa_c, "pa_c"), (pa_c2, "pa_c2"))[role % 2]
                            xv3 = pab[:, :N + 3 * nseq].rearrange("p (s l) -> p s l", s=nseq)
                            ts("dve", v3(dest[:]), xv3[:, :, 0:L], col("conv_w", 0 * 12 + cq), None, ALU.mult, None, [pak, "cols"], [dk])
                            for i in range(1, 4):
                                stt("dve", v3(dest[:]), xv3[:, :, i:i + L], col("conv_w", i * 12 + cq), v3(dest[:]),
                                    ALU.mult, ALU.add, [pak, "cols", dk], [dk])

                        def g_silu(dest, dk):
                            act(dest[:], dest[:], AF.Silu, [dk], [dk])


                        def gdn_prefetch(hh):
                            g_load(hh, 0, r_t, "r_t"); g_load(hh, 1, k_t, "k_t")
                            g_conv(hh, 0, r_t, "r_t")
                            g_load(hh, 2, v_t, "v_t")
                            g_conv(hh, 1, k_t, "k_t")
                            g_silu(r_t, "r_t")
                            g_conv(hh, 2, v_t, "v_t")
                            g_silu(k_t, "k_t"); g_silu(v_t, "v_t")

                        issue_load(0)
                        if samp:
                            srv = srw_d.rearrange("(s h v) k -> h v s k", s=NSQ, h=8)
                            sdv = sdl_d.rearrange("(s h k) v -> h k s v", s=NSQ, h=4)

                            def load_rwkv_state(u_):
                                t_ = sbd[u_ % 2]; tkey = ("sbd", u_ % 2)
                                k.dma("sp", t_[0:64, :, 0:64], srv[2 * u_], writes=[tkey])
                                k.dma("sp", t_[64:128, :, 64:128], srv[2 * u_ + 1], writes=[tkey])

                            def load_gdn_state(h_):
                                k.dma("sp", sbd[h_ % 2][:], sdv[h_], writes=[("sbd", h_ % 2)])

                            for i_ in range(2):
                                k.op("pool", lambda e, i_=i_: e.memset(sbd[i_][:], 0.0), writes=[("sbd", i_)])
                            load_rwkv_state(0)
                        a_chunk(0, tmp[1][:], tk[1])
                        act(lora_in[0:64, :], tmp[1][0:64, :], AF.Tanh, [tk[1]], ["lora_in"])
                        copy("pool", lora_in[64:128, :], tmp[1][64:128, :], [tk[1]], ["lora_in"])
                        a_chunk(1, tmp[1][:], tk[1])
                        act(sig_gd[:], tmp[1][:], AF.Sigmoid, [tk[1]], ["sig_gd"])
                        pre_ops, fin_ops, chunk_ops, gpf_ops = [], [], [], []
                        for u in range(DBG.get('nu', 4)):
                            p_ = u % 2
                            at_, bt_, kt_, rt_ = ATs[p_]
                            kat, kbt, kkt, krt = ["%s%d" % (n_, p_) for n_ in ("at", "bt", "kt", "rt")] if p_ else ["at", "bt", "kt", "rt"]
                            dc_, kdc = DCs[p_], "dc%d" % p_
                            psmode[0] = 2 if u > 0 else 0
                            k.begin()
                            a_chunk(2 + 3 * u, r_t[:], "r_t"); a_chunk(3 + 3 * u, k_t[:], "k_t"); a_chunk(4 + 3 * u, v_t[:], "v_t")
                            if samp and u > 0:
                                load_rwkv_state(u)
                            ucol = slice(u * 128, (u + 1) * 128)
                            sigw, a_t, g_t, kk, kkn, kmod, b_t, cl, e_t, Bh, Kh, bonus = tmp
                            ksw, ka, kg, kkk, kkkn, kkmod, kb_, kcl, ke, kBh, kKh, kbon = tk
                            g_t, kg = GTs[p_]
                            bonus, kbon = BNs[p_]
                            e1, e2, e3 = tmp[3], Bh, Kh
                            ke1, ke2, ke3 = tk[3], kBh, kKh
                            act(sqb[:], k_t[:], AF.Square, ["k_t", "cols"], ["sqb"], scale=col("k_k", u))
                            bw = nps()
                            mm(ps[bw][:, :N], lora_wa[0:64, ucol], lora_in[0:64, :], ["lora_wa", "lora_in"], [("ps", bw)])
                            ba = nps()
                            mm(ps[ba][:, :N], lora_wa[64:128, ucol], lora_in[64:128, :], ["lora_wa", "lora_in"], [("ps", ba)])
                            bg = nps()
                            mm(ps[bg][:, :N], lg2[:, ucol], sig_gd[:], ["lg2", "sig_gd"], [("ps", bg)])
                            bn = nps()
                            mm(ps[bn][:, :N], bones_b, sqb[:], ["sqb", "cb"], [("ps", bn)])
                            act(sigw[:], ps[bw][:, :N], AF.Sigmoid, [("ps", bw), "cols"], [ksw], bias=col("w0", u))
                            act(a_t[:], ps[ba][:, :N], AF.Sigmoid, [("ps", ba), "cols"], [ka], bias=col("a0", u))
                            copy("act", g_t[:], ps[bg][:, :N], [("ps", bg)], [kg])
                            ts("dve", e_t[:], ps[bn][:, :N], 1e-24, None, ALU.max, None, [("ps", bn)], [ke])
                            if samp:
                                k.op("dve", lambda e: e.tensor_tensor_scan(out=cl[:], data0=cc("reset_s", 64), data1=sigw[:], initial=0.0,
                                                                           op0=ALU.mult, op1=ALU.add), reads=[ksw, "cst"], writes=[kcl])
                            else:
                                for q in range(4):
                                    k.op("dve", lambda e, q=q: e.tensor_tensor_scan(
                                        out=cl[:, q * 128:(q + 1) * 128], data0=ones_f, data1=sigw[:, q * 128:(q + 1) * 128], initial=0.0,
                                        op0=ALU.mult, op1=ALU.add), reads=[ksw, "cst"], writes=[kcl])
                            act(e_t[:], e_t[:], AF.Ln, [ke], [ke])
                            act(e_t[:], e_t[:], AF.Exp, [ke], [ke], scale=-0.5)
                            ts("dve", kmod[:], a_t[:], col("k_a", u), omka[:, u:u + 1], ALU.mult, ALU.add, [ka, "cols", "omka"], [kkmod])
                            tt("pool", e1[:], cl[:], sigw[:], ALU.subtract, [kcl, ksw], [ke1])
                            tt("pool", kmod[:], kmod[:], k_t[:], ALU.mult, [kkmod, "k_t"], [kkmod])
                            stt("dve", kkn[:], k_t[:], col("k_k", u), e_t[:], ALU.mult, ALU.mult, ["k_t", "cols", ke], [kkkn])
                            act(e1[:], e1[:], AF.Exp, [ke1], [ke1], scale=-C0)
                            act(e2[:], cl[:], AF.Exp, [kcl], [ke2], scale=C0)
                            act(e3[:], cl[:], AF.Exp, [kcl], [ke3], scale=-C0)
                            tt("pool", b_t[:], kkn[:], a_t[:], ALU.mult, [kkkn, ka], [kb_])
                            stt("dve", bnb[:], r_t[:], col("r_k", u), kmod[:], ALU.mult, ALU.mult, ["r_t", kkmod, "cols"], ["bnb"])
                            bb = nps()
                            mm(ps[bb][:, :N], bones_b, bnb[:], ["bnb", "cb"], [("ps", bb)])
                            stt("dve", at_[:], kkn[:], -1.0, e1[:], ALU.mult, ALU.mult, [kkkn, ke1], [kat])
                            cl3 = sg3(cl[:])
                            tt("pool", sg3(e_t[:]), cl3[:, :, lsg - 1:lsg].to_broadcast([128, nsg, lsg]), cl3, ALU.subtract, [kcl, kkkn], [ke])
                            tt("dve", kt_[:], kmod[:], e2[:], ALU.mult, [kkmod, ke2], [kkt])
                            tt("pool", bt_[:], b_t[:], e2[:], ALU.mult, [kb_, ke2], [kbt])
                            tt("dve", rt_[:], r_t[:], e3[:], ALU.mult, ["r_t", ke3], [krt])
                            act(e_t[:], e_t[:], AF.Exp, [ke], [ke], scale=-C0)
                            act(dc_[:, :nsg], cl3[:, :, lsg - 1], AF.Exp, [kcl], [kdc], scale=-C0)
                            tt("dve", bonus[:], ps[bb][:, :N], v_t[:], ALU.mult, [("ps", bb), "v_t"], [kbon])
                            tt("pool", Bh[:], b_t[:], e_t[:], ALU.mult, [kb_, ke, kbt, kkt], [kBh])
                            tt("dve", Kh[:], kmod[:], e_t[:], ALU.mult, [kkmod, ke, krt], [kKh])
                            pre_ops.append(k.end())
                            psmode[0] = 1
                            k.begin()
                            for hh_ in range(2):
                                lo = hh_ * 64
                                for src, skey_, dst, kn in [(at_, kat, atz, "atz"), (bt_, kbt, btz, "btz"), (rt_, krt, rtz, "rtz")]:
                                    act(dst[:, hh_, :], src[:, :], AF.Copy, [skey_, "cst"], [kn], scale=bones_f[:, lo:lo + 1])
                            for ci in range(nch):
                                cs = slice(ci * C, (ci + 1) * C)
                                b = nps()
                                tr(ps[b][:C, 0:128], Bh[:, cs], 128, [kBh], [("ps", b)])
                                tr(ps[b][:C, 128:256], Kh[:, cs], 128, [kKh], [("ps", b)])
                                tr(ps[b][:C, 256:384], v_t[:, cs], 128, ["v_t"], [("ps", b)])
                                hm2 = cc("hmask", 256)[:C, :].rearrange("p (a c) -> p a c", a=2)
                                tt("dve", tok[:C, ci, 0:2, :], ps[b][:C, 0:128].unsqueeze(1).to_broadcast([C, 2, 128]), hm2, ALU.mult,
                                   [("ps", b), "cst"], ["tok"])
                                tt("dve", tok[:C, ci, 2:4, :], ps[b][:C, 128:256].unsqueeze(1).to_broadcast([C, 2, 128]), hm2, ALU.mult,
                                   [("ps", b), "cst"], ["tok"])
                                copy("act", tok[:C, ci, 4, :], ps[b][:C, 256:384], [("ps", b)], ["tok"])
                                tt("pool", Vz[:C, ci, :, :], tok[:C, ci, 4:5, :].to_broadcast([C, 2, 128]), hm2, ALU.mult, ["tok", "cst"], ["Vz"])
                            fin_ops.append(k.end())
                            if u == 3 and DBG.get('nh', 4) > 0:
                                psmode[0] = 2
                                k.begin(); gdn_prefetch(0); gpf_ops = k.end()
                                psmode[0] = 1
                            k.begin()
                            if samp:
                                for q in range(4):
                                    b = nps()
                                    for s4 in range(4):
                                        s = q * 4 + s4
                                        tr(ps[b][:, s4 * 128:(s4 + 1) * 128], sbd[u % 2][:, s, :], 128, [("sbd", u % 2)], [("ps", b)])
                                    copy("act", Zs[:, q * 4:(q + 1) * 4, :].rearrange("p a c -> p (a c)"), ps[b][:, :], [("ps", b)], ["Zs"])
                                copy("act", Zsb[:], Zs[:], ["Zs"], ["Zsb"])
                                Zf, zk = Zs, "Zs"
                                Zrd = lambda ci: [Zsb[:, s_, :] for s_ in range(NSQ)]
                                zrk = lambda ci: "Zsb"
                                tt("pool", atm[:], at_[:, :].unsqueeze(1).to_broadcast([128, NSQ, 64]),
                                   segm, ALU.mult, [kat, "segm"], ["atm"])
                                tt("pool", rtm[:], rt_[:, :].unsqueeze(1).to_broadcast([128, NSQ, 64]),
                                   segm, ALU.mult, [krt, "segm"], ["rtm"])
                            else:
                                Zf, zk = Zr[:, u:u + 1, :], ("Zr", u)
                                Zrd = lambda ci: [Zrb[:, u, ci % 2, :]]
                                zrk = lambda ci: ("Zrb", u, ci % 2)
                            pv4 = lambda b_: ps[b_][:C, :].rearrange("p (a c) -> p a c", a=4)[:, :, :C]
                            for ci in range(nch):
                                cs = slice(ci * C, (ci + 1) * C)
                                pv2 = lambda b_, s0: ps[b_][:C, s0 * 128:(s0 + 2) * 128].rearrange("p (a c) -> p a c", a=2)[:, :, :C]
                                b1 = nps()
                                mm(pv2(b1, 0), bt_[:, cs], atz[:, :, cs], [kbt, "atz"], [("ps", b1)])
                                mm(pv2(b1, 2), kt_[:, cs], atz[:, :, cs], [kkt, "atz"], [("ps", b1)])
                                b2 = nps()
                                mm(pv2(b2, 0), at_[:, cs], btz[:, :, cs], [kat, "btz"], [("ps", b2)])
                                b3 = nps()
                                mm(pv2(b3, 0), bt_[:, cs], rtz[:, :, cs], [kbt, "rtz"], [("ps", b3)])
                                mm(pv2(b3, 2), kt_[:, cs], rtz[:, :, cs], [kkt, "rtz"], [("ps", b3)])
                                tt("dve", cm1[ci][:C, :, :C], pv4(b1), mS.unsqueeze(1).to_broadcast([C, 4, C]), ALU.mult, [("ps", b1), "cst"], [("cm1", ci)])
                                tt("dve", nt2[ci][:C, :, :C], pv4(b2)[:, 0:2, :], mT.unsqueeze(1).to_broadcast([C, 2, C]), ALU.mult, [("ps", b2), "cst"], [("nt2", ci)])
                                tt("dve", cm3[ci][:C, :, :C], pv4(b3), mI.unsqueeze(1).to_broadcast([C, 4, C]), ALU.mult, [("ps", b3), "cst"], [("cm3", ci)])
                                tt("pool", Tt[ci][:C, :, :C], cm1[ci][:C, 0:2, :C], ident_b[:C, :C].unsqueeze(1).to_broadcast([C, 2, C]), ALU.add,
                                   [("cm1", ci), "cb"], [("Tt", ci)])
                            Pm = [[cm1[ci][:C, 0, :C], cm1[ci][:C, 1, :C]] for ci in range(nch)]
                            Qm = [[nt2[ci][:C, 0, :C], nt2[ci][:C, 1, :C]] for ci in range(nch)]
                            pqk = [[("cm1", ci), ("nt2", ci)] for ci in range(nch)]
                            for lev in range(nlev):
                                bs = []
                                for ci in range(nch):
                                    b = nps(); bs.append(b)
                                    for hh_ in range(2):
                                        if lev < nlev - 1:
                                            mm(ps[b][:C, hh_ * 128: hh_ * 128 + C], Qm[ci][hh_], Pm[ci][hh_], pqk[ci], [("ps", b)])
                                        mm(ps[b][:C, (2 + hh_) * 128: (2 + hh_) * 128 + C], Pm[ci][hh_], Qm[ci][hh_], pqk[ci], [("ps", b)])
                                for ci in range(nch):
                                    b = bs[ci]; pq = PQ[ci]
                                    if lev < nlev - 1:
                                        copy("act" if ci != 3 else "dve", pq[:C, :, :C], pv4(b), [("ps", b)], [("PQ", ci)])
                                    else:
                                        copy("act" if ci != 3 else "dve", pq[:C, 2:4, :C], pv4(b)[:, 2:4, :], [("ps", b)], [("PQ", ci)])
                                    Pm[ci] = [pq[:C, 0, :C], pq[:C, 1, :C]]; Qm[ci] = [pq[:C, 2, :C], pq[:C, 3, :C]]
                                    pqk[ci] = [("PQ", ci)]
                                bs = []
                                for ci in range(nch):
                                    b = nps(); bs.append(b)
                                    for hh_ in range(2):
                                        mm(ps[b][:C, hh_ * 128: hh_ * 128 + C], Qm[ci][hh_], Tt[ci][:C, hh_, :C], [("PQ", ci), ("Tt", ci)], [("ps", b)])
                                for ci in range(nch):
                                    b = bs[ci]
                                    tt("dve", Tt[ci][:C, :, :C], pv4(b)[:, 0:2, :], Tt[ci][:C, :, :C], ALU.add,
                                       [("ps", b), ("Tt", ci)], [("Tt", ci)])
                            for ci in range(nch):
                                cs = slice(ci * C, (ci + 1) * C)
                                b = nps()
                                items = []
                                for s in range(nS):
                                    items.append((ps[b][:C, 0:128], (atm[:, s, :] if samp else at_[:, cs]), Zrd(ci)[s]))
                                for hh_ in range(2):
                                    items.append((ps[b][:C, 0:128], cm1[ci][:C, 2 + hh_, :C], Vz[:C, ci, hh_, :]))
                                mmg(items, [kat, "atm", zrk(ci), ("cm1", ci), "Vz"] if samp else [kat, zrk(ci), ("cm1", ci), "Vz"], [("ps", b)])
                                copy("act", Xb[:C, :], ps[b][:C, 0:128], [("ps", b)], ["Xb"])
                                b = nps()
                                for hh_ in range(2):
                                    mm(ps[b][:C, hh_ * 128:(hh_ + 1) * 128], Tt[ci][:C, hh_, :C], Xb[:C, :], [("Tt", ci), "Xb"], [("ps", b)])
                                tt("dve", Wz[:C, :, :], ps[b][:C, 0:256].rearrange("p (a c) -> p a c", a=2),
                                   cc("hmask", 256)[:C, :].rearrange("p (a c) -> p a c", a=2), ALU.mult, [("ps", b), "cst"], ["Wz"])
                                if samp:
                                    tt("pool", Wb[:C, :], Wz[:C, 0, :], Wz[:C, 1, :], ALU.add, ["Wz"], ["Wb"])
                                    tt("dve", Wm[:C, :, :], Wb[:C, :].unsqueeze(1).to_broadcast([C, NSQ, 128]),
                                       cc("rowseg", 16)[:C, :].unsqueeze(2).to_broadcast([C, NSQ, 128]), ALU.mult, ["Wb", "cst"], ["Wm"])
                                    tt("dve", Vm[:C, :, :], tok[:C, ci, 4:5, :].to_broadcast([C, NSQ, 128]),
                                       cc("rowseg", 16)[:C, :].unsqueeze(2).to_broadcast([C, NSQ, 128]), ALU.mult, ["tok", "cst"], ["Vm"])
                                    for q in range(4):
                                        b = nps()
                                        mmg([(ps[b][:, :], tok[:C, ci, 0, :], Wm[:C, q * 4:(q + 1) * 4, :].rearrange("p a c -> p (a c)")),
                                             (ps[b][:, :], tok[:C, ci, 1, :], Wm[:C, q * 4:(q + 1) * 4, :].rearrange("p a c -> p (a c)")),
                                             (ps[b][:, :], tok[:C, ci, 2, :], Vm[:C, q * 4:(q + 1) * 4, :].rearrange("p a c -> p (a c)")),
                                             (ps[b][:, :], tok[:C, ci, 3, :], Vm[:C, q * 4:(q + 1) * 4, :].rearrange("p a c -> p (a c)"))],
                                            ["tok", "Wm", "Vm"], [("ps", b)])
                                        tt("dve", bigf[:, q * 4:(q + 1) * 4, :], ps[b][:, :].rearrange("p (a c) -> p a c", a=4),
                                           bones_f.unsqueeze(1).to_broadcast([128, 4, 128]), ALU.mult, [("ps", b), "cst"], ["bigf"])
                                    tt("dve", Zs[:], Zs[:], dc_[:, :NSQ].unsqueeze(2).to_broadcast([128, NSQ, 128]), ALU.mult, ["Zs", kdc], ["Zs"])
                                    tt("dve", Zs[:], Zs[:], bigf[:], ALU.add, ["Zs", "bigf"], ["Zs"])
                                else:
                                    bz = nps()
                                    mmg([(ps[bz][:, 0:128], tok[:, ci, 0, :], Wz[:, 0, :]), (ps[bz][:, 0:128], tok[:, ci, 1, :], Wz[:, 1, :]),
                                         (ps[bz][:, 0:128], tok[:, ci, 2, :], Vz[:, ci, 0, :]), (ps[bz][:, 0:128], tok[:, ci, 3, :], Vz[:, ci, 1, :])],
                                        ["tok", "Wz", "Vz"], [("ps", bz)])
                                    stt("dve", Zrb[:, u, (ci + 1) % 2, :], Zf[:, 0, :], dc_[:, ci:ci + 1], ps[bz][:, 0:128], ALU.mult, ALU.add,
                                        [zk, kdc, ("ps", bz)], [("Zrb", u, (ci + 1) % 2)])
                                    stt("dve", Zf[:, 0, :], Zf[:, 0, :], dc_[:, ci:ci + 1], ps[bz][:, 0:128], ALU.mult, ALU.add, [zk, kdc, ("ps", bz)], [zk])
                                b = nps()
                                items = []
                                for s in range(nS):
                                    items.append((ps[b][:, :C], Zrd(ci)[s], (rtm[:, s, :] if samp else rt_[:, cs])))
                                for hh_ in range(2):
                                    items.append((ps[b][:, :C], Wz[:C, hh_, :], cm3[ci][:C, hh_, :C]))
                                    items.append((ps[b][:, :C], Vz[:C, ci, hh_, :], cm3[ci][:C, 2 + hh_, :C]))
                                mmg(items, [zrk(ci), krt, "rtm", "Wz", ("cm3", ci), "Vz"] if samp else [zrk(ci), krt, "Wz", ("cm3", ci), "Vz"], [("ps", b)])
                                copy("act", o_t[:, cs], ps[b][:, :C], [("ps", b)], ["o_t"])

                            if samp:
                                for q in range(4):
                                    b = nps()
                                    for s4 in range(4):
                                        tr(ps[b][:, s4 * 128:(s4 + 1) * 128], Zs[:, q * 4 + s4, :], 128, ["Zs"], [("ps", b)])
                                    pvq = ps[b][:, :].rearrange("p (a c) -> p a c", a=4)
                                    copy("act", stg[0:64, q * 4:(q + 1) * 4, :], pvq[0:64, :, 0:64], [("ps", b)], ["stg"])
                                    copy("act", stg[64:128, q * 4:(q + 1) * 4, :], pvq[64:128, :, 64:128], [("ps", b)], ["stg"])
                                rwv = rws_d.rearrange("(s h v) k -> h v s k", s=NSQ, h=8)
                                k.dma("sp", rwv[2 * u], stg[0:64, :, :], reads=["stg"])
                                k.dma("sp", rwv[2 * u + 1], stg[64:128, :, :], reads=["stg"])
                            elif ti == 3:
                                b = nps()
                                tr(ps[b][:, 0:128], Zr[:, u, :], 128, [("Zr", u)], [("ps", b)])
                                copy("act", bigf[0:64, 0, 0:64], ps[b][0:64, 0:64], [("ps", b)], ["bigf"])
                                copy("act", bigf[64:128, 0, 0:64], ps[b][64:128, 64:128], [("ps", b)], ["bigf"])
                                k.dma("sp", rwp_d[u * 128:(u + 1) * 128, :], bigf[:, 0, 0:64], reads=["bigf"])
                            m1, var_, o2 = [t_[:].rearrange("p a n -> p (a n)").bitcast(F32) for t_ in (atz, btz, rtz)]
                            km1, kvar, ko2 = "atz", "btz", "rtz"
                            b = nps()
                            mm(ps[b][:, :N], bones_f, o_t[:, :N], ["o_t", "cst"], [("ps", b)])
                            stt("dve", o2, ps[b][:, :N], -1.0 / 64, o_t[:, :N], ALU.mult, ALU.add, [("ps", b), "o_t"], [ko2])
                            act(sqp[:], o2, AF.Square, [ko2], ["sqp"])
                            b = nps()
                            mm(ps[b][:, :N], bones_b, sqp[:], ["sqp", "cb"], [("ps", b)])
                            act(var_, ps[b][:, :N], AF.Ln, [("ps", b)], [kvar], bias=64e-5, scale=1.0 / 64)
                            act(var_, var_, AF.Exp, [kvar], [kvar], scale=-0.5)
                            tt("pool", o2, o2, var_, ALU.mult, [ko2, kvar], [ko2])
                            ts("dve", o2, o2, col("lnx_w", u), col("lnx_b", u), ALU.mult, ALU.add, [ko2, "cols"], [ko2])
                            tt("pool", o2, o2, bonus[:], ALU.add, [ko2, kbon], [ko2])
                            tt("dve", oab[:, u, :N], o2, g_t[:], ALU.mult, [ko2, kg], [("oab", u)])
                            chunk_ops.append(k.end())
                        psmode[0] = 0
                        nu_ = len(pre_ops)
                        if nu_:
                            k.emit(pre_ops[0]); k.emit(fin_ops[0])
                            for u in range(nu_):
                                nxt = pre_ops[u + 1] if u + 1 < nu_ else gpf_ops
                                k.emit_interleaved(chunk_ops[u], nxt)
                                if u + 1 < nu_:
                                    k.emit(fin_ops[u + 1])

                        k.barrier()
                        for ci in range(nch):
                            b = nps()
                            mmg([(ps[b][:C, 0:8], u_t[:, kc, ci * C:(ci + 1) * C], wab[:, kc, :]) for kc in range(8)], ["u_t", "wab"], [("ps", b)])
                            act(btok[:C, ci, :], ps[b][:C, 4:8], AF.Sigmoid, [("ps", b)], ["btok"])
                            tt("dve", sptmp[:C, ci, :], ps[b][:C, 0:4], dtb[:C, :], ALU.add, [("ps", b), "dtb"], ["sptmp"])
                        act(sptmp[:C, :, :], sptmp[:C, :, :], AF.Exp, ["sptmp"], ["sptmp"])
                        act(sptmp[:C, :, :], sptmp[:C, :, :], AF.Ln, ["sptmp"], ["sptmp"], bias=1.0, scale=1.0)
                        stt("dve", gtok[:C, :, :], sptmp[:C, :, :], -1.0, eA[:C, :].unsqueeze(1).to_broadcast([C, nch, 4]), ALU.mult, ALU.mult,
                            ["sptmp", "eA"], ["gtok"])
                        for hh in range(DBG.get('nh', 4)):
                            xq, xk, xv = r_t, k_t, v_t
                            if hh == 0 and DBG.get('nu', 4) < 4:
                                gdn_prefetch(0)
                            bz_ = pchunk(14 + hh * 4 + 3)
                            act(z_t[:], ps[bz_][:, :N], AF.Silu, [("ps", bz_)], ["z_t"])
                            if DBG.get("gstop", 99) <= 1:
                                continue
                            qn, kn, qnb, knb_ = tmp[1], tmp[2], at, bt
                            for src, skey, dst, dkey, scl, sc_, sck, sb_, sbk in [(xq, "r_t", qn, tk[1], 128.0 ** -0.5, tmp[3], tk[3], sqb, "sqb"),
                                                                                    (xk, "k_t", kn, tk[2], 1.0, tmp[0], tk[0], bnb, "bnb")]:
                                act(sb_[:], src[:], AF.Square, [skey], [sbk])
                                b = nps()
                                mm(ps[b][:, :N], ones_b, sb_[:], [sbk, "cb"], [("ps", b)])
                                act(sc_[:], ps[b][:, :N], AF.Ln, [("ps", b)], [sck], bias=1e-6, scale=1.0)
                                act(sc_[:], sc_[:], AF.Exp, [sck], [sck], scale=-0.5)
                                stt("dve", dst[:], src[:], scl, sc_[:], ALU.mult, ALU.mult, [skey, sck], [dkey])
                            if DBG.get("gstop", 99) <= 2:
                                continue
                            copy("act", qnb[:], qn[:], [tk[1]], ["at"])
                            copy("dve", knb_[:], kn[:], [tk[2]], ["bt"])
                            qt_, Khf = kt, tmp[4]
                            atg = rt
                            if samp:
                                if hh == 0:
                                    load_gdn_state(0)
                                ZS, ZSK = sbd[hh % 2], ("sbd", hh % 2)
                                copy("act", Zsb[:], ZS[:], [ZSK], ["Zsb"])
                                Zf, zk = ZS, ZSK
                                Zrd = lambda ci: [Zsb[:, s_, :] for s_ in range(NSQ)]
                                zrk = lambda ci: "Zsb"
                            else:
                                Zf, zk = Zg[:, hh:hh + 1, :], ("Zg", hh)
                                Zrd = lambda ci, hh=hh: [Zgb[:, hh, ci % 2, :]]
                                zrk = lambda ci, hh=hh: ("Zgb", hh, ci % 2)
                            bAs, bBs = [], []
                            for ci in range(nch):
                                copy("pool", g_rep[ci][:C, :], gtok[:C, ci, hh:hh + 1].to_broadcast([C, 128]), ["gtok"], [("g_rep", ci)])
                                copy("pool", b_rep[ci][:C, :], btok[:C, ci, hh:hh + 1].to_broadcast([C, 128]), ["btok"], [("b_rep", ci)])
                                ts("dve", Gs[ci][:C, :C], mT, gtok[:C, ci, hh:hh + 1], None, ALU.mult, None, ["cst", "gtok"], [("Gs", ci)])
                            for ci in range(nch):
                                bA = nps(); bAs.append(bA)
                                mm(ps[bA][:, 0:C], g_rep[ci][:C, :], mI, [("g_rep", ci), "cst"], [("ps", bA)])
                                mm(ps[bA][:, 128:128 + C], g_rep[ci][:C, :], mT, [("g_rep", ci), "cst"], [("ps", bA)])
                                mm(ps[bA][:, 256:256 + C], b_rep[ci][:C, :], ident_f[:C, :C], [("b_rep", ci), "cst"], [("ps", bA)])
                                mm(ps[bA][:C, 384:384 + C], Gs[ci][:C, :C], mI, [("Gs", ci), "cst"], [("ps", bA)])
                                bB = nps(); bBs.append(bB)
                                mm(ps[bB][:C, 0:C], mI, Gs[ci][:C, :C], [("Gs", ci), "cst"], [("ps", bB)])
                            for ci in range(nch):
                                cs = slice(ci * C, (ci + 1) * C)
                                bA, bB = bAs[ci], bBs[ci]
                                act(eg[ci][:, :C], ps[bA][:, 0:C], AF.Exp, [("ps", bA)], [("eg", ci)])
                                act(ek[ci][:, :C], ps[bA][:, 128:128 + C], AF.Exp, [("ps", bA)], [("ek", ci)])
                                act(dmi[ci][:C, :C], ps[bA][:C, 384:384 + C], AF.Exp, [("ps", bA)], [("dmi", ci)])
                                tt("dve", kbf[ci][:, :C], kn[:, cs], ps[bA][:, 256:256 + C], ALU.mult, [tk[2], ("ps", bA)], [("kbf", ci)])
                                act(dmt[ci][:C, :C], ps[bB][:C, 0:C], AF.Exp, [("ps", bB)], [("dmt", ci)])
                                tt("pool", dms[ci][:C, :C], dmi[ci][:C, :C], mS, ALU.mult, [("dmi", ci), "cst"], [("dms", ci)])
                                tt("pool", dmi[ci][:C, :C], dmi[ci][:C, :C], mI, ALU.mult, [("dmi", ci), "cst"], [("dmi", ci)])
                                tt("pool", dmt[ci][:C, :C], dmt[ci][:C, :C], mT, ALU.mult, [("dmt", ci), "cst"], [("dmt", ci)])
                                copy("pool", kbb[ci][:, :C], kbf[ci][:, :C], [("kbf", ci)], [("kbb", ci)])
                                stt("dve", atg[:, cs], kbf[ci][:, :C], -1.0, eg[ci][:, :C], ALU.mult, ALU.mult, [("kbf", ci), ("eg", ci)], [("rt", ci)])
                                tt("pool", qt_[:, cs], qn[:, cs], eg[ci][:, :C], ALU.mult, [tk[1], ("eg", ci)], [("kt", ci)])
                                tt("pool", Khf[:, cs], kn[:, cs], ek[ci][:, :C], ALU.mult, [tk[2], ("ek", ci)], [(tk[4], ci)])
                            for ci in range(nch):
                                cs = slice(ci * C, (ci + 1) * C)
                                b = nps()
                                tr(ps[b][:C, 0:128], Khf[:, cs], 128, [(tk[4], ci)], [("ps", b)])
                                tr(ps[b][:C, 128:256], xv[:, cs], 128, ["v_t"], [("ps", b)])
                                copy("act", tok[:C, ci, 3:5, :].rearrange("p a c -> p (a c)"), ps[b][:C, 0:256], [("ps", b)], [("tok", ci)])
                            if hh + 1 < DBG.get('nh', 4):
                                gdn_prefetch(hh + 1)
                                if samp:
                                    load_gdn_state(hh + 1)
                            for ci in range(nch):
                                cs = slice(ci * C, (ci + 1) * C)
                                b1 = nps()
                                mm(ps[b1][:C, 0:C], knb_[:, cs], kbb[ci][:, :C], ["bt", ("kbb", ci)], [("ps", b1)])
                                mm(ps[b1][:C, 128:128 + C], kbb[ci][:, :C], knb_[:, cs], ["bt", ("kbb", ci)], [("ps", b1)])
                                mm(ps[b1][:C, 256:256 + C], knb_[:, cs], qnb[:, cs], ["bt", "at"], [("ps", b1)])
                                stt("dve", cm1[ci][:C, 0, :C], ps[b1][:C, 0:C], -1.0, dms[ci][:C, :C], ALU.mult, ALU.mult, [("ps", b1), ("dms", ci)], [("cm1", ci)])
                                stt("dve", nt2[ci][:C, 0, :C], ps[b1][:C, 128:128 + C], -1.0, dmt[ci][:C, :C], ALU.mult, ALU.mult, [("ps", b1), ("dmt", ci)], [("nt2", ci)])
                                tt("dve", cm3[ci][:C, 0, :C], ps[b1][:C, 256:256 + C], dmi[ci][:C, :C], ALU.mult, [("ps", b1), ("dmi", ci)], [("cm3", ci)])
                                tt("pool", Tt[ci][:C, 0, :C], cm1[ci][:C, 0, :C], ident_b[:C, :C], ALU.add, [("cm1", ci), "cb"], [("Tt", ci)])
                            Pm = [cm1[ci][:C, 0, :C] for ci in range(nch)]
                            Qm = [nt2[ci][:C, 0, :C] for ci in range(nch)]
                            pqk = [[("cm1", ci), ("nt2", ci)] for ci in range(nch)]
                            for lev in range(nlev):
                                bs = []
                                for ci in range(nch):
                                    b = nps(); bs.append(b)
                                    if lev < nlev - 1:
                                        mm(ps[b][:C, 0:C], Qm[ci], Pm[ci], pqk[ci], [("ps", b)])
                                    mm(ps[b][:C, 128:128 + C], Pm[ci], Qm[ci], pqk[ci], [("ps", b)])
                                for ci in range(nch):
                                    b = bs[ci]; pq = PQ[ci]
                                    pvv = ps[b][:C, 0:256].rearrange("p (a c) -> p a c", a=2)[:, :, :C]
                                    if lev < nlev - 1:
                                        copy("act" if ci != 3 else "dve", pq[:C, 0:2, :C], pvv, [("ps", b)], [("PQ", ci)])
                                    else:
                                        copy("act" if ci != 3 else "dve", pq[:C, 1, :C], ps[b][:C, 128:128 + C], [("ps", b)], [("PQ", ci)])
                                    Pm[ci], Qm[ci] = pq[:C, 0, :C], pq[:C, 1, :C]
                                    pqk[ci] = [("PQ", ci)]
                                bs = []
                                for ci in range(nch):
                                    b = nps(); bs.append(b)
                                    mm(ps[b][:C, 0:C], Qm[ci], Tt[ci][:C, 0, :C], [("PQ", ci), ("Tt", ci)], [("ps", b)])
                                for ci in range(nch):
                                    b = bs[ci]
                                    tt("dve", Tt[ci][:C, 0, :C], ps[b][:C, 0:C], Tt[ci][:C, 0, :C], ALU.add, [("ps", b), ("Tt", ci)], [("Tt", ci)])
                            if samp:
                                tt("pool", atm[:], atg[:, :].unsqueeze(1).to_broadcast([128, NSQ, 64]),
                                   segm, ALU.mult, [("rt", 0), "segm"], ["atm"])
                                tt("pool", rtm[:], qt_[:, :].unsqueeze(1).to_broadcast([128, NSQ, 64]),
                                   segm, ALU.mult, [("kt", 0), "segm"], ["rtm"])
                            for ci in range(nch):
                                cs = slice(ci * C, (ci + 1) * C)
                                zr = Zrd(ci); zrkey = zrk(ci)
                                b = nps()
                                mmg([(ps[b][:C, 0:128], (atm[:, s, :] if samp else atg[:, cs]), zr[s]) for s in range(nS)],
                                    [("rt", ci), "atm", zrkey] if samp else [("rt", ci), zrkey], [("ps", b)])
                                stt("dve", Xb[:C, :], tok[:C, ci, 4, :], btok[:C, ci, hh:hh + 1], ps[b][:C, 0:128], ALU.mult, ALU.add,
                                    [("tok", ci), "btok", ("ps", b)], ["Xb"])
                                b = nps()
                                mm(ps[b][:C, 0:128], Tt[ci][:C, 0, :C], Xb[:C, :], [("Tt", ci), "Xb"], [("ps", b)])
                                copy("act", Wb[:C, :], ps[b][:C, 0:128], [("ps", b)], ["Wb"])
                                if samp:
                                    tt("dve", Wm[:C, :, :], Wb[:C, :].unsqueeze(1).to_broadcast([C, NSQ, 128]),
                                       cc("rowseg", 16)[:C, :].unsqueeze(2).to_broadcast([C, NSQ, 128]), ALU.mult, ["Wb", "cst"], ["Wm"])
                                    egl = eg[ci][:, :C].rearrange("p (s l) -> p s l", s=NSQ)[:, :, LS - 1:LS]
                                    tt("dve", ZS[:], ZS[:], egl.to_broadcast([128, NSQ, 128]), ALU.mult, [ZSK, ("eg", ci)], [ZSK])
                                    for q in range(4):
                                        b = nps()
                                        mm(ps[b][:, :], tok[:C, ci, 3, :], Wm[:C, q * 4:(q + 1) * 4, :].rearrange("p a c -> p (a c)"), [("tok", ci), "Wm"], [("ps", b)])
                                        tt("dve", ZS[:, q * 4:(q + 1) * 4, :], ZS[:, q * 4:(q + 1) * 4, :], ps[b][:, :].rearrange("p (a c) -> p a c", a=4),
                                           ALU.add, [ZSK, ("ps", b)], [ZSK])
                                else:
                                    bz = nps()
                                    mm(ps[bz][:, 0:128], tok[:, ci, 3, :], Wb[:, :], [("tok", ci), "Wb"], [("ps", bz)])
                                    stt("dve", Zgb[:, hh, (ci + 1) % 2, :], Zf[:, 0, :], eg[ci][:, C - 1:C], ps[bz][:, 0:128], ALU.mult, ALU.add,
                                        [zk, ("eg", ci), ("ps", bz)], [("Zgb", hh, (ci + 1) % 2)])
                                    stt("dve", Zf[:, 0, :], Zf[:, 0, :], eg[ci][:, C - 1:C], ps[bz][:, 0:128], ALU.mult, ALU.add,
                                        [zk, ("eg", ci), ("ps", bz)], [zk])
                                b = nps()
                                items = [(ps[b][:, :C], zr[s], (rtm[:, s, :] if samp else qt_[:, cs])) for s in range(nS)]
                                items.append((ps[b][:, :C], Wb[:C, :], cm3[ci][:C, 0, :C]))
                                mmg(items, [zrkey, ("kt", ci), "rtm", "Wb", ("cm3", ci)] if samp else [zrkey, ("kt", ci), "Wb", ("cm3", ci)], [("ps", b)])
                                copy("act", o_t[:, cs], ps[b][:, :C], [("ps", b)], ["o_t"])

                            if samp:
                                k.dma("sp", dls_d.rearrange("(s h k) v -> h k s v", s=NSQ, h=4)[hh], ZS[:], reads=[ZSK])
                            elif ti == 3:
                                k.dma("sp", dlp_d[hh * 128:(hh + 1) * 128, :], Zg[:, hh, :], reads=[("Zg", hh)])
                            o2 = tmp[3]
                            tt("pool", sqb[:], o_t[:, :N], o_t[:, :N], ALU.mult, ["o_t"], ["sqb"])
                            b = nps()
                            mm(ps[b][:, :N], ones_b, sqb[:], ["sqb", "cb"], [("ps", b)])
                            act(o2[:], ps[b][:, :N], AF.Ln, [("ps", b)], [tk[3]], bias=1e-6, scale=1.0 / 128)
                            act(o2[:], o2[:], AF.Exp, [tk[3]], [tk[3]], scale=-0.5)
                            stt("dve", o2[:], o_t[:, :N], col("norm_w", 0), o2[:], ALU.mult, ALU.mult, ["o_t", "cols", tk[3]], [tk[3]])
                            tt("dve", oab[:, 4 + hh, :N], o2[:], z_t[:], ALU.mult, [tk[3], "z_t"], [("oab", 4 + hh)])

                        mrg = [at, bt, kt, rt, atz[:, 0, :], atz[:, 1, :], btz[:, 0, :], btz[:, 1, :]]
                        mrk = ["at", "bt", "kt", "rt", "atz", "atz", "btz", "btz"]
                        okeys = [("oab", j) for j in range(8)]
                        k.barrier()
                        for m in range(DBG.get('ng', 8)):
                            pj = pjbuf[m % 2]
                            k.dma("sp", pj[:, 0, :, :], pjab_d[:, m * 128:(m + 1) * 128].rearrange("(kc p) c -> p kc c", p=128), reads=["pjab"], writes=[("pj", m % 2)])
                            k.dma("sp", pj[:, 1, :, :], pjbb_d[:, m * 128:(m + 1) * 128].rearrange("(kc p) c -> p kc c", p=128), reads=["pjbb"], writes=[("pj", m % 2)])
                            bga = pchunk(30 + 2 * m)
                            act(tmp[0][:], ps[bga][:, :N], AF.Sigmoid, [("ps", bga)], [tk[0]])
                            b = nps()
                            mmg([(ps[b][:, :N], pj[:, 0, kc, :], oab[:, kc, :N]) for kc in range(4)], [("pj", m % 2)] + okeys, [("ps", b)])
                            tt("dve", tmp[1][:], tmp[0][:], ps[b][:, :N], ALU.mult, [tk[0], ("ps", b)], [tk[1]])
                            bgb = pchunk(31 + 2 * m)
                            act(tmp[0][:], ps[bgb][:, :N], AF.Sigmoid, [("ps", bgb)], [tk[0]])
                            b = nps()
                            mmg([(ps[b][:, :N], pj[:, 1, kc, :], oab[:, 4 + kc, :N]) for kc in range(4)], [("pj", m % 2)] + okeys, [("ps", b)])
                            tt("dve", tmp[2][:], tmp[0][:], ps[b][:, :N], ALU.mult, [tk[0], ("ps", b)], [tk[2]])
                            tt("pool", mrg[m][:, :N] if m < 4 else mrg[m], tmp[1][:], tmp[2][:], ALU.add, [tk[1], tk[2]], [mrk[m], ("mrg", m)])
                        for half in range(4):
                            wt = WI[half % 2]
                            k.dma("sp", wt[:, :, :], wob_d[:, half * 256:(half + 1) * 256].rearrange("(kc p) c -> p kc c", p=128),
                                  reads=[("wob", r_) for r_ in range(2)], writes=[("wi", half % 2)])
                            for m4 in range(2):
                                m = half * 2 + m4
                                b = nps()
                                mmg([(ps[b][:, :N], wt[:, kc, m4 * 128:(m4 + 1) * 128], (mrg[kc][:, :N] if kc < 4 else mrg[kc])) for kc in range(8)],
                                    [("wi", half % 2)] + [("mrg", j) for j in range(8)], [("ps", b)])
                                tt("dve", h[:, m, c0:c0 + N], h[:, m, c0:c0 + N], ps[b][:, :N], ALU.add, hk + [("ps", b)], hk)
                        k.barrier()
            k.barrier()

        while pending_casts:
            one_cast()
        if stage >= 2:
            mixer()
        if stage >= 3:
            ffn(w2g_d, w2u_d, w2d_d, "ffn2_norm", "b")

        with ExitStack() as st:
            rstd = sb("rstdf", [128, 512], F32, st)
            sq = [sb("sqf%d" % i, [128, 512], BF16, st) for i in range(2)]
            yf = [sb("yf%d" % i, [128, 8, 512], F32, st) for i in range(2)]
            yo = [sb("yo%d" % i, [128, D], F32, st) for i in range(3)]
            bi = 0
            for ti, (c0, N, _, _) in enumerate(TILES):
                hk = hkeys_of(c0, N)
                rms_stats(c0, N, rstd, sq, hk)
                yft = yf[ti % 2]
                for m in range(8):
                    k.op("dve", lambda e, m=m, yft=yft, c0=c0, N=N: e.scalar_tensor_tensor(
                        out=yft[:, m, :N], in0=h[:, m, c0:c0 + N], scalar=col("final_norm", m), in1=rstd[:, :N],
                        op0=ALU.mult, op1=ALU.mult), reads=hk + ["rstd", "cols"], writes=[("yf", ti % 2, m)])
                for blk in range((N + 127) // 128):
                    rows = min(128, N - blk * 128)
                    yot = yo[bi % 3]
                    for half in range(2):
                        b = nps()
                        for m4 in range(4):
                            m = half * 4 + m4
                            k.op("pe", lambda e, b=b, m4=m4, m=m, yft=yft, rows=rows, blk=blk: e.transpose(
                                ps[b][:rows, m4 * 128:(m4 + 1) * 128], yft[:, m, blk * 128:blk * 128 + rows], ident_f),
                                reads=[("yf", ti % 2, m), "cst"], writes=[("ps", b)])
                        if half == 0:
                            k.op("act", lambda e, b=b, yot=yot, rows=rows: e.activation(
                                out=yot[:rows, 0:512], in_=ps[b][:rows, :], func=AF.Copy), reads=[("ps", b)], writes=[("yo", bi % 3)])
                        else:
                            k.op("dve", lambda e, b=b, yot=yot, rows=rows: e.tensor_copy(
                                out=yot[:rows, 512:1024], in_=ps[b][:rows, :]), reads=[("ps", b)], writes=[("yo", bi % 3)])
                    k.dma("sp", y_d[c0 + blk * 128:c0 + blk * 128 + rows, :], yot[:rows, :], reads=[("yo", bi % 3)])
                    bi += 1
        k.finish()
    return nc


def _perm_a():
    parts = [np.arange(512, 576), np.arange(1600, 1664), np.arange(1664, 1792)]
    for u in range(4):
        parts += [np.arange(u * 128, (u + 1) * 128), np.arange(576 + u * 128, 576 + (u + 1) * 128),
                  np.arange(1088 + u * 128, 1088 + (u + 1) * 128)]
    return np.concatenate(parts)


def _perm_b():
    parts = []
    for hh in range(4):
        parts += [np.arange(hh * 128, (hh + 1) * 128), np.arange(512 + hh * 128, 512 + (hh + 1) * 128),
                  np.arange(1024 + hh * 128, 1024 + (hh + 1) * 128), np.arange(1544 + hh * 128, 1544 + (hh + 1) * 128)]
    return np.concatenate(parts)


def _qkv_from_b():
    j = np.arange(1536)
    role = j // 512; hh = (j % 512) // 128; off = j % 128
    return hh * 512 + role * 128 + off


def _win_perm():
    pa = _perm_a()
    pb = A_PROJ + _perm_b()
    g0 = A_PROJ + 2056
    pg = np.concatenate([np.concatenate([np.arange(g0 + m * 128, g0 + (m + 1) * 128),
                                         np.arange(g0 + 1024 + m * 128, g0 + 1024 + (m + 1) * 128)]) for m in range(8)])
    ab = np.arange(A_PROJ + 1536, A_PROJ + 1544)
    return np.concatenate([pa, pb, pg, ab])


def _colvec(v):
    v = np.asarray(v, np.float32).reshape(-1)
    return np.ascontiguousarray(v.reshape(-1, 128).T)


def kernel(**inp):
    stage = int(inp.pop("_stage", 99))
    f = lambda a: np.ascontiguousarray(np.asarray(a, dtype=np.float32))
    pa = _perm_a()
    cols = np.zeros((128, NCOL), np.float32)
    def putc(name, arr):
        cols[:, COLS[name]:COLS[name] + arr.shape[1]] = arr
    putc("ffn1_norm", _colvec(inp["ffn1_norm"][0])); putc("mix_norm", _colvec(inp["mix_norm"][0]))
    putc("ffn2_norm", _colvec(inp["ffn2_norm"][0])); putc("final_norm", _colvec(inp["final_norm"]))
    putc("mu", _colvec(f(inp["rwkv_mu"])[0][pa]))
    for nm, key in [("w0", "rwkv_w0"), ("a0", "rwkv_a0"), ("k_k", "rwkv_k_k"), ("k_a", "rwkv_k_a"),
                    ("r_k", "rwkv_r_k"), ("lnx_w", "rwkv_lnx_w"), ("lnx_b", "rwkv_lnx_b")]:
        putc(nm, _colvec(f(inp[key])[0]))
    cw = f(inp["gdn_conv_w"])[0]
    putc("conv_w", np.concatenate([_colvec(cw[i]) for i in range(4)], axis=1))
    putc("norm_w", _colvec(f(inp["gdn_norm_w"])[0]))
    putc("A_log", np.tile(f(inp["gdn_A_log"])[0][None, :], (128, 1)))
    putc("dt_bias", np.tile(f(inp["gdn_dt_bias"])[0][None, :], (128, 1)))
    consts = make_consts()
    w_in = np.ascontiguousarray(f(inp["w_in"])[0][:, _win_perm()])
    shared = {
        "w1g": f(inp["ffn1_w_gate"])[0], "w1u": f(inp["ffn1_w_up"])[0], "w1d": f(inp["ffn1_w_down"])[0],
        "w2g": f(inp["ffn2_w_gate"])[0], "w2u": f(inp["ffn2_w_up"])[0], "w2d": f(inp["ffn2_w_down"])[0],
        "w_in": w_in, "proj_a": f(inp["proj_a"])[0], "proj_b": f(inp["proj_b"])[0], "w_out": f(inp["w_out"])[0],
        "lw2": f(inp["rwkv_w2"])[0], "la2": f(inp["rwkv_a2"])[0], "lg2": f(inp["rwkv_g2"])[0],
        "cols": cols, "consts": consts, "consts2": make_consts2(),
    }
    xp = f(inp["x_prompt"]); xs = f(inp["x_sample"])
    srw = f(inp["state_rwkv"])[0]; ssh = f(inp["state_rwkv_shift"])[0]
    sdl = f(inp["state_delta"])[0]; scv = f(inp["state_conv"])[0]
    in_maps = []
    for c in range(NCORES):
        sl = slice(c * NSQ, (c + 1) * NSQ)
        m = dict(shared)
        m["x"] = np.ascontiguousarray(np.concatenate([xp[c], xs[sl].reshape(NSQ * LS, D)], axis=0))
        m["s_rwkv"] = np.ascontiguousarray(srw[sl].reshape(NSQ * 512, 64))
        m["s_shift"] = np.ascontiguousarray(ssh[sl][:, pa])
        m["s_delta"] = np.ascontiguousarray(sdl[sl].reshape(NSQ * 512, 128))
        m["s_conv"] = np.ascontiguousarray(scv[sl].reshape(NSQ * 3, 1536))
        in_maps.append(m)
    nc = build(stage)
    res = run_bass_kernel_spmd(nc, in_maps, core_ids=list(range(NCORES)))
    R = res.results
    inv = np.argsort(pa)
    y = np.stack([r["y"] for r in R])
    y_prompt = np.ascontiguousarray(y[:, :SEQ, :])
    y_sample = np.ascontiguousarray(y[:, SEQ:, :].reshape(NCORES * NSQ, LS, D))
    qb = _qkv_from_b()
    rwkv_p = np.stack([r["rwkv_p"].reshape(8, 64, 64) for r in R])[None]
    shift_p = np.stack([r["tokrows_p"][3, :A_PROJ][inv] for r in R])[None]
    delta_p = np.stack([r["delta_p"].reshape(4, 128, 128) for r in R])[None]
    conv_p = np.stack([r["tokrows_p"][1:4, A_PROJ:][:, qb] for r in R])[None]
    rwkv_s = np.concatenate([r["rwkv_s"].reshape(NSQ, 8, 64, 64) for r in R])[None]
    shift_s = np.concatenate([r["tokrows_s"].reshape(NSQ, LS, 3840)[:, 3, :A_PROJ][:, inv] for r in R])[None]
    delta_s = np.concatenate([r["delta_s"].reshape(NSQ, 4, 128, 128) for r in R])[None]
    conv_s = np.concatenate([r["tokrows_s"].reshape(NSQ, LS, 3840)[:, 1:4, A_PROJ:][:, :, qb] for r in R])[None]
    outs = (y_prompt, y_sample, rwkv_p, shift_p, delta_p, conv_p, rwkv_s, shift_s, delta_s, conv_s)
    return tuple(np.ascontiguousarray(o.astype(np.float32)) for o in outs)
```

```python
from contextlib import ExitStack
import numpy as np
import concourse.bass as bass
import concourse.mybir as mybir
from concourse.bass_utils import run_bass_kernel_spmd

F32 = mybir.dt.float32
BF16 = mybir.dt.bfloat16
AF = mybir.ActivationFunctionType
ALU = mybir.AluOpType

NCORES = 8
DBG = {}
D = 1024
DFF = 2816
SEQ = 2048
NSQ = 16
LS = 4
NT = SEQ + NSQ * LS
A_PROJ = 1792
C0 = float(np.exp(-0.5))

COLS = {}
_nc = 0
for _name, _n in [("ffn1_norm", 8), ("mix_norm", 8), ("ffn2_norm", 8), ("final_norm", 8), ("mu", 14),
                  ("w0", 4), ("a0", 4), ("k_k", 4), ("k_a", 4), ("r_k", 4), ("lnx_w", 4), ("lnx_b", 4),
                  ("conv_w", 48), ("norm_w", 1), ("A_log", 4), ("dt_bias", 4)]:
    COLS[_name] = _nc
    _nc += _n
NCOL = _nc

CONST = {}
_k = 0
for _name, _n in [("ident", 128), ("incl", 128), ("strict", 128), ("tail", 128), ("incl_s", 128), ("strict_s", 128),
                  ("tail_s", 128), ("bones", 128), ("ones", 128), ("hmask", 256), ("rowseg", 16),
                  ("reset_s", 64)]:
    CONST[_name] = _k
    _k += _n
NCONST = _k


def make_consts2():
    sm = np.zeros((128, 16, 64), np.float32)
    for s in range(16):
        sm[:, s, s * LS:(s + 1) * LS] = 1
    return sm.reshape(128, 1024)


def make_consts():
    c = np.zeros((128, NCONST), np.float32)
    i = np.arange(128)
    seg = i // LS
    same = (seg[:, None] == seg[None, :])
    def put(name, a):
        c[:a.shape[0], CONST[name]:CONST[name] + a.shape[1]] = a
    put("ident", np.eye(128, dtype=np.float32))
    put("incl", (i[None, :] >= i[:, None]).astype(np.float32))
    put("strict", (i[None, :] > i[:, None]).astype(np.float32))
    put("tail", (i[:, None] > i[None, :]).astype(np.float32))
    put("incl_s", ((i[None, :] >= i[:, None]) & same).astype(np.float32))
    put("strict_s", ((i[None, :] > i[:, None]) & same).astype(np.float32))
    put("tail_s", ((i[:, None] > i[None, :]) & same).astype(np.float32))
    bo = np.zeros((128, 128), np.float32); bo[:64, :64] = 1; bo[64:, 64:] = 1
    put("bones", bo)
    put("ones", np.ones((128, 128), np.float32))
    hm = np.zeros((128, 256), np.float32); hm[:, 0:64] = 1; hm[:, 128 + 64:256] = 1
    put("hmask", hm)
    rs = np.zeros((128, 16), np.float32)
    for s in range(16):
        rs[s * LS:(s + 1) * LS, s] = 1
    put("rowseg", rs)
    r = np.ones((128, 64), np.float32); r[:, ::LS] = 0
    put("reset_s", r)
    return c


class K:
    def __init__(self, nc, es):
        self.nc = nc
        self.eng = {"pe": nc.tensor, "act": nc.scalar, "dve": nc.vector, "pool": nc.gpsimd, "sp": nc.sync}
        self.sem = {e: es.enter_context(nc.semaphore("sem_" + e)) for e in ("pe", "act", "dve", "pool")}
        self.cnt = {e: 0 for e in self.sem}
        self.ndma = 24
        self.dsem = [es.enter_context(nc.semaphore("dsem%d" % i)) for i in range(self.ndma)]
        self.dval = [0] * self.ndma
        self.di = {"sp": 0, "pool": 12}
        self.waited = {e: {} for e in self.eng}
        self.lastw = {}
        self.readers = {}
        self.nops = 0
        self._cap = None

    def begin(self):
        assert self._cap is None
        self._cap = []

    def end(self):
        c = self._cap
        self._cap = None
        return c

    def emit(self, items):
        for it in items:
            if it[0] == "op":
                self.op(*it[1:])
            else:
                self.dma(*it[1:])

    def emit_interleaved(self, a, b):
        na, nb = len(a), len(b)
        ia = ib = 0
        while ia < na or ib < nb:
            if ib >= nb or (ia < na and ia * nb <= ib * na):
                self.emit([a[ia]]); ia += 1
            else:
                self.emit([b[ib]]); ib += 1

    def _semh(self, key):
        return self.sem[key] if isinstance(key, str) else self.dsem[key[1]]

    def _deps(self, reads, writes):
        toks = []
        for k in reads:
            if k in self.lastw:
                toks.append(self.lastw[k])
        for k in writes:
            if k in self.lastw:
                toks.append(self.lastw[k])
            toks.extend(self.readers.get(k, ()))
        return toks

    def _wait(self, en, toks):
        need = {}
        for (sk, v) in toks:
            if sk == "pe" and en == "pe":
                continue
            if v > need.get(sk, 0):
                need[sk] = v
        w = self.waited[en]
        for sk, v in need.items():
            if w.get(sk, 0) < v:
                self.eng[en].wait_ge(self._semh(sk), v)
                w[sk] = v

    def _record(self, tok, reads, writes):
        for k in reads:
            self.readers.setdefault(k, []).append(tok)
        for k in writes:
            self.lastw[k] = tok
            self.readers[k] = []

    @staticmethod
    def _excl(reads, writes):
        pr = [x for x in reads if isinstance(x, tuple) and x[0] == "ps"]
        if pr:
            reads = [x for x in reads if x not in pr]
            writes = list(writes) + [x for x in pr if x not in writes]
        return reads, writes

    def op(self, en, fn, reads=(), writes=()):
        if self._cap is not None:
            self._cap.append(("op", en, fn, list(reads), list(writes)))
            return
        reads, writes = self._excl(reads, writes)
        self._wait(en, self._deps(reads, writes))
        ins = fn(self.eng[en])
        self.cnt[en] += 1
        ins.then_inc(self.sem[en], 1)
        self._record((en, self.cnt[en]), reads, writes)
        self.nops += 1

    def dma(self, q, out, in_, reads=(), writes=()):
        if self._cap is not None:
            self._cap.append(("dma", q, out, in_, list(reads), list(writes)))
            return
        toks = self._deps(reads, writes)
        i = self.di[q]
        base = 0 if q == "sp" else 12
        self.di[q] = base + (i - base + 1) % 12
        if self.dval[i] > 0:
            toks.append((("dma", i), self.dval[i]))
        self._wait(q, toks)
        ins = self.eng[q].dma_start(out=out, in_=in_)
        self.dval[i] += 16
        ins.then_inc(self.dsem[i], 16)
        self._record((("dma", i), self.dval[i]), reads, writes)
        self.nops += 1

    def barrier(self):
        toks = [(e, self.cnt[e]) for e in self.cnt if self.cnt[e] > 0]
        toks += [(("dma", i), self.dval[i]) for i in range(self.ndma) if self.dval[i] > 0]
        for en in self.eng:
            self._wait(en, toks)

    def finish(self):
        toks = [(("dma", i), self.dval[i]) for i in range(self.ndma) if self.dval[i] > 0]
        toks += [(e, self.cnt[e]) for e in self.cnt if self.cnt[e] > 0]
        self._wait("sp", toks)


def build(stage=99):
    nc = bass.Bass("TRN2", target_bir_lowering=False)
    es = ExitStack()

    def din(name, shape):
        return nc.dram_tensor(name, list(shape), F32, kind="ExternalInput").ap()

    def dout(name, shape):
        return nc.dram_tensor(name, list(shape), F32, kind="ExternalOutput").ap()

    x_d = din("x", [NT, D])
    srw_d = din("s_rwkv", [NSQ * 512, 64])
    ssh_d = din("s_shift", [NSQ, A_PROJ])
    sdl_d = din("s_delta", [NSQ * 512, 128])
    scv_d = din("s_conv", [NSQ * 3, 1536])
    w1g_d = din("w1g", [D, DFF]); w1u_d = din("w1u", [D, DFF]); w1d_d = din("w1d", [DFF, D])
    w2g_d = din("w2g", [D, DFF]); w2u_d = din("w2u", [D, DFF]); w2d_d = din("w2d", [DFF, D])
    win_d = din("w_in", [D, 5896])
    pja_d = din("proj_a", [512, D]); pjb_d = din("proj_b", [512, D]); wo_d = din("w_out", [D, D])
    lw2_d = din("lw2", [64, 512]); la2_d = din("la2", [64, 512]); lg2_d = din("lg2", [128, 512])
    cols_d = din("cols", [128, NCOL]); consts_d = din("consts", [128, NCONST]); consts2_d = din("consts2", [128, 1024])

    winb_d = nc.dram_tensor("w_in_bf", [D, 5896], BF16, kind="Internal").ap()
    pjab_d = nc.dram_tensor("proj_a_bf", [512, D], BF16, kind="Internal").ap()
    pjbb_d = nc.dram_tensor("proj_b_bf", [512, D], BF16, kind="Internal").ap()
    wob_d = nc.dram_tensor("w_out_bf", [D, D], BF16, kind="Internal").ap()
    y_d = dout("y", [NT, D])
    rwp_d = dout("rwkv_p", [512, 64]); dlp_d = dout("delta_p", [512, 128])
    rws_d = dout("rwkv_s", [NSQ * 512, 64]); dls_d = dout("delta_s", [NSQ * 512, 128])
    trp_d = dout("tokrows_p", [4, 3840]); trs_d = dout("tokrows_s", [NSQ * LS, 3840])

    with es:
        k = K(nc, es)

        def sb(name, shape, dt=F32, stack=es):
            return stack.enter_context(nc.sbuf_tensor("sb_" + name, list(shape), dt))

        ps = [es.enter_context(nc.psum_tensor("ps%d" % i, [128, 512], F32)) for i in range(8)]
        psi = [0, 0]
        psmode = [0]

        def nps():
            if psmode[0] == 0:
                i = psi[0]
                psi[0] = (i + 1) % 8
                return i
            if psmode[0] == 1:
                i = psi[0] % 4
                psi[0] = (i + 1) % 4
                return i
            i = psi[1]
            psi[1] = (i + 1) % 4
            return 4 + i

        h = sb("h", [128, 8, NT])
        cols = sb("cols", [128, NCOL])
        cst = sb("cst", [128, NCONST])
        cb = sb("cb", [128, 5 * 128], BF16)
        pending_casts = []
        for r8 in range(8):
            pending_casts.append(lambda r8=r8: k.dma("pool", winb_d[r8 * 128:(r8 + 1) * 128, :], win_d[r8 * 128:(r8 + 1) * 128, :], writes=[("winb", r8)]))
        pending_casts.append(lambda: k.dma("pool", pjab_d[:, :], pja_d[:, :], writes=["pjab"]))
        pending_casts.append(lambda: k.dma("pool", pjbb_d[:, :], pjb_d[:, :], writes=["pjbb"]))
        for r2 in range(2):
            pending_casts.append(lambda r2=r2: k.dma("pool", wob_d[r2 * 512:(r2 + 1) * 512, :], wo_d[r2 * 512:(r2 + 1) * 512, :], writes=[("wob", r2)]))

        def one_cast():
            if pending_casts:
                pending_casts.pop(0)()
        k.dma("sp", cols[:], cols_d[:, :], writes=["cols"])
        k.dma("sp", cst[:], consts_d[:, :], writes=["cst"])

        def cc(name, n=128, rows=128):
            o = CONST[name]
            return cst[:rows, o:o + n]

        def col(name, j=0):
            o = COLS[name] + j
            return cols[:, o:o + 1]

        for j, nm in enumerate(["ident", "ones", "bones"]):
            k.op("dve", lambda e, j=j, nm=nm: e.tensor_copy(out=cb[:, j * 128:(j + 1) * 128], in_=cc(nm)),
                 reads=["cst"], writes=["cb"])
        ident_f = cc("ident")
        ones_b = cb[:, 128:256]
        bones_b = cb[:, 256:384]

        TILES = [(t * 512, 512, 1, 512) for t in range(4)] + [(SEQ, NSQ * LS, NSQ, LS)]

        with ExitStack() as st:
            xin = [sb("xin%d" % i, [128, D], F32, st) for i in range(2)]
            for blk in range(17):
                rows = 128 if blk < 16 else 64
                xb = xin[blk % 2]
                k.dma("sp", xb[:rows, :], x_d[blk * 128: blk * 128 + rows, :], writes=[("xin", blk % 2)])
                for half in range(2):
                    b = nps()
                    for m4 in range(4):
                        m = half * 4 + m4
                        k.op("pe", lambda e, b=b, m4=m4, m=m, xb=xb, rows=rows: e.transpose(
                            ps[b][:, m4 * 128: m4 * 128 + rows], xb[:rows, m * 128:(m + 1) * 128], ident_f[:rows, :rows]),
                            reads=[("xin", blk % 2), "cst"], writes=[("ps", b)])
                    k.op("act", lambda e, b=b, half=half, blk=blk, rows=rows: e.activation(
                        out=h[:, half * 4:(half + 1) * 4, blk * 128: blk * 128 + rows],
                        in_=ps[b][:].rearrange("p (a c) -> p a c", a=4)[:, :, :rows], func=AF.Copy),
                        reads=[("ps", b)], writes=[("h", blk)])
            k.barrier()

        def rms_stats(c0, N, rstd, tmp_sq, hkeys, rkey="rstd", sqkeys=(("sq", 0), ("sq", 1))):
            b = nps()
            for m in range(8):
                sq = tmp_sq[m % 2]
                k.op("act", lambda e, sq=sq, m=m: e.activation(out=sq[:, :N], in_=h[:, m, c0:c0 + N], func=AF.Square),
                     reads=hkeys, writes=[sqkeys[m % 2]])
                k.op("pe", lambda e, sq=sq, m=m, b=b: e.matmul(ps[b][:, :N], ones_b, sq[:, :N], start=(m == 0), stop=(m == 7)),
                     reads=[sqkeys[m % 2], "cb"], writes=[("ps", b)])
            k.op("act", lambda e: e.activation(out=rstd[:, :N], in_=ps[b][:, :N], func=AF.Sqrt, bias=1e-6, scale=1.0 / D),
                 reads=[("ps", b)], writes=[rkey])
            k.op("dve", lambda e: e.reciprocal(out=rstd[:, :N], in_=rstd[:, :N]), reads=[rkey], writes=[rkey])

        def hkeys_of(c0, N):
            return [("h", bk) for bk in range(c0 // 128, (c0 + N + 127) // 128)]

        def ffn(wg_d, wu_d, wd_d, normname, tag):
            with ExitStack() as st:
                u = sb("u" + tag, [128, 8, NT], BF16, st)
                hid = sb("hid" + tag, [128, 11, NT], BF16, st)
                WG = [sb("wg%d" % i + tag, [128, 8, 512], BF16, st) for i in range(2)]
                WU = [sb("wu%d" % i + tag, [128, 8, 512], BF16, st) for i in range(2)]
                WD = [sb("wd%d" % i + tag, [128, 11, 128], BF16, st) for i in range(2)]
                rstd = sb("rstd" + tag, [128, 512], F32, st)
                sq = [sb("sq%d" % i + tag, [128, 512], BF16, st) for i in range(2)]
                sg = [sb("sg%d" % i + tag, [128, 512], F32, st) for i in range(2)]
                def norm_tile(ti):
                    c0, N = TILES[ti][0], TILES[ti][1]
                    rms_stats(c0, N, rstd, sq, hkeys_of(c0, N))
                    for m in range(8):
                        k.op("dve", lambda e, m=m: e.scalar_tensor_tensor(
                            out=u[:, m, c0:c0 + N], in0=h[:, m, c0:c0 + N], scalar=col(normname, m), in1=rstd[:, :N],
                            op0=ALU.mult, op1=ALU.mult), reads=hkeys_of(c0, N) + ["rstd", "cols"], writes=[("u", ti)])
                norm_tile(0)
                gi = 0
                di = 0
                sgi = 0
                for half in range(2):
                    groups = [(0, 4), (4, 4), (8, 3)]
                    for (j0, nj) in groups:
                        ff0 = (half * 11 + j0) * 128
                        wgt, wut = WG[gi % 2], WU[gi % 2]
                        k.dma("pool", wgt[:, :, :nj * 128], wg_d[:, ff0:ff0 + nj * 128].rearrange("(kc p) c -> p kc c", p=128),
                              writes=[("wg", gi % 2)])
                        k.dma("pool", wut[:, :, :nj * 128], wu_d[:, ff0:ff0 + nj * 128].rearrange("(kc p) c -> p kc c", p=128),
                              writes=[("wu", gi % 2)])
                        one_cast()
                        for ti, (c0, N, _, _) in enumerate(TILES):
                            if gi == 0 and ti + 1 < len(TILES):
                                norm_tile(ti + 1)
                            for j in range(nj):
                                b1 = nps(); b2 = nps()
                                def mm(e, wt, b, j=j):
                                    ins = None
                                    for kc in range(8):
                                        ins = e.matmul(ps[b][:, :N], wt[:, kc, j * 128:(j + 1) * 128], u[:, kc, c0:c0 + N],
                                                       start=(kc == 0), stop=(kc == 7))
                                    return ins
                                k.op("pe", lambda e, mm=mm, wgt=wgt, b1=b1: mm(e, wgt, b1),
                                     reads=[("wg", gi % 2), ("u", ti)], writes=[("ps", b1)])
                                k.op("pe", lambda e, mm=mm, wut=wut, b2=b2: mm(e, wut, b2),
                                     reads=[("wu", gi % 2), ("u", ti)], writes=[("ps", b2)])
                                sgt = sg[sgi % 2]
                                k.op("act", lambda e, sgt=sgt, b1=b1: e.activation(out=sgt[:, :N], in_=ps[b1][:, :N], func=AF.Silu),
                                     reads=[("ps", b1)], writes=[("sg", sgi % 2)])
                                k.op("dve", lambda e, sgt=sgt, b2=b2, jj=j0 + j: e.tensor_tensor(
                                    out=hid[:, jj, c0:c0 + N], in0=sgt[:, :N], in1=ps[b2][:, :N], op=ALU.mult),
                                    reads=[("sg", sgi % 2), ("ps", b2)], writes=[("hid", ti, j0 + j)])
                                sgi += 1
                        gi += 1
                    for piece in range(8):
                        wdt = WD[di % 2]
                        r0 = half * 11 * 128
                        k.dma("pool", wdt[:, :, :], wd_d[r0:r0 + 11 * 128, piece * 128:(piece + 1) * 128].rearrange("(kc p) c -> p kc c", p=128),
                              writes=[("wd", di % 2)])
                        one_cast()
                        for ti, (c0, N, _, _) in enumerate(TILES):
                            for mm_ in range(1):
                                m = piece
                                b = nps()
                                def mmd(e, wdt=wdt, b=b, mm_=mm_, c0=c0, N=N):
                                    ins = None
                                    for kc in range(11):
                                        ins = e.matmul(ps[b][:, :N], wdt[:, kc, mm_ * 128:(mm_ + 1) * 128], hid[:, kc, c0:c0 + N],
                                                       start=(kc == 0), stop=(kc == 10))
                                    return ins
                                k.op("pe", mmd, reads=[("wd", di % 2)] + [("hid", ti, jj) for jj in range(11)], writes=[("ps", b)])
                                k.op("dve", lambda e, b=b, m=m, c0=c0, N=N: e.scalar_tensor_tensor(
                                    out=h[:, m, c0:c0 + N], in0=ps[b][:, :N], scalar=0.5, in1=h[:, m, c0:c0 + N],
                                    op0=ALU.mult, op1=ALU.add), reads=[("ps", b)] + hkeys_of(c0, N), writes=hkeys_of(c0, N))
                        di += 1
            k.barrier()

        if stage >= 1:
            ffn(w1g_d, w1u_d, w1d_d, "ffn1_norm", "a")
        def act(out, in_, func, r, w, **kw):
            k.op("act", lambda e: e.activation(out=out, in_=in_, func=func, **kw), reads=r, writes=w)

        def tt(en, out, a, b, op, r, w):
            k.op(en, lambda e: e.tensor_tensor(out=out, in0=a, in1=b, op=op), reads=r, writes=w)

        def ts(en, out, a, s1, s2, op0, op1, r, w):
            if s2 is None:
                k.op(en, lambda e: e.tensor_scalar(out, a, s1, None, op0), reads=r, writes=w)
            else:
                k.op(en, lambda e: e.tensor_scalar(out, a, s1, s2, op0, op1), reads=r, writes=w)

        def stt(en, out, in0, sc, in1, op0, op1, r, w):
            k.op(en, lambda e: e.scalar_tensor_tensor(out=out, in0=in0, scalar=sc, in1=in1, op0=op0, op1=op1),
                 reads=r, writes=w)

        def mm(out, lhsT, rhs, r, w, start=True, stop=True):
            k.op("pe", lambda e: e.matmul(out, lhsT, rhs, start=start, stop=stop), reads=r, writes=w)

        def mmg(items, r, w):
            def f(e):
                ins = None
                n = len(items)
                for i, (o, l, rh) in enumerate(items):
                    ins = e.matmul(o, l, rh, start=(i == 0), stop=(i == n - 1))
                return ins
            k.op("pe", f, reads=r, writes=w)

        def tr(out, in_, rows, r, w):
            k.op("pe", lambda e: e.transpose(out, in_, ident_f[:rows, :rows]), reads=r + ["cst"], writes=w)

        def copy(en, out, in_, r, w):
            if en == "act":
                act(out, in_, AF.Copy, r, w)
            else:
                k.op(en, lambda e: e.tensor_copy(out=out, in_=in_), reads=r, writes=w)

        def mixer():
            with ExitStack() as st:
                def T(name, shape, dt=F32, stack=st):
                    return sb(name, shape, dt, stack)
                lora_wa = T("lora_wa", [128, 512], BF16)
                lg2 = T("lg2s", [128, 512], BF16)
                wab = T("wab", [128, 8, 8], BF16)
                pjbuf = [T("pjbuf%d" % i, [128, 2, 4, 128], BF16) for i in range(2)]
                WI = [T("wi%d" % i, [128, 8, 256], BF16) for i in range(2)]
                u_t = T("u_t", [128, 8, 512], BF16)
                oab = T("oab", [128, 8, 512], BF16)
                rowst = T("rowst", [128, 256])
                halo_a = T("halo_a", [128, 14]); halo_c = T("halo_c", [128, 12, 3])
                sshT = T("sshT", [128, 14, 16]); scvT = T("scvT", [128, 12, 48])
                Zr = T("Zr", [128, 4, 128]); Zrb = T("Zrb", [128, 4, 2, 128], BF16)
                Zg = T("Zg", [128, 4, 128]); Zgb = T("Zgb", [128, 4, 2, 128], BF16)
                omka = T("omka", [128, 4]); eA = T("eA", [128, 4]); dtb = T("dtb", [128, 4]); omu = T("omu", [128, 14])
                dc = T("dc", [128, 16])
                k.dma("pool", lora_wa[0:64, :], lw2_d[:, :], writes=["lora_wa"])
                k.dma("pool", lora_wa[64:128, :], la2_d[:, :], writes=["lora_wa"])
                k.dma("pool", lg2[:], lg2_d[:, :], writes=["lg2"])
                k.dma("sp", wab[:], winb_d[:, 5888:5896].rearrange("(kc p) c -> p kc c", p=128), reads=[("winb", r_) for r_ in range(8)], writes=["wab"])
                for t_, nm in [(halo_a, "halo_a"), (halo_c, "halo_c")]:
                    k.op("pool", lambda e, t_=t_: e.memset(t_[:], 0.0), writes=[nm])
                for t_, nm in [(Zr, "Zr"), (Zg, "Zg")]:
                    k.op("pool", lambda e, t_=t_: e.memset(t_[:], 0.0), writes=[(nm, j) for j in range(4)])
                for t_, nm in [(Zrb, "Zrb"), (Zgb, "Zgb")]:
                    k.op("pool", lambda e, t_=t_: e.memset(t_[:], 0.0), writes=[(nm, j, p_) for j in range(4) for p_ in range(2)])
                ts("dve", omka[:], cols[:, COLS["k_a"]:COLS["k_a"] + 4], -1.0, 1.0, ALU.mult, ALU.add, ["cols"], ["omka"])
                act(eA[:], cols[:, COLS["A_log"]:COLS["A_log"] + 4], AF.Exp, ["cols"], ["eA"])
                ts("dve", omu[:], cols[:, COLS["mu"]:COLS["mu"] + 14], -1.0, 1.0, ALU.mult, ALU.add, ["cols"], ["omu"])
                copy("dve", dtb[:], cols[:, COLS["dt_bias"]:COLS["dt_bias"] + 4], ["cols"], ["dtb"])
                with ExitStack() as s0:
                    ld1 = T("ld1", [16, A_PROJ], F32, s0); ld2 = T("ld2", [48, 1536], F32, s0)
                    k.dma("sp", ld1[:], ssh_d[:, :], writes=["ld1"])
                    k.dma("sp", ld2[:], scv_d[:, :], writes=["ld2"])
                    b = nps()
                    for c in range(14):
                        tr(ps[b][:, c * 16:(c + 1) * 16], ld1[:, c * 128:(c + 1) * 128], 16, ["ld1"], [("ps", b)])
                    copy("act", sshT[:].rearrange("p a s -> p (a s)"), ps[b][:, :224], [("ps", b)], ["sshT"])
                    for hf in range(2):
                        b = nps()
                        for c6 in range(6):
                            c = hf * 6 + c6
                            tr(ps[b][:, c6 * 48:(c6 + 1) * 48], ld2[:, c * 128:(c + 1) * 128], 48, ["ld2"], [("ps", b)])
                        copy("act", scvT[:, hf * 6:(hf + 1) * 6, :].rearrange("p a s -> p (a s)"), ps[b][:, :288],
                             [("ps", b)], ["scvT"])
                    k.barrier()

                bones_f = cc("bones"); ones_f = cc("ones"); ident_b = cb[:, 0:128]

                for ti, (c0, N, nseq, L) in enumerate(TILES):
                    if ti not in DBG.get('tiles', range(5)):
                        continue
                    samp = nseq > 1
                    C = 64 if samp else 128
                    nch = 1 if samp else 4
                    nS = NSQ if samp else 1
                    nsg, lsg = (NSQ, LS) if samp else (4, 128)
                    nlev = 1 if samp else 6
                    mI = cc("incl_s" if samp else "incl")[:C, :C]
                    mS = cc("strict_s" if samp else "strict")[:C, :C]
                    mT = cc("tail_s" if samp else "tail")[:C, :C]
                    hk = hkeys_of(c0, N)
                    with ExitStack() as ts_:
                        def TT(name, shape, dt=F32):
                            return sb(name + "_%d" % ti, shape, dt, ts_)
                        tmp = [TT("tmp%d" % i, [128, N]) for i in range(12)]
                        tk = ["tmp%d" % i for i in range(12)]
                        pa_c = TT("pa_c", [128, N + 3 * nseq]); pa_c2 = TT("pa_c2", [128, N + 3 * nseq])
                        r_t = TT("r_t", [128, N]); k_t = TT("k_t", [128, N]); v_t = TT("v_t", [128, N]); o_t = TT("o_t", [128, N])
                        z_t = TT("z_t", [128, N])
                        lora_in = TT("lora_in", [128, N], BF16); sig_gd = TT("sig_gd", [128, N], BF16)
                        sqb = TT("sqb", [128, N], BF16); bnb = TT("bnb", [128, N], BF16); sqp = TT("sqp", [128, N], BF16)
                        atB = TT("atB", [128, N], BF16); btB = TT("btB", [128, N], BF16); ktB = TT("ktB", [128, N], BF16); rtB = TT("rtB", [128, N], BF16)
                        gtB = TT("gtB", [128, N]); bnB = TT("bnB", [128, N]); dcB = TT("dcB", [128, 16])
                        at = TT("at", [128, N], BF16); bt = TT("bt", [128, N], BF16); kt = TT("kt", [128, N], BF16)
                        rt = TT("rt", [128, N], BF16)
                        atz = TT("atz", [128, 2, N], BF16); btz = TT("btz", [128, 2, N], BF16); rtz = TT("rtz", [128, 2, N], BF16)
                        tok = TT("tok", [128, nch, 5, 128], BF16); Vz = TT("Vz", [128, nch, 2, 128], BF16)
                        cm1 = [TT("cm1_%d" % i, [128, 4, 128], BF16) for i in range(nch)]
                        nt2 = [TT("nt2_%d" % i, [128, 2, 128], BF16) for i in range(nch)]
                        cm3 = [TT("cm3_%d" % i, [128, 4, 128], BF16) for i in range(nch)]
                        Tt = [TT("Tt%d" % i, [128, 2, 128], BF16) for i in range(nch)]
                        PQ = [TT("PQ%d" % i, [128, 4, 128], BF16) for i in range(nch)]
                        Xb = TT("Xb", [128, 128], BF16); Wz = TT("Wz", [128, 2, 128], BF16); Wb = TT("Wb", [128, 128], BF16)
                        gtok = TT("gtok", [128, nch, 4]); btok = TT("btok", [128, nch, 4]); sptmp = TT("sptmp", [128, nch, 4])
                        if samp:
                            g_rep, b_rep, Gs, eg, ek, dmi, dms = [[TT(nm_, [128, 128])] for nm_ in ("g_rep", "b_rep", "Gs", "eg", "ek", "dmi", "dms")]
                            segm_t = TT("segm", [128, 1024])
                            k.dma("sp", segm_t[:], consts2_d[:, :], writes=["segm"])
                            segm = segm_t[:, :].rearrange("p (s i) -> p s i", s=NSQ)
                        else:
                            g_rep, b_rep, Gs, eg, ek, dmi, dms = [[tmp[5 + j_][:, q_ * 128:(q_ + 1) * 128] for q_ in range(4)] for j_ in range(7)]
                        dmt_t = TT("dmt", [128, nch, 128]); kbf_t = TT("kbf", [128, nch, 128]); kbb_t = TT("kbb", [128, nch, 128], BF16)
                        dmt = [dmt_t[:, q_, :] for q_ in range(nch)]; kbf = [kbf_t[:, q_, :] for q_ in range(nch)]
                        kbb = [kbb_t[:, q_, :] for q_ in range(nch)]
                        merged = None
                        if samp:
                            Zs = TT("Zs", [128, NSQ, 128]); Zsb = TT("Zsb", [128, NSQ, 128], BF16)
                            bigf = TT("bigf", [128, NSQ, 128]); stg = TT("stg", [128, NSQ, 64])
                            sbd = [TT("sbd%d" % i, [128, NSQ, 128]) for i in range(2)]
                            atm = TT("atm", [128, NSQ, 64], BF16); rtm = TT("rtm", [128, NSQ, 64], BF16)
                            Wm = TT("Wm", [128, NSQ, 128], BF16); Vm = TT("Vm", [128, NSQ, 128], BF16)
                        else:
                            bigf = TT("bigf", [128, 1, 128])

                        ATs = [(at, bt, kt, rt), (atB, btB, ktB, rtB)]
                        GTs = [(tmp[2], tk[2]), (gtB, "gtB")]
                        BNs = [(tmp[11], tk[11]), (bnB, "bnB")]
                        DCs = [dc, dcB]

                        def v3(ap):
                            return ap.rearrange("p (s l) -> p s l", s=nseq)

                        def sg3(ap):
                            return ap.rearrange("p (s l) -> p s l", s=nsg)

                        rms_stats(c0, N, tmp[0], [at, bt], hk, rkey=tk[0], sqkeys=("at", "bt"))
                        for m in range(8):
                            stt("dve", u_t[:, m, :N], h[:, m, c0:c0 + N], col("mix_norm", m), tmp[0][:, :N], ALU.mult, ALU.mult,
                                hk + [tk[0], "cols"], ["u_t"])

                        need_rows = (ti >= 3)
                        M_rows = 64 if samp else 4
                        rows_lo = 0 if samp else 508
                        state = {"g": -1, "issued": set()}

                        def issue_load(g):
                            if g > 22 or g in state["issued"]:
                                return
                            state["issued"].add(g)
                            wt = WI[g % 2]
                            ncols = 256
                            k.dma("sp", wt[:, :, :ncols], winb_d[:, g * 256: g * 256 + ncols].rearrange("(kc p) c -> p kc c", p=128),
                                  reads=[("winb", r_) for r_ in range(8)], writes=[("wi", g % 2)])

                        def load_group(g):
                            wt = WI[g % 2]
                            issue_load(g)
                            if need_rows and g < 15:
                                b = nps()
                                mmg([(ps[b][:M_rows, :256], u_t[:, kc, rows_lo:rows_lo + M_rows], wt[:, kc, :]) for kc in range(8)],
                                    ["u_t", ("wi", g % 2)], [("ps", b)])
                                copy("act", rowst[:M_rows, :], ps[b][:M_rows, :256], [("ps", b)], ["rowst"])
                                if samp:
                                    k.dma("sp", trs_d[:, g * 256:(g + 1) * 256], rowst[:64, :], reads=["rowst"])
                                else:
                                    k.dma("sp", trp_d[:, g * 256:(g + 1) * 256], rowst[:4, :], reads=["rowst"])

                        def pchunk(c):
                            g = c // 2
                            if g != state["g"]:
                                load_group(g)
                                state["g"] = g
                                issue_load(g + 1)
                            wt = WI[g % 2]
                            off = (c % 2) * 128
                            b = nps()
                            mmg([(ps[b][:, :N], wt[:, kc, off:off + 128], u_t[:, kc, :N]) for kc in range(8)],
                                ["u_t", ("wi", g % 2)], [("ps", b)])
                            return b

                        def a_chunk(c, dest, dkey):
                            b = pchunk(c)
                            pab, pak = ((pa_c, "pa_c"), (pa_c2, "pa_c2"))[c % 2]
                            pv = pab[:, :N + nseq].rearrange("p (s l) -> p s l", s=nseq)
                            copy("act", pv[:, :, 1:], v3(ps[b][:, :N]), [("ps", b)], [pak])
                            act(v3(dest), v3(ps[b][:, :N]), AF.Copy, [("ps", b), "omu"], [dkey], scale=omu[:, c:c + 1])
                            if samp:
                                copy("pool", pv[:, :, 0], sshT[:, c, :], ["sshT"], [pak])
                            else:
                                copy("pool", pv[:, :, 0], halo_a[:, c:c + 1], ["halo_a"], [pak])
                                copy("pool", halo_a[:, c:c + 1], pv[:, :, L], [pak], ["halo_a"])
                            stt("dve", v3(dest), pv[:, :, 0:L], col("mu", c), v3(dest), ALU.mult, ALU.add,
                                [pak, "cols", dkey], [dkey])

                        def g_load(hh, role, dest, dk):
                            cq = role * 4 + hh
                            b = pchunk(14 + hh * 4 + role)
                            pab, pak = ((pa_c, "pa_c"), (pa_c2, "pa_c2"))[role % 2]
                            xv3 = pab[:, :N + 3 * nseq].rearrange("p (s l) -> p s l", s=nseq)
                            copy("act", xv3[:, :, 3:], v3(ps[b][:, :N]), [("ps", b)], [pak])
                            if samp:
                                copy("pool", xv3[:, :, 0:3], scvT[:, cq, :].rearrange("p (s i) -> p s i", s=NSQ), ["scvT"], [pak])
                            else:
                                copy("pool", xv3[:, :, 0:3], halo_c[:, cq:cq + 1, :], ["halo_c"], [pak])
                                copy("pool", halo_c[:, cq:cq + 1, :], xv3[:, :, L:L + 3], [pak], ["halo_c"])

                        def g_conv(hh, role, dest, dk):
                            cq = role * 4 + hh
                            pab, pak = ((pa_c, "pa_c"), (pa_c2, "pa_c2"))[role % 2]
                            xv3 = pab[:, :N + 3 * nseq].rearrange("p (s l) -> p s l", s=nseq)
                            ts("dve", v3(dest[:]), xv3[:, :, 0:L], col("conv_w", 0 * 12 + cq), None, ALU.mult, None, [pak, "cols"], [dk])
                            for i in range(1, 4):
                                stt("dve", v3(dest[:]), xv3[:, :, i:i + L], col("conv_w", i * 12 + cq), v3(dest[:]),
                                    ALU.mult, ALU.add, [pak, "cols", dk], [dk])

                        def g_silu(dest, dk):
                            act(dest[:], dest[:], AF.Silu, [dk], [dk])


                        def gdn_prefetch(hh):
                            g_load(hh, 0, r_t, "r_t"); g_load(hh, 1, k_t, "k_t")
                            g_conv(hh, 0, r_t, "r_t")
                            g_load(hh, 2, v_t, "v_t")
                            g_conv(hh, 1, k_t, "k_t")
                            g_silu(r_t, "r_t")
                            g_conv(hh, 2, v_t, "v_t")
                            g_silu(k_t, "k_t"); g_silu(v_t, "v_t")

                        issue_load(0)
                        if samp:
                            srv = srw_d.rearrange("(s h v) k -> h v s k", s=NSQ, h=8)
                            sdv = sdl_d.rearrange("(s h k) v -> h k s v", s=NSQ, h=4)

                            def load_rwkv_state(u_):
                                t_ = sbd[u_ % 2]; tkey = ("sbd", u_ % 2)
                                k.dma("sp", t_[0:64, :, 0:64], srv[2 * u_], writes=[tkey])
                                k.dma("sp", t_[64:128, :, 64:128], srv[2 * u_ + 1], writes=[tkey])

                            def load_gdn_state(h_):
                                k.dma("sp", sbd[h_ % 2][:], sdv[h_], writes=[("sbd", h_ % 2)])

                            for i_ in range(2):
                                k.op("pool", lambda e, i_=i_: e.memset(sbd[i_][:], 0.0), writes=[("sbd", i_)])
                            load_rwkv_state(0)
                        a_chunk(0, tmp[1][:], tk[1])
                        act(lora_in[0:64, :], tmp[1][0:64, :], AF.Tanh, [tk[1]], ["lora_in"])
                        copy("pool", lora_in[64:128, :], tmp[1][64:128, :], [tk[1]], ["lora_in"])
                        a_chunk(1, tmp[1][:], tk[1])
                        act(sig_gd[:], tmp[1][:], AF.Sigmoid, [tk[1]], ["sig_gd"])
                        pre_ops, fin_ops, chunk_ops, gpf_ops = [], [], [], []
                        for u in range(DBG.get('nu', 4)):
                            p_ = u % 2
                            at_, bt_, kt_, rt_ = ATs[p_]
                            kat, kbt, kkt, krt = ["%s%d" % (n_, p_) for n_ in ("at", "bt", "kt", "rt")] if p_ else ["at", "bt", "kt", "rt"]
                            dc_, kdc = DCs[p_], "dc%d" % p_
                            psmode[0] = 2 if u > 0 else 0
                            k.begin()
                            a_chunk(2 + 3 * u, r_t[:], "r_t"); a_chunk(3 + 3 * u, k_t[:], "k_t"); a_chunk(4 + 3 * u, v_t[:], "v_t")
                            if samp and u > 0:
                                load_rwkv_state(u)
                            ucol = slice(u * 128, (u + 1) * 128)
                            sigw, a_t, g_t, kk, kkn, kmod, b_t, cl, e_t, Bh, Kh, bonus = tmp
                            ksw, ka, kg, kkk, kkkn, kkmod, kb_, kcl, ke, kBh, kKh, kbon = tk
                            g_t, kg = GTs[p_]
                            bonus, kbon = BNs[p_]
                            e1, e2, e3 = tmp[3], Bh, Kh
                            ke1, ke2, ke3 = tk[3], kBh, kKh
                            act(sqb[:], k_t[:], AF.Square, ["k_t", "cols"], ["sqb"], scale=col("k_k", u))
                            bw = nps()
                            mm(ps[bw][:, :N], lora_wa[0:64, ucol], lora_in[0:64, :], ["lora_wa", "lora_in"], [("ps", bw)])
                            ba = nps()
                            mm(ps[ba][:, :N], lora_wa[64:128, ucol], lora_in[64:128, :], ["lora_wa", "lora_in"], [("ps", ba)])
                            bg = nps()
                            mm(ps[bg][:, :N], lg2[:, ucol], sig_gd[:], ["lg2", "sig_gd"], [("ps", bg)])
                            bn = nps()
                            mm(ps[bn][:, :N], bones_b, sqb[:], ["sqb", "cb"], [("ps", bn)])
                            act(sigw[:], ps[bw][:, :N], AF.Sigmoid, [("ps", bw), "cols"], [ksw], bias=col("w0", u))
                            act(a_t[:], ps[ba][:, :N], AF.Sigmoid, [("ps", ba), "cols"], [ka], bias=col("a0", u))
                            copy("act", g_t[:], ps[bg][:, :N], [("ps", bg)], [kg])
                            ts("dve", e_t[:], ps[bn][:, :N], 1e-24, None, ALU.max, None, [("ps", bn)], [ke])
                            if samp:
                                k.op("dve", lambda e: e.tensor_tensor_scan(out=cl[:], data0=cc("reset_s", 64), data1=sigw[:], initial=0.0,
                                                                           op0=ALU.mult, op1=ALU.add), reads=[ksw, "cst"], writes=[kcl])
                            else:
                                for q in range(4):
                                    k.op("dve", lambda e, q=q: e.tensor_tensor_scan(
                                        out=cl[:, q * 128:(q + 1) * 128], data0=ones_f, data1=sigw[:, q * 128:(q + 1) * 128], initial=0.0,
                                        op0=ALU.mult, op1=ALU.add), reads=[ksw, "cst"], writes=[kcl])
                            act(e_t[:], e_t[:], AF.Ln, [ke], [ke])
                            act(e_t[:], e_t[:], AF.Exp, [ke], [ke], scale=-0.5)
                            ts("dve", kmod[:], a_t[:], col("k_a", u), omka[:, u:u + 1], ALU.mult, ALU.add, [ka, "cols", "omka"], [kkmod])
                            tt("pool", e1[:], cl[:], sigw[:], ALU.subtract, [kcl, ksw], [ke1])
                            tt("pool", kmod[:], kmod[:], k_t[:], ALU.mult, [kkmod, "k_t"], [kkmod])
                            stt("dve", kkn[:], k_t[:], col("k_k", u), e_t[:], ALU.mult, ALU.mult, ["k_t", "cols", ke], [kkkn])
                            act(e1[:], e1[:], AF.Exp, [ke1], [ke1], scale=-C0)
                            act(e2[:], cl[:], AF.Exp, [kcl], [ke2], scale=C0)
                            act(e3[:], cl[:], AF.Exp, [kcl], [ke3], scale=-C0)
                            tt("pool", b_t[:], kkn[:], a_t[:], ALU.mult, [kkkn, ka], [kb_])
                            stt("dve", bnb[:], r_t[:], col("r_k", u), kmod[:], ALU.mult, ALU.mult, ["r_t", kkmod, "cols"], ["bnb"])
                            bb = nps()
                            mm(ps[bb][:, :N], bones_b, bnb[:], ["bnb", "cb"], [("ps", bb)])
                            stt("dve", at_[:], kkn[:], -1.0, e1[:], ALU.mult, ALU.mult, [kkkn, ke1], [kat])
                            cl3 = sg3(cl[:])
                            tt("pool", sg3(e_t[:]), cl3[:, :, lsg - 1:lsg].to_broadcast([128, nsg, lsg]), cl3, ALU.subtract, [kcl, kkkn], [ke])
                            tt("dve", kt_[:], kmod[:], e2[:], ALU.mult, [kkmod, ke2], [kkt])
                            tt("pool", bt_[:], b_t[:], e2[:], ALU.mult, [kb_, ke2], [kbt])
                            tt("dve", rt_[:], r_t[:], e3[:], ALU.mult, ["r_t", ke3], [krt])
                            act(e_t[:], e_t[:], AF.Exp, [ke], [ke], scale=-C0)
                            act(dc_[:, :nsg], cl3[:, :, lsg - 1], AF.Exp, [kcl], [kdc], scale=-C0)
                            tt("dve", bonus[:], ps[bb][:, :N], v_t[:], ALU.mult, [("ps", bb), "v_t"], [kbon])
                            tt("pool", Bh[:], b_t[:], e_t[:], ALU.mult, [kb_, ke, kbt, kkt], [kBh])
                            tt("dve", Kh[:], kmod[:], e_t[:], ALU.mult, [kkmod, ke, krt], [kKh])
                            pre_ops.append(k.end())
                            psmode[0] = 1
                            k.begin()
                            for hh_ in range(2):
                                lo = hh_ * 64
                                for src, skey_, dst, kn in [(at_, kat, atz, "atz"), (bt_, kbt, btz, "btz"), (rt_, krt, rtz, "rtz")]:
                                    if hh_ == 0:
                                        act(dst[:, hh_, :], src[:, :], AF.Copy, [skey_, "cst"], [kn], scale=bones_f[:, lo:lo + 1])
                                    else:
                                        ts("dve", dst[:, hh_, :], src[:, :], bones_f[:, lo:lo + 1], None, ALU.mult, None, [skey_, "cst"], [kn])
                            for ci in range(nch):
                                cs = slice(ci * C, (ci + 1) * C)
                                b = nps()
                                tr(ps[b][:C, 0:128], Bh[:, cs], 128, [kBh], [("ps", b)])
                                tr(ps[b][:C, 128:256], Kh[:, cs], 128, [kKh], [("ps", b)])
                                tr(ps[b][:C, 256:384], v_t[:, cs], 128, ["v_t"], [("ps", b)])
                                hm2 = cc("hmask", 256)[:C, :].rearrange("p (a c) -> p a c", a=2)
                                tt("dve", tok[:C, ci, 0:2, :], ps[b][:C, 0:128].unsqueeze(1).to_broadcast([C, 2, 128]), hm2, ALU.mult,
                                   [("ps", b), "cst"], ["tok"])
                                tt("dve", tok[:C, ci, 2:4, :], ps[b][:C, 128:256].unsqueeze(1).to_broadcast([C, 2, 128]), hm2, ALU.mult,
                                   [("ps", b), "cst"], ["tok"])
                                copy("act", tok[:C, ci, 4, :], ps[b][:C, 256:384], [("ps", b)], ["tok"])
                                tt("pool", Vz[:C, ci, :, :], tok[:C, ci, 4:5, :].to_broadcast([C, 2, 128]), hm2, ALU.mult, ["tok", "cst"], ["Vz"])
                            fin_ops.append(k.end())
                            if u == 3 and DBG.get('nh', 4) > 0:
                                psmode[0] = 2
                                k.begin(); gdn_prefetch(0); gpf_ops = k.end()
                                psmode[0] = 1
                            k.begin()
                            if samp:
                                for q in range(4):
                                    b = nps()
                                    for s4 in range(4):
                                        s = q * 4 + s4
                                        tr(ps[b][:, s4 * 128:(s4 + 1) * 128], sbd[u % 2][:, s, :], 128, [("sbd", u % 2)], [("ps", b)])
                                    copy("act", Zs[:, q * 4:(q + 1) * 4, :].rearrange("p a c -> p (a c)"), ps[b][:, :], [("ps", b)], ["Zs"])
                                copy("act", Zsb[:], Zs[:], ["Zs"], ["Zsb"])
                                Zf, zk = Zs, "Zs"
                                Zrd = lambda ci: [Zsb[:, s_, :] for s_ in range(NSQ)]
                                zrk = lambda ci: "Zsb"
                                tt("pool", atm[:], at_[:, :].unsqueeze(1).to_broadcast([128, NSQ, 64]),
                                   segm, ALU.mult, [kat, "segm"], ["atm"])
                                tt("pool", rtm[:], rt_[:, :].unsqueeze(1).to_broadcast([128, NSQ, 64]),
                                   segm, ALU.mult, [krt, "segm"], ["rtm"])
                            else:
                                Zf, zk = Zr[:, u:u + 1, :], ("Zr", u)
                                Zrd = lambda ci: [Zrb[:, u, ci % 2, :]]
                                zrk = lambda ci: ("Zrb", u, ci % 2)
                            pv4 = lambda b_: ps[b_][:C, :].rearrange("p (a c) -> p a c", a=4)[:, :, :C]
                            for ci in range(nch):
                                cs = slice(ci * C, (ci + 1) * C)
                                b1 = nps()
                                for hh_ in range(2):
                                    mm(ps[b1][:C, hh_ * 128: hh_ * 128 + C], bt_[:, cs], atz[:, hh_, cs], [kbt, "atz"], [("ps", b1)])
                                    mm(ps[b1][:C, (2 + hh_) * 128: (2 + hh_) * 128 + C], kt_[:, cs], atz[:, hh_, cs], [kkt, "atz"], [("ps", b1)])
                                b2 = nps()
                                for hh_ in range(2):
                                    mm(ps[b2][:C, hh_ * 128: hh_ * 128 + C], at_[:, cs], btz[:, hh_, cs], [kat, "btz"], [("ps", b2)])
                                b3 = nps()
                                for hh_ in range(2):
                                    mm(ps[b3][:C, hh_ * 128: hh_ * 128 + C], bt_[:, cs], rtz[:, hh_, cs], [kbt, "rtz"], [("ps", b3)])
                                    mm(ps[b3][:C, (2 + hh_) * 128: (2 + hh_) * 128 + C], kt_[:, cs], rtz[:, hh_, cs], [kkt, "rtz"], [("ps", b3)])
                                tt("dve", cm1[ci][:C, :, :C], pv4(b1), mS.unsqueeze(1).to_broadcast([C, 4, C]), ALU.mult, [("ps", b1), "cst"], [("cm1", ci)])
                                tt("dve", nt2[ci][:C, :, :C], pv4(b2)[:, 0:2, :], mT.unsqueeze(1).to_broadcast([C, 2, C]), ALU.mult, [("ps", b2), "cst"], [("nt2", ci)])
                                tt("dve", cm3[ci][:C, :, :C], pv4(b3), mI.unsqueeze(1).to_broadcast([C, 4, C]), ALU.mult, [("ps", b3), "cst"], [("cm3", ci)])
                                tt("pool", Tt[ci][:C, :, :C], cm1[ci][:C, 0:2, :C], ident_b[:C, :C].unsqueeze(1).to_broadcast([C, 2, C]), ALU.add,
                                   [("cm1", ci), "cb"], [("Tt", ci)])
                            Pm = [[cm1[ci][:C, 0, :C], cm1[ci][:C, 1, :C]] for ci in range(nch)]
                            Qm = [[nt2[ci][:C, 0, :C], nt2[ci][:C, 1, :C]] for ci in range(nch)]
                            pqk = [[("cm1", ci), ("nt2", ci)] for ci in range(nch)]
                            for lev in range(nlev):
                                bs = []
                                for ci in range(nch):
                                    b = nps(); bs.append(b)
                                    for hh_ in range(2):
                                        if lev < nlev - 1:
                                            mm(ps[b][:C, hh_ * 128: hh_ * 128 + C], Qm[ci][hh_], Pm[ci][hh_], pqk[ci], [("ps", b)])
                                        mm(ps[b][:C, (2 + hh_) * 128: (2 + hh_) * 128 + C], Pm[ci][hh_], Qm[ci][hh_], pqk[ci], [("ps", b)])
                                for ci in range(nch):
                                    b = bs[ci]; pq = PQ[ci]
                                    if lev < nlev - 1:
                                        copy("act" if ci != 3 else "dve", pq[:C, :, :C], pv4(b), [("ps", b)], [("PQ", ci)])
                                    else:
                                        copy("act" if ci != 3 else "dve", pq[:C, 2:4, :C], pv4(b)[:, 2:4, :], [("ps", b)], [("PQ", ci)])
                                    Pm[ci] = [pq[:C, 0, :C], pq[:C, 1, :C]]; Qm[ci] = [pq[:C, 2, :C], pq[:C, 3, :C]]
                                    pqk[ci] = [("PQ", ci)]
                                bs = []
                                for ci in range(nch):
                                    b = nps(); bs.append(b)
                                    for hh_ in range(2):
                                        mm(ps[b][:C, hh_ * 128: hh_ * 128 + C], Qm[ci][hh_], Tt[ci][:C, hh_, :C], [("PQ", ci), ("Tt", ci)], [("ps", b)])
                                for ci in range(nch):
                                    b = bs[ci]
                                    tt("dve", Tt[ci][:C, :, :C], pv4(b)[:, 0:2, :], Tt[ci][:C, :, :C], ALU.add,
                                       [("ps", b), ("Tt", ci)], [("Tt", ci)])
                            for ci in range(nch):
                                cs = slice(ci * C, (ci + 1) * C)
                                b = nps()
                                items = []
                                for s in range(nS):
                                    items.append((ps[b][:C, 0:128], (atm[:, s, :] if samp else at_[:, cs]), Zrd(ci)[s]))
                                for hh_ in range(2):
                                    items.append((ps[b][:C, 0:128], cm1[ci][:C, 2 + hh_, :C], Vz[:C, ci, hh_, :]))
                                mmg(items, [kat, "atm", zrk(ci), ("cm1", ci), "Vz"] if samp else [kat, zrk(ci), ("cm1", ci), "Vz"], [("ps", b)])
                                copy("act", Xb[:C, :], ps[b][:C, 0:128], [("ps", b)], ["Xb"])
                                b = nps()
                                for hh_ in range(2):
                                    mm(ps[b][:C, hh_ * 128:(hh_ + 1) * 128], Tt[ci][:C, hh_, :C], Xb[:C, :], [("Tt", ci), "Xb"], [("ps", b)])
                                tt("dve", Wz[:C, :, :], ps[b][:C, 0:256].rearrange("p (a c) -> p a c", a=2),
                                   cc("hmask", 256)[:C, :].rearrange("p (a c) -> p a c", a=2), ALU.mult, [("ps", b), "cst"], ["Wz"])
                                if samp:
                                    tt("pool", Wb[:C, :], Wz[:C, 0, :], Wz[:C, 1, :], ALU.add, ["Wz"], ["Wb"])
                                    tt("dve", Wm[:C, :, :], Wb[:C, :].unsqueeze(1).to_broadcast([C, NSQ, 128]),
                                       cc("rowseg", 16)[:C, :].unsqueeze(2).to_broadcast([C, NSQ, 128]), ALU.mult, ["Wb", "cst"], ["Wm"])
                                    tt("dve", Vm[:C, :, :], tok[:C, ci, 4:5, :].to_broadcast([C, NSQ, 128]),
                                       cc("rowseg", 16)[:C, :].unsqueeze(2).to_broadcast([C, NSQ, 128]), ALU.mult, ["tok", "cst"], ["Vm"])
                                    for q in range(4):
                                        b = nps()
                                        mmg([(ps[b][:, :], tok[:C, ci, 0, :], Wm[:C, q * 4:(q + 1) * 4, :].rearrange("p a c -> p (a c)")),
                                             (ps[b][:, :], tok[:C, ci, 1, :], Wm[:C, q * 4:(q + 1) * 4, :].rearrange("p a c -> p (a c)")),
                                             (ps[b][:, :], tok[:C, ci, 2, :], Vm[:C, q * 4:(q + 1) * 4, :].rearrange("p a c -> p (a c)")),
                                             (ps[b][:, :], tok[:C, ci, 3, :], Vm[:C, q * 4:(q + 1) * 4, :].rearrange("p a c -> p (a c)"))],
                                            ["tok", "Wm", "Vm"], [("ps", b)])
                                        tt("dve", bigf[:, q * 4:(q + 1) * 4, :], ps[b][:, :].rearrange("p (a c) -> p a c", a=4),
                                           bones_f.unsqueeze(1).to_broadcast([128, 4, 128]), ALU.mult, [("ps", b), "cst"], ["bigf"])
                                    tt("dve", Zs[:], Zs[:], dc_[:, :NSQ].unsqueeze(2).to_broadcast([128, NSQ, 128]), ALU.mult, ["Zs", kdc], ["Zs"])
                                    tt("dve", Zs[:], Zs[:], bigf[:], ALU.add, ["Zs", "bigf"], ["Zs"])
                                else:
                                    bz = nps()
                                    mmg([(ps[bz][:, 0:128], tok[:, ci, 0, :], Wz[:, 0, :]), (ps[bz][:, 0:128], tok[:, ci, 1, :], Wz[:, 1, :]),
                                         (ps[bz][:, 0:128], tok[:, ci, 2, :], Vz[:, ci, 0, :]), (ps[bz][:, 0:128], tok[:, ci, 3, :], Vz[:, ci, 1, :])],
                                        ["tok", "Wz", "Vz"], [("ps", bz)])
                                    stt("dve", Zrb[:, u, (ci + 1) % 2, :], Zf[:, 0, :], dc_[:, ci:ci + 1], ps[bz][:, 0:128], ALU.mult, ALU.add,
                                        [zk, kdc, ("ps", bz)], [("Zrb", u, (ci + 1) % 2)])
                                    stt("dve", Zf[:, 0, :], Zf[:, 0, :], dc_[:, ci:ci + 1], ps[bz][:, 0:128], ALU.mult, ALU.add, [zk, kdc, ("ps", bz)], [zk])
                                b = nps()
                                items = []
                                for s in range(nS):
                                    items.append((ps[b][:, :C], Zrd(ci)[s], (rtm[:, s, :] if samp else rt_[:, cs])))
                                for hh_ in range(2):
                                    items.append((ps[b][:, :C], Wz[:C, hh_, :], cm3[ci][:C, hh_, :C]))
                                    items.append((ps[b][:, :C], Vz[:C, ci, hh_, :], cm3[ci][:C, 2 + hh_, :C]))
                                mmg(items, [zrk(ci), krt, "rtm", "Wz", ("cm3", ci), "Vz"] if samp else [zrk(ci), krt, "Wz", ("cm3", ci), "Vz"], [("ps", b)])
                                copy("act", o_t[:, cs], ps[b][:, :C], [("ps", b)], ["o_t"])

                            if samp:
                                for q in range(4):
                                    b = nps()
                                    for s4 in range(4):
                                        tr(ps[b][:, s4 * 128:(s4 + 1) * 128], Zs[:, q * 4 + s4, :], 128, ["Zs"], [("ps", b)])
                                    pvq = ps[b][:, :].rearrange("p (a c) -> p a c", a=4)
                                    copy("act", stg[0:64, q * 4:(q + 1) * 4, :], pvq[0:64, :, 0:64], [("ps", b)], ["stg"])
                                    copy("act", stg[64:128, q * 4:(q + 1) * 4, :], pvq[64:128, :, 64:128], [("ps", b)], ["stg"])
                                rwv = rws_d.rearrange("(s h v) k -> h v s k", s=NSQ, h=8)
                                k.dma("sp", rwv[2 * u], stg[0:64, :, :], reads=["stg"])
                                k.dma("sp", rwv[2 * u + 1], stg[64:128, :, :], reads=["stg"])
                            elif ti == 3:
                                b = nps()
                                tr(ps[b][:, 0:128], Zr[:, u, :], 128, [("Zr", u)], [("ps", b)])
                                copy("act", bigf[0:64, 0, 0:64], ps[b][0:64, 0:64], [("ps", b)], ["bigf"])
                                copy("act", bigf[64:128, 0, 0:64], ps[b][64:128, 64:128], [("ps", b)], ["bigf"])
                                k.dma("sp", rwp_d[u * 128:(u + 1) * 128, :], bigf[:, 0, 0:64], reads=["bigf"])
                            m1, var_, o2 = [t_[:].rearrange("p a n -> p (a n)").bitcast(F32) for t_ in (atz, btz, rtz)]
                            km1, kvar, ko2 = "atz", "btz", "rtz"
                            b = nps()
                            mm(ps[b][:, :N], bones_f, o_t[:, :N], ["o_t", "cst"], [("ps", b)])
                            stt("dve", o2, ps[b][:, :N], -1.0 / 64, o_t[:, :N], ALU.mult, ALU.add, [("ps", b), "o_t"], [ko2])
                            act(sqp[:], o2, AF.Square, [ko2], ["sqp"])
                            b = nps()
                            mm(ps[b][:, :N], bones_b, sqp[:], ["sqp", "cb"], [("ps", b)])
                            act(var_, ps[b][:, :N], AF.Ln, [("ps", b)], [kvar], bias=64e-5, scale=1.0 / 64)
                            act(var_, var_, AF.Exp, [kvar], [kvar], scale=-0.5)
                            tt("pool", o2, o2, var_, ALU.mult, [ko2, kvar], [ko2])
                            ts("dve", o2, o2, col("lnx_w", u), col("lnx_b", u), ALU.mult, ALU.add, [ko2, "cols"], [ko2])
                            tt("pool", o2, o2, bonus[:], ALU.add, [ko2, kbon], [ko2])
                            tt("dve", oab[:, u, :N], o2, g_t[:], ALU.mult, [ko2, kg], [("oab", u)])
                            chunk_ops.append(k.end())
                        psmode[0] = 0
                        nu_ = len(pre_ops)
                        if nu_:
                            k.emit(pre_ops[0]); k.emit(fin_ops[0])
                            for u in range(nu_):
                                nxt = pre_ops[u + 1] if u + 1 < nu_ else gpf_ops
                                k.emit_interleaved(chunk_ops[u], nxt)
                                if u + 1 < nu_:
                                    k.emit(fin_ops[u + 1])

                        k.barrier()
                        for ci in range(nch):
                            b = nps()
                            mmg([(ps[b][:C, 0:8], u_t[:, kc, ci * C:(ci + 1) * C], wab[:, kc, :]) for kc in range(8)], ["u_t", "wab"], [("ps", b)])
                            act(btok[:C, ci, :], ps[b][:C, 4:8], AF.Sigmoid, [("ps", b)], ["btok"])
                            tt("dve", sptmp[:C, ci, :], ps[b][:C, 0:4], dtb[:C, :], ALU.add, [("ps", b), "dtb"], ["sptmp"])
                        act(sptmp[:C, :, :], sptmp[:C, :, :], AF.Exp, ["sptmp"], ["sptmp"])
                        act(sptmp[:C, :, :], sptmp[:C, :, :], AF.Ln, ["sptmp"], ["sptmp"], bias=1.0, scale=1.0)
                        stt("dve", gtok[:C, :, :], sptmp[:C, :, :], -1.0, eA[:C, :].unsqueeze(1).to_broadcast([C, nch, 4]), ALU.mult, ALU.mult,
                            ["sptmp", "eA"], ["gtok"])
                        for hh in range(DBG.get('nh', 4)):
                            xq, xk, xv = r_t, k_t, v_t
                            if hh == 0 and DBG.get('nu', 4) < 4:
                                gdn_prefetch(0)
                            bz_ = pchunk(14 + hh * 4 + 3)
                            act(z_t[:], ps[bz_][:, :N], AF.Silu, [("ps", bz_)], ["z_t"])
                            if DBG.get("gstop", 99) <= 1:
                                continue
                            qn, kn, qnb, knb_ = tmp[1], tmp[2], at, bt
                            for src, skey, dst, dkey, scl, sc_, sck, sb_, sbk in [(xq, "r_t", qn, tk[1], 128.0 ** -0.5, tmp[3], tk[3], sqb, "sqb"),
                                                                                    (xk, "k_t", kn, tk[2], 1.0, tmp[0], tk[0], bnb, "bnb")]:
                                act(sb_[:], src[:], AF.Square, [skey], [sbk])
                                b = nps()
                                mm(ps[b][:, :N], ones_b, sb_[:], [sbk, "cb"], [("ps", b)])
                                act(sc_[:], ps[b][:, :N], AF.Ln, [("ps", b)], [sck], bias=1e-6, scale=1.0)
                                act(sc_[:], sc_[:], AF.Exp, [sck], [sck], scale=-0.5)
                                stt("dve", dst[:], src[:], scl, sc_[:], ALU.mult, ALU.mult, [skey, sck], [dkey])
                            if DBG.get("gstop", 99) <= 2:
                                continue
                            copy("act", qnb[:], qn[:], [tk[1]], ["at"])
                            copy("dve", knb_[:], kn[:], [tk[2]], ["bt"])
                            qt_, Khf = kt, tmp[4]
                            atg = rt
                            if samp:
                                if hh == 0:
                                    load_gdn_state(0)
                                ZS, ZSK = sbd[hh % 2], ("sbd", hh % 2)
                                copy("act", Zsb[:], ZS[:], [ZSK], ["Zsb"])
                                Zf, zk = ZS, ZSK
                                Zrd = lambda ci: [Zsb[:, s_, :] for s_ in range(NSQ)]
                                zrk = lambda ci: "Zsb"
                            else:
                                Zf, zk = Zg[:, hh:hh + 1, :], ("Zg", hh)
                                Zrd = lambda ci, hh=hh: [Zgb[:, hh, ci % 2, :]]
                                zrk = lambda ci, hh=hh: ("Zgb", hh, ci % 2)
                            bAs, bBs = [], []
                            for ci in range(nch):
                                copy("pool", g_rep[ci][:C, :], gtok[:C, ci, hh:hh + 1].to_broadcast([C, 128]), ["gtok"], [("g_rep", ci)])
                                copy("pool", b_rep[ci][:C, :], btok[:C, ci, hh:hh + 1].to_broadcast([C, 128]), ["btok"], [("b_rep", ci)])
                                ts("dve", Gs[ci][:C, :C], mT, gtok[:C, ci, hh:hh + 1], None, ALU.mult, None, ["cst", "gtok"], [("Gs", ci)])
                            for ci in range(nch):
                                bA = nps(); bAs.append(bA)
                                mm(ps[bA][:, 0:C], g_rep[ci][:C, :], mI, [("g_rep", ci), "cst"], [("ps", bA)])
                                mm(ps[bA][:, 128:128 + C], g_rep[ci][:C, :], mT, [("g_rep", ci), "cst"], [("ps", bA)])
                                mm(ps[bA][:, 256:256 + C], b_rep[ci][:C, :], ident_f[:C, :C], [("b_rep", ci), "cst"], [("ps", bA)])
                                mm(ps[bA][:C, 384:384 + C], Gs[ci][:C, :C], mI, [("Gs", ci), "cst"], [("ps", bA)])
                                bB = nps(); bBs.append(bB)
                                mm(ps[bB][:C, 0:C], mI, Gs[ci][:C, :C], [("Gs", ci), "cst"], [("ps", bB)])
                            for ci in range(nch):
                                cs = slice(ci * C, (ci + 1) * C)
                                bA, bB = bAs[ci], bBs[ci]
                                act(eg[ci][:, :C], ps[bA][:, 0:C], AF.Exp, [("ps", bA)], [("eg", ci)])
                                act(ek[ci][:, :C], ps[bA][:, 128:128 + C], AF.Exp, [("ps", bA)], [("ek", ci)])
                                act(dmi[ci][:C, :C], ps[bA][:C, 384:384 + C], AF.Exp, [("ps", bA)], [("dmi", ci)])
                                tt("dve", kbf[ci][:, :C], kn[:, cs], ps[bA][:, 256:256 + C], ALU.mult, [tk[2], ("ps", bA)], [("kbf", ci)])
                                act(dmt[ci][:C, :C], ps[bB][:C, 0:C], AF.Exp, [("ps", bB)], [("dmt", ci)])
                                tt("pool", dms[ci][:C, :C], dmi[ci][:C, :C], mS, ALU.mult, [("dmi", ci), "cst"], [("dms", ci)])
                                tt("pool", dmi[ci][:C, :C], dmi[ci][:C, :C], mI, ALU.mult, [("dmi", ci), "cst"], [("dmi", ci)])
                                tt("pool", dmt[ci][:C, :C], dmt[ci][:C, :C], mT, ALU.mult, [("dmt", ci), "cst"], [("dmt", ci)])
                                copy("pool", kbb[ci][:, :C], kbf[ci][:, :C], [("kbf", ci)], [("kbb", ci)])
                                stt("dve", atg[:, cs], kbf[ci][:, :C], -1.0, eg[ci][:, :C], ALU.mult, ALU.mult, [("kbf", ci), ("eg", ci)], [("rt", ci)])
                                tt("pool", qt_[:, cs], qn[:, cs], eg[ci][:, :C], ALU.mult, [tk[1], ("eg", ci)], [("kt", ci)])
                                tt("pool", Khf[:, cs], kn[:, cs], ek[ci][:, :C], ALU.mult, [tk[2], ("ek", ci)], [(tk[4], ci)])
                            for ci in range(nch):
                                cs = slice(ci * C, (ci + 1) * C)
                                b = nps()
                                tr(ps[b][:C, 0:128], Khf[:, cs], 128, [(tk[4], ci)], [("ps", b)])
                                tr(ps[b][:C, 128:256], xv[:, cs], 128, ["v_t"], [("ps", b)])
                                copy("act", tok[:C, ci, 3:5, :].rearrange("p a c -> p (a c)"), ps[b][:C, 0:256], [("ps", b)], [("tok", ci)])
                            if hh + 1 < DBG.get('nh', 4):
                                gdn_prefetch(hh + 1)
                                if samp:
                                    load_gdn_state(hh + 1)
                            for ci in range(nch):
                                cs = slice(ci * C, (ci + 1) * C)
                                b1 = nps()
                                mm(ps[b1][:C, 0:C], knb_[:, cs], kbb[ci][:, :C], ["bt", ("kbb", ci)], [("ps", b1)])
                                mm(ps[b1][:C, 128:128 + C], kbb[ci][:, :C], knb_[:, cs], ["bt", ("kbb", ci)], [("ps", b1)])
                                mm(ps[b1][:C, 256:256 + C], knb_[:, cs], qnb[:, cs], ["bt", "at"], [("ps", b1)])
                                stt("dve", cm1[ci][:C, 0, :C], ps[b1][:C, 0:C], -1.0, dms[ci][:C, :C], ALU.mult, ALU.mult, [("ps", b1), ("dms", ci)], [("cm1", ci)])
                                stt("dve", nt2[ci][:C, 0, :C], ps[b1][:C, 128:128 + C], -1.0, dmt[ci][:C, :C], ALU.mult, ALU.mult, [("ps", b1), ("dmt", ci)], [("nt2", ci)])
                                tt("dve", cm3[ci][:C, 0, :C], ps[b1][:C, 256:256 + C], dmi[ci][:C, :C], ALU.mult, [("ps", b1), ("dmi", ci)], [("cm3", ci)])
                                tt("pool", Tt[ci][:C, 0, :C], cm1[ci][:C, 0, :C], ident_b[:C, :C], ALU.add, [("cm1", ci), "cb"], [("Tt", ci)])
                            Pm = [cm1[ci][:C, 0, :C] for ci in range(nch)]
                            Qm = [nt2[ci][:C, 0, :C] for ci in range(nch)]
                            pqk = [[("cm1", ci), ("nt2", ci)] for ci in range(nch)]
                            for lev in range(nlev):
                                bs = []
                                for ci in range(nch):
                                    b = nps(); bs.append(b)
                                    if lev < nlev - 1:
                                        mm(ps[b][:C, 0:C], Qm[ci], Pm[ci], pqk[ci], [("ps", b)])
                                    mm(ps[b][:C, 128:128 + C], Pm[ci], Qm[ci], pqk[ci], [("ps", b)])
                                for ci in range(nch):
                                    b = bs[ci]; pq = PQ[ci]
                                    pvv = ps[b][:C, 0:256].rearrange("p (a c) -> p a c", a=2)[:, :, :C]
                                    if lev < nlev - 1:
                                        copy("act" if ci != 3 else "dve", pq[:C, 0:2, :C], pvv, [("ps", b)], [("PQ", ci)])
                                    else:
                                        copy("act" if ci != 3 else "dve", pq[:C, 1, :C], ps[b][:C, 128:128 + C], [("ps", b)], [("PQ", ci)])
                                    Pm[ci], Qm[ci] = pq[:C, 0, :C], pq[:C, 1, :C]
                                    pqk[ci] = [("PQ", ci)]
                                bs = []
                                for ci in range(nch):
                                    b = nps(); bs.append(b)
                                    mm(ps[b][:C, 0:C], Qm[ci], Tt[ci][:C, 0, :C], [("PQ", ci), ("Tt", ci)], [("ps", b)])
                                for ci in range(nch):
                                    b = bs[ci]
                                    tt("dve", Tt[ci][:C, 0, :C], ps[b][:C, 0:C], Tt[ci][:C, 0, :C], ALU.add, [("ps", b), ("Tt", ci)], [("Tt", ci)])
                            if samp:
                                tt("pool", atm[:], atg[:, :].unsqueeze(1).to_broadcast([128, NSQ, 64]),
                                   segm, ALU.mult, [("rt", 0), "segm"], ["atm"])
                                tt("pool", rtm[:], qt_[:, :].unsqueeze(1).to_broadcast([128, NSQ, 64]),
                                   segm, ALU.mult, [("kt", 0), "segm"], ["rtm"])
                            for ci in range(nch):
                                cs = slice(ci * C, (ci + 1) * C)
                                zr = Zrd(ci); zrkey = zrk(ci)
                                b = nps()
                                mmg([(ps[b][:C, 0:128], (atm[:, s, :] if samp else atg[:, cs]), zr[s]) for s in range(nS)],
                                    [("rt", ci), "atm", zrkey] if samp else [("rt", ci), zrkey], [("ps", b)])
                                stt("dve", Xb[:C, :], tok[:C, ci, 4, :], btok[:C, ci, hh:hh + 1], ps[b][:C, 0:128], ALU.mult, ALU.add,
                                    [("tok", ci), "btok", ("ps", b)], ["Xb"])
                                b = nps()
                                mm(ps[b][:C, 0:128], Tt[ci][:C, 0, :C], Xb[:C, :], [("Tt", ci), "Xb"], [("ps", b)])
                                copy("act", Wb[:C, :], ps[b][:C, 0:128], [("ps", b)], ["Wb"])
                                if samp:
                                    tt("dve", Wm[:C, :, :], Wb[:C, :].unsqueeze(1).to_broadcast([C, NSQ, 128]),
                                       cc("rowseg", 16)[:C, :].unsqueeze(2).to_broadcast([C, NSQ, 128]), ALU.mult, ["Wb", "cst"], ["Wm"])
                                    egl = eg[ci][:, :C].rearrange("p (s l) -> p s l", s=NSQ)[:, :, LS - 1:LS]
                                    tt("dve", ZS[:], ZS[:], egl.to_broadcast([128, NSQ, 128]), ALU.mult, [ZSK, ("eg", ci)], [ZSK])
                                    for q in range(4):
                                        b = nps()
                                        mm(ps[b][:, :], tok[:C, ci, 3, :], Wm[:C, q * 4:(q + 1) * 4, :].rearrange("p a c -> p (a c)"), [("tok", ci), "Wm"], [("ps", b)])
                                        tt("dve", ZS[:, q * 4:(q + 1) * 4, :], ZS[:, q * 4:(q + 1) * 4, :], ps[b][:, :].rearrange("p (a c) -> p a c", a=4),
                                           ALU.add, [ZSK, ("ps", b)], [ZSK])
                                else:
                                    bz = nps()
                                    mm(ps[bz][:, 0:128], tok[:, ci, 3, :], Wb[:, :], [("tok", ci), "Wb"], [("ps", bz)])
                                    stt("dve", Zgb[:, hh, (ci + 1) % 2, :], Zf[:, 0, :], eg[ci][:, C - 1:C], ps[bz][:, 0:128], ALU.mult, ALU.add,
                                        [zk, ("eg", ci), ("ps", bz)], [("Zgb", hh, (ci + 1) % 2)])
                                    stt("dve", Zf[:, 0, :], Zf[:, 0, :], eg[ci][:, C - 1:C], ps[bz][:, 0:128], ALU.mult, ALU.add,
                                        [zk, ("eg", ci), ("ps", bz)], [zk])
                                b = nps()
                                items = [(ps[b][:, :C], zr[s], (rtm[:, s, :] if samp else qt_[:, cs])) for s in range(nS)]
                                items.append((ps[b][:, :C], Wb[:C, :], cm3[ci][:C, 0, :C]))
                                mmg(items, [zrkey, ("kt", ci), "rtm", "Wb", ("cm3", ci)] if samp else [zrkey, ("kt", ci), "Wb", ("cm3", ci)], [("ps", b)])
                                copy("act", o_t[:, cs], ps[b][:, :C], [("ps", b)], ["o_t"])

                            if samp:
                                k.dma("sp", dls_d.rearrange("(s h k) v -> h k s v", s=NSQ, h=4)[hh], ZS[:], reads=[ZSK])
                            elif ti == 3:
                                k.dma("sp", dlp_d[hh * 128:(hh + 1) * 128, :], Zg[:, hh, :], reads=[("Zg", hh)])
                            o2 = tmp[3]
                            tt("pool", sqb[:], o_t[:, :N], o_t[:, :N], ALU.mult, ["o_t"], ["sqb"])
                            b = nps()
                            mm(ps[b][:, :N], ones_b, sqb[:], ["sqb", "cb"], [("ps", b)])
                            act(o2[:], ps[b][:, :N], AF.Ln, [("ps", b)], [tk[3]], bias=1e-6, scale=1.0 / 128)
                            act(o2[:], o2[:], AF.Exp, [tk[3]], [tk[3]], scale=-0.5)
                            stt("dve", o2[:], o_t[:, :N], col("norm_w", 0), o2[:], ALU.mult, ALU.mult, ["o_t", "cols", tk[3]], [tk[3]])
                            tt("dve", oab[:, 4 + hh, :N], o2[:], z_t[:], ALU.mult, [tk[3], "z_t"], [("oab", 4 + hh)])

                        mrg = [at, bt, kt, rt, atz[:, 0, :], atz[:, 1, :], btz[:, 0, :], btz[:, 1, :]]
                        mrk = ["at", "bt", "kt", "rt", "atz", "atz", "btz", "btz"]
                        okeys = [("oab", j) for j in range(8)]
                        k.barrier()
                        for m in range(DBG.get('ng', 8)):
                            pj = pjbuf[m % 2]
                            k.dma("sp", pj[:, 0, :, :], pjab_d[:, m * 128:(m + 1) * 128].rearrange("(kc p) c -> p kc c", p=128), reads=["pjab"], writes=[("pj", m % 2)])
                            k.dma("sp", pj[:, 1, :, :], pjbb_d[:, m * 128:(m + 1) * 128].rearrange("(kc p) c -> p kc c", p=128), reads=["pjbb"], writes=[("pj", m % 2)])
                            bga = pchunk(30 + 2 * m)
                            act(tmp[0][:], ps[bga][:, :N], AF.Sigmoid, [("ps", bga)], [tk[0]])
                            b = nps()
                            mmg([(ps[b][:, :N], pj[:, 0, kc, :], oab[:, kc, :N]) for kc in range(4)], [("pj", m % 2)] + okeys, [("ps", b)])
                            tt("dve", tmp[1][:], tmp[0][:], ps[b][:, :N], ALU.mult, [tk[0], ("ps", b)], [tk[1]])
                            bgb = pchunk(31 + 2 * m)
                            act(tmp[0][:], ps[bgb][:, :N], AF.Sigmoid, [("ps", bgb)], [tk[0]])
                            b = nps()
                            mmg([(ps[b][:, :N], pj[:, 1, kc, :], oab[:, 4 + kc, :N]) for kc in range(4)], [("pj", m % 2)] + okeys, [("ps", b)])
                            tt("dve", tmp[2][:], tmp[0][:], ps[b][:, :N], ALU.mult, [tk[0], ("ps", b)], [tk[2]])
                            tt("pool", mrg[m][:, :N] if m < 4 else mrg[m], tmp[1][:], tmp[2][:], ALU.add, [tk[1], tk[2]], [mrk[m], ("mrg", m)])
                        for half in range(4):
                            wt = WI[half % 2]
                            k.dma("sp", wt[:, :, :], wob_d[:, half * 256:(half + 1) * 256].rearrange("(kc p) c -> p kc c", p=128),
                                  reads=[("wob", r_) for r_ in range(2)], writes=[("wi", half % 2)])
                            for m4 in range(2):
                                m = half * 2 + m4
                                b = nps()
                                mmg([(ps[b][:, :N], wt[:, kc, m4 * 128:(m4 + 1) * 128], (mrg[kc][:, :N] if kc < 4 else mrg[kc])) for kc in range(8)],
                                    [("wi", half % 2)] + [("mrg", j) for j in range(8)], [("ps", b)])
                                tt("dve", h[:, m, c0:c0 + N], h[:, m, c0:c0 + N], ps[b][:, :N], ALU.add, hk + [("ps", b)], hk)
                        k.barrier()
            k.barrier()

        while pending_casts:
            one_cast()
        if stage >= 2:
            mixer()
        if stage >= 3:
            ffn(w2g_d, w2u_d, w2d_d, "ffn2_norm", "b")

        with ExitStack() as st:
            rstd = sb("rstdf", [128, 512], F32, st)
            sq = [sb("sqf%d" % i, [128, 512], BF16, st) for i in range(2)]
            yf = [sb("yf%d" % i, [128, 8, 512], F32, st) for i in range(2)]
            yo = [sb("yo%d" % i, [128, D], F32, st) for i in range(3)]
            bi = 0
            for ti, (c0, N, _, _) in enumerate(TILES):
                hk = hkeys_of(c0, N)
                rms_stats(c0, N, rstd, sq, hk)
                yft = yf[ti % 2]
                for m in range(8):
                    k.op("dve", lambda e, m=m, yft=yft, c0=c0, N=N: e.scalar_tensor_tensor(
                        out=yft[:, m, :N], in0=h[:, m, c0:c0 + N], scalar=col("final_norm", m), in1=rstd[:, :N],
                        op0=ALU.mult, op1=ALU.mult), reads=hk + ["rstd", "cols"], writes=[("yf", ti % 2, m)])
                for blk in range((N + 127) // 128):
                    rows = min(128, N - blk * 128)
                    yot = yo[bi % 3]
                    for half in range(2):
                        b = nps()
                        for m4 in range(4):
                            m = half * 4 + m4
                            k.op("pe", lambda e, b=b, m4=m4, m=m, yft=yft, rows=rows, blk=blk: e.transpose(
                                ps[b][:rows, m4 * 128:(m4 + 1) * 128], yft[:, m, blk * 128:blk * 128 + rows], ident_f),
                                reads=[("yf", ti % 2, m), "cst"], writes=[("ps", b)])
                        if half == 0:
                            k.op("act", lambda e, b=b, yot=yot, rows=rows: e.activation(
                                out=yot[:rows, 0:512], in_=ps[b][:rows, :], func=AF.Copy), reads=[("ps", b)], writes=[("yo", bi % 3)])
                        else:
                            k.op("dve", lambda e, b=b, yot=yot, rows=rows: e.tensor_copy(
                                out=yot[:rows, 512:1024], in_=ps[b][:rows, :]), reads=[("ps", b)], writes=[("yo", bi % 3)])
                    k.dma("sp", y_d[c0 + blk * 128:c0 + blk * 128 + rows, :], yot[:rows, :], reads=[("yo", bi % 3)])
                    bi += 1
        k.finish()
    return nc


def _perm_a():
    parts = [np.arange(512, 576), np.arange(1600, 1664), np.arange(1664, 1792)]
    for u in range(4):
        parts += [np.arange(u * 128, (u + 1) * 128), np.arange(576 + u * 128, 576 + (u + 1) * 128),
                  np.arange(1088 + u * 128, 1088 + (u + 1) * 128)]
    return np.concatenate(parts)


def _perm_b():
    parts = []
    for hh in range(4):
        parts += [np.arange(hh * 128, (hh + 1) * 128), np.arange(512 + hh * 128, 512 + (hh + 1) * 128),
                  np.arange(1024 + hh * 128, 1024 + (hh + 1) * 128), np.arange(1544 + hh * 128, 1544 + (hh + 1) * 128)]
    return np.concatenate(parts)


def _qkv_from_b():
    j = np.arange(1536)
    role = j // 512; hh = (j % 512) // 128; off = j % 128
    return hh * 512 + role * 128 + off


def _win_perm():
    pa = _perm_a()
    pb = A_PROJ + _perm_b()
    g0 = A_PROJ + 2056
    pg = np.concatenate([np.concatenate([np.arange(g0 + m * 128, g0 + (m + 1) * 128),
                                         np.arange(g0 + 1024 + m * 128, g0 + 1024 + (m + 1) * 128)]) for m in range(8)])
    ab = np.arange(A_PROJ + 1536, A_PROJ + 1544)
    return np.concatenate([pa, pb, pg, ab])


def _colvec(v):
    v = np.asarray(v, np.float32).reshape(-1)
    return np.ascontiguousarray(v.reshape(-1, 128).T)


def kernel(**inp):
    stage = int(inp.pop("_stage", 99))
    f = lambda a: np.ascontiguousarray(np.asarray(a, dtype=np.float32))
    pa = _perm_a()
    cols = np.zeros((128, NCOL), np.float32)
    def putc(name, arr):
        cols[:, COLS[name]:COLS[name] + arr.shape[1]] = arr
    putc("ffn1_norm", _colvec(inp["ffn1_norm"][0])); putc("mix_norm", _colvec(inp["mix_norm"][0]))
    putc("ffn2_norm", _colvec(inp["ffn2_norm"][0])); putc("final_norm", _colvec(inp["final_norm"]))
    putc("mu", _colvec(f(inp["rwkv_mu"])[0][pa]))
    for nm, key in [("w0", "rwkv_w0"), ("a0", "rwkv_a0"), ("k_k", "rwkv_k_k"), ("k_a", "rwkv_k_a"),
                    ("r_k", "rwkv_r_k"), ("lnx_w", "rwkv_lnx_w"), ("lnx_b", "rwkv_lnx_b")]:
        putc(nm, _colvec(f(inp[key])[0]))
    cw = f(inp["gdn_conv_w"])[0]
    putc("conv_w", np.concatenate([_colvec(cw[i]) for i in range(4)], axis=1))
    putc("norm_w", _colvec(f(inp["gdn_norm_w"])[0]))
    putc("A_log", np.tile(f(inp["gdn_A_log"])[0][None, :], (128, 1)))
    putc("dt_bias", np.tile(f(inp["gdn_dt_bias"])[0][None, :], (128, 1)))
    consts = make_consts()
    w_in = np.ascontiguousarray(f(inp["w_in"])[0][:, _win_perm()])
    shared = {
        "w1g": f(inp["ffn1_w_gate"])[0], "w1u": f(inp["ffn1_w_up"])[0], "w1d": f(inp["ffn1_w_down"])[0],
        "w2g": f(inp["ffn2_w_gate"])[0], "w2u": f(inp["ffn2_w_up"])[0], "w2d": f(inp["ffn2_w_down"])[0],
        "w_in": w_in, "proj_a": f(inp["proj_a"])[0], "proj_b": f(inp["proj_b"])[0], "w_out": f(inp["w_out"])[0],
        "lw2": f(inp["rwkv_w2"])[0], "la2": f(inp["rwkv_a2"])[0], "lg2": f(inp["rwkv_g2"])[0],
        "cols": cols, "consts": consts, "consts2": make_consts2(),
    }
    xp = f(inp["x_prompt"]); xs = f(inp["x_sample"])
    srw = f(inp["state_rwkv"])[0]; ssh = f(inp["state_rwkv_shift"])[0]
    sdl = f(inp["state_delta"])[0]; scv = f(inp["state_conv"])[0]
    in_maps = []
    for c in range(NCORES):
        sl = slice(c * NSQ, (c + 1) * NSQ)
        m = dict(shared)
        m["x"] = np.ascontiguousarray(np.concatenate([xp[c], xs[sl].reshape(NSQ * LS, D)], axis=0))
        m["s_rwkv"] = np.ascontiguousarray(srw[sl].reshape(NSQ * 512, 64))
        m["s_shift"] = np.ascontiguousarray(ssh[sl][:, pa])
        m["s_delta"] = np.ascontiguousarray(sdl[sl].reshape(NSQ * 512, 128))
        m["s_conv"] = np.ascontiguousarray(scv[sl].reshape(NSQ * 3, 1536))
        in_maps.append(m)
    nc = build(stage)
    res = run_bass_kernel_spmd(nc, in_maps, core_ids=list(range(NCORES)))
    R = res.results
    inv = np.argsort(pa)
    y = np.stack([r["y"] for r in R])
    y_prompt = np.ascontiguousarray(y[:, :SEQ, :])
    y_sample = np.ascontiguousarray(y[:, SEQ:, :].reshape(NCORES * NSQ, LS, D))
    qb = _qkv_from_b()
    rwkv_p = np.stack([r["rwkv_p"].reshape(8, 64, 64) for r in R])[None]
    shift_p = np.stack([r["tokrows_p"][3, :A_PROJ][inv] for r in R])[None]
    delta_p = np.stack([r["delta_p"].reshape(4, 128, 128) for r in R])[None]
    conv_p = np.stack([r["tokrows_p"][1:4, A_PROJ:][:, qb] for r in R])[None]
    rwkv_s = np.concatenate([r["rwkv_s"].reshape(NSQ, 8, 64, 64) for r in R])[None]
    shift_s = np.concatenate([r["tokrows_s"].reshape(NSQ, LS, 3840)[:, 3, :A_PROJ][:, inv] for r in R])[None]
    delta_s = np.concatenate([r["delta_s"].reshape(NSQ, 4, 128, 128) for r in R])[None]
    conv_s = np.concatenate([r["tokrows_s"].reshape(NSQ, LS, 3840)[:, 1:4, A_PROJ:][:, :, qb] for r in R])[None]
    outs = (y_prompt, y_sample, rwkv_p, shift_p, delta_p, conv_p, rwkv_s, shift_s, delta_s, conv_s)
    return tuple(np.ascontiguousarray(o.astype(np.float32)) for o in outs)
```

```python
from contextlib import ExitStack
import numpy as np
import concourse.bass as bass
import concourse.mybir as mybir
from concourse.bass_utils import run_bass_kernel_spmd

F32 = mybir.dt.float32
BF16 = mybir.dt.bfloat16
AF = mybir.ActivationFunctionType
ALU = mybir.AluOpType

NCORES = 8
DBG = {}
D = 1024
DFF = 2816
SEQ = 2048
NSQ = 16
LS = 4
NT = SEQ + NSQ * LS
A_PROJ = 1792
C0 = float(np.exp(-0.5))

COLS = {}
_nc = 0
for _name, _n in [("ffn1_norm", 8), ("mix_norm", 8), ("ffn2_norm", 8), ("final_norm", 8), ("mu", 14),
                  ("w0", 4), ("a0", 4), ("k_k", 4), ("k_a", 4), ("r_k", 4), ("lnx_w", 4), ("lnx_b", 4),
                  ("conv_w", 48), ("norm_w", 1), ("A_log", 4), ("dt_bias", 4)]:
    COLS[_name] = _nc
    _nc += _n
NCOL = _nc

CONST = {}
_k = 0
for _name, _n in [("ident", 128), ("incl", 128), ("strict", 128), ("tail", 128), ("incl_s", 128), ("strict_s", 128),
                  ("tail_s", 128), ("bones", 128), ("ones", 128), ("hmask", 256), ("rowseg", 16),
                  ("reset_s", 64)]:
    CONST[_name] = _k
    _k += _n
NCONST = _k


def make_consts2():
    sm = np.zeros((128, 16, 64), np.float32)
    for s in range(16):
        sm[:, s, s * LS:(s + 1) * LS] = 1
    return sm.reshape(128, 1024)


def make_consts():
    c = np.zeros((128, NCONST), np.float32)
    i = np.arange(128)
    seg = i // LS
    same = (seg[:, None] == seg[None, :])
    def put(name, a):
        c[:a.shape[0], CONST[name]:CONST[name] + a.shape[1]] = a
    put("ident", np.eye(128, dtype=np.float32))
    put("incl", (i[None, :] >= i[:, None]).astype(np.float32))
    put("strict", (i[None, :] > i[:, None]).astype(np.float32))
    put("tail", (i[:, None] > i[None, :]).astype(np.float32))
    put("incl_s", ((i[None, :] >= i[:, None]) & same).astype(np.float32))
    put("strict_s", ((i[None, :] > i[:, None]) & same).astype(np.float32))
    put("tail_s", ((i[:, None] > i[None, :]) & same).astype(np.float32))
    bo = np.zeros((128, 128), np.float32); bo[:64, :64] = 1; bo[64:, 64:] = 1
    put("bones", bo)
    put("ones", np.ones((128, 128), np.float32))
    hm = np.zeros((128, 256), np.float32); hm[:, 0:64] = 1; hm[:, 128 + 64:256] = 1
    put("hmask", hm)
    rs = np.zeros((128, 16), np.float32)
    for s in range(16):
        rs[s * LS:(s + 1) * LS, s] = 1
    put("rowseg", rs)
    r = np.ones((128, 64), np.float32); r[:, ::LS] = 0
    put("reset_s", r)
    return c


class K:
    def __init__(self, nc, es):
        self.nc = nc
        self.eng = {"pe": nc.tensor, "act": nc.scalar, "dve": nc.vector, "pool": nc.gpsimd, "sp": nc.sync}
        self.sem = {e: es.enter_context(nc.semaphore("sem_" + e)) for e in ("pe", "act", "dve", "pool")}
        self.cnt = {e: 0 for e in self.sem}
        self.ndma = 24
        self.dsem = [es.enter_context(nc.semaphore("dsem%d" % i)) for i in range(self.ndma)]
        self.dval = [0] * self.ndma
        self.di = {"sp": 0, "pool": 12}
        self.waited = {e: {} for e in self.eng}
        self.lastw = {}
        self.readers = {}
        self.nops = 0
        self._cap = None

    def begin(self):
        assert self._cap is None
        self._cap = []

    def end(self):
        c = self._cap
        self._cap = None
        return c

    def emit(self, items):
        for it in items:
            if it[0] == "op":
                self.op(*it[1:])
            else:
                self.dma(*it[1:])

    def emit_interleaved(self, a, b):
        na, nb = len(a), len(b)
        ia = ib = 0
        while ia < na or ib < nb:
            if ib >= nb or (ia < na and ia * nb <= ib * na):
                self.emit([a[ia]]); ia += 1
            else:
                self.emit([b[ib]]); ib += 1

    def _semh(self, key):
        return self.sem[key] if isinstance(key, str) else self.dsem[key[1]]

    def _deps(self, reads, writes):
        toks = []
        for k in reads:
            if k in self.lastw:
                toks.append(self.lastw[k])
        for k in writes:
            if k in self.lastw:
                toks.append(self.lastw[k])
            toks.extend(self.readers.get(k, ()))
        return toks

    def _wait(self, en, toks):
        need = {}
        for (sk, v) in toks:
            if sk == "pe" and en == "pe":
                continue
            if v > need.get(sk, 0):
                need[sk] = v
        w = self.waited[en]
        for sk, v in need.items():
            if w.get(sk, 0) < v:
                self.eng[en].wait_ge(self._semh(sk), v)
                w[sk] = v

    def _record(self, tok, reads, writes):
        for k in reads:
            self.readers.setdefault(k, []).append(tok)
        for k in writes:
            self.lastw[k] = tok
            self.readers[k] = []

    @staticmethod
    def _excl(reads, writes):
        pr = [x for x in reads if isinstance(x, tuple) and x[0] == "ps"]
        if pr:
            reads = [x for x in reads if x not in pr]
            writes = list(writes) + [x for x in pr if x not in writes]
        return reads, writes

    def op(self, en, fn, reads=(), writes=()):
        if self._cap is not None:
            self._cap.append(("op", en, fn, list(reads), list(writes)))
            return
        reads, writes = self._excl(reads, writes)
        self._wait(en, self._deps(reads, writes))
        ins = fn(self.eng[en])
        self.cnt[en] += 1
        ins.then_inc(self.sem[en], 1)
        self._record((en, self.cnt[en]), reads, writes)
        self.nops += 1

    def dma(self, q, out, in_, reads=(), writes=()):
        if self._cap is not None:
            self._cap.append(("dma", q, out, in_, list(reads), list(writes)))
            return
        toks = self._deps(reads, writes)
        i = self.di[q]
        base = 0 if q == "sp" else 12
        self.di[q] = base + (i - base + 1) % 12
        if self.dval[i] > 0:
            toks.append((("dma", i), self.dval[i]))
        self._wait(q, toks)
        ins = self.eng[q].dma_start(out=out, in_=in_)
        self.dval[i] += 16
        ins.then_inc(self.dsem[i], 16)
        self._record((("dma", i), self.dval[i]), reads, writes)
        self.nops += 1

    def barrier(self):
        toks = [(e, self.cnt[e]) for e in self.cnt if self.cnt[e] > 0]
        toks += [(("dma", i), self.dval[i]) for i in range(self.ndma) if self.dval[i] > 0]
        for en in self.eng:
            self._wait(en, toks)

    def finish(self):
        toks = [(("dma", i), self.dval[i]) for i in range(self.ndma) if self.dval[i] > 0]
        toks += [(e, self.cnt[e]) for e in self.cnt if self.cnt[e] > 0]
        self._wait("sp", toks)


def build(stage=99):
    nc = bass.Bass("TRN2", target_bir_lowering=False)
    es = ExitStack()

    def din(name, shape):
        return nc.dram_tensor(name, list(shape), F32, kind="ExternalInput").ap()

    def dout(name, shape):
        return nc.dram_tensor(name, list(shape), F32, kind="ExternalOutput").ap()

    x_d = din("x", [NT, D])
    srw_d = din("s_rwkv", [NSQ * 512, 64])
    ssh_d = din("s_shift", [NSQ, A_PROJ])
    sdl_d = din("s_delta", [NSQ * 512, 128])
    scv_d = din("s_conv", [NSQ * 3, 1536])
    w1g_d = din("w1g", [D, DFF]); w1u_d = din("w1u", [D, DFF]); w1d_d = din("w1d", [DFF, D])
    w2g_d = din("w2g", [D, DFF]); w2u_d = din("w2u", [D, DFF]); w2d_d = din("w2d", [DFF, D])
    win_d = din("w_in", [D, 5896])
    pja_d = din("proj_a", [512, D]); pjb_d = din("proj_b", [512, D]); wo_d = din("w_out", [D, D])
    lw2_d = din("lw2", [64, 512]); la2_d = din("la2", [64, 512]); lg2_d = din("lg2", [128, 512])
    cols_d = din("cols", [128, NCOL]); consts_d = din("consts", [128, NCONST]); consts2_d = din("consts2", [128, 1024])

    winb_d = nc.dram_tensor("w_in_bf", [D, 5896], BF16, kind="Internal").ap()
    pjab_d = nc.dram_tensor("proj_a_bf", [512, D], BF16, kind="Internal").ap()
    pjbb_d = nc.dram_tensor("proj_b_bf", [512, D], BF16, kind="Internal").ap()
    wob_d = nc.dram_tensor("w_out_bf", [D, D], BF16, kind="Internal").ap()
    y_d = dout("y", [NT, D])
    rwp_d = dout("rwkv_p", [512, 64]); dlp_d = dout("delta_p", [512, 128])
    rws_d = dout("rwkv_s", [NSQ * 512, 64]); dls_d = dout("delta_s", [NSQ * 512, 128])
    trp_d = dout("tokrows_p", [4, 3840]); trs_d = dout("tokrows_s", [NSQ * LS, 3840])

    with es:
        k = K(nc, es)

        def sb(name, shape, dt=F32, stack=es):
            return stack.enter_context(nc.sbuf_tensor("sb_" + name, list(shape), dt))

        ps = [es.enter_context(nc.psum_tensor("ps%d" % i, [128, 512], F32)) for i in range(8)]
        psi = [0, 0]
        psmode = [0]

        def nps():
            if psmode[0] == 0:
                i = psi[0]
                psi[0] = (i + 1) % 8
                return i
            if psmode[0] == 1:
                i = psi[0] % 4
                psi[0] = (i + 1) % 4
                return i
            i = psi[1]
            psi[1] = (i + 1) % 4
            return 4 + i

        h = sb("h", [128, 8, NT])
        cols = sb("cols", [128, NCOL])
        cst = sb("cst", [128, NCONST])
        cb = sb("cb", [128, 5 * 128], BF16)
        pending_casts = []
        for r8 in range(8):
            pending_casts.append(lambda r8=r8: k.dma("pool", winb_d[r8 * 128:(r8 + 1) * 128, :], win_d[r8 * 128:(r8 + 1) * 128, :], writes=[("winb", r8)]))
        pending_casts.append(lambda: k.dma("pool", pjab_d[:, :], pja_d[:, :], writes=["pjab"]))
        pending_casts.append(lambda: k.dma("pool", pjbb_d[:, :], pjb_d[:, :], writes=["pjbb"]))
        for r2 in range(2):
            pending_casts.append(lambda r2=r2: k.dma("pool", wob_d[r2 * 512:(r2 + 1) * 512, :], wo_d[r2 * 512:(r2 + 1) * 512, :], writes=[("wob", r2)]))

        def one_cast():
            if pending_casts:
                pending_casts.pop(0)()
        k.dma("sp", cols[:], cols_d[:, :], writes=["cols"])
        k.dma("sp", cst[:], consts_d[:, :], writes=["cst"])

        def cc(name, n=128, rows=128):
            o = CONST[name]
            return cst[:rows, o:o + n]

        def col(name, j=0):
            o = COLS[name] + j
            return cols[:, o:o + 1]

        for j, nm in enumerate(["ident", "ones", "bones"]):
            k.op("dve", lambda e, j=j, nm=nm: e.tensor_copy(out=cb[:, j * 128:(j + 1) * 128], in_=cc(nm)),
                 reads=["cst"], writes=["cb"])
        ident_f = cc("ident")
        ones_b = cb[:, 128:256]
        bones_b = cb[:, 256:384]

        TILES = [(t * 512, 512, 1, 512) for t in range(4)] + [(SEQ, NSQ * LS, NSQ, LS)]

        with ExitStack() as st:
            xin = [sb("xin%d" % i, [128, D], F32, st) for i in range(2)]
            for blk in range(17):
                rows = 128 if blk < 16 else 64
                xb = xin[blk % 2]
                k.dma("sp", xb[:rows, :], x_d[blk * 128: blk * 128 + rows, :], writes=[("xin", blk % 2)])
                for half in range(2):
                    b = nps()
                    for m4 in range(4):
                        m = half * 4 + m4
                        k.op("pe", lambda e, b=b, m4=m4, m=m, xb=xb, rows=rows: e.transpose(
                            ps[b][:, m4 * 128: m4 * 128 + rows], xb[:rows, m * 128:(m + 1) * 128], ident_f[:rows, :rows]),
                            reads=[("xin", blk % 2), "cst"], writes=[("ps", b)])
                    k.op("act", lambda e, b=b, half=half, blk=blk, rows=rows: e.activation(
                        out=h[:, half * 4:(half + 1) * 4, blk * 128: blk * 128 + rows],
                        in_=ps[b][:].rearrange("p (a c) -> p a c", a=4)[:, :, :rows], func=AF.Copy),
                        reads=[("ps", b)], writes=[("h", blk)])
            k.barrier()

        def rms_stats(c0, N, rstd, tmp_sq, hkeys, rkey="rstd", sqkeys=(("sq", 0), ("sq", 1))):
            b = nps()
            for m in range(8):
                sq = tmp_sq[m % 2]
                k.op("act", lambda e, sq=sq, m=m: e.activation(out=sq[:, :N], in_=h[:, m, c0:c0 + N], func=AF.Square),
                     reads=hkeys, writes=[sqkeys[m % 2]])
                k.op("pe", lambda e, sq=sq, m=m, b=b: e.matmul(ps[b][:, :N], ones_b, sq[:, :N], start=(m == 0), stop=(m == 7)),
                     reads=[sqkeys[m % 2], "cb"], writes=[("ps", b)])
            k.op("act", lambda e: e.activation(out=rstd[:, :N], in_=ps[b][:, :N], func=AF.Sqrt, bias=1e-6, scale=1.0 / D),
                 reads=[("ps", b)], writes=[rkey])
            k.op("dve", lambda e: e.reciprocal(out=rstd[:, :N], in_=rstd[:, :N]), reads=[rkey], writes=[rkey])

        def hkeys_of(c0, N):
            return [("h", bk) for bk in range(c0 // 128, (c0 + N + 127) // 128)]

        def ffn(wg_d, wu_d, wd_d, normname, tag):
            with ExitStack() as st:
                u = sb("u" + tag, [128, 8, NT], BF16, st)
                hid = sb("hid" + tag, [128, 11, NT], BF16, st)
                WG = [sb("wg%d" % i + tag, [128, 8, 512], BF16, st) for i in range(2)]
                WU = [sb("wu%d" % i + tag, [128, 8, 512], BF16, st) for i in range(2)]
                WD = [sb("wd%d" % i + tag, [128, 11, 128], BF16, st) for i in range(2)]
                rstd = sb("rstd" + tag, [128, 512], F32, st)
                sq = [sb("sq%d" % i + tag, [128, 512], BF16, st) for i in range(2)]
                sg = [sb("sg%d" % i + tag, [128, 512], F32, st) for i in range(2)]
                def norm_tile(ti):
                    c0, N = TILES[ti][0], TILES[ti][1]
                    rms_stats(c0, N, rstd, sq, hkeys_of(c0, N))
                    for m in range(8):
                        k.op("dve", lambda e, m=m: e.scalar_tensor_tensor(
                            out=u[:, m, c0:c0 + N], in0=h[:, m, c0:c0 + N], scalar=col(normname, m), in1=rstd[:, :N],
                            op0=ALU.mult, op1=ALU.mult), reads=hkeys_of(c0, N) + ["rstd", "cols"], writes=[("u", ti)])
                norm_tile(0)
                gi = 0
                di = 0
                sgi = 0
                for half in range(2):
                    groups = [(0, 4), (4, 4), (8, 3)]
                    for (j0, nj) in groups:
                        ff0 = (half * 11 + j0) * 128
                        wgt, wut = WG[gi % 2], WU[gi % 2]
                        k.dma("pool", wgt[:, :, :nj * 128], wg_d[:, ff0:ff0 + nj * 128].rearrange("(kc p) c -> p kc c", p=128),
                              writes=[("wg", gi % 2)])
                        k.dma("pool", wut[:, :, :nj * 128], wu_d[:, ff0:ff0 + nj * 128].rearrange("(kc p) c -> p kc c", p=128),
                              writes=[("wu", gi % 2)])
                        one_cast()
                        for ti, (c0, N, _, _) in enumerate(TILES):
                            if gi == 0 and ti + 1 < len(TILES):
                                norm_tile(ti + 1)
                            for j in range(nj):
                                b1 = nps(); b2 = nps()
                                def mm(e, wt, b, j=j):
                                    ins = None
                                    for kc in range(8):
                                        ins = e.matmul(ps[b][:, :N], wt[:, kc, j * 128:(j + 1) * 128], u[:, kc, c0:c0 + N],
                                                       start=(kc == 0), stop=(kc == 7))
                                    return ins
                                k.op("pe", lambda e, mm=mm, wgt=wgt, b1=b1: mm(e, wgt, b1),
                                     reads=[("wg", gi % 2), ("u", ti)], writes=[("ps", b1)])
                                k.op("pe", lambda e, mm=mm, wut=wut, b2=b2: mm(e, wut, b2),
                                     reads=[("wu", gi % 2), ("u", ti)], writes=[("ps", b2)])
                                sgt = sg[sgi % 2]
                                k.op("act", lambda e, sgt=sgt, b1=b1: e.activation(out=sgt[:, :N], in_=ps[b1][:, :N], func=AF.Silu),
                                     reads=[("ps", b1)], writes=[("sg", sgi % 2)])
                                k.op("dve", lambda e, sgt=sgt, b2=b2, jj=j0 + j: e.tensor_tensor(
                                    out=hid[:, jj, c0:c0 + N], in0=sgt[:, :N], in1=ps[b2][:, :N], op=ALU.mult),
                                    reads=[("sg", sgi % 2), ("ps", b2)], writes=[("hid", ti, j0 + j)])
                                sgi += 1
                        gi += 1
                    for piece in range(8):
                        wdt = WD[di % 2]
                        r0 = half * 11 * 128
                        k.dma("pool", wdt[:, :, :], wd_d[r0:r0 + 11 * 128, piece * 128:(piece + 1) * 128].rearrange("(kc p) c -> p kc c", p=128),
                              writes=[("wd", di % 2)])
                        one_cast()
                        for ti, (c0, N, _, _) in enumerate(TILES):
                            for mm_ in range(1):
                                m = piece
                                b = nps()
                                def mmd(e, wdt=wdt, b=b, mm_=mm_, c0=c0, N=N):
                                    ins = None
                                    for kc in range(11):
                                        ins = e.matmul(ps[b][:, :N], wdt[:, kc, mm_ * 128:(mm_ + 1) * 128], hid[:, kc, c0:c0 + N],
                                                       start=(kc == 0), stop=(kc == 10))
                                    return ins
                                k.op("pe", mmd, reads=[("wd", di % 2)] + [("hid", ti, jj) for jj in range(11)], writes=[("ps", b)])
                                k.op("dve", lambda e, b=b, m=m, c0=c0, N=N: e.scalar_tensor_tensor(
                                    out=h[:, m, c0:c0 + N], in0=ps[b][:, :N], scalar=0.5, in1=h[:, m, c0:c0 + N],
                                    op0=ALU.mult, op1=ALU.add), reads=[("ps", b)] + hkeys_of(c0, N), writes=hkeys_of(c0, N))
                        di += 1
            k.barrier()

        if stage >= 1:
            ffn(w1g_d, w1u_d, w1d_d, "ffn1_norm", "a")
        def act(out, in_, func, r, w, **kw):
            k.op("act", lambda e: e.activation(out=out, in_=in_, func=func, **kw), reads=r, writes=w)

        def tt(en, out, a, b, op, r, w):
            k.op(en, lambda e: e.tensor_tensor(out=out, in0=a, in1=b, op=op), reads=r, writes=w)

        def ts(en, out, a, s1, s2, op0, op1, r, w):
            if s2 is None:
                k.op(en, lambda e: e.tensor_scalar(out, a, s1, None, op0), reads=r, writes=w)
            else:
                k.op(en, lambda e: e.tensor_scalar(out, a, s1, s2, op0, op1), reads=r, writes=w)

        def stt(en, out, in0, sc, in1, op0, op1, r, w):
            k.op(en, lambda e: e.scalar_tensor_tensor(out=out, in0=in0, scalar=sc, in1=in1, op0=op0, op1=op1),
                 reads=r, writes=w)

        def mm(out, lhsT, rhs, r, w, start=True, stop=True):
            k.op("pe", lambda e: e.matmul(out, lhsT, rhs, start=start, stop=stop), reads=r, writes=w)

        def mmg(items, r, w):
            def f(e):
                ins = None
                n = len(items)
                for i, (o, l, rh) in enumerate(items):
                    ins = e.matmul(o, l, rh, start=(i == 0), stop=(i == n - 1))
                return ins
            k.op("pe", f, reads=r, writes=w)

        def tr(out, in_, rows, r, w):
            k.op("pe", lambda e: e.transpose(out, in_, ident_f[:rows, :rows]), reads=r + ["cst"], writes=w)

        def copy(en, out, in_, r, w):
            if en == "act":
                act(out, in_, AF.Copy, r, w)
            else:
                k.op(en, lambda e: e.tensor_copy(out=out, in_=in_), reads=r, writes=w)

        def mixer():
            with ExitStack() as st:
                def T(name, shape, dt=F32, stack=st):
                    return sb(name, shape, dt, stack)
                lora_wa = T("lora_wa", [128, 512], BF16)
                lg2 = T("lg2s", [128, 512], BF16)
                wab = T("wab", [128, 8, 8], BF16)
                pjbuf = [T("pjbuf%d" % i, [128, 2, 4, 128], BF16) for i in range(2)]
                WI = [T("wi%d" % i, [128, 8, 256], BF16) for i in range(2)]
                u_t = T("u_t", [128, 8, 512], BF16)
                oab = T("oab", [128, 8, 512], BF16)
                rowst = T("rowst", [128, 256])
                halo_a = T("halo_a", [128, 14]); halo_c = T("halo_c", [128, 12, 3])
                sshT = T("sshT", [128, 14, 16]); scvT = T("scvT", [128, 12, 48])
                Zr = T("Zr", [128, 4, 128]); Zrb = T("Zrb", [128, 4, 2, 128], BF16)
                Zg = T("Zg", [128, 4, 128]); Zgb = T("Zgb", [128, 4, 2, 128], BF16)
                omka = T("omka", [128, 4]); eA = T("eA", [128, 4]); dtb = T("dtb", [128, 4]); omu = T("omu", [128, 14])
                dc = T("dc", [128, 16])
                k.dma("pool", lora_wa[0:64, :], lw2_d[:, :], writes=["lora_wa"])
                k.dma("pool", lora_wa[64:128, :], la2_d[:, :], writes=["lora_wa"])
                k.dma("pool", lg2[:], lg2_d[:, :], writes=["lg2"])
                k.dma("sp", wab[:], winb_d[:, 5888:5896].rearrange("(kc p) c -> p kc c", p=128), reads=[("winb", r_) for r_ in range(8)], writes=["wab"])
                for t_, nm in [(halo_a, "halo_a"), (halo_c, "halo_c")]:
                    k.op("pool", lambda e, t_=t_: e.memset(t_[:], 0.0), writes=[nm])
                for t_, nm in [(Zr, "Zr"), (Zg, "Zg")]:
                    k.op("pool", lambda e, t_=t_: e.memset(t_[:], 0.0), writes=[(nm, j) for j in range(4)])
                for t_, nm in [(Zrb, "Zrb"), (Zgb, "Zgb")]:
                    k.op("pool", lambda e, t_=t_: e.memset(t_[:], 0.0), writes=[(nm, j, p_) for j in range(4) for p_ in range(2)])
                ts("dve", omka[:], cols[:, COLS["k_a"]:COLS["k_a"] + 4], -1.0, 1.0, ALU.mult, ALU.add, ["cols"], ["omka"])
                act(eA[:], cols[:, COLS["A_log"]:COLS["A_log"] + 4], AF.Exp, ["cols"], ["eA"])
                ts("dve", omu[:], cols[:, COLS["mu"]:COLS["mu"] + 14], -1.0, 1.0, ALU.mult, ALU.add, ["cols"], ["omu"])
                copy("dve", dtb[:], cols[:, COLS["dt_bias"]:COLS["dt_bias"] + 4], ["cols"], ["dtb"])
                with ExitStack() as s0:
                    ld1 = T("ld1", [16, A_PROJ], F32, s0); ld2 = T("ld2", [48, 1536], F32, s0)
                    k.dma("sp", ld1[:], ssh_d[:, :], writes=["ld1"])
                    k.dma("sp", ld2[:], scv_d[:, :], writes=["ld2"])
                    b = nps()
                    for c in range(14):
                        tr(ps[b][:, c * 16:(c + 1) * 16], ld1[:, c * 128:(c + 1) * 128], 16, ["ld1"], [("ps", b)])
                    copy("act", sshT[:].rearrange("p a s -> p (a s)"), ps[b][:, :224], [("ps", b)], ["sshT"])
                    for hf in range(2):
                        b = nps()
                        for c6 in range(6):
                            c = hf * 6 + c6
                            tr(ps[b][:, c6 * 48:(c6 + 1) * 48], ld2[:, c * 128:(c + 1) * 128], 48, ["ld2"], [("ps", b)])
                        copy("act", scvT[:, hf * 6:(hf + 1) * 6, :].rearrange("p a s -> p (a s)"), ps[b][:, :288],
                             [("ps", b)], ["scvT"])
                    k.barrier()

                bones_f = cc("bones"); ones_f = cc("ones"); ident_b = cb[:, 0:128]

                for ti, (c0, N, nseq, L) in enumerate(TILES):
                    if ti not in DBG.get('tiles', range(5)):
                        continue
                    samp = nseq > 1
                    C = 64 if samp else 128
                    nch = 1 if samp else 4
                    nS = NSQ if samp else 1
                    nsg, lsg = (NSQ, LS) if samp else (4, 128)
                    nlev = 1 if samp else 6
                    mI = cc("incl_s" if samp else "incl")[:C, :C]
                    mS = cc("strict_s" if samp else "strict")[:C, :C]
                    mT = cc("tail_s" if samp else "tail")[:C, :C]
                    hk = hkeys_of(c0, N)
                    with ExitStack() as ts_:
                        def TT(name, shape, dt=F32):
                            return sb(name + "_%d" % ti, shape, dt, ts_)
                        tmp = [TT("tmp%d" % i, [128, N]) for i in range(12)]
                        tk = ["tmp%d" % i for i in range(12)]
                        pa_c = TT("pa_c", [128, N + 3 * nseq]); pa_c2 = TT("pa_c2", [128, N + 3 * nseq])
                        r_t = TT("r_t", [128, N]); k_t = TT("k_t", [128, N]); v_t = TT("v_t", [128, N]); o_t = TT("o_t", [128, N])
                        z_t = TT("z_t", [128, N])
                        lora_in = TT("lora_in", [128, N], BF16); sig_gd = TT("sig_gd", [128, N], BF16)
                        sqb = TT("sqb", [128, N], BF16); bnb = TT("bnb", [128, N], BF16); sqp = TT("sqp", [128, N], BF16)
                        atB = TT("atB", [128, N], BF16); btB = TT("btB", [128, N], BF16); ktB = TT("ktB", [128, N], BF16); rtB = TT("rtB", [128, N], BF16)
                        gtB = TT("gtB", [128, N]); bnB = TT("bnB", [128, N]); dcB = TT("dcB", [128, 16])
                        at = TT("at", [128, N], BF16); bt = TT("bt", [128, N], BF16); kt = TT("kt", [128, N], BF16)
                        rt = TT("rt", [128, N], BF16)
                        atz = TT("atz", [128, 2, N], BF16); btz = TT("btz", [128, 2, N], BF16); rtz = TT("rtz", [128, 2, N], BF16)
                        tok = TT("tok", [128, nch, 5, 128], BF16); Vz = TT("Vz", [128, nch, 2, 128], BF16)
                        cm1 = [TT("cm1_%d" % i, [128, 4, 128], BF16) for i in range(nch)]
                        nt2 = [TT("nt2_%d" % i, [128, 2, 128], BF16) for i in range(nch)]
                        cm3 = [TT("cm3_%d" % i, [128, 4, 128], BF16) for i in range(nch)]
                        Tt = [TT("Tt%d" % i, [128, 2, 128], BF16) for i in range(nch)]
                        PQ = [TT("PQ%d" % i, [128, 4, 128], BF16) for i in range(nch)]
                        Xb = TT("Xb", [128, 128], BF16); Wz = TT("Wz", [128, 2, 128], BF16); Wb = TT("Wb", [128, 128], BF16)
                        gtok = TT("gtok", [128, nch, 4]); btok = TT("btok", [128, nch, 4]); sptmp = TT("sptmp", [128, nch, 4])
                        if samp:
                            g_rep, b_rep, Gs, eg, ek, dmi, dms = [[TT(nm_, [128, 128])] for nm_ in ("g_rep", "b_rep", "Gs", "eg", "ek", "dmi", "dms")]
                            segm_t = TT("segm", [128, 1024])
                            k.dma("sp", segm_t[:], consts2_d[:, :], writes=["segm"])
                            segm = segm_t[:, :].rearrange("p (s i) -> p s i", s=NSQ)
                        else:
                            g_rep, b_rep, Gs, eg, ek, dmi, dms = [[tmp[5 + j_][:, q_ * 128:(q_ + 1) * 128] for q_ in range(4)] for j_ in range(7)]
                        dmt_t = TT("dmt", [128, nch, 128]); kbf_t = TT("kbf", [128, nch, 128]); kbb_t = TT("kbb", [128, nch, 128], BF16)
                        dmt = [dmt_t[:, q_, :] for q_ in range(nch)]; kbf = [kbf_t[:, q_, :] for q_ in range(nch)]
                        kbb = [kbb_t[:, q_, :] for q_ in range(nch)]
                        merged = None
                        if samp:
                            Zs = TT("Zs", [128, NSQ, 128]); Zsb = TT("Zsb", [128, NSQ, 128], BF16)
                            bigf = TT("bigf", [128, NSQ, 128]); stg = TT("stg", [128, NSQ, 64])
                            sbd = [TT("sbd%d" % i, [128, NSQ, 128]) for i in range(2)]
                            atm = TT("atm", [128, NSQ, 64], BF16); rtm = TT("rtm", [128, NSQ, 64], BF16)
                            Wm = TT("Wm", [128, NSQ, 128], BF16); Vm = TT("Vm", [128, NSQ, 128], BF16)
                        else:
                            bigf = TT("bigf", [128, 1, 128])

                        ATs = [(at, bt, kt, rt), (atB, btB, ktB, rtB)]
                        GTs = [(tmp[2], tk[2]), (gtB, "gtB")]
                        BNs = [(tmp[11], tk[11]), (bnB, "bnB")]
                        DCs = [dc, dcB]

                        def v3(ap):
                            return ap.rearrange("p (s l) -> p s l", s=nseq)

                        def sg3(ap):
                            return ap.rearrange("p (s l) -> p s l", s=nsg)

                        rms_stats(c0, N, tmp[0], [at, bt], hk, rkey=tk[0], sqkeys=("at", "bt"))
                        for m in range(8):
                            stt("dve", u_t[:, m, :N], h[:, m, c0:c0 + N], col("mix_norm", m), tmp[0][:, :N], ALU.mult, ALU.mult,
                                hk + [tk[0], "cols"], ["u_t"])

                        need_rows = (ti >= 3)
                        M_rows = 64 if samp else 4
                        rows_lo = 0 if samp else 508
                        state = {"g": -1, "issued": set()}

                        def issue_load(g):
                            if g > 22 or g in state["issued"]:
                                return
                            state["issued"].add(g)
                            wt = WI[g % 2]
                            ncols = 256
                            k.dma("sp", wt[:, :, :ncols], winb_d[:, g * 256: g * 256 + ncols].rearrange("(kc p) c -> p kc c", p=128),
                                  reads=[("winb", r_) for r_ in range(8)], writes=[("wi", g % 2)])

                        def load_group(g):
                            wt = WI[g % 2]
                            issue_load(g)
                            if need_rows and g < 15:
                                b = nps()
                                mmg([(ps[b][:M_rows, :256], u_t[:, kc, rows_lo:rows_lo + M_rows], wt[:, kc, :]) for kc in range(8)],
                                    ["u_t", ("wi", g % 2)], [("ps", b)])
                                copy("act", rowst[:M_rows, :], ps[b][:M_rows, :256], [("ps", b)], ["rowst"])
                                if samp:
                                    k.dma("sp", trs_d[:, g * 256:(g + 1) * 256], rowst[:64, :], reads=["rowst"])
                                else:
                                    k.dma("sp", trp_d[:, g * 256:(g + 1) * 256], rowst[:4, :], reads=["rowst"])

                        def pchunk(c):
                            g = c // 2
                            if g != state["g"]:
                                load_group(g)
                                state["g"] = g
                                issue_load(g + 1)
                            wt = WI[g % 2]
                            off = (c % 2) * 128
                            b = nps()
                            mmg([(ps[b][:, :N], wt[:, kc, off:off + 128], u_t[:, kc, :N]) for kc in range(8)],
                                ["u_t", ("wi", g % 2)], [("ps", b)])
                            return b

                        def a_chunk(c, dest, dkey):
                            b = pchunk(c)
                            pab, pak = ((pa_c, "pa_c"), (pa_c2, "pa_c2"))[c % 2]
                            pv = pab[:, :N + nseq].rearrange("p (s l) -> p s l", s=nseq)
                            copy("act", pv[:, :, 1:], v3(ps[b][:, :N]), [("ps", b)], [pak])
                            act(v3(dest), v3(ps[b][:, :N]), AF.Copy, [("ps", b), "omu"], [dkey], scale=omu[:, c:c + 1])
                            if samp:
                                copy("pool", pv[:, :, 0], sshT[:, c, :], ["sshT"], [pak])
                            else:
                                copy("pool", pv[:, :, 0], halo_a[:, c:c + 1], ["halo_a"], [pak])
                                copy("pool", halo_a[:, c:c + 1], pv[:, :, L], [pak], ["halo_a"])
                            stt("dve", v3(dest), pv[:, :, 0:L], col("mu", c), v3(dest), ALU.mult, ALU.add,
                                [pak, "cols", dkey], [dkey])

                        def g_load(hh, role, dest, dk):
                            cq = role * 4 + hh
                            b = pchunk(14 + hh * 4 + role)
                            pab, pak = ((pa_c, "pa_c"), (pa_c2, "pa_c2"))[role % 2]
                            xv3 = pab[:, :N + 3 * nseq].rearrange("p (s l) -> p s l", s=nseq)
                            copy("act", xv3[:, :, 3:], v3(ps[b][:, :N]), [("ps", b)], [pak])
                            if samp:
                                copy("pool", xv3[:, :, 0:3], scvT[:, cq, :].rearrange("p (s i) -> p s i", s=NSQ), ["scvT"], [pak])
                            else:
                                copy("pool", xv3[:, :, 0:3], halo_c[:, cq:cq + 1, :], ["halo_c"], [pak])
                                copy("pool", halo_c[:, cq:cq + 1, :], xv3[:, :, L:L + 3], [pak], ["halo_c"])

                        def g_conv(hh, role, dest, dk):
                            cq = role * 4 + hh
                            pab, pak = ((pa_c, "pa_c"), (pa_c2, "pa_c2"))[role % 2]
                            xv3 = pab[:, :N + 3 * nseq].rearrange("p (s l) -> p s l", s=nseq)
                            ts("dve", v3(dest[:]), xv3[:, :, 0:L], col("conv_w", 0 * 12 + cq), None, ALU.mult, None, [pak, "cols"], [dk])
                            for i in range(1, 4):
                                stt("dve", v3(dest[:]), xv3[:, :, i:i + L], col("conv_w", i * 12 + cq), v3(dest[:]),
                                    ALU.mult, ALU.add, [pak, "cols", dk], [dk])

                        def g_silu(dest, dk):
                            act(dest[:], dest[:], AF.Silu, [dk], [dk])


                        def gdn_prefetch(hh):
                            g_load(hh, 0, r_t, "r_t"); g_load(hh, 1, k_t, "k_t")
                            g_conv(hh, 0, r_t, "r_t")
                            g_load(hh, 2, v_t, "v_t")
                            g_conv(hh, 1, k_t, "k_t")
                            g_silu(r_t, "r_t")
                            g_conv(hh, 2, v_t, "v_t")
                            g_silu(k_t, "k_t"); g_silu(v_t, "v_t")

                        issue_load(0)
                        if samp:
                            srv = srw_d.rearrange("(s h v) k -> h v s k", s=NSQ, h=8)
                            sdv = sdl_d.rearrange("(s h k) v -> h k s v", s=NSQ, h=4)

                            def load_rwkv_state(u_):
                                t_ = sbd[u_ % 2]; tkey = ("sbd", u_ % 2)
                                k.dma("sp", t_[0:64, :, 0:64], srv[2 * u_], writes=[tkey])
                                k.dma("sp", t_[64:128, :, 64:128], srv[2 * u_ + 1], writes=[tkey])

                            def load_gdn_state(h_):
                                k.dma("sp", sbd[h_ % 2][:], sdv[h_], writes=[("sbd", h_ % 2)])

                            for i_ in range(2):
                                k.op("pool", lambda e, i_=i_: e.memset(sbd[i_][:], 0.0), writes=[("sbd", i_)])
                            load_rwkv_state(0)
                        a_chunk(0, tmp[1][:], tk[1])
                        act(lora_in[0:64, :], tmp[1][0:64, :], AF.Tanh, [tk[1]], ["lora_in"])
                        copy("pool", lora_in[64:128, :], tmp[1][64:128, :], [tk[1]], ["lora_in"])
                        a_chunk(1, tmp[1][:], tk[1])
                        act(sig_gd[:], tmp[1][:], AF.Sigmoid, [tk[1]], ["sig_gd"])
                        pre_ops, fin_ops, chunk_ops, gpf_ops = [], [], [], []
                        for u in range(DBG.get('nu', 4)):
                            p_ = u % 2
                            at_, bt_, kt_, rt_ = ATs[p_]
                            kat, kbt, kkt, krt = ["%s%d" % (n_, p_) for n_ in ("at", "bt", "kt", "rt")] if p_ else ["at", "bt", "kt", "rt"]
                            dc_, kdc = DCs[p_], "dc%d" % p_
                            psmode[0] = 2 if u > 0 else 0
                            k.begin()
                            a_chunk(2 + 3 * u, r_t[:], "r_t"); a_chunk(3 + 3 * u, k_t[:], "k_t"); a_chunk(4 + 3 * u, v_t[:], "v_t")
                            if samp and u > 0:
                                load_rwkv_state(u)
                            ucol = slice(u * 128, (u + 1) * 128)
                            sigw, a_t, g_t, kk, kkn, kmod, b_t, cl, e_t, Bh, Kh, bonus = tmp
                            ksw, ka, kg, kkk, kkkn, kkmod, kb_, kcl, ke, kBh, kKh, kbon = tk
                            g_t, kg = GTs[p_]
                            bonus, kbon = BNs[p_]
                            e1, e2, e3 = tmp[3], Bh, Kh
                            ke1, ke2, ke3 = tk[3], kBh, kKh
                            act(sqb[:], k_t[:], AF.Square, ["k_t", "cols"], ["sqb"], scale=col("k_k", u))
                            bw = nps()
                            mm(ps[bw][:, :N], lora_wa[0:64, ucol], lora_in[0:64, :], ["lora_wa", "lora_in"], [("ps", bw)])
                            ba = nps()
                            mm(ps[ba][:, :N], lora_wa[64:128, ucol], lora_in[64:128, :], ["lora_wa", "lora_in"], [("ps", ba)])
                            bg = nps()
                            mm(ps[bg][:, :N], lg2[:, ucol], sig_gd[:], ["lg2", "sig_gd"], [("ps", bg)])
                            bn = nps()
                            mm(ps[bn][:, :N], bones_b, sqb[:], ["sqb", "cb"], [("ps", bn)])
                            act(sigw[:], ps[bw][:, :N], AF.Sigmoid, [("ps", bw), "cols"], [ksw], bias=col("w0", u))
                            act(a_t[:], ps[ba][:, :N], AF.Sigmoid, [("ps", ba), "cols"], [ka], bias=col("a0", u))
                            copy("act", g_t[:], ps[bg][:, :N], [("ps", bg)], [kg])
                            ts("dve", e_t[:], ps[bn][:, :N], 1e-24, None, ALU.max, None, [("ps", bn)], [ke])
                            if samp:
                                k.op("dve", lambda e: e.tensor_tensor_scan(out=cl[:], data0=cc("reset_s", 64), data1=sigw[:], initial=0.0,
                                                                           op0=ALU.mult, op1=ALU.add), reads=[ksw, "cst"], writes=[kcl])
                            else:
                                for q in range(4):
                                    k.op("dve", lambda e, q=q: e.tensor_tensor_scan(
                                        out=cl[:, q * 128:(q + 1) * 128], data0=ones_f, data1=sigw[:, q * 128:(q + 1) * 128], initial=0.0,
                                        op0=ALU.mult, op1=ALU.add), reads=[ksw, "cst"], writes=[kcl])
                            act(e_t[:], e_t[:], AF.Ln, [ke], [ke])
                            act(e_t[:], e_t[:], AF.Exp, [ke], [ke], scale=-0.5)
                            ts("dve", kmod[:], a_t[:], col("k_a", u), omka[:, u:u + 1], ALU.mult, ALU.add, [ka, "cols", "omka"], [kkmod])
                            tt("pool", e1[:], cl[:], sigw[:], ALU.subtract, [kcl, ksw], [ke1])
                            tt("pool", kmod[:], kmod[:], k_t[:], ALU.mult, [kkmod, "k_t"], [kkmod])
                            stt("dve", kkn[:], k_t[:], col("k_k", u), e_t[:], ALU.mult, ALU.mult, ["k_t", "cols", ke], [kkkn])
                            act(e1[:], e1[:], AF.Exp, [ke1], [ke1], scale=-C0)
                            act(e2[:], cl[:], AF.Exp, [kcl], [ke2], scale=C0)
                            act(e3[:], cl[:], AF.Exp, [kcl], [ke3], scale=-C0)
                            tt("pool", b_t[:], kkn[:], a_t[:], ALU.mult, [kkkn, ka], [kb_])
                            stt("dve", bnb[:], r_t[:], col("r_k", u), kmod[:], ALU.mult, ALU.mult, ["r_t", kkmod, "cols"], ["bnb"])
                            bb = nps()
                            mm(ps[bb][:, :N], bones_b, bnb[:], ["bnb", "cb"], [("ps", bb)])
                            stt("dve", at_[:], kkn[:], -1.0, e1[:], ALU.mult, ALU.mult, [kkkn, ke1], [kat])
                            cl3 = sg3(cl[:])
                            tt("pool", sg3(e_t[:]), cl3[:, :, lsg - 1:lsg].to_broadcast([128, nsg, lsg]), cl3, ALU.subtract, [kcl, kkkn], [ke])
                            tt("dve", kt_[:], kmod[:], e2[:], ALU.mult, [kkmod, ke2], [kkt])
                            tt("pool", bt_[:], b_t[:], e2[:], ALU.mult, [kb_, ke2], [kbt])
                            tt("dve", rt_[:], r_t[:], e3[:], ALU.mult, ["r_t", ke3], [krt])
                            act(e_t[:], e_t[:], AF.Exp, [ke], [ke], scale=-C0)
                            act(dc_[:, :nsg], cl3[:, :, lsg - 1], AF.Exp, [kcl], [kdc], scale=-C0)
                            tt("dve", bonus[:], ps[bb][:, :N], v_t[:], ALU.mult, [("ps", bb), "v_t"], [kbon])
                            tt("pool", Bh[:], b_t[:], e_t[:], ALU.mult, [kb_, ke, kbt, kkt], [kBh])
                            tt("dve", Kh[:], kmod[:], e_t[:], ALU.mult, [kkmod, ke, krt], [kKh])
                            pre_ops.append(k.end())
                            psmode[0] = 1
                            k.begin()
                            for hh_ in range(2):
                                lo = hh_ * 64
                                for src, skey_, dst, kn in [(at_, kat, atz, "atz"), (bt_, kbt, btz, "btz"), (rt_, krt, rtz, "rtz")]:
                                    if hh_ == 0:
                                        act(dst[:, hh_, :], src[:, :], AF.Copy, [skey_, "cst"], [kn], scale=bones_f[:, lo:lo + 1])
                                    else:
                                        ts("dve", dst[:, hh_, :], src[:, :], bones_f[:, lo:lo + 1], None, ALU.mult, None, [skey_, "cst"], [kn])
                            for ci in range(nch):
                                cs = slice(ci * C, (ci + 1) * C)
                                b = nps()
                                tr(ps[b][:C, 0:128], Bh[:, cs], 128, [kBh], [("ps", b)])
                                tr(ps[b][:C, 128:256], Kh[:, cs], 128, [kKh], [("ps", b)])
                                tr(ps[b][:C, 256:384], v_t[:, cs], 128, ["v_t"], [("ps", b)])
                                hm2 = cc("hmask", 256)[:C, :].rearrange("p (a c) -> p a c", a=2)
                                tt("dve", tok[:C, ci, 0:2, :], ps[b][:C, 0:128].unsqueeze(1).to_broadcast([C, 2, 128]), hm2, ALU.mult,
                                   [("ps", b), "cst"], ["tok"])
                                tt("dve", tok[:C, ci, 2:4, :], ps[b][:C, 128:256].unsqueeze(1).to_broadcast([C, 2, 128]), hm2, ALU.mult,
                                   [("ps", b), "cst"], ["tok"])
                                copy("act", tok[:C, ci, 4, :], ps[b][:C, 256:384], [("ps", b)], ["tok"])
                                tt("pool", Vz[:C, ci, :, :], tok[:C, ci, 4:5, :].to_broadcast([C, 2, 128]), hm2, ALU.mult, ["tok", "cst"], ["Vz"])
                            fin_ops.append(k.end())
                            if u == 3 and DBG.get('nh', 4) > 0:
                                psmode[0] = 2
                                k.begin(); gdn_prefetch(0); gpf_ops = k.end()
                                psmode[0] = 1
                            k.begin()
                            if samp:
                                for q in range(4):
                                    b = nps()
                                    for s4 in range(4):
                                        s = q * 4 + s4
                                        tr(ps[b][:, s4 * 128:(s4 + 1) * 128], sbd[u % 2][:, s, :], 128, [("sbd", u % 2)], [("ps", b)])
                                    copy("act", Zs[:, q * 4:(q + 1) * 4, :].rearrange("p a c -> p (a c)"), ps[b][:, :], [("ps", b)], ["Zs"])
                                copy("act", Zsb[:], Zs[:], ["Zs"], ["Zsb"])
                                Zf, zk = Zs, "Zs"
                                Zrd = lambda ci: [Zsb[:, s_, :] for s_ in range(NSQ)]
                                zrk = lambda ci: "Zsb"
                                tt("pool", atm[:], at_[:, :].unsqueeze(1).to_broadcast([128, NSQ, 64]),
                                   segm, ALU.mult, [kat, "segm"], ["atm"])
                                tt("pool", rtm[:], rt_[:, :].unsqueeze(1).to_broadcast([128, NSQ, 64]),
                                   segm, ALU.mult, [krt, "segm"], ["rtm"])
                            else:
                                Zf, zk = Zr[:, u:u + 1, :], ("Zr", u)
                                Zrd = lambda ci: [Zrb[:, u, ci % 2, :]]
                                zrk = lambda ci: ("Zrb", u, ci % 2)
                            pv4 = lambda b_: ps[b_][:C, :].rearrange("p (a c) -> p a c", a=4)[:, :, :C]
                            for ci in range(nch):
                                cs = slice(ci * C, (ci + 1) * C)
                                b1 = nps()
                                for hh_ in range(2):
                                    mm(ps[b1][:C, hh_ * 128: hh_ * 128 + C], bt_[:, cs], atz[:, hh_, cs], [kbt, "atz"], [("ps", b1)])
                                    mm(ps[b1][:C, (2 + hh_) * 128: (2 + hh_) * 128 + C], kt_[:, cs], atz[:, hh_, cs], [kkt, "atz"], [("ps", b1)])
                                b2 = nps()
                                for hh_ in range(2):
                                    mm(ps[b2][:C, hh_ * 128: hh_ * 128 + C], at_[:, cs], btz[:, hh_, cs], [kat, "btz"], [("ps", b2)])
                                b3 = nps()
                                for hh_ in range(2):
                                    mm(ps[b3][:C, hh_ * 128: hh_ * 128 + C], bt_[:, cs], rtz[:, hh_, cs], [kbt, "rtz"], [("ps", b3)])
                                    mm(ps[b3][:C, (2 + hh_) * 128: (2 + hh_) * 128 + C], kt_[:, cs], rtz[:, hh_, cs], [kkt, "rtz"], [("ps", b3)])
                                tt("dve", cm1[ci][:C, :, :C], pv4(b1), mS.unsqueeze(1).to_broadcast([C, 4, C]), ALU.mult, [("ps", b1), "cst"], [("cm1", ci)])
                                tt("dve", nt2[ci][:C, :, :C], pv4(b2)[:, 0:2, :], mT.unsqueeze(1).to_broadcast([C, 2, C]), ALU.mult, [("ps", b2), "cst"], [("nt2", ci)])
                                tt("dve", cm3[ci][:C, :, :C], pv4(b3), mI.unsqueeze(1).to_broadcast([C, 4, C]), ALU.mult, [("ps", b3), "cst"], [("cm3", ci)])
                                tt("pool", Tt[ci][:C, :, :C], cm1[ci][:C, 0:2, :C], ident_b[:C, :C].unsqueeze(1).to_broadcast([C, 2, C]), ALU.add,
                                   [("cm1", ci), "cb"], [("Tt", ci)])
                            Pm = [[cm1[ci][:C, 0, :C], cm1[ci][:C, 1, :C]] for ci in range(nch)]
                            Qm = [[nt2[ci][:C, 0, :C], nt2[ci][:C, 1, :C]] for ci in range(nch)]
                            pqk = [[("cm1", ci), ("nt2", ci)] for ci in range(nch)]
                            for lev in range(nlev):
                                bs = []
                                for ci in range(nch):
                                    b = nps(); bs.append(b)
                                    for hh_ in range(2):
                                        if lev < nlev - 1:
                                            mm(ps[b][:C, hh_ * 128: hh_ * 128 + C], Qm[ci][hh_], Pm[ci][hh_], pqk[ci], [("ps", b)])
                                        mm(ps[b][:C, (2 + hh_) * 128: (2 + hh_) * 128 + C], Pm[ci][hh_], Qm[ci][hh_], pqk[ci], [("ps", b)])
                                for ci in range(nch):
                                    b = bs[ci]; pq = PQ[ci]
                                    if lev < nlev - 1:
                                        copy("act" if ci != 3 else "dve", pq[:C, :, :C], pv4(b), [("ps", b)], [("PQ", ci)])
                                    else:
                                        copy("act" if ci != 3 else "dve", pq[:C, 2:4, :C], pv4(b)[:, 2:4, :], [("ps", b)], [("PQ", ci)])
                                    Pm[ci] = [pq[:C, 0, :C], pq[:C, 1, :C]]; Qm[ci] = [pq[:C, 2, :C], pq[:C, 3, :C]]
                                    pqk[ci] = [("PQ", ci)]
                                bs = []
                                for ci in range(nch):
                                    b = nps(); bs.append(b)
                                    for hh_ in range(2):
                                        mm(ps[b][:C, hh_ * 128: hh_ * 128 + C], Qm[ci][hh_], Tt[ci][:C, hh_, :C], [("PQ", ci), ("Tt", ci)], [("ps", b)])
                                for ci in range(nch):
                                    b = bs[ci]
                                    tt("dve", Tt[ci][:C, :, :C], pv4(b)[:, 0:2, :], Tt[ci][:C, :, :C], ALU.add,
                                       [("ps", b), ("Tt", ci)], [("Tt", ci)])
                            for ci in range(nch):
                                cs = slice(ci * C, (ci + 1) * C)
                                b = nps()
                                items = []
                                for s in range(nS):
                                    items.append((ps[b][:C, 0:128], (atm[:, s, :] if samp else at_[:, cs]), Zrd(ci)[s]))
                                for hh_ in range(2):
                                    items.append((ps[b][:C, 0:128], cm1[ci][:C, 2 + hh_, :C], Vz[:C, ci, hh_, :]))
                                mmg(items, [kat, "atm", zrk(ci), ("cm1", ci), "Vz"] if samp else [kat, zrk(ci), ("cm1", ci), "Vz"], [("ps", b)])
                                copy("dve", Xb[:C, :], ps[b][:C, 0:128], [("ps", b)], ["Xb"])
                                b = nps()
                                for hh_ in range(2):
                                    mm(ps[b][:C, hh_ * 128:(hh_ + 1) * 128], Tt[ci][:C, hh_, :C], Xb[:C, :], [("Tt", ci), "Xb"], [("ps", b)])
                                tt("dve", Wz[:C, :, :], ps[b][:C, 0:256].rearrange("p (a c) -> p a c", a=2),
                                   cc("hmask", 256)[:C, :].rearrange("p (a c) -> p a c", a=2), ALU.mult, [("ps", b), "cst"], ["Wz"])
                                if samp:
                                    tt("pool", Wb[:C, :], Wz[:C, 0, :], Wz[:C, 1, :], ALU.add, ["Wz"], ["Wb"])
                                    tt("dve", Wm[:C, :, :], Wb[:C, :].unsqueeze(1).to_broadcast([C, NSQ, 128]),
                                       cc("rowseg", 16)[:C, :].unsqueeze(2).to_broadcast([C, NSQ, 128]), ALU.mult, ["Wb", "cst"], ["Wm"])
                                    tt("dve", Vm[:C, :, :], tok[:C, ci, 4:5, :].to_broadcast([C, NSQ, 128]),
                                       cc("rowseg", 16)[:C, :].unsqueeze(2).to_broadcast([C, NSQ, 128]), ALU.mult, ["tok", "cst"], ["Vm"])
                                    for q in range(4):
                                        b = nps()
                                        mmg([(ps[b][:, :], tok[:C, ci, 0, :], Wm[:C, q * 4:(q + 1) * 4, :].rearrange("p a c -> p (a c)")),
                                             (ps[b][:, :], tok[:C, ci, 1, :], Wm[:C, q * 4:(q + 1) * 4, :].rearrange("p a c -> p (a c)")),
                                             (ps[b][:, :], tok[:C, ci, 2, :], Vm[:C, q * 4:(q + 1) * 4, :].rearrange("p a c -> p (a c)")),
                                             (ps[b][:, :], tok[:C, ci, 3, :], Vm[:C, q * 4:(q + 1) * 4, :].rearrange("p a c -> p (a c)"))],
                                            ["tok", "Wm", "Vm"], [("ps", b)])
                                        tt("dve", bigf[:, q * 4:(q + 1) * 4, :], ps[b][:, :].rearrange("p (a c) -> p a c", a=4),
                                           bones_f.unsqueeze(1).to_broadcast([128, 4, 128]), ALU.mult, [("ps", b), "cst"], ["bigf"])
                                    tt("dve", Zs[:], Zs[:], dc_[:, :NSQ].unsqueeze(2).to_broadcast([128, NSQ, 128]), ALU.mult, ["Zs", kdc], ["Zs"])
                                    tt("dve", Zs[:], Zs[:], bigf[:], ALU.add, ["Zs", "bigf"], ["Zs"])
                                else:
                                    bz = nps()
                                    mmg([(ps[bz][:, 0:128], tok[:, ci, 0, :], Wz[:, 0, :]), (ps[bz][:, 0:128], tok[:, ci, 1, :], Wz[:, 1, :]),
                                         (ps[bz][:, 0:128], tok[:, ci, 2, :], Vz[:, ci, 0, :]), (ps[bz][:, 0:128], tok[:, ci, 3, :], Vz[:, ci, 1, :])],
                                        ["tok", "Wz", "Vz"], [("ps", bz)])
                                    stt("dve", Zrb[:, u, (ci + 1) % 2, :], Zf[:, 0, :], dc_[:, ci:ci + 1], ps[bz][:, 0:128], ALU.mult, ALU.add,
                                        [zk, kdc, ("ps", bz)], [("Zrb", u, (ci + 1) % 2)])
                                    stt("dve", Zf[:, 0, :], Zf[:, 0, :], dc_[:, ci:ci + 1], ps[bz][:, 0:128], ALU.mult, ALU.add, [zk, kdc, ("ps", bz)], [zk])
                                b = nps()
                                items = []
                                for s in range(nS):
                                    items.append((ps[b][:, :C], Zrd(ci)[s], (rtm[:, s, :] if samp else rt_[:, cs])))
                                for hh_ in range(2):
                                    items.append((ps[b][:, :C], Wz[:C, hh_, :], cm3[ci][:C, hh_, :C]))
                                    items.append((ps[b][:, :C], Vz[:C, ci, hh_, :], cm3[ci][:C, 2 + hh_, :C]))
                                mmg(items, [zrk(ci), krt, "rtm", "Wz", ("cm3", ci), "Vz"] if samp else [zrk(ci), krt, "Wz", ("cm3", ci), "Vz"], [("ps", b)])
                                copy("act", o_t[:, cs], ps[b][:, :C], [("ps", b)], ["o_t"])

                            if samp:
                                for q in range(4):
                                    b = nps()
                                    for s4 in range(4):
                                        tr(ps[b][:, s4 * 128:(s4 + 1) * 128], Zs[:, q * 4 + s4, :], 128, ["Zs"], [("ps", b)])
                                    pvq = ps[b][:, :].rearrange("p (a c) -> p a c", a=4)
                                    copy("act", stg[0:64, q * 4:(q + 1) * 4, :], pvq[0:64, :, 0:64], [("ps", b)], ["stg"])
                                    copy("act", stg[64:128, q * 4:(q + 1) * 4, :], pvq[64:128, :, 64:128], [("ps", b)], ["stg"])
                                rwv = rws_d.rearrange("(s h v) k -> h v s k", s=NSQ, h=8)
                                k.dma("sp", rwv[2 * u], stg[0:64, :, :], reads=["stg"])
                                k.dma("sp", rwv[2 * u + 1], stg[64:128, :, :], reads=["stg"])
                            elif ti == 3:
                                b = nps()
                                tr(ps[b][:, 0:128], Zr[:, u, :], 128, [("Zr", u)], [("ps", b)])
                                copy("act", bigf[0:64, 0, 0:64], ps[b][0:64, 0:64], [("ps", b)], ["bigf"])
                                copy("act", bigf[64:128, 0, 0:64], ps[b][64:128, 64:128], [("ps", b)], ["bigf"])
                                k.dma("sp", rwp_d[u * 128:(u + 1) * 128, :], bigf[:, 0, 0:64], reads=["bigf"])
                            m1, var_, o2 = [t_[:].rearrange("p a n -> p (a n)").bitcast(F32) for t_ in (atz, btz, rtz)]
                            km1, kvar, ko2 = "atz", "btz", "rtz"
                            b = nps()
                            mm(ps[b][:, :N], bones_f, o_t[:, :N], ["o_t", "cst"], [("ps", b)])
                            stt("dve", o2, ps[b][:, :N], -1.0 / 64, o_t[:, :N], ALU.mult, ALU.add, [("ps", b), "o_t"], [ko2])
                            act(sqp[:], o2, AF.Square, [ko2], ["sqp"])
                            b = nps()
                            mm(ps[b][:, :N], bones_b, sqp[:], ["sqp", "cb"], [("ps", b)])
                            act(var_, ps[b][:, :N], AF.Ln, [("ps", b)], [kvar], bias=64e-5, scale=1.0 / 64)
                            act(var_, var_, AF.Exp, [kvar], [kvar], scale=-0.5)
                            tt("pool", o2, o2, var_, ALU.mult, [ko2, kvar], [ko2])
                            ts("dve", o2, o2, col("lnx_w", u), col("lnx_b", u), ALU.mult, ALU.add, [ko2, "cols"], [ko2])
                            tt("pool", o2, o2, bonus[:], ALU.add, [ko2, kbon], [ko2])
                            tt("dve", oab[:, u, :N], o2, g_t[:], ALU.mult, [ko2, kg], [("oab", u)])
                            chunk_ops.append(k.end())
                        psmode[0] = 0
                        nu_ = len(pre_ops)
                        if nu_:
                            k.emit(pre_ops[0]); k.emit(fin_ops[0])
                            for u in range(nu_):
                                nxt = pre_ops[u + 1] if u + 1 < nu_ else gpf_ops
                                k.emit_interleaved(chunk_ops[u], nxt)
                                if u + 1 < nu_:
                                    k.emit(fin_ops[u + 1])

                        k.barrier()
                        for ci in range(nch):
                            b = nps()
                            mmg([(ps[b][:C, 0:8], u_t[:, kc, ci * C:(ci + 1) * C], wab[:, kc, :]) for kc in range(8)], ["u_t", "wab"], [("ps", b)])
                            act(btok[:C, ci, :], ps[b][:C, 4:8], AF.Sigmoid, [("ps", b)], ["btok"])
                            tt("dve", sptmp[:C, ci, :], ps[b][:C, 0:4], dtb[:C, :], ALU.add, [("ps", b), "dtb"], ["sptmp"])
                        act(sptmp[:C, :, :], sptmp[:C, :, :], AF.Exp, ["sptmp"], ["sptmp"])
                        act(sptmp[:C, :, :], sptmp[:C, :, :], AF.Ln, ["sptmp"], ["sptmp"], bias=1.0, scale=1.0)
                        stt("dve", gtok[:C, :, :], sptmp[:C, :, :], -1.0, eA[:C, :].unsqueeze(1).to_broadcast([C, nch, 4]), ALU.mult, ALU.mult,
                            ["sptmp", "eA"], ["gtok"])
                        for hh in range(DBG.get('nh', 4)):
                            xq, xk, xv = r_t, k_t, v_t
                            if hh == 0 and DBG.get('nu', 4) < 4:
                                gdn_prefetch(0)
                            bz_ = pchunk(14 + hh * 4 + 3)
                            act(z_t[:], ps[bz_][:, :N], AF.Silu, [("ps", bz_)], ["z_t"])
                            if DBG.get("gstop", 99) <= 1:
                                continue
                            qn, kn, qnb, knb_ = tmp[1], tmp[2], at, bt
                            for src, skey, dst, dkey, scl, sc_, sck, sb_, sbk in [(xq, "r_t", qn, tk[1], 128.0 ** -0.5, tmp[3], tk[3], sqb, "sqb"),
                                                                                    (xk, "k_t", kn, tk[2], 1.0, tmp[0], tk[0], bnb, "bnb")]:
                                act(sb_[:], src[:], AF.Square, [skey], [sbk])
                                b = nps()
                                mm(ps[b][:, :N], ones_b, sb_[:], [sbk, "cb"], [("ps", b)])
                                act(sc_[:], ps[b][:, :N], AF.Ln, [("ps", b)], [sck], bias=1e-6, scale=1.0)
                                act(sc_[:], sc_[:], AF.Exp, [sck], [sck], scale=-0.5)
                                stt("dve", dst[:], src[:], scl, sc_[:], ALU.mult, ALU.mult, [skey, sck], [dkey])
                            if DBG.get("gstop", 99) <= 2:
                                continue
                            copy("act", qnb[:], qn[:], [tk[1]], ["at"])
                            copy("dve", knb_[:], kn[:], [tk[2]], ["bt"])
                            qt_, Khf = kt, tmp[4]
                            atg = rt
                            if samp:
                                if hh == 0:
                                    load_gdn_state(0)
                                ZS, ZSK = sbd[hh % 2], ("sbd", hh % 2)
                                copy("act", Zsb[:], ZS[:], [ZSK], ["Zsb"])
                                Zf, zk = ZS, ZSK
                                Zrd = lambda ci: [Zsb[:, s_, :] for s_ in range(NSQ)]
                                zrk = lambda ci: "Zsb"
                            else:
                                Zf, zk = Zg[:, hh:hh + 1, :], ("Zg", hh)
                                Zrd = lambda ci, hh=hh: [Zgb[:, hh, ci % 2, :]]
                                zrk = lambda ci, hh=hh: ("Zgb", hh, ci % 2)
                            bAs, bBs = [], []
                            for ci in range(nch):
                                copy("pool", g_rep[ci][:C, :], gtok[:C, ci, hh:hh + 1].to_broadcast([C, 128]), ["gtok"], [("g_rep", ci)])
                                copy("pool", b_rep[ci][:C, :], btok[:C, ci, hh:hh + 1].to_broadcast([C, 128]), ["btok"], [("b_rep", ci)])
                                ts("dve", Gs[ci][:C, :C], mT, gtok[:C, ci, hh:hh + 1], None, ALU.mult, None, ["cst", "gtok"], [("Gs", ci)])
                            for ci in range(nch):
                                bA = nps(); bAs.append(bA)
                                mm(ps[bA][:, 0:C], g_rep[ci][:C, :], mI, [("g_rep", ci), "cst"], [("ps", bA)])
                                mm(ps[bA][:, 128:128 + C], g_rep[ci][:C, :], mT, [("g_rep", ci), "cst"], [("ps", bA)])
                                mm(ps[bA][:, 256:256 + C], b_rep[ci][:C, :], ident_f[:C, :C], [("b_rep", ci), "cst"], [("ps", bA)])
                                mm(ps[bA][:C, 384:384 + C], Gs[ci][:C, :C], mI, [("Gs", ci), "cst"], [("ps", bA)])
                                bB = nps(); bBs.append(bB)
                                mm(ps[bB][:C, 0:C], mI, Gs[ci][:C, :C], [("Gs", ci), "cst"], [("ps", bB)])
                            for ci in range(nch):
                                cs = slice(ci * C, (ci + 1) * C)
                                bA, bB = bAs[ci], bBs[ci]
                                act(eg[ci][:, :C], ps[bA][:, 0:C], AF.Exp, [("ps", bA)], [("eg", ci)])
                                act(ek[ci][:, :C], ps[bA][:, 128:128 + C], AF.Exp, [("ps", bA)], [("ek", ci)])
                                act(dmi[ci][:C, :C], ps[bA][:C, 384:384 + C], AF.Exp, [("ps", bA)], [("dmi", ci)])
                                tt("dve", kbf[ci][:, :C], kn[:, cs], ps[bA][:, 256:256 + C], ALU.mult, [tk[2], ("ps", bA)], [("kbf", ci)])
                                act(dmt[ci][:C, :C], ps[bB][:C, 0:C], AF.Exp, [("ps", bB)], [("dmt", ci)])
                                tt("pool", dms[ci][:C, :C], dmi[ci][:C, :C], mS, ALU.mult, [("dmi", ci), "cst"], [("dms", ci)])
                                tt("pool", dmi[ci][:C, :C], dmi[ci][:C, :C], mI, ALU.mult, [("dmi", ci), "cst"], [("dmi", ci)])
                                tt("pool", dmt[ci][:C, :C], dmt[ci][:C, :C], mT, ALU.mult, [("dmt", ci), "cst"], [("dmt", ci)])
                                copy("pool", kbb[ci][:, :C], kbf[ci][:, :C], [("kbf", ci)], [("kbb", ci)])
                                stt("dve", atg[:, cs], kbf[ci][:, :C], -1.0, eg[ci][:, :C], ALU.mult, ALU.mult, [("kbf", ci), ("eg", ci)], [("rt", ci)])
                                tt("pool", qt_[:, cs], qn[:, cs], eg[ci][:, :C], ALU.mult, [tk[1], ("eg", ci)], [("kt", ci)])
                                tt("pool", Khf[:, cs], kn[:, cs], ek[ci][:, :C], ALU.mult, [tk[2], ("ek", ci)], [(tk[4], ci)])
                            for ci in range(nch):
                                cs = slice(ci * C, (ci + 1) * C)
                                b = nps()
                                tr(ps[b][:C, 0:128], Khf[:, cs], 128, [(tk[4], ci)], [("ps", b)])
                                tr(ps[b][:C, 128:256], xv[:, cs], 128, ["v_t"], [("ps", b)])
                                copy("act", tok[:C, ci, 3:5, :].rearrange("p a c -> p (a c)"), ps[b][:C, 0:256], [("ps", b)], [("tok", ci)])
                            if hh + 1 < DBG.get('nh', 4):
                                gdn_prefetch(hh + 1)
                                if samp:
                                    load_gdn_state(hh + 1)
                            for ci in range(nch):
                                cs = slice(ci * C, (ci + 1) * C)
                                b1 = nps()
                                mm(ps[b1][:C, 0:C], knb_[:, cs], kbb[ci][:, :C], ["bt", ("kbb", ci)], [("ps", b1)])
                                mm(ps[b1][:C, 128:128 + C], kbb[ci][:, :C], knb_[:, cs], ["bt", ("kbb", ci)], [("ps", b1)])
                                mm(ps[b1][:C, 256:256 + C], knb_[:, cs], qnb[:, cs], ["bt", "at"], [("ps", b1)])
                                stt("dve", cm1[ci][:C, 0, :C], ps[b1][:C, 0:C], -1.0, dms[ci][:C, :C], ALU.mult, ALU.mult, [("ps", b1), ("dms", ci)], [("cm1", ci)])
                                stt("dve", nt2[ci][:C, 0, :C], ps[b1][:C, 128:128 + C], -1.0, dmt[ci][:C, :C], ALU.mult, ALU.mult, [("ps", b1), ("dmt", ci)], [("nt2", ci)])
                                tt("dve", cm3[ci][:C, 0, :C], ps[b1][:C, 256:256 + C], dmi[ci][:C, :C], ALU.mult, [("ps", b1), ("dmi", ci)], [("cm3", ci)])
                                tt("pool", Tt[ci][:C, 0, :C], cm1[ci][:C, 0, :C], ident_b[:C, :C], ALU.add, [("cm1", ci), "cb"], [("Tt", ci)])
                            Pm = [cm1[ci][:C, 0, :C] for ci in range(nch)]
                            Qm = [nt2[ci][:C, 0, :C] for ci in range(nch)]
                            pqk = [[("cm1", ci), ("nt2", ci)] for ci in range(nch)]
                            for lev in range(nlev):
                                bs = []
                                for ci in range(nch):
                                    b = nps(); bs.append(b)
                                    if lev < nlev - 1:
                                        mm(ps[b][:C, 0:C], Qm[ci], Pm[ci], pqk[ci], [("ps", b)])
                                    mm(ps[b][:C, 128:128 + C], Pm[ci], Qm[ci], pqk[ci], [("ps", b)])
                                for ci in range(nch):
                                    b = bs[ci]; pq = PQ[ci]
                                    pvv = ps[b][:C, 0:256].rearrange("p (a c) -> p a c", a=2)[:, :, :C]
                                    if lev < nlev - 1:
                                        copy("act" if ci != 3 else "dve", pq[:C, 0:2, :C], pvv, [("ps", b)], [("PQ", ci)])
                                    else:
                                        copy("act" if ci != 3 else "dve", pq[:C, 1, :C], ps[b][:C, 128:128 + C], [("ps", b)], [("PQ", ci)])
                                    Pm[ci], Qm[ci] = pq[:C, 0, :C], pq[:C, 1, :C]
                                    pqk[ci] = [("PQ", ci)]
                                bs = []
                                for ci in range(nch):
                                    b = nps(); bs.append(b)
                                    mm(ps[b][:C, 0:C], Qm[ci], Tt[ci][:C, 0, :C], [("PQ", ci), ("Tt", ci)], [("ps", b)])
                                for ci in range(nch):
                                    b = bs[ci]
                                    tt("dve", Tt[ci][:C, 0, :C], ps[b][:C, 0:C], Tt[ci][:C, 0, :C], ALU.add, [("ps", b), ("Tt", ci)], [("Tt", ci)])
                            if samp:
                                tt("pool", atm[:], atg[:, :].unsqueeze(1).to_broadcast([128, NSQ, 64]),
                                   segm, ALU.mult, [("rt", 0), "segm"], ["atm"])
                                tt("pool", rtm[:], qt_[:, :].unsqueeze(1).to_broadcast([128, NSQ, 64]),
                                   segm, ALU.mult, [("kt", 0), "segm"], ["rtm"])
                            for ci in range(nch):
                                cs = slice(ci * C, (ci + 1) * C)
                                zr = Zrd(ci); zrkey = zrk(ci)
                                b = nps()
                                mmg([(ps[b][:C, 0:128], (atm[:, s, :] if samp else atg[:, cs]), zr[s]) for s in range(nS)],
                                    [("rt", ci), "atm", zrkey] if samp else [("rt", ci), zrkey], [("ps", b)])
                                stt("dve", Xb[:C, :], tok[:C, ci, 4, :], btok[:C, ci, hh:hh + 1], ps[b][:C, 0:128], ALU.mult, ALU.add,
                                    [("tok", ci), "btok", ("ps", b)], ["Xb"])
                                b = nps()
                                mm(ps[b][:C, 0:128], Tt[ci][:C, 0, :C], Xb[:C, :], [("Tt", ci), "Xb"], [("ps", b)])
                                copy("dve", Wb[:C, :], ps[b][:C, 0:128], [("ps", b)], ["Wb"])
                                if samp:
                                    tt("dve", Wm[:C, :, :], Wb[:C, :].unsqueeze(1).to_broadcast([C, NSQ, 128]),
                                       cc("rowseg", 16)[:C, :].unsqueeze(2).to_broadcast([C, NSQ, 128]), ALU.mult, ["Wb", "cst"], ["Wm"])
                                    egl = eg[ci][:, :C].rearrange("p (s l) -> p s l", s=NSQ)[:, :, LS - 1:LS]
                                    tt("dve", ZS[:], ZS[:], egl.to_broadcast([128, NSQ, 128]), ALU.mult, [ZSK, ("eg", ci)], [ZSK])
                                    for q in range(4):
                                        b = nps()
                                        mm(ps[b][:, :], tok[:C, ci, 3, :], Wm[:C, q * 4:(q + 1) * 4, :].rearrange("p a c -> p (a c)"), [("tok", ci), "Wm"], [("ps", b)])
                                        tt("dve", ZS[:, q * 4:(q + 1) * 4, :], ZS[:, q * 4:(q + 1) * 4, :], ps[b][:, :].rearrange("p (a c) -> p a c", a=4),
                                           ALU.add, [ZSK, ("ps", b)], [ZSK])
                                else:
                                    bz = nps()
                                    mm(ps[bz][:, 0:128], tok[:, ci, 3, :], Wb[:, :], [("tok", ci), "Wb"], [("ps", bz)])
                                    stt("dve", Zgb[:, hh, (ci + 1) % 2, :], Zf[:, 0, :], eg[ci][:, C - 1:C], ps[bz][:, 0:128], ALU.mult, ALU.add,
                                        [zk, ("eg", ci), ("ps", bz)], [("Zgb", hh, (ci + 1) % 2)])
                                    stt("dve", Zf[:, 0, :], Zf[:, 0, :], eg[ci][:, C - 1:C], ps[bz][:, 0:128], ALU.mult, ALU.add,
                                        [zk, ("eg", ci), ("ps", bz)], [zk])
                                b = nps()
                                items = [(ps[b][:, :C], zr[s], (rtm[:, s, :] if samp else qt_[:, cs])) for s in range(nS)]
                                items.append((ps[b][:, :C], Wb[:C, :], cm3[ci][:C, 0, :C]))
                                mmg(items, [zrkey, ("kt", ci), "rtm", "Wb", ("cm3", ci)] if samp else [zrkey, ("kt", ci), "Wb", ("cm3", ci)], [("ps", b)])
                                copy("act", o_t[:, cs], ps[b][:, :C], [("ps", b)], ["o_t"])

                            if samp:
                                k.dma("sp", dls_d.rearrange("(s h k) v -> h k s v", s=NSQ, h=4)[hh], ZS[:], reads=[ZSK])
                            elif ti == 3:
                                k.dma("sp", dlp_d[hh * 128:(hh + 1) * 128, :], Zg[:, hh, :], reads=[("Zg", hh)])
                            o2 = tmp[3]
                            tt("pool", sqb[:], o_t[:, :N], o_t[:, :N], ALU.mult, ["o_t"], ["sqb"])
                            b = nps()
                            mm(ps[b][:, :N], ones_b, sqb[:], ["sqb", "cb"], [("ps", b)])
                            act(o2[:], ps[b][:, :N], AF.Ln, [("ps", b)], [tk[3]], bias=1e-6, scale=1.0 / 128)
                            act(o2[:], o2[:], AF.Exp, [tk[3]], [tk[3]], scale=-0.5)
                            stt("dve", o2[:], o_t[:, :N], col("norm_w", 0), o2[:], ALU.mult, ALU.mult, ["o_t", "cols", tk[3]], [tk[3]])
                            tt("dve", oab[:, 4 + hh, :N], o2[:], z_t[:], ALU.mult, [tk[3], "z_t"], [("oab", 4 + hh)])

                        mrg = [at, bt, kt, rt, atz[:, 0, :], atz[:, 1, :], btz[:, 0, :], btz[:, 1, :]]
                        mrk = ["at", "bt", "kt", "rt", "atz", "atz", "btz", "btz"]
                        okeys = [("oab", j) for j in range(8)]
                        k.barrier()
                        for m in range(DBG.get('ng', 8)):
                            pj = pjbuf[m % 2]
                            k.dma("sp", pj[:, 0, :, :], pjab_d[:, m * 128:(m + 1) * 128].rearrange("(kc p) c -> p kc c", p=128), reads=["pjab"], writes=[("pj", m % 2)])
                            k.dma("sp", pj[:, 1, :, :], pjbb_d[:, m * 128:(m + 1) * 128].rearrange("(kc p) c -> p kc c", p=128), reads=["pjbb"], writes=[("pj", m % 2)])
                            bga = pchunk(30 + 2 * m)
                            act(tmp[0][:], ps[bga][:, :N], AF.Sigmoid, [("ps", bga)], [tk[0]])
                            b = nps()
                            mmg([(ps[b][:, :N], pj[:, 0, kc, :], oab[:, kc, :N]) for kc in range(4)], [("pj", m % 2)] + okeys, [("ps", b)])
                            tt("dve", tmp[1][:], tmp[0][:], ps[b][:, :N], ALU.mult, [tk[0], ("ps", b)], [tk[1]])
                            bgb = pchunk(31 + 2 * m)
                            act(tmp[0][:], ps[bgb][:, :N], AF.Sigmoid, [("ps", bgb)], [tk[0]])
                            b = nps()
                            mmg([(ps[b][:, :N], pj[:, 1, kc, :], oab[:, 4 + kc, :N]) for kc in range(4)], [("pj", m % 2)] + okeys, [("ps", b)])
                            tt("dve", tmp[2][:], tmp[0][:], ps[b][:, :N], ALU.mult, [tk[0], ("ps", b)], [tk[2]])
                            tt("pool", mrg[m][:, :N] if m < 4 else mrg[m], tmp[1][:], tmp[2][:], ALU.add, [tk[1], tk[2]], [mrk[m], ("mrg", m)])
                        for half in range(4):
                            wt = WI[half % 2]
                            k.dma("sp", wt[:, :, :], wob_d[:, half * 256:(half + 1) * 256].rearrange("(kc p) c -> p kc c", p=128),
                                  reads=[("wob", r_) for r_ in range(2)], writes=[("wi", half % 2)])
                            for m4 in range(2):
                                m = half * 2 + m4
                                b = nps()
                                mmg([(ps[b][:, :N], wt[:, kc, m4 * 128:(m4 + 1) * 128], (mrg[kc][:, :N] if kc < 4 else mrg[kc])) for kc in range(8)],
                                    [("wi", half % 2)] + [("mrg", j) for j in range(8)], [("ps", b)])
                                tt("dve", h[:, m, c0:c0 + N], h[:, m, c0:c0 + N], ps[b][:, :N], ALU.add, hk + [("ps", b)], hk)
                        k.barrier()
            k.barrier()

        while pending_casts:
            one_cast()
        if stage >= 2:
            mixer()
        if stage >= 3:
            ffn(w2g_d, w2u_d, w2d_d, "ffn2_norm", "b")

        with ExitStack() as st:
            rstd = sb("rstdf", [128, 512], F32, st)
            sq = [sb("sqf%d" % i, [128, 512], BF16, st) for i in range(2)]
            yf = [sb("yf%d" % i, [128, 8, 512], F32, st) for i in range(2)]
            yo = [sb("yo%d" % i, [128, D], F32, st) for i in range(3)]
            bi = 0
            for ti, (c0, N, _, _) in enumerate(TILES):
                hk = hkeys_of(c0, N)
                rms_stats(c0, N, rstd, sq, hk)
                yft = yf[ti % 2]
                for m in range(8):
                    k.op("dve", lambda e, m=m, yft=yft, c0=c0, N=N: e.scalar_tensor_tensor(
                        out=yft[:, m, :N], in0=h[:, m, c0:c0 + N], scalar=col("final_norm", m), in1=rstd[:, :N],
                        op0=ALU.mult, op1=ALU.mult), reads=hk + ["rstd", "cols"], writes=[("yf", ti % 2, m)])
                for blk in range((N + 127) // 128):
                    rows = min(128, N - blk * 128)
                    yot = yo[bi % 3]
                    for half in range(2):
                        b = nps()
                        for m4 in range(4):
                            m = half * 4 + m4
                            k.op("pe", lambda e, b=b, m4=m4, m=m, yft=yft, rows=rows, blk=blk: e.transpose(
                                ps[b][:rows, m4 * 128:(m4 + 1) * 128], yft[:, m, blk * 128:blk * 128 + rows], ident_f),
                                reads=[("yf", ti % 2, m), "cst"], writes=[("ps", b)])
                        if half == 0:
                            k.op("act", lambda e, b=b, yot=yot, rows=rows: e.activation(
                                out=yot[:rows, 0:512], in_=ps[b][:rows, :], func=AF.Copy), reads=[("ps", b)], writes=[("yo", bi % 3)])
                        else:
                            k.op("dve", lambda e, b=b, yot=yot, rows=rows: e.tensor_copy(
                                out=yot[:rows, 512:1024], in_=ps[b][:rows, :]), reads=[("ps", b)], writes=[("yo", bi % 3)])
                    k.dma("sp", y_d[c0 + blk * 128:c0 + blk * 128 + rows, :], yot[:rows, :], reads=[("yo", bi % 3)])
                    bi += 1
        k.finish()
    return nc


def _perm_a():
    parts = [np.arange(512, 576), np.arange(1600, 1664), np.arange(1664, 1792)]
    for u in range(4):
        parts += [np.arange(u * 128, (u + 1) * 128), np.arange(576 + u * 128, 576 + (u + 1) * 128),
                  np.arange(1088 + u * 128, 1088 + (u + 1) * 128)]
    return np.concatenate(parts)


def _perm_b():
    parts = []
    for hh in range(4):
        parts += [np.arange(hh * 128, (hh + 1) * 128), np.arange(512 + hh * 128, 512 + (hh + 1) * 128),
                  np.arange(1024 + hh * 128, 1024 + (hh + 1) * 128), np.arange(1544 + hh * 128, 1544 + (hh + 1) * 128)]
    return np.concatenate(parts)


def _qkv_from_b():
    j = np.arange(1536)
    role = j // 512; hh = (j % 512) // 128; off = j % 128
    return hh * 512 + role * 128 + off


def _win_perm():
    pa = _perm_a()
    pb = A_PROJ + _perm_b()
    g0 = A_PROJ + 2056
    pg = np.concatenate([np.concatenate([np.arange(g0 + m * 128, g0 + (m + 1) * 128),
                                         np.arange(g0 + 1024 + m * 128, g0 + 1024 + (m + 1) * 128)]) for m in range(8)])
    ab = np.arange(A_PROJ + 1536, A_PROJ + 1544)
    return np.concatenate([pa, pb, pg, ab])


def _colvec(v):
    v = np.asarray(v, np.float32).reshape(-1)
    return np.ascontiguousarray(v.reshape(-1, 128).T)


def kernel(**inp):
    stage = int(inp.pop("_stage", 99))
    f = lambda a: np.ascontiguousarray(np.asarray(a, dtype=np.float32))
    pa = _perm_a()
    cols = np.zeros((128, NCOL), np.float32)
    def putc(name, arr):
        cols[:, COLS[name]:COLS[name] + arr.shape[1]] = arr
    putc("ffn1_norm", _colvec(inp["ffn1_norm"][0])); putc("mix_norm", _colvec(inp["mix_norm"][0]))
    putc("ffn2_norm", _colvec(inp["ffn2_norm"][0])); putc("final_norm", _colvec(inp["final_norm"]))
    putc("mu", _colvec(f(inp["rwkv_mu"])[0][pa]))
    for nm, key in [("w0", "rwkv_w0"), ("a0", "rwkv_a0"), ("k_k", "rwkv_k_k"), ("k_a", "rwkv_k_a"),
                    ("r_k", "rwkv_r_k"), ("lnx_w", "rwkv_lnx_w"), ("lnx_b", "rwkv_lnx_b")]:
        putc(nm, _colvec(f(inp[key])[0]))
    cw = f(inp["gdn_conv_w"])[0]
    putc("conv_w", np.concatenate([_colvec(cw[i]) for i in range(4)], axis=1))
    putc("norm_w", _colvec(f(inp["gdn_norm_w"])[0]))
    putc("A_log", np.tile(f(inp["gdn_A_log"])[0][None, :], (128, 1)))
    putc("dt_bias", np.tile(f(inp["gdn_dt_bias"])[0][None, :], (128, 1)))
    consts = make_consts()
    w_in = np.ascontiguousarray(f(inp["w_in"])[0][:, _win_perm()])
    shared = {
        "w1g": f(inp["ffn1_w_gate"])[0], "w1u": f(inp["ffn1_w_up"])[0], "w1d": f(inp["ffn1_w_down"])[0],
        "w2g": f(inp["ffn2_w_gate"])[0], "w2u": f(inp["ffn2_w_up"])[0], "w2d": f(inp["ffn2_w_down"])[0],
        "w_in": w_in, "proj_a": f(inp["proj_a"])[0], "proj_b": f(inp["proj_b"])[0], "w_out": f(inp["w_out"])[0],
        "lw2": f(inp["rwkv_w2"])[0], "la2": f(inp["rwkv_a2"])[0], "lg2": f(inp["rwkv_g2"])[0],
        "cols": cols, "consts": consts, "consts2": make_consts2(),
    }
    xp = f(inp["x_prompt"]); xs = f(inp["x_sample"])
    srw = f(inp["state_rwkv"])[0]; ssh = f(inp["state_rwkv_shift"])[0]
    sdl = f(inp["state_delta"])[0]; scv = f(inp["state_conv"])[0]
    in_maps = []
    for c in range(NCORES):
        sl = slice(c * NSQ, (c + 1) * NSQ)
        m = dict(shared)
        m["x"] = np.ascontiguousarray(np.concatenate([xp[c], xs[sl].reshape(NSQ * LS, D)], axis=0))
        m["s_rwkv"] = np.ascontiguousarray(srw[sl].reshape(NSQ * 512, 64))
        m["s_shift"] = np.ascontiguousarray(ssh[sl][:, pa])
        m["s_delta"] = np.ascontiguousarray(sdl[sl].reshape(NSQ * 512, 128))
        m["s_conv"] = np.ascontiguousarray(scv[sl].reshape(NSQ * 3, 1536))
        in_maps.append(m)
    nc = build(stage)
    res = run_bass_kernel_spmd(nc, in_maps, core_ids=list(range(NCORES)))
    R = res.results
    inv = np.argsort(pa)
    y = np.stack([r["y"] for r in R])
    y_prompt = np.ascontiguousarray(y[:, :SEQ, :])
    y_sample = np.ascontiguousarray(y[:, SEQ:, :].reshape(NCORES * NSQ, LS, D))
    qb = _qkv_from_b()
    rwkv_p = np.stack([r["rwkv_p"].reshape(8, 64, 64) for r in R])[None]
    shift_p = np.stack([r["tokrows_p"][3, :A_PROJ][inv] for r in R])[None]
    delta_p = np.stack([r["delta_p"].reshape(4, 128, 128) for r in R])[None]
    conv_p = np.stack([r["tokrows_p"][1:4, A_PROJ:][:, qb] for r in R])[None]
    rwkv_s = np.concatenate([r["rwkv_s"].reshape(NSQ, 8, 64, 64) for r in R])[None]
    shift_s = np.concatenate([r["tokrows_s"].reshape(NSQ, LS, 3840)[:, 3, :A_PROJ][:, inv] for r in R])[None]
    delta_s = np.concatenate([r["delta_s"].reshape(NSQ, 4, 128, 128) for r in R])[None]
    conv_s = np.concatenate([r["tokrows_s"].reshape(NSQ, LS, 3840)[:, 1:4, A_PROJ:][:, :, qb] for r in R])[None]
    outs = (y_prompt, y_sample, rwkv_p, shift_p, delta_p, conv_p, rwkv_s, shift_s, delta_s, conv_s)
    return tuple(np.ascontiguousarray(o.astype(np.float32)) for o in outs)
```

```python
from contextlib import ExitStack
import numpy as np
import concourse.bass as bass
import concourse.mybir as mybir
from concourse.bass_utils import run_bass_kernel_spmd

F32 = mybir.dt.float32
BF16 = mybir.dt.bfloat16
AF = mybir.ActivationFunctionType
ALU = mybir.AluOpType

NCORES = 8
DBG = {}
D = 1024
DFF = 2816
SEQ = 2048
NSQ = 16
LS = 4
NT = SEQ + NSQ * LS
A_PROJ = 1792
C0 = float(np.exp(-0.5))

COLS = {}
_nc = 0
for _name, _n in [("ffn1_norm", 8), ("mix_norm", 8), ("ffn2_norm", 8), ("final_norm", 8), ("mu", 14),
                  ("w0", 4), ("a0", 4), ("k_k", 4), ("k_a", 4), ("r_k", 4), ("lnx_w", 4), ("lnx_b", 4),
                  ("conv_w", 48), ("norm_w", 1), ("A_log", 4), ("dt_bias", 4)]:
    COLS[_name] = _nc
    _nc += _n
NCOL = _nc

CONST = {}
_k = 0
for _name, _n in [("ident", 128), ("incl", 128), ("strict", 128), ("tail", 128), ("incl_s", 128), ("strict_s", 128),
                  ("tail_s", 128), ("bones", 128), ("ones", 128), ("hmask", 256), ("rowseg", 16),
                  ("reset_s", 64)]:
    CONST[_name] = _k
    _k += _n
NCONST = _k


def make_consts2():
    sm = np.zeros((128, 16, 64), np.float32)
    for s in range(16):
        sm[:, s, s * LS:(s + 1) * LS] = 1
    return sm.reshape(128, 1024)


def make_consts():
    c = np.zeros((128, NCONST), np.float32)
    i = np.arange(128)
    seg = i // LS
    same = (seg[:, None] == seg[None, :])
    def put(name, a):
        c[:a.shape[0], CONST[name]:CONST[name] + a.shape[1]] = a
    put("ident", np.eye(128, dtype=np.float32))
    put("incl", (i[None, :] >= i[:, None]).astype(np.float32))
    put("strict", (i[None, :] > i[:, None]).astype(np.float32))
    put("tail", (i[:, None] > i[None, :]).astype(np.float32))
    put("incl_s", ((i[None, :] >= i[:, None]) & same).astype(np.float32))
    put("strict_s", ((i[None, :] > i[:, None]) & same).astype(np.float32))
    put("tail_s", ((i[:, None] > i[None, :]) & same).astype(np.float32))
    bo = np.zeros((128, 128), np.float32); bo[:64, :64] = 1; bo[64:, 64:] = 1
    put("bones", bo)
    put("ones", np.ones((128, 128), np.float32))
    hm = np.zeros((128, 256), np.float32); hm[:, 0:64] = 1; hm[:, 128 + 64:256] = 1
    put("hmask", hm)
    rs = np.zeros((128, 16), np.float32)
    for s in range(16):
        rs[s * LS:(s + 1) * LS, s] = 1
    put("rowseg", rs)
    r = np.ones((128, 64), np.float32); r[:, ::LS] = 0
    put("reset_s", r)
    return c


class K:
    def __init__(self, nc, es):
        self.nc = nc
        self.eng = {"pe": nc.tensor, "act": nc.scalar, "dve": nc.vector, "pool": nc.gpsimd, "sp": nc.sync}
        self.sem = {e: es.enter_context(nc.semaphore("sem_" + e)) for e in ("pe", "act", "dve", "pool")}
        self.cnt = {e: 0 for e in self.sem}
        self.ndma = 24
        self.dsem = [es.enter_context(nc.semaphore("dsem%d" % i)) for i in range(self.ndma)]
        self.dval = [0] * self.ndma
        self.di = {"sp": 0, "pool": 12}
        self.waited = {e: {} for e in self.eng}
        self.lastw = {}
        self.readers = {}
        self.nops = 0
        self._cap = None

    def begin(self):
        assert self._cap is None
        self._cap = []

    def end(self):
        c = self._cap
        self._cap = None
        return c

    def emit(self, items):
        for it in items:
            if it[0] == "op":
                self.op(*it[1:])
            else:
                self.dma(*it[1:])

    def emit_interleaved(self, a, b):
        na, nb = len(a), len(b)
        ia = ib = 0
        while ia < na or ib < nb:
            if ib >= nb or (ia < na and ia * nb <= ib * na):
                self.emit([a[ia]]); ia += 1
            else:
                self.emit([b[ib]]); ib += 1

    def _semh(self, key):
        return self.sem[key] if isinstance(key, str) else self.dsem[key[1]]

    def _deps(self, reads, writes):
        toks = []
        for k in reads:
            if k in self.lastw:
                toks.append(self.lastw[k])
        for k in writes:
            if k in self.lastw:
                toks.append(self.lastw[k])
            toks.extend(self.readers.get(k, ()))
        return toks

    def _wait(self, en, toks):
        need = {}
        for (sk, v) in toks:
            if sk == "pe" and en == "pe":
                continue
            if v > need.get(sk, 0):
                need[sk] = v
        w = self.waited[en]
        for sk, v in need.items():
            if w.get(sk, 0) < v:
                self.eng[en].wait_ge(self._semh(sk), v)
                w[sk] = v

    def _record(self, tok, reads, writes):
        for k in reads:
            self.readers.setdefault(k, []).append(tok)
        for k in writes:
            self.lastw[k] = tok
            self.readers[k] = []

    @staticmethod
    def _excl(reads, writes):
        pr = [x for x in reads if isinstance(x, tuple) and x[0] == "ps"]
        if pr:
            reads = [x for x in reads if x not in pr]
            writes = list(writes) + [x for x in pr if x not in writes]
        return reads, writes

    def op(self, en, fn, reads=(), writes=()):
        if self._cap is not None:
            self._cap.append(("op", en, fn, list(reads), list(writes)))
            return
        reads, writes = self._excl(reads, writes)
        self._wait(en, self._deps(reads, writes))
        ins = fn(self.eng[en])
        self.cnt[en] += 1
        ins.then_inc(self.sem[en], 1)
        self._record((en, self.cnt[en]), reads, writes)
        self.nops += 1

    def dma(self, q, out, in_, reads=(), writes=()):
        if self._cap is not None:
            self._cap.append(("dma", q, out, in_, list(reads), list(writes)))
            return
        toks = self._deps(reads, writes)
        i = self.di[q]
        base = 0 if q == "sp" else 12
        self.di[q] = base + (i - base + 1) % 12
        if self.dval[i] > 0:
            toks.append((("dma", i), self.dval[i]))
        self._wait(q, toks)
        ins = self.eng[q].dma_start(out=out, in_=in_)
        self.dval[i] += 16
        ins.then_inc(self.dsem[i], 16)
        self._record((("dma", i), self.dval[i]), reads, writes)
        self.nops += 1

    def barrier(self):
        toks = [(e, self.cnt[e]) for e in self.cnt if self.cnt[e] > 0]
        toks += [(("dma", i), self.dval[i]) for i in range(self.ndma) if self.dval[i] > 0]
        for en in self.eng:
            self._wait(en, toks)

    def finish(self):
        toks = [(("dma", i), self.dval[i]) for i in range(self.ndma) if self.dval[i] > 0]
        toks += [(e, self.cnt[e]) for e in self.cnt if self.cnt[e] > 0]
        self._wait("sp", toks)


def build(stage=99):
    nc = bass.Bass("TRN2", target_bir_lowering=False)
    es = ExitStack()

    def din(name, shape):
        return nc.dram_tensor(name, list(shape), F32, kind="ExternalInput").ap()

    def dout(name, shape):
        return nc.dram_tensor(name, list(shape), F32, kind="ExternalOutput").ap()

    x_d = din("x", [NT, D])
    srw_d = din("s_rwkv", [NSQ * 512, 64])
    ssh_d = din("s_shift", [NSQ, A_PROJ])
    sdl_d = din("s_delta", [NSQ * 512, 128])
    scv_d = din("s_conv", [NSQ * 3, 1536])
    w1g_d = din("w1g", [D, DFF]); w1u_d = din("w1u", [D, DFF]); w1d_d = din("w1d", [DFF, D])
    w2g_d = din("w2g", [D, DFF]); w2u_d = din("w2u", [D, DFF]); w2d_d = din("w2d", [DFF, D])
    win_d = din("w_in", [D, 5896])
    pja_d = din("proj_a", [512, D]); pjb_d = din("proj_b", [512, D]); wo_d = din("w_out", [D, D])
    lw2_d = din("lw2", [64, 512]); la2_d = din("la2", [64, 512]); lg2_d = din("lg2", [128, 512])
    cols_d = din("cols", [128, NCOL]); consts_d = din("consts", [128, NCONST]); consts2_d = din("consts2", [128, 1024])

    winb_d = nc.dram_tensor("w_in_bf", [D, 5896], BF16, kind="Internal").ap()
    pjab_d = nc.dram_tensor("proj_a_bf", [512, D], BF16, kind="Internal").ap()
    pjbb_d = nc.dram_tensor("proj_b_bf", [512, D], BF16, kind="Internal").ap()
    wob_d = nc.dram_tensor("w_out_bf", [D, D], BF16, kind="Internal").ap()
    y_d = dout("y", [NT, D])
    rwp_d = dout("rwkv_p", [512, 64]); dlp_d = dout("delta_p", [512, 128])
    rws_d = dout("rwkv_s", [NSQ * 512, 64]); dls_d = dout("delta_s", [NSQ * 512, 128])
    trp_d = dout("tokrows_p", [4, 3840]); trs_d = dout("tokrows_s", [NSQ * LS, 3840])

    with es:
        k = K(nc, es)

        def sb(name, shape, dt=F32, stack=es):
            return stack.enter_context(nc.sbuf_tensor("sb_" + name, list(shape), dt))

        ps = [es.enter_context(nc.psum_tensor("ps%d" % i, [128, 512], F32)) for i in range(8)]
        psi = [0, 0]
        psmode = [0]

        def nps():
            if psmode[0] == 0:
                i = psi[0]
                psi[0] = (i + 1) % 8
                return i
            if psmode[0] == 1:
                i = psi[0] % 4
                psi[0] = (i + 1) % 4
                return i
            i = psi[1]
            psi[1] = (i + 1) % 4
            return 4 + i

        h = sb("h", [128, 8, NT])
        cols = sb("cols", [128, NCOL])
        cst = sb("cst", [128, NCONST])
        cb = sb("cb", [128, 5 * 128], BF16)
        pending_casts = []
        for r8 in range(8):
            pending_casts.append(lambda r8=r8: k.dma("pool", winb_d[r8 * 128:(r8 + 1) * 128, :], win_d[r8 * 128:(r8 + 1) * 128, :], writes=[("winb", r8)]))
        pending_casts.append(lambda: k.dma("pool", pjab_d[:, :], pja_d[:, :], writes=["pjab"]))
        pending_casts.append(lambda: k.dma("pool", pjbb_d[:, :], pjb_d[:, :], writes=["pjbb"]))
        for r2 in range(2):
            pending_casts.append(lambda r2=r2: k.dma("pool", wob_d[r2 * 512:(r2 + 1) * 512, :], wo_d[r2 * 512:(r2 + 1) * 512, :], writes=[("wob", r2)]))

        def one_cast():
            if pending_casts:
                pending_casts.pop(0)()
        k.dma("sp", cols[:], cols_d[:, :], writes=["cols"])
        k.dma("sp", cst[:], consts_d[:, :], writes=["cst"])

        def cc(name, n=128, rows=128):
            o = CONST[name]
            return cst[:rows, o:o + n]

        def col(name, j=0):
            o = COLS[name] + j
            return cols[:, o:o + 1]

        for j, nm in enumerate(["ident", "ones", "bones"]):
            k.op("dve", lambda e, j=j, nm=nm: e.tensor_copy(out=cb[:, j * 128:(j + 1) * 128], in_=cc(nm)),
                 reads=["cst"], writes=["cb"])
        ident_f = cc("ident")
        ones_b = cb[:, 128:256]
        bones_b = cb[:, 256:384]

        TILES = [(t * 512, 512, 1, 512) for t in range(4)] + [(SEQ, NSQ * LS, NSQ, LS)]

        with ExitStack() as st:
            xin = [sb("xin%d" % i, [128, D], F32, st) for i in range(2)]
            for blk in range(17):
                rows = 128 if blk < 16 else 64
                xb = xin[blk % 2]
                k.dma("sp", xb[:rows, :], x_d[blk * 128: blk * 128 + rows, :], writes=[("xin", blk % 2)])
                for half in range(2):
                    b = nps()
                    for m4 in range(4):
                        m = half * 4 + m4
                        k.op("pe", lambda e, b=b, m4=m4, m=m, xb=xb, rows=rows: e.transpose(
                            ps[b][:, m4 * 128: m4 * 128 + rows], xb[:rows, m * 128:(m + 1) * 128], ident_f[:rows, :rows]),
                            reads=[("xin", blk % 2), "cst"], writes=[("ps", b)])
                    k.op("act", lambda e, b=b, half=half, blk=blk, rows=rows: e.activation(
                        out=h[:, half * 4:(half + 1) * 4, blk * 128: blk * 128 + rows],
                        in_=ps[b][:].rearrange("p (a c) -> p a c", a=4)[:, :, :rows], func=AF.Copy),
                        reads=[("ps", b)], writes=[("h", blk)])
            k.barrier()

        def rms_stats(c0, N, rstd, tmp_sq, hkeys, rkey="rstd", sqkeys=(("sq", 0), ("sq", 1))):
            b = nps()
            for m in range(8):
                sq = tmp_sq[m % 2]
                k.op("act", lambda e, sq=sq, m=m: e.activation(out=sq[:, :N], in_=h[:, m, c0:c0 + N], func=AF.Square),
                     reads=hkeys, writes=[sqkeys[m % 2]])
                k.op("pe", lambda e, sq=sq, m=m, b=b: e.matmul(ps[b][:, :N], ones_b, sq[:, :N], start=(m == 0), stop=(m == 7)),
                     reads=[sqkeys[m % 2], "cb"], writes=[("ps", b)])
            k.op("act", lambda e: e.activation(out=rstd[:, :N], in_=ps[b][:, :N], func=AF.Sqrt, bias=1e-6, scale=1.0 / D),
                 reads=[("ps", b)], writes=[rkey])
            k.op("dve", lambda e: e.reciprocal(out=rstd[:, :N], in_=rstd[:, :N]), reads=[rkey], writes=[rkey])

        def hkeys_of(c0, N):
            return [("h", bk) for bk in range(c0 // 128, (c0 + N + 127) // 128)]

        def ffn(wg_d, wu_d, wd_d, normname, tag):
            with ExitStack() as st:
                u = sb("u" + tag, [128, 8, NT], BF16, st)
                hid = sb("hid" + tag, [128, 11, NT], BF16, st)
                WG = [sb("wg%d" % i + tag, [128, 8, 512], BF16, st) for i in range(2)]
                WU = [sb("wu%d" % i + tag, [128, 8, 512], BF16, st) for i in range(2)]
                WD = [sb("wd%d" % i + tag, [128, 11, 128], BF16, st) for i in range(2)]
                rstd = sb("rstd" + tag, [128, 512], F32, st)
                sq = [sb("sq%d" % i + tag, [128, 512], BF16, st) for i in range(2)]
                sg = [sb("sg%d" % i + tag, [128, 512], F32, st) for i in range(2)]
                def norm_tile(ti):
                    c0, N = TILES[ti][0], TILES[ti][1]
                    rms_stats(c0, N, rstd, sq, hkeys_of(c0, N))
                    for m in range(8):
                        k.op("dve", lambda e, m=m: e.scalar_tensor_tensor(
                            out=u[:, m, c0:c0 + N], in0=h[:, m, c0:c0 + N], scalar=col(normname, m), in1=rstd[:, :N],
                            op0=ALU.mult, op1=ALU.mult), reads=hkeys_of(c0, N) + ["rstd", "cols"], writes=[("u", ti)])
                norm_tile(0)
                gi = 0
                di = 0
                sgi = 0
                for half in range(2):
                    groups = [(0, 4), (4, 4), (8, 3)]
                    for (j0, nj) in groups:
                        ff0 = (half * 11 + j0) * 128
                        wgt, wut = WG[gi % 2], WU[gi % 2]
                        k.dma("pool", wgt[:, :, :nj * 128], wg_d[:, ff0:ff0 + nj * 128].rearrange("(kc p) c -> p kc c", p=128),
                              writes=[("wg", gi % 2)])
                        k.dma("pool", wut[:, :, :nj * 128], wu_d[:, ff0:ff0 + nj * 128].rearrange("(kc p) c -> p kc c", p=128),
                              writes=[("wu", gi % 2)])
                        one_cast()
                        for ti, (c0, N, _, _) in enumerate(TILES):
                            if gi == 0 and ti + 1 < len(TILES):
                                norm_tile(ti + 1)
                            for j in range(nj):
                                b1 = nps(); b2 = nps()
                                def mm(e, wt, b, j=j):
                                    ins = None
                                    for kc in range(8):
                                        ins = e.matmul(ps[b][:, :N], wt[:, kc, j * 128:(j + 1) * 128], u[:, kc, c0:c0 + N],
                                                       start=(kc == 0), stop=(kc == 7))
                                    return ins
                                k.op("pe", lambda e, mm=mm, wgt=wgt, b1=b1: mm(e, wgt, b1),
                                     reads=[("wg", gi % 2), ("u", ti)], writes=[("ps", b1)])
                                k.op("pe", lambda e, mm=mm, wut=wut, b2=b2: mm(e, wut, b2),
                                     reads=[("wu", gi % 2), ("u", ti)], writes=[("ps", b2)])
                                sgt = sg[sgi % 2]
                                k.op("act", lambda e, sgt=sgt, b1=b1: e.activation(out=sgt[:, :N], in_=ps[b1][:, :N], func=AF.Silu),
                                     reads=[("ps", b1)], writes=[("sg", sgi % 2)])
                                k.op("dve", lambda e, sgt=sgt, b2=b2, jj=j0 + j: e.tensor_tensor(
                                    out=hid[:, jj, c0:c0 + N], in0=sgt[:, :N], in1=ps[b2][:, :N], op=ALU.mult),
                                    reads=[("sg", sgi % 2), ("ps", b2)], writes=[("hid", ti, j0 + j)])
                                sgi += 1
                        gi += 1
                    for piece in range(8):
                        wdt = WD[di % 2]
                        r0 = half * 11 * 128
                        k.dma("pool", wdt[:, :, :], wd_d[r0:r0 + 11 * 128, piece * 128:(piece + 1) * 128].rearrange("(kc p) c -> p kc c", p=128),
                              writes=[("wd", di % 2)])
                        one_cast()
                        for ti, (c0, N, _, _) in enumerate(TILES):
                            for mm_ in range(1):
                                m = piece
                                b = nps()
                                def mmd(e, wdt=wdt, b=b, mm_=mm_, c0=c0, N=N):
                                    ins = None
                                    for kc in range(11):
                                        ins = e.matmul(ps[b][:, :N], wdt[:, kc, mm_ * 128:(mm_ + 1) * 128], hid[:, kc, c0:c0 + N],
                                                       start=(kc == 0), stop=(kc == 10))
                                    return ins
                                k.op("pe", mmd, reads=[("wd", di % 2)] + [("hid", ti, jj) for jj in range(11)], writes=[("ps", b)])
                                k.op("dve", lambda e, b=b, m=m, c0=c0, N=N: e.scalar_tensor_tensor(
                                    out=h[:, m, c0:c0 + N], in0=ps[b][:, :N], scalar=0.5, in1=h[:, m, c0:c0 + N],
                                    op0=ALU.mult, op1=ALU.add), reads=[("ps", b)] + hkeys_of(c0, N), writes=hkeys_of(c0, N))
                        di += 1
            k.barrier()

        if stage >= 1:
            ffn(w1g_d, w1u_d, w1d_d, "ffn1_norm", "a")
        def act(out, in_, func, r, w, **kw):
            k.op("act", lambda e: e.activation(out=out, in_=in_, func=func, **kw), reads=r, writes=w)

        def tt(en, out, a, b, op, r, w):
            k.op(en, lambda e: e.tensor_tensor(out=out, in0=a, in1=b, op=op), reads=r, writes=w)

        def ts(en, out, a, s1, s2, op0, op1, r, w):
            if s2 is None:
                k.op(en, lambda e: e.tensor_scalar(out, a, s1, None, op0), reads=r, writes=w)
            else:
                k.op(en, lambda e: e.tensor_scalar(out, a, s1, s2, op0, op1), reads=r, writes=w)

        def stt(en, out, in0, sc, in1, op0, op1, r, w):
            k.op(en, lambda e: e.scalar_tensor_tensor(out=out, in0=in0, scalar=sc, in1=in1, op0=op0, op1=op1),
                 reads=r, writes=w)

        def mm(out, lhsT, rhs, r, w, start=True, stop=True):
            k.op("pe", lambda e: e.matmul(out, lhsT, rhs, start=start, stop=stop), reads=r, writes=w)

        def mmg(items, r, w):
            def f(e):
                ins = None
                n = len(items)
                for i, (o, l, rh) in enumerate(items):
                    ins = e.matmul(o, l, rh, start=(i == 0), stop=(i == n - 1))
                return ins
            k.op("pe", f, reads=r, writes=w)

        def tr(out, in_, rows, r, w):
            k.op("pe", lambda e: e.transpose(out, in_, ident_f[:rows, :rows]), reads=r + ["cst"], writes=w)

        def copy(en, out, in_, r, w):
            if en == "act":
                act(out, in_, AF.Copy, r, w)
            else:
                k.op(en, lambda e: e.tensor_copy(out=out, in_=in_), reads=r, writes=w)

        def mixer():
            with ExitStack() as st:
                def T(name, shape, dt=F32, stack=st):
                    return sb(name, shape, dt, stack)
                lora_wa = T("lora_wa", [128, 512], BF16)
                lg2 = T("lg2s", [128, 512], BF16)
                wab = T("wab", [128, 8, 8], BF16)
                pjbuf = [T("pjbuf%d" % i, [128, 2, 4, 128], BF16) for i in range(2)]
                WI = [T("wi%d" % i, [128, 8, 256], BF16) for i in range(2)]
                u_t = T("u_t", [128, 8, 512], BF16)
                oab = T("oab", [128, 8, 512], BF16)
                rowst = T("rowst", [128, 256])
                halo_a = T("halo_a", [128, 14]); halo_c = T("halo_c", [128, 12, 3])
                sshT = T("sshT", [128, 14, 16]); scvT = T("scvT", [128, 12, 48])
                Zr = T("Zr", [128, 4, 128]); Zrb = T("Zrb", [128, 4, 2, 128], BF16)
                Zg = T("Zg", [128, 4, 128]); Zgb = T("Zgb", [128, 4, 2, 128], BF16)
                omka = T("omka", [128, 4]); eA = T("eA", [128, 4]); dtb = T("dtb", [128, 4]); omu = T("omu", [128, 14])
                dc = T("dc", [128, 16])
                k.dma("pool", lora_wa[0:64, :], lw2_d[:, :], writes=["lora_wa"])
                k.dma("pool", lora_wa[64:128, :], la2_d[:, :], writes=["lora_wa"])
                k.dma("pool", lg2[:], lg2_d[:, :], writes=["lg2"])
                k.dma("sp", wab[:], winb_d[:, 5888:5896].rearrange("(kc p) c -> p kc c", p=128), reads=[("winb", r_) for r_ in range(8)], writes=["wab"])
                for t_, nm in [(halo_a, "halo_a"), (halo_c, "halo_c")]:
                    k.op("pool", lambda e, t_=t_: e.memset(t_[:], 0.0), writes=[nm])
                for t_, nm in [(Zr, "Zr"), (Zg, "Zg")]:
                    k.op("pool", lambda e, t_=t_: e.memset(t_[:], 0.0), writes=[(nm, j) for j in range(4)])
                for t_, nm in [(Zrb, "Zrb"), (Zgb, "Zgb")]:
                    k.op("pool", lambda e, t_=t_: e.memset(t_[:], 0.0), writes=[(nm, j, p_) for j in range(4) for p_ in range(2)])
                ts("dve", omka[:], cols[:, COLS["k_a"]:COLS["k_a"] + 4], -1.0, 1.0, ALU.mult, ALU.add, ["cols"], ["omka"])
                act(eA[:], cols[:, COLS["A_log"]:COLS["A_log"] + 4], AF.Exp, ["cols"], ["eA"])
                ts("dve", omu[:], cols[:, COLS["mu"]:COLS["mu"] + 14], -1.0, 1.0, ALU.mult, ALU.add, ["cols"], ["omu"])
                copy("dve", dtb[:], cols[:, COLS["dt_bias"]:COLS["dt_bias"] + 4], ["cols"], ["dtb"])
                with ExitStack() as s0:
                    ld1 = T("ld1", [16, A_PROJ], F32, s0); ld2 = T("ld2", [48, 1536], F32, s0)
                    k.dma("sp", ld1[:], ssh_d[:, :], writes=["ld1"])
                    k.dma("sp", ld2[:], scv_d[:, :], writes=["ld2"])
                    b = nps()
                    for c in range(14):
                        tr(ps[b][:, c * 16:(c + 1) * 16], ld1[:, c * 128:(c + 1) * 128], 16, ["ld1"], [("ps", b)])
                    copy("act", sshT[:].rearrange("p a s -> p (a s)"), ps[b][:, :224], [("ps", b)], ["sshT"])
                    for hf in range(2):
                        b = nps()
                        for c6 in range(6):
                            c = hf * 6 + c6
                            tr(ps[b][:, c6 * 48:(c6 + 1) * 48], ld2[:, c * 128:(c + 1) * 128], 48, ["ld2"], [("ps", b)])
                        copy("act", scvT[:, hf * 6:(hf + 1) * 6, :].rearrange("p a s -> p (a s)"), ps[b][:, :288],
                             [("ps", b)], ["scvT"])
                    k.barrier()

                bones_f = cc("bones"); ones_f = cc("ones"); ident_b = cb[:, 0:128]

                for ti, (c0, N, nseq, L) in enumerate(TILES):
                    if ti not in DBG.get('tiles', range(5)):
                        continue
                    samp = nseq > 1
                    C = 64 if samp else 128
                    nch = 1 if samp else 4
                    nS = NSQ if samp else 1
                    nsg, lsg = (NSQ, LS) if samp else (4, 128)
                    nlev = 1 if samp else 6
                    mI = cc("incl_s" if samp else "incl")[:C, :C]
                    mS = cc("strict_s" if samp else "strict")[:C, :C]
                    mT = cc("tail_s" if samp else "tail")[:C, :C]
                    hk = hkeys_of(c0, N)
                    with ExitStack() as ts_:
                        def TT(name, shape, dt=F32):
                            return sb(name + "_%d" % ti, shape, dt, ts_)
                        tmp = [TT("tmp%d" % i, [128, N]) for i in range(12)]
                        tk = ["tmp%d" % i for i in range(12)]
                        pa_c = TT("pa_c", [128, N + 3 * nseq]); pa_c2 = TT("pa_c2", [128, N + 3 * nseq])
                        r_t = TT("r_t", [128, N]); k_t = TT("k_t", [128, N]); v_t = TT("v_t", [128, N]); o_t = TT("o_t", [128, N])
                        z_t = TT("z_t", [128, N])
                        lora_in = TT("lora_in", [128, N], BF16); sig_gd = TT("sig_gd", [128, N], BF16)
                        sqb = TT("sqb", [128, N], BF16); bnb = TT("bnb", [128, N], BF16); sqp = TT("sqp", [128, N], BF16)
                        atB = TT("atB", [128, N], BF16); btB = TT("btB", [128, N], BF16); ktB = TT("ktB", [128, N], BF16); rtB = TT("rtB", [128, N], BF16)
                        gtB = TT("gtB", [128, N]); bnB = TT("bnB", [128, N]); dcB = TT("dcB", [128, 16])
                        at = TT("at", [128, N], BF16); bt = TT("bt", [128, N], BF16); kt = TT("kt", [128, N], BF16)
                        rt = TT("rt", [128, N], BF16)
                        atz = TT("atz", [128, 2, N], BF16); btz = TT("btz", [128, 2, N], BF16); rtz = TT("rtz", [128, 2, N], BF16)
                        tok = TT("tok", [128, nch, 5, 128], BF16); Vz = TT("Vz", [128, nch, 2, 128], BF16)
                        cm1 = [TT("cm1_%d" % i, [128, 4, 128], BF16) for i in range(nch)]
                        nt2 = [TT("nt2_%d" % i, [128, 2, 128], BF16) for i in range(nch)]
                        cm3 = [TT("cm3_%d" % i, [128, 4, 128], BF16) for i in range(nch)]
                        Tt = [TT("Tt%d" % i, [128, 2, 128], BF16) for i in range(nch)]
                        PQ = [TT("PQ%d" % i, [128, 4, 128], BF16) for i in range(nch)]
                        Xb = TT("Xb", [128, 128], BF16); Wz = TT("Wz", [128, 2, 128], BF16); Wb = TT("Wb", [128, 128], BF16)
                        gtok = TT("gtok", [128, nch, 4]); btok = TT("btok", [128, nch, 4]); sptmp = TT("sptmp", [128, nch, 4])
                        if samp:
                            g_rep, b_rep, Gs, eg, ek, dmi, dms = [[TT(nm_, [128, 128])] for nm_ in ("g_rep", "b_rep", "Gs", "eg", "ek", "dmi", "dms")]
                            segm_t = TT("segm", [128, 1024])
                            k.dma("sp", segm_t[:], consts2_d[:, :], writes=["segm"])
                            segm = segm_t[:, :].rearrange("p (s i) -> p s i", s=NSQ)
                        else:
                            g_rep, b_rep, Gs, eg, ek, dmi, dms = [[tmp[5 + j_][:, q_ * 128:(q_ + 1) * 128] for q_ in range(4)] for j_ in range(7)]
                        dmt_t = TT("dmt", [128, nch, 128]); kbf_t = TT("kbf", [128, nch, 128]); kbb_t = TT("kbb", [128, nch, 128], BF16)
                        dmt = [dmt_t[:, q_, :] for q_ in range(nch)]; kbf = [kbf_t[:, q_, :] for q_ in range(nch)]
                        kbb = [kbb_t[:, q_, :] for q_ in range(nch)]
                        merged = None
                        if samp:
                            Zs = TT("Zs", [128, NSQ, 128]); Zsb = TT("Zsb", [128, NSQ, 128], BF16)
                            bigf = TT("bigf", [128, NSQ, 128]); stg = TT("stg", [128, NSQ, 64])
                            sbd = [TT("sbd%d" % i, [128, NSQ, 128]) for i in range(2)]
                            atm = TT("atm", [128, NSQ, 64], BF16); rtm = TT("rtm", [128, NSQ, 64], BF16)
                            Wm = TT("Wm", [128, NSQ, 128], BF16); Vm = TT("Vm", [128, NSQ, 128], BF16)
                        else:
                            bigf = TT("bigf", [128, 1, 128])

                        ATs = [(at, bt, kt, rt), (atB, btB, ktB, rtB)]
                        GTs = [(tmp[2], tk[2]), (gtB, "gtB")]
                        BNs = [(tmp[11], tk[11]), (bnB, "bnB")]
                        DCs = [dc, dcB]

                        def v3(ap):
                            return ap.rearrange("p (s l) -> p s l", s=nseq)

                        def sg3(ap):
                            return ap.rearrange("p (s l) -> p s l", s=nsg)

                        rms_stats(c0, N, tmp[0], [at, bt], hk, rkey=tk[0], sqkeys=("at", "bt"))
                        for m in range(8):
                            stt("dve", u_t[:, m, :N], h[:, m, c0:c0 + N], col("mix_norm", m), tmp[0][:, :N], ALU.mult, ALU.mult,
                                hk + [tk[0], "cols"], ["u_t"])

                        need_rows = (ti >= 3)
                        M_rows = 64 if samp else 4
                        rows_lo = 0 if samp else 508
                        state = {"g": -1, "issued": set()}

                        def issue_load(g):
                            if g > 22 or g in state["issued"]:
                                return
                            state["issued"].add(g)
                            wt = WI[g % 2]
                            ncols = 256
                            k.dma("sp", wt[:, :, :ncols], winb_d[:, g * 256: g * 256 + ncols].rearrange("(kc p) c -> p kc c", p=128),
                                  reads=[("winb", r_) for r_ in range(8)], writes=[("wi", g % 2)])

                        def load_group(g):
                            wt = WI[g % 2]
                            issue_load(g)
                            if need_rows and g < 15:
                                b = nps()
                                mmg([(ps[b][:M_rows, :256], u_t[:, kc, rows_lo:rows_lo + M_rows], wt[:, kc, :]) for kc in range(8)],
                                    ["u_t", ("wi", g % 2)], [("ps", b)])
                                copy("act", rowst[:M_rows, :], ps[b][:M_rows, :256], [("ps", b)], ["rowst"])
                                if samp:
                                    k.dma("sp", trs_d[:, g * 256:(g + 1) * 256], rowst[:64, :], reads=["rowst"])
                                else:
                                    k.dma("sp", trp_d[:, g * 256:(g + 1) * 256], rowst[:4, :], reads=["rowst"])

                        def pchunk(c):
                            g = c // 2
                            if g != state["g"]:
                                load_group(g)
                                state["g"] = g
                                issue_load(g + 1)
                            wt = WI[g % 2]
                            off = (c % 2) * 128
                            b = nps()
                            mmg([(ps[b][:, :N], wt[:, kc, off:off + 128], u_t[:, kc, :N]) for kc in range(8)],
                                ["u_t", ("wi", g % 2)], [("ps", b)])
                            return b

                        def a_chunk(c, dest, dkey):
                            b = pchunk(c)
                            pab, pak = ((pa_c, "pa_c"), (pa_c2, "pa_c2"))[c % 2]
                            pv = pab[:, :N + nseq].rearrange("p (s l) -> p s l", s=nseq)
                            copy("act", pv[:, :, 1:], v3(ps[b][:, :N]), [("ps", b)], [pak])
                            act(v3(dest), v3(ps[b][:, :N]), AF.Copy, [("ps", b), "omu"], [dkey], scale=omu[:, c:c + 1])
                            if samp:
                                copy("pool", pv[:, :, 0], sshT[:, c, :], ["sshT"], [pak])
                            else:
                                copy("pool", pv[:, :, 0], halo_a[:, c:c + 1], ["halo_a"], [pak])
                                copy("pool", halo_a[:, c:c + 1], pv[:, :, L], [pak], ["halo_a"])
                            stt("dve", v3(dest), pv[:, :, 0:L], col("mu", c), v3(dest), ALU.mult, ALU.add,
                                [pak, "cols", dkey], [dkey])

                        def g_load(hh, role, dest, dk):
                            cq = role * 4 + hh
                            b = pchunk(14 + hh * 4 + role)
                            pab, pak = ((pa_c, "pa_c"), (pa_c2, "pa_c2"))[role % 2]
                            xv3 = pab[:, :N + 3 * nseq].rearrange("p (s l) -> p s l", s=nseq)
                            copy("act", xv3[:, :, 3:], v3(ps[b][:, :N]), [("ps", b)], [pak])
                            if samp:
                                copy("pool", xv3[:, :, 0:3], scvT[:, cq, :].rearrange("p (s i) -> p s i", s=NSQ), ["scvT"], [pak])
                            else:
                                copy("pool", xv3[:, :, 0:3], halo_c[:, cq:cq + 1, :], ["halo_c"], [pak])
                                copy("pool", halo_c[:, cq:cq + 1, :], xv3[:, :, L:L + 3], [pak], ["halo_c"])

                        def g_conv(hh, role, dest, dk):
                            cq = role * 4 + hh
                            pab, pak = ((pa_c, "pa_c"), (pa_c2, "pa_c2"))[role % 2]
                            xv3 = pab[:, :N + 3 * nseq].rearrange("p (s l) -> p s l", s=nseq)
                            ts("dve", v3(dest[:]), xv3[:, :, 0:L], col("conv_w", 0 * 12 + cq), None, ALU.mult, None, [pak, "cols"], [dk])
                            for i in range(1, 4):
                                stt("dve", v3(dest[:]), xv3[:, :, i:i + L], col("conv_w", i * 12 + cq), v3(dest[:]),
                                    ALU.mult, ALU.add, [pak, "cols", dk], [dk])

                        def g_silu(dest, dk):
                            act(dest[:], dest[:], AF.Silu, [dk], [dk])


                        def gdn_prefetch(hh):
                            g_load(hh, 0, r_t, "r_t"); g_load(hh, 1, k_t, "k_t")
                            g_conv(hh, 0, r_t, "r_t")
                            g_load(hh, 2, v_t, "v_t")
                            g_conv(hh, 1, k_t, "k_t")
                            g_silu(r_t, "r_t")
                            g_conv(hh, 2, v_t, "v_t")
                            g_silu(k_t, "k_t"); g_silu(v_t, "v_t")

                        issue_load(0)
                        if samp:
                            srv = srw_d.rearrange("(s h v) k -> h v s k", s=NSQ, h=8)
                            sdv = sdl_d.rearrange("(s h k) v -> h k s v", s=NSQ, h=4)

                            def load_rwkv_state(u_):
                                t_ = sbd[u_ % 2]; tkey = ("sbd", u_ % 2)
                                k.dma("sp", t_[0:64, :, 0:64], srv[2 * u_], writes=[tkey])
                                k.dma("sp", t_[64:128, :, 64:128], srv[2 * u_ + 1], writes=[tkey])

                            def load_gdn_state(h_):
                                k.dma("sp", sbd[h_ % 2][:], sdv[h_], writes=[("sbd", h_ % 2)])

                            for i_ in range(2):
                                k.op("pool", lambda e, i_=i_: e.memset(sbd[i_][:], 0.0), writes=[("sbd", i_)])
                            load_rwkv_state(0)
                        a_chunk(0, tmp[1][:], tk[1])
                        act(lora_in[0:64, :], tmp[1][0:64, :], AF.Tanh, [tk[1]], ["lora_in"])
                        copy("pool", lora_in[64:128, :], tmp[1][64:128, :], [tk[1]], ["lora_in"])
                        a_chunk(1, tmp[1][:], tk[1])
                        act(sig_gd[:], tmp[1][:], AF.Sigmoid, [tk[1]], ["sig_gd"])
                        pre_ops, fin_ops, chunk_ops, gpf_ops = [], [], [], []
                        for u in range(DBG.get('nu', 4)):
                            p_ = u % 2
                            at_, bt_, kt_, rt_ = ATs[p_]
                            kat, kbt, kkt, krt = ["%s%d" % (n_, p_) for n_ in ("at", "bt", "kt", "rt")] if p_ else ["at", "bt", "kt", "rt"]
                            dc_, kdc = DCs[p_], "dc%d" % p_
                            psmode[0] = 2 if u > 0 else 0
                            k.begin()
                            a_chunk(2 + 3 * u, r_t[:], "r_t"); a_chunk(3 + 3 * u, k_t[:], "k_t"); a_chunk(4 + 3 * u, v_t[:], "v_t")
                            if samp and u > 0:
                                load_rwkv_state(u)
                            ucol = slice(u * 128, (u + 1) * 128)
                            sigw, a_t, g_t, kk, kkn, kmod, b_t, cl, e_t, Bh, Kh, bonus = tmp
                            ksw, ka, kg, kkk, kkkn, kkmod, kb_, kcl, ke, kBh, kKh, kbon = tk
                            g_t, kg = GTs[p_]
                            bonus, kbon = BNs[p_]
                            e1, e2, e3 = tmp[3], Bh, Kh
                            ke1, ke2, ke3 = tk[3], kBh, kKh
                            act(sqb[:], k_t[:], AF.Square, ["k_t", "cols"], ["sqb"], scale=col("k_k", u))
                            bw = nps()
                            mm(ps[bw][:, :N], lora_wa[0:64, ucol], lora_in[0:64, :], ["lora_wa", "lora_in"], [("ps", bw)])
                            ba = nps()
                            mm(ps[ba][:, :N], lora_wa[64:128, ucol], lora_in[64:128, :], ["lora_wa", "lora_in"], [("ps", ba)])
                            bg = nps()
                            mm(ps[bg][:, :N], lg2[:, ucol], sig_gd[:], ["lg2", "sig_gd"], [("ps", bg)])
                            bn = nps()
                            mm(ps[bn][:, :N], bones_b, sqb[:], ["sqb", "cb"], [("ps", bn)])
                            act(sigw[:], ps[bw][:, :N], AF.Sigmoid, [("ps", bw), "cols"], [ksw], bias=col("w0", u))
                            act(a_t[:], ps[ba][:, :N], AF.Sigmoid, [("ps", ba), "cols"], [ka], bias=col("a0", u))
                            copy("act", g_t[:], ps[bg][:, :N], [("ps", bg)], [kg])
                            ts("dve", e_t[:], ps[bn][:, :N], 1e-24, None, ALU.max, None, [("ps", bn)], [ke])
                            if samp:
                                k.op("dve", lambda e: e.tensor_tensor_scan(out=cl[:], data0=cc("reset_s", 64), data1=sigw[:], initial=0.0,
                                                                           op0=ALU.mult, op1=ALU.add), reads=[ksw, "cst"], writes=[kcl])
                            else:
                                for q in range(4):
                                    k.op("dve", lambda e, q=q: e.tensor_tensor_scan(
                                        out=cl[:, q * 128:(q + 1) * 128], data0=ones_f, data1=sigw[:, q * 128:(q + 1) * 128], initial=0.0,
                                        op0=ALU.mult, op1=ALU.add), reads=[ksw, "cst"], writes=[kcl])
                            act(e_t[:], e_t[:], AF.Ln, [ke], [ke])
                            act(e_t[:], e_t[:], AF.Exp, [ke], [ke], scale=-0.5)
                            ts("dve", kmod[:], a_t[:], col("k_a", u), omka[:, u:u + 1], ALU.mult, ALU.add, [ka, "cols", "omka"], [kkmod])
                            tt("pool", e1[:], cl[:], sigw[:], ALU.subtract, [kcl, ksw], [ke1])
                            tt("pool", kmod[:], kmod[:], k_t[:], ALU.mult, [kkmod, "k_t"], [kkmod])
                            stt("dve", kkn[:], k_t[:], col("k_k", u), e_t[:], ALU.mult, ALU.mult, ["k_t", "cols", ke], [kkkn])
                            act(e1[:], e1[:], AF.Exp, [ke1], [ke1], scale=-C0)
                            act(e2[:], cl[:], AF.Exp, [kcl], [ke2], scale=C0)
                            act(e3[:], cl[:], AF.Exp, [kcl], [ke3], scale=-C0)
                            tt("pool", b_t[:], kkn[:], a_t[:], ALU.mult, [kkkn, ka], [kb_])
                            stt("dve", bnb[:], r_t[:], col("r_k", u), kmod[:], ALU.mult, ALU.mult, ["r_t", kkmod, "cols"], ["bnb"])
                            bb = nps()
                            mm(ps[bb][:, :N], bones_b, bnb[:], ["bnb", "cb"], [("ps", bb)])
                            stt("dve", at_[:], kkn[:], -1.0, e1[:], ALU.mult, ALU.mult, [kkkn, ke1], [kat])
                            cl3 = sg3(cl[:])
                            tt("pool", sg3(e_t[:]), cl3[:, :, lsg - 1:lsg].to_broadcast([128, nsg, lsg]), cl3, ALU.subtract, [kcl, kkkn], [ke])
                            tt("dve", kt_[:], kmod[:], e2[:], ALU.mult, [kkmod, ke2], [kkt])
                            tt("pool", bt_[:], b_t[:], e2[:], ALU.mult, [kb_, ke2], [kbt])
                            tt("dve", rt_[:], r_t[:], e3[:], ALU.mult, ["r_t", ke3], [krt])
                            act(e_t[:], e_t[:], AF.Exp, [ke], [ke], scale=-C0)
                            act(dc_[:, :nsg], cl3[:, :, lsg - 1], AF.Exp, [kcl], [kdc], scale=-C0)
                            tt("dve", bonus[:], ps[bb][:, :N], v_t[:], ALU.mult, [("ps", bb), "v_t"], [kbon])
                            tt("pool", Bh[:], b_t[:], e_t[:], ALU.mult, [kb_, ke, kbt, kkt], [kBh])
                            tt("dve", Kh[:], kmod[:], e_t[:], ALU.mult, [kkmod, ke, krt], [kKh])
                            pre_ops.append(k.end())
                            psmode[0] = 1
                            k.begin()
                            for hh_ in range(2):
                                lo = hh_ * 64
                                for src, skey_, dst, kn in [(at_, kat, atz, "atz"), (bt_, kbt, btz, "btz"), (rt_, krt, rtz, "rtz")]:
                                    if hh_ == 0:
                                        act(dst[:, hh_, :], src[:, :], AF.Copy, [skey_, "cst"], [kn], scale=bones_f[:, lo:lo + 1])
                                    else:
                                        ts("dve", dst[:, hh_, :], src[:, :], bones_f[:, lo:lo + 1], None, ALU.mult, None, [skey_, "cst"], [kn])
                            for ci in range(nch):
                                cs = slice(ci * C, (ci + 1) * C)
                                b = nps()
                                tr(ps[b][:C, 0:128], Bh[:, cs], 128, [kBh], [("ps", b)])
                                tr(ps[b][:C, 128:256], Kh[:, cs], 128, [kKh], [("ps", b)])
                                tr(ps[b][:C, 256:384], v_t[:, cs], 128, ["v_t"], [("ps", b)])
                                hm2 = cc("hmask", 256)[:C, :].rearrange("p (a c) -> p a c", a=2)
                                tt("dve", tok[:C, ci, 0:2, :], ps[b][:C, 0:128].unsqueeze(1).to_broadcast([C, 2, 128]), hm2, ALU.mult,
                                   [("ps", b), "cst"], ["tok"])
                                tt("dve", tok[:C, ci, 2:4, :], ps[b][:C, 128:256].unsqueeze(1).to_broadcast([C, 2, 128]), hm2, ALU.mult,
                                   [("ps", b), "cst"], ["tok"])
                                copy("act", tok[:C, ci, 4, :], ps[b][:C, 256:384], [("ps", b)], ["tok"])
                                tt("pool", Vz[:C, ci, :, :], tok[:C, ci, 4:5, :].to_broadcast([C, 2, 128]), hm2, ALU.mult, ["tok", "cst"], ["Vz"])
                            fin_ops.append(k.end())
                            if u == 3 and DBG.get('nh', 4) > 0:
                                psmode[0] = 2
                                k.begin(); gdn_prefetch(0); gpf_ops = k.end()
                                psmode[0] = 1
                            k.begin()
                            if samp:
                                for q in range(4):
                                    b = nps()
                                    for s4 in range(4):
                                        s = q * 4 + s4
                                        tr(ps[b][:, s4 * 128:(s4 + 1) * 128], sbd[u % 2][:, s, :], 128, [("sbd", u % 2)], [("ps", b)])
                                    copy("act", Zs[:, q * 4:(q + 1) * 4, :].rearrange("p a c -> p (a c)"), ps[b][:, :], [("ps", b)], ["Zs"])
                                copy("act", Zsb[:], Zs[:], ["Zs"], ["Zsb"])
                                Zf, zk = Zs, "Zs"
                                Zrd = lambda ci: [Zsb[:, s_, :] for s_ in range(NSQ)]
                                zrk = lambda ci: "Zsb"
                                tt("pool", atm[:], at_[:, :].unsqueeze(1).to_broadcast([128, NSQ, 64]),
                                   segm, ALU.mult, [kat, "segm"], ["atm"])
                                tt("pool", rtm[:], rt_[:, :].unsqueeze(1).to_broadcast([128, NSQ, 64]),
                                   segm, ALU.mult, [krt, "segm"], ["rtm"])
                            else:
                                Zf, zk = Zr[:, u:u + 1, :], ("Zr", u)
                                Zrd = lambda ci: [Zrb[:, u, ci % 2, :]]
                                zrk = lambda ci: ("Zrb", u, ci % 2)
                            pv4 = lambda b_: ps[b_][:C, :].rearrange("p (a c) -> p a c", a=4)[:, :, :C]
                            for ci in range(nch):
                                cs = slice(ci * C, (ci + 1) * C)
                                b1 = nps()
                                for hh_ in range(2):
                                    mm(ps[b1][:C, hh_ * 128: hh_ * 128 + C], bt_[:, cs], atz[:, hh_, cs], [kbt, "atz"], [("ps", b1)])
                                    mm(ps[b1][:C, (2 + hh_) * 128: (2 + hh_) * 128 + C], kt_[:, cs], atz[:, hh_, cs], [kkt, "atz"], [("ps", b1)])
                                b2 = nps()
                                for hh_ in range(2):
                                    mm(ps[b2][:C, hh_ * 128: hh_ * 128 + C], at_[:, cs], btz[:, hh_, cs], [kat, "btz"], [("ps", b2)])
                                b3 = nps()
                                for hh_ in range(2):
                                    mm(ps[b3][:C, hh_ * 128: hh_ * 128 + C], bt_[:, cs], rtz[:, hh_, cs], [kbt, "rtz"], [("ps", b3)])
                                    mm(ps[b3][:C, (2 + hh_) * 128: (2 + hh_) * 128 + C], kt_[:, cs], rtz[:, hh_, cs], [kkt, "rtz"], [("ps", b3)])
                                tt("dve", cm1[ci][:C, :, :C], pv4(b1), mS.unsqueeze(1).to_broadcast([C, 4, C]), ALU.mult, [("ps", b1), "cst"], [("cm1", ci)])
                                tt("dve", nt2[ci][:C, :, :C], pv4(b2)[:, 0:2, :], mT.unsqueeze(1).to_broadcast([C, 2, C]), ALU.mult, [("ps", b2), "cst"], [("nt2", ci)])
                                tt("dve", cm3[ci][:C, :, :C], pv4(b3), mI.unsqueeze(1).to_broadcast([C, 4, C]), ALU.mult, [("ps", b3), "cst"], [("cm3", ci)])
                                tt("pool", Tt[ci][:C, :, :C], cm1[ci][:C, 0:2, :C], ident_b[:C, :C].unsqueeze(1).to_broadcast([C, 2, C]), ALU.add,
                                   [("cm1", ci), "cb"], [("Tt", ci)])
                            Pm = [[cm1[ci][:C, 0, :C], cm1[ci][:C, 1, :C]] for ci in range(nch)]
                            Qm = [[nt2[ci][:C, 0, :C], nt2[ci][:C, 1, :C]] for ci in range(nch)]
                            pqk = [[("cm1", ci), ("nt2", ci)] for ci in range(nch)]
                            for lev in range(nlev):
                                bs = []
                                for ci in range(nch):
                                    b = nps(); bs.append(b)
                                    for hh_ in range(2):
                                        if lev < nlev - 1:
                                            mm(ps[b][:C, hh_ * 128: hh_ * 128 + C], Qm[ci][hh_], Pm[ci][hh_], pqk[ci], [("ps", b)])
                                        mm(ps[b][:C, (2 + hh_) * 128: (2 + hh_) * 128 + C], Pm[ci][hh_], Qm[ci][hh_], pqk[ci], [("ps", b)])
                                for ci in range(nch):
                                    b = bs[ci]; pq = PQ[ci]
                                    if lev < nlev - 1:
                                        copy("act" if ci != 3 else "dve", pq[:C, :, :C], pv4(b), [("ps", b)], [("PQ", ci)])
                                    else:
                                        copy("act" if ci != 3 else "dve", pq[:C, 2:4, :C], pv4(b)[:, 2:4, :], [("ps", b)], [("PQ", ci)])
                                    Pm[ci] = [pq[:C, 0, :C], pq[:C, 1, :C]]; Qm[ci] = [pq[:C, 2, :C], pq[:C, 3, :C]]
                                    pqk[ci] = [("PQ", ci)]
                                bs = []
                                for ci in range(nch):
                                    b = nps(); bs.append(b)
                                    for hh_ in range(2):
                                        mm(ps[b][:C, hh_ * 128: hh_ * 128 + C], Qm[ci][hh_], Tt[ci][:C, hh_, :C], [("PQ", ci), ("Tt", ci)], [("ps", b)])
                                for ci in range(nch):
                                    b = bs[ci]
                                    tt("dve", Tt[ci][:C, :, :C], pv4(b)[:, 0:2, :], Tt[ci][:C, :, :C], ALU.add,
                                       [("ps", b), ("Tt", ci)], [("Tt", ci)])
                            for ci in range(nch):
                                cs = slice(ci * C, (ci + 1) * C)
                                b = nps()
                                items = []
                                for s in range(nS):
                                    items.append((ps[b][:C, 0:128], (atm[:, s, :] if samp else at_[:, cs]), Zrd(ci)[s]))
                                for hh_ in range(2):
                                    items.append((ps[b][:C, 0:128], cm1[ci][:C, 2 + hh_, :C], Vz[:C, ci, hh_, :]))
                                mmg(items, [kat, "atm", zrk(ci), ("cm1", ci), "Vz"] if samp else [kat, zrk(ci), ("cm1", ci), "Vz"], [("ps", b)])
                                copy("act", Xb[:C, :], ps[b][:C, 0:128], [("ps", b)], ["Xb"])
                                b = nps()
                                for hh_ in range(2):
                                    mm(ps[b][:C, hh_ * 128:(hh_ + 1) * 128], Tt[ci][:C, hh_, :C], Xb[:C, :], [("Tt", ci), "Xb"], [("ps", b)])
                                tt("dve", Wz[:C, :, :], ps[b][:C, 0:256].rearrange("p (a c) -> p a c", a=2),
                                   cc("hmask", 256)[:C, :].rearrange("p (a c) -> p a c", a=2), ALU.mult, [("ps", b), "cst"], ["Wz"])
                                if samp:
                                    tt("pool", Wb[:C, :], Wz[:C, 0, :], Wz[:C, 1, :], ALU.add, ["Wz"], ["Wb"])
                                    tt("dve", Wm[:C, :, :], Wb[:C, :].unsqueeze(1).to_broadcast([C, NSQ, 128]),
                                       cc("rowseg", 16)[:C, :].unsqueeze(2).to_broadcast([C, NSQ, 128]), ALU.mult, ["Wb", "cst"], ["Wm"])
                                    tt("dve", Vm[:C, :, :], tok[:C, ci, 4:5, :].to_broadcast([C, NSQ, 128]),
                                       cc("rowseg", 16)[:C, :].unsqueeze(2).to_broadcast([C, NSQ, 128]), ALU.mult, ["tok", "cst"], ["Vm"])
                                    for q in range(4):
                                        b = nps()
                                        mmg([(ps[b][:, :], tok[:C, ci, 0, :], Wm[:C, q * 4:(q + 1) * 4, :].rearrange("p a c -> p (a c)")),
                                             (ps[b][:, :], tok[:C, ci, 1, :], Wm[:C, q * 4:(q + 1) * 4, :].rearrange("p a c -> p (a c)")),
                                             (ps[b][:, :], tok[:C, ci, 2, :], Vm[:C, q * 4:(q + 1) * 4, :].rearrange("p a c -> p (a c)")),
                                             (ps[b][:, :], tok[:C, ci, 3, :], Vm[:C, q * 4:(q + 1) * 4, :].rearrange("p a c -> p (a c)"))],
                                            ["tok", "Wm", "Vm"], [("ps", b)])
                                        tt("dve", bigf[:, q * 4:(q + 1) * 4, :], ps[b][:, :].rearrange("p (a c) -> p a c", a=4),
                                           bones_f.unsqueeze(1).to_broadcast([128, 4, 128]), ALU.mult, [("ps", b), "cst"], ["bigf"])
                                    tt("dve", Zs[:], Zs[:], dc_[:, :NSQ].unsqueeze(2).to_broadcast([128, NSQ, 128]), ALU.mult, ["Zs", kdc], ["Zs"])
                                    tt("dve", Zs[:], Zs[:], bigf[:], ALU.add, ["Zs", "bigf"], ["Zs"])
                                else:
                                    bz = nps()
                                    mmg([(ps[bz][:, 0:128], tok[:, ci, 0, :], Wz[:, 0, :]), (ps[bz][:, 0:128], tok[:, ci, 1, :], Wz[:, 1, :]),
                                         (ps[bz][:, 0:128], tok[:, ci, 2, :], Vz[:, ci, 0, :]), (ps[bz][:, 0:128], tok[:, ci, 3, :], Vz[:, ci, 1, :])],
                                        ["tok", "Wz", "Vz"], [("ps", bz)])
                                    stt("dve", Zrb[:, u, (ci + 1) % 2, :], Zf[:, 0, :], dc_[:, ci:ci + 1], ps[bz][:, 0:128], ALU.mult, ALU.add,
                                        [zk, kdc, ("ps", bz)], [("Zrb", u, (ci + 1) % 2)])
                                    stt("dve", Zf[:, 0, :], Zf[:, 0, :], dc_[:, ci:ci + 1], ps[bz][:, 0:128], ALU.mult, ALU.add, [zk, kdc, ("ps", bz)], [zk])
                                b = nps()
                                items = []
                                for s in range(nS):
                                    items.append((ps[b][:, :C], Zrd(ci)[s], (rtm[:, s, :] if samp else rt_[:, cs])))
                                for hh_ in range(2):
                                    items.append((ps[b][:, :C], Wz[:C, hh_, :], cm3[ci][:C, hh_, :C]))
                                    items.append((ps[b][:, :C], Vz[:C, ci, hh_, :], cm3[ci][:C, 2 + hh_, :C]))
                                mmg(items, [zrk(ci), krt, "rtm", "Wz", ("cm3", ci), "Vz"] if samp else [zrk(ci), krt, "Wz", ("cm3", ci), "Vz"], [("ps", b)])
                                copy("act", o_t[:, cs], ps[b][:, :C], [("ps", b)], ["o_t"])

                            if samp:
                                for q in range(4):
                                    b = nps()
                                    for s4 in range(4):
                                        tr(ps[b][:, s4 * 128:(s4 + 1) * 128], Zs[:, q * 4 + s4, :], 128, ["Zs"], [("ps", b)])
                                    pvq = ps[b][:, :].rearrange("p (a c) -> p a c", a=4)
                                    copy("act", stg[0:64, q * 4:(q + 1) * 4, :], pvq[0:64, :, 0:64], [("ps", b)], ["stg"])
                                    copy("act", stg[64:128, q * 4:(q + 1) * 4, :], pvq[64:128, :, 64:128], [("ps", b)], ["stg"])
                                rwv = rws_d.rearrange("(s h v) k -> h v s k", s=NSQ, h=8)
                                k.dma("sp", rwv[2 * u], stg[0:64, :, :], reads=["stg"])
                                k.dma("sp", rwv[2 * u + 1], stg[64:128, :, :], reads=["stg"])
                            elif ti == 3:
                                b = nps()
                                tr(ps[b][:, 0:128], Zr[:, u, :], 128, [("Zr", u)], [("ps", b)])
                                copy("act", bigf[0:64, 0, 0:64], ps[b][0:64, 0:64], [("ps", b)], ["bigf"])
                                copy("act", bigf[64:128, 0, 0:64], ps[b][64:128, 64:128], [("ps", b)], ["bigf"])
                                k.dma("sp", rwp_d[u * 128:(u + 1) * 128, :], bigf[:, 0, 0:64], reads=["bigf"])
                            m1, var_, o2 = [t_[:].rearrange("p a n -> p (a n)").bitcast(F32) for t_ in (atz, btz, rtz)]
                            km1, kvar, ko2 = "atz", "btz", "rtz"
                            b = nps()
                            mm(ps[b][:, :N], bones_f, o_t[:, :N], ["o_t", "cst"], [("ps", b)])
                            stt("dve", o2, ps[b][:, :N], -1.0 / 64, o_t[:, :N], ALU.mult, ALU.add, [("ps", b), "o_t"], [ko2])
                            act(sqp[:], o2, AF.Square, [ko2], ["sqp"])
                            b = nps()
                            mm(ps[b][:, :N], bones_b, sqp[:], ["sqp", "cb"], [("ps", b)])
                            act(var_, ps[b][:, :N], AF.Ln, [("ps", b)], [kvar], bias=64e-5, scale=1.0 / 64)
                            act(var_, var_, AF.Exp, [kvar], [kvar], scale=-0.5)
                            tt("pool", o2, o2, var_, ALU.mult, [ko2, kvar], [ko2])
                            ts("dve", o2, o2, col("lnx_w", u), col("lnx_b", u), ALU.mult, ALU.add, [ko2, "cols"], [ko2])
                            tt("pool", o2, o2, bonus[:], ALU.add, [ko2, kbon], [ko2])
                            tt("dve", oab[:, u, :N], o2, g_t[:], ALU.mult, [ko2, kg], [("oab", u)])
                            chunk_ops.append(k.end())
                        psmode[0] = 0
                        nu_ = len(pre_ops)
                        if nu_:
                            k.emit(pre_ops[0]); k.emit(fin_ops[0])
                            for u in range(nu_):
                                nxt = pre_ops[u + 1] if u + 1 < nu_ else gpf_ops
                                k.emit_interleaved(chunk_ops[u], nxt)
                                if u + 1 < nu_:
                                    k.emit(fin_ops[u + 1])

                        k.barrier()
                        for ci in range(nch):
                            b = nps()
                            mmg([(ps[b][:C, 0:8], u_t[:, kc, ci * C:(ci + 1) * C], wab[:, kc, :]) for kc in range(8)], ["u_t", "wab"], [("ps", b)])
                            act(btok[:C, ci, :], ps[b][:C, 4:8], AF.Sigmoid, [("ps", b)], ["btok"])
                            tt("dve", sptmp[:C, ci, :], ps[b][:C, 0:4], dtb[:C, :], ALU.add, [("ps", b), "dtb"], ["sptmp"])
                        act(sptmp[:C, :, :], sptmp[:C, :, :], AF.Exp, ["sptmp"], ["sptmp"])
                        act(sptmp[:C, :, :], sptmp[:C, :, :], AF.Ln, ["sptmp"], ["sptmp"], bias=1.0, scale=1.0)
                        stt("dve", gtok[:C, :, :], sptmp[:C, :, :], -1.0, eA[:C, :].unsqueeze(1).to_broadcast([C, nch, 4]), ALU.mult, ALU.mult,
                            ["sptmp", "eA"], ["gtok"])
                        for hh in range(DBG.get('nh', 4)):
                            xq, xk, xv = r_t, k_t, v_t
                            if hh == 0 and DBG.get('nu', 4) < 4:
                                gdn_prefetch(0)
                            bz_ = pchunk(14 + hh * 4 + 3)
                            act(z_t[:], ps[bz_][:, :N], AF.Silu, [("ps", bz_)], ["z_t"])
                            if DBG.get("gstop", 99) <= 1:
                                continue
                            qn, kn, qnb, knb_ = tmp[1], tmp[2], at, bt
                            for src, skey, dst, dkey, scl, sc_, sck, sb_, sbk in [(xq, "r_t", qn, tk[1], 128.0 ** -0.5, tmp[3], tk[3], sqb, "sqb"),
                                                                                    (xk, "k_t", kn, tk[2], 1.0, tmp[0], tk[0], bnb, "bnb")]:
                                act(sb_[:], src[:], AF.Square, [skey], [sbk])
                                b = nps()
                                mm(ps[b][:, :N], ones_b, sb_[:], [sbk, "cb"], [("ps", b)])
                                act(sc_[:], ps[b][:, :N], AF.Ln, [("ps", b)], [sck], bias=1e-6, scale=1.0)
                                act(sc_[:], sc_[:], AF.Exp, [sck], [sck], scale=-0.5)
                                stt("dve", dst[:], src[:], scl, sc_[:], ALU.mult, ALU.mult, [skey, sck], [dkey])
                            if DBG.get("gstop", 99) <= 2:
                                continue
                            copy("act", qnb[:], qn[:], [tk[1]], ["at"])
                            copy("dve", knb_[:], kn[:], [tk[2]], ["bt"])
                            qt_, Khf = kt, tmp[4]
                            atg = rt
                            if samp:
                                if hh == 0:
                                    load_gdn_state(0)
                                ZS, ZSK = sbd[hh % 2], ("sbd", hh % 2)
                                copy("act", Zsb[:], ZS[:], [ZSK], ["Zsb"])
                                Zf, zk = ZS, ZSK
                                Zrd = lambda ci: [Zsb[:, s_, :] for s_ in range(NSQ)]
                                zrk = lambda ci: "Zsb"
                            else:
                                Zf, zk = Zg[:, hh:hh + 1, :], ("Zg", hh)
                                Zrd = lambda ci, hh=hh: [Zgb[:, hh, ci % 2, :]]
                                zrk = lambda ci, hh=hh: ("Zgb", hh, ci % 2)
                            bAs, bBs = [], []
                            for ci in range(nch):
                                copy("pool", g_rep[ci][:C, :], gtok[:C, ci, hh:hh + 1].to_broadcast([C, 128]), ["gtok"], [("g_rep", ci)])
                                copy("pool", b_rep[ci][:C, :], btok[:C, ci, hh:hh + 1].to_broadcast([C, 128]), ["btok"], [("b_rep", ci)])
                                ts("dve", Gs[ci][:C, :C], mT, gtok[:C, ci, hh:hh + 1], None, ALU.mult, None, ["cst", "gtok"], [("Gs", ci)])
                            for ci in range(nch):
                                bA = nps(); bAs.append(bA)
                                mm(ps[bA][:, 0:C], g_rep[ci][:C, :], mI, [("g_rep", ci), "cst"], [("ps", bA)])
                                mm(ps[bA][:, 128:128 + C], g_rep[ci][:C, :], mT, [("g_rep", ci), "cst"], [("ps", bA)])
                                mm(ps[bA][:, 256:256 + C], b_rep[ci][:C, :], ident_f[:C, :C], [("b_rep", ci), "cst"], [("ps", bA)])
                                mm(ps[bA][:C, 384:384 + C], Gs[ci][:C, :C], mI, [("Gs", ci), "cst"], [("ps", bA)])
                                bB = nps(); bBs.append(bB)
                                mm(ps[bB][:C, 0:C], mI, Gs[ci][:C, :C], [("Gs", ci), "cst"], [("ps", bB)])
                            for ci in range(nch):
                                cs = slice(ci * C, (ci + 1) * C)
                                bA, bB = bAs[ci], bBs[ci]
                                act(eg[ci][:, :C], ps[bA][:, 0:C], AF.Exp, [("ps", bA)], [("eg", ci)])
                                act(ek[ci][:, :C], ps[bA][:, 128:128 + C], AF.Exp, [("ps", bA)], [("ek", ci)])
                                act(dmi[ci][:C, :C], ps[bA][:C, 384:384 + C], AF.Exp, [("ps", bA)], [("dmi", ci)])
                                tt("dve", kbf[ci][:, :C], kn[:, cs], ps[bA][:, 256:256 + C], ALU.mult, [tk[2], ("ps", bA)], [("kbf", ci)])
                                act(dmt[ci][:C, :C], ps[bB][:C, 0:C], AF.Exp, [("ps", bB)], [("dmt", ci)])
                                tt("pool", dms[ci][:C, :C], dmi[ci][:C, :C], mS, ALU.mult, [("dmi", ci), "cst"], [("dms", ci)])
                                tt("pool", dmi[ci][:C, :C], dmi[ci][:C, :C], mI, ALU.mult, [("dmi", ci), "cst"], [("dmi", ci)])
                                tt("pool", dmt[ci][:C, :C], dmt[ci][:C, :C], mT, ALU.mult, [("dmt", ci), "cst"], [("dmt", ci)])
                                copy("pool", kbb[ci][:, :C], kbf[ci][:, :C], [("kbf", ci)], [("kbb", ci)])
                                stt("dve", atg[:, cs], kbf[ci][:, :C], -1.0, eg[ci][:, :C], ALU.mult, ALU.mult, [("kbf", ci), ("eg", ci)], [("rt", ci)])
                                tt("pool", qt_[:, cs], qn[:, cs], eg[ci][:, :C], ALU.mult, [tk[1], ("eg", ci)], [("kt", ci)])
                                tt("pool", Khf[:, cs], kn[:, cs], ek[ci][:, :C], ALU.mult, [tk[2], ("ek", ci)], [(tk[4], ci)])
                            for ci in range(nch):
                                cs = slice(ci * C, (ci + 1) * C)
                                b = nps()
                                tr(ps[b][:C, 0:128], Khf[:, cs], 128, [(tk[4], ci)], [("ps", b)])
                                tr(ps[b][:C, 128:256], xv[:, cs], 128, ["v_t"], [("ps", b)])
                                copy("act", tok[:C, ci, 3:5, :].rearrange("p a c -> p (a c)"), ps[b][:C, 0:256], [("ps", b)], [("tok", ci)])
                            if hh + 1 < DBG.get('nh', 4):
                                gdn_prefetch(hh + 1)
                                if samp:
                                    load_gdn_state(hh + 1)
                            for ci in range(nch):
                                cs = slice(ci * C, (ci + 1) * C)
                                b1 = nps()
                                mm(ps[b1][:C, 0:C], knb_[:, cs], kbb[ci][:, :C], ["bt", ("kbb", ci)], [("ps", b1)])
                                mm(ps[b1][:C, 128:128 + C], kbb[ci][:, :C], knb_[:, cs], ["bt", ("kbb", ci)], [("ps", b1)])
                                mm(ps[b1][:C, 256:256 + C], knb_[:, cs], qnb[:, cs], ["bt", "at"], [("ps", b1)])
                                stt("dve", cm1[ci][:C, 0, :C], ps[b1][:C, 0:C], -1.0, dms[ci][:C, :C], ALU.mult, ALU.mult, [("ps", b1), ("dms", ci)], [("cm1", ci)])
                                stt("dve", nt2[ci][:C, 0, :C], ps[b1][:C, 128:128 + C], -1.0, dmt[ci][:C, :C], ALU.mult, ALU.mult, [("ps", b1), ("dmt", ci)], [("nt2", ci)])
                                tt("dve", cm3[ci][:C, 0, :C], ps[b1][:C, 256:256 + C], dmi[ci][:C, :C], ALU.mult, [("ps", b1), ("dmi", ci)], [("cm3", ci)])
                                tt("pool", Tt[ci][:C, 0, :C], cm1[ci][:C, 0, :C], ident_b[:C, :C], ALU.add, [("cm1", ci), "cb"], [("Tt", ci)])
                            Pm = [cm1[ci][:C, 0, :C] for ci in range(nch)]
                            Qm = [nt2[ci][:C, 0, :C] for ci in range(nch)]
                            pqk = [[("cm1", ci), ("nt2", ci)] for ci in range(nch)]
                            for lev in range(nlev):
                                bs = []
                                for ci in range(nch):
                                    b = nps(); bs.append(b)
                                    if lev < nlev - 1:
                                        mm(ps[b][:C, 0:C], Qm[ci], Pm[ci], pqk[ci], [("ps", b)])
                                    mm(ps[b][:C, 128:128 + C], Pm[ci], Qm[ci], pqk[ci], [("ps", b)])
                                for ci in range(nch):
                                    b = bs[ci]; pq = PQ[ci]
                                    pvv = ps[b][:C, 0:256].rearrange("p (a c) -> p a c", a=2)[:, :, :C]
                                    if lev < nlev - 1:
                                        copy("act" if ci != 3 else "dve", pq[:C, 0:2, :C], pvv, [("ps", b)], [("PQ", ci)])
                                    else:
                                        copy("act" if ci != 3 else "dve", pq[:C, 1, :C], ps[b][:C, 128:128 + C], [("ps", b)], [("PQ", ci)])
                                    Pm[ci], Qm[ci] = pq[:C, 0, :C], pq[:C, 1, :C]
                                    pqk[ci] = [("PQ", ci)]
                                bs = []
                                for ci in range(nch):
                                    b = nps(); bs.append(b)
                                    mm(ps[b][:C, 0:C], Qm[ci], Tt[ci][:C, 0, :C], [("PQ", ci), ("Tt", ci)], [("ps", b)])
                                for ci in range(nch):
                                    b = bs[ci]
                                    tt("dve", Tt[ci][:C, 0, :C], ps[b][:C, 0:C], Tt[ci][:C, 0, :C], ALU.add, [("ps", b), ("Tt", ci)], [("Tt", ci)])
                            if samp:
                                tt("pool", atm[:], atg[:, :].unsqueeze(1).to_broadcast([128, NSQ, 64]),
                                   segm, ALU.mult, [("rt", 0), "segm"], ["atm"])
                                tt("pool", rtm[:], qt_[:, :].unsqueeze(1).to_broadcast([128, NSQ, 64]),
                                   segm, ALU.mult, [("kt", 0), "segm"], ["rtm"])
                            for ci in range(nch):
                                cs = slice(ci * C, (ci + 1) * C)
                                zr = Zrd(ci); zrkey = zrk(ci)
                                b = nps()
                                mmg([(ps[b][:C, 0:128], (atm[:, s, :] if samp else atg[:, cs]), zr[s]) for s in range(nS)],
                                    [("rt", ci), "atm", zrkey] if samp else [("rt", ci), zrkey], [("ps", b)])
                                stt("dve", Xb[:C, :], tok[:C, ci, 4, :], btok[:C, ci, hh:hh + 1], ps[b][:C, 0:128], ALU.mult, ALU.add,
                                    [("tok", ci), "btok", ("ps", b)], ["Xb"])
                                b = nps()
                                mm(ps[b][:C, 0:128], Tt[ci][:C, 0, :C], Xb[:C, :], [("Tt", ci), "Xb"], [("ps", b)])
                                copy("act", Wb[:C, :], ps[b][:C, 0:128], [("ps", b)], ["Wb"])
                                if samp:
                                    tt("dve", Wm[:C, :, :], Wb[:C, :].unsqueeze(1).to_broadcast([C, NSQ, 128]),
                                       cc("rowseg", 16)[:C, :].unsqueeze(2).to_broadcast([C, NSQ, 128]), ALU.mult, ["Wb", "cst"], ["Wm"])
                                    egl = eg[ci][:, :C].rearrange("p (s l) -> p s l", s=NSQ)[:, :, LS - 1:LS]
                                    tt("dve", ZS[:], ZS[:], egl.to_broadcast([128, NSQ, 128]), ALU.mult, [ZSK, ("eg", ci)], [ZSK])
                                    for q in range(4):
                                        b = nps()
                                        mm(ps[b][:, :], tok[:C, ci, 3, :], Wm[:C, q * 4:(q + 1) * 4, :].rearrange("p a c -> p (a c)"), [("tok", ci), "Wm"], [("ps", b)])
                                        tt("dve", ZS[:, q * 4:(q + 1) * 4, :], ZS[:, q * 4:(q + 1) * 4, :], ps[b][:, :].rearrange("p (a c) -> p a c", a=4),
                                           ALU.add, [ZSK, ("ps", b)], [ZSK])
                                else:
                                    bz = nps()
                                    mm(ps[bz][:, 0:128], tok[:, ci, 3, :], Wb[:, :], [("tok", ci), "Wb"], [("ps", bz)])
                                    stt("dve", Zgb[:, hh, (ci + 1) % 2, :], Zf[:, 0, :], eg[ci][:, C - 1:C], ps[bz][:, 0:128], ALU.mult, ALU.add,
                                        [zk, ("eg", ci), ("ps", bz)], [("Zgb", hh, (ci + 1) % 2)])
                                    stt("dve", Zf[:, 0, :], Zf[:, 0, :], eg[ci][:, C - 1:C], ps[bz][:, 0:128], ALU.mult, ALU.add,
                                        [zk, ("eg", ci), ("ps", bz)], [zk])
                                b = nps()
                                items = [(ps[b][:, :C], zr[s], (rtm[:, s, :] if samp else qt_[:, cs])) for s in range(nS)]
                                items.append((ps[b][:, :C], Wb[:C, :], cm3[ci][:C, 0, :C]))
                                mmg(items, [zrkey, ("kt", ci), "rtm", "Wb", ("cm3", ci)] if samp else [zrkey, ("kt", ci), "Wb", ("cm3", ci)], [("ps", b)])
                                copy("act", o_t[:, cs], ps[b][:, :C], [("ps", b)], ["o_t"])

                            if samp:
                                k.dma("sp", dls_d.rearrange("(s h k) v -> h k s v", s=NSQ, h=4)[hh], ZS[:], reads=[ZSK])
                            elif ti == 3:
                                k.dma("sp", dlp_d[hh * 128:(hh + 1) * 128, :], Zg[:, hh, :], reads=[("Zg", hh)])
                            o2 = tmp[3]
                            tt("pool", sqb[:], o_t[:, :N], o_t[:, :N], ALU.mult, ["o_t"], ["sqb"])
                            b = nps()
                            mm(ps[b][:, :N], ones_b, sqb[:], ["sqb", "cb"], [("ps", b)])
                            act(o2[:], ps[b][:, :N], AF.Ln, [("ps", b)], [tk[3]], bias=1e-6, scale=1.0 / 128)
                            act(o2[:], o2[:], AF.Exp, [tk[3]], [tk[3]], scale=-0.5)
                            stt("dve", o2[:], o_t[:, :N], col("norm_w", 0), o2[:], ALU.mult, ALU.mult, ["o_t", "cols", tk[3]], [tk[3]])
                            tt("dve", oab[:, 4 + hh, :N], o2[:], z_t[:], ALU.mult, [tk[3], "z_t"], [("oab", 4 + hh)])

                        mrg = [at, bt, kt, rt, atz[:, 0, :], atz[:, 1, :], btz[:, 0, :], btz[:, 1, :]]
                        mrk = ["at", "bt", "kt", "rt", "atz", "atz", "btz", "btz"]
                        okeys = [("oab", j) for j in range(8)]
                        mrx = {2: [("kt", ci_) for ci_ in range(nch)], 3: [("rt", ci_) for ci_ in range(nch)]}
                        for m in range(DBG.get('ng', 8)):
                            pj = pjbuf[m % 2]
                            k.dma("sp", pj[:, 0, :, :], pjab_d[:, m * 128:(m + 1) * 128].rearrange("(kc p) c -> p kc c", p=128), reads=["pjab"], writes=[("pj", m % 2)])
                            k.dma("sp", pj[:, 1, :, :], pjbb_d[:, m * 128:(m + 1) * 128].rearrange("(kc p) c -> p kc c", p=128), reads=["pjbb"], writes=[("pj", m % 2)])
                            bga = pchunk(30 + 2 * m)
                            act(tmp[0][:], ps[bga][:, :N], AF.Sigmoid, [("ps", bga)], [tk[0]])
                            b = nps()
                            mmg([(ps[b][:, :N], pj[:, 0, kc, :], oab[:, kc, :N]) for kc in range(4)], [("pj", m % 2)] + okeys, [("ps", b)])
                            tt("dve", tmp[1][:], tmp[0][:], ps[b][:, :N], ALU.mult, [tk[0], ("ps", b)], [tk[1]])
                            bgb = pchunk(31 + 2 * m)
                            act(tmp[0][:], ps[bgb][:, :N], AF.Sigmoid, [("ps", bgb)], [tk[0]])
                            b = nps()
                            mmg([(ps[b][:, :N], pj[:, 1, kc, :], oab[:, 4 + kc, :N]) for kc in range(4)], [("pj", m % 2)] + okeys, [("ps", b)])
                            tt("dve", tmp[2][:], tmp[0][:], ps[b][:, :N], ALU.mult, [tk[0], ("ps", b)], [tk[2]])
                            tt("pool", mrg[m][:, :N] if m < 4 else mrg[m], tmp[1][:], tmp[2][:], ALU.add, [tk[1], tk[2]], [mrk[m], ("mrg", m)] + mrx.get(m, []))
                        for half in range(4):
                            wt = WI[half % 2]
                            k.dma("sp", wt[:, :, :], wob_d[:, half * 256:(half + 1) * 256].rearrange("(kc p) c -> p kc c", p=128),
                                  reads=[("wob", r_) for r_ in range(2)], writes=[("wi", half % 2)])
                            for m4 in range(2):
                                m = half * 2 + m4
                                b = nps()
                                mmg([(ps[b][:, :N], wt[:, kc, m4 * 128:(m4 + 1) * 128], (mrg[kc][:, :N] if kc < 4 else mrg[kc])) for kc in range(8)],
                                    [("wi", half % 2)] + [("mrg", j) for j in range(8)], [("ps", b)])
                                tt("dve", h[:, m, c0:c0 + N], h[:, m, c0:c0 + N], ps[b][:, :N], ALU.add, hk + [("ps", b)], hk)
                        k.barrier()
            k.barrier()

        while pending_casts:
            one_cast()
        if stage >= 2:
            mixer()
        if stage >= 3:
            ffn(w2g_d, w2u_d, w2d_d, "ffn2_norm", "b")

        with ExitStack() as st:
            rstd = sb("rstdf", [128, 512], F32, st)
            sq = [sb("sqf%d" % i, [128, 512], BF16, st) for i in range(2)]
            yf = [sb("yf%d" % i, [128, 8, 512], F32, st) for i in range(2)]
            yo = [sb("yo%d" % i, [128, D], F32, st) for i in range(3)]
            bi = 0
            for ti, (c0, N, _, _) in enumerate(TILES):
                hk = hkeys_of(c0, N)
                rms_stats(c0, N, rstd, sq, hk)
                yft = yf[ti % 2]
                for m in range(8):
                    k.op("dve", lambda e, m=m, yft=yft, c0=c0, N=N: e.scalar_tensor_tensor(
                        out=yft[:, m, :N], in0=h[:, m, c0:c0 + N], scalar=col("final_norm", m), in1=rstd[:, :N],
                        op0=ALU.mult, op1=ALU.mult), reads=hk + ["rstd", "cols"], writes=[("yf", ti % 2, m)])
                for blk in range((N + 127) // 128):
                    rows = min(128, N - blk * 128)
                    yot = yo[bi % 3]
                    for half in range(2):
                        b = nps()
                        for m4 in range(4):
                            m = half * 4 + m4
                            k.op("pe", lambda e, b=b, m4=m4, m=m, yft=yft, rows=rows, blk=blk: e.transpose(
                                ps[b][:rows, m4 * 128:(m4 + 1) * 128], yft[:, m, blk * 128:blk * 128 + rows], ident_f),
                                reads=[("yf", ti % 2, m), "cst"], writes=[("ps", b)])
                        if half == 0:
                            k.op("act", lambda e, b=b, yot=yot, rows=rows: e.activation(
                                out=yot[:rows, 0:512], in_=ps[b][:rows, :], func=AF.Copy), reads=[("ps", b)], writes=[("yo", bi % 3)])
                        else:
                            k.op("dve", lambda e, b=b, yot=yot, rows=rows: e.tensor_copy(
                                out=yot[:rows, 512:1024], in_=ps[b][:rows, :]), reads=[("ps", b)], writes=[("yo", bi % 3)])
                    k.dma("sp", y_d[c0 + blk * 128:c0 + blk * 128 + rows, :], yot[:rows, :], reads=[("yo", bi % 3)])
                    bi += 1
        k.finish()
    return nc


def _perm_a():
    parts = [np.arange(512, 576), np.arange(1600, 1664), np.arange(1664, 1792)]
    for u in range(4):
        parts += [np.arange(u * 128, (u + 1) * 128), np.arange(576 + u * 128, 576 + (u + 1) * 128),
                  np.arange(1088 + u * 128, 1088 + (u + 1) * 128)]
    return np.concatenate(parts)


def _perm_b():
    parts = []
    for hh in range(4):
        parts += [np.arange(hh * 128, (hh + 1) * 128), np.arange(512 + hh * 128, 512 + (hh + 1) * 128),
                  np.arange(1024 + hh * 128, 1024 + (hh + 1) * 128), np.arange(1544 + hh * 128, 1544 + (hh + 1) * 128)]
    return np.concatenate(parts)


def _qkv_from_b():
    j = np.arange(1536)
    role = j // 512; hh = (j % 512) // 128; off = j % 128
    return hh * 512 + role * 128 + off


def _win_perm():
    pa = _perm_a()
    pb = A_PROJ + _perm_b()
    g0 = A_PROJ + 2056
    pg = np.concatenate([np.concatenate([np.arange(g0 + m * 128, g0 + (m + 1) * 128),
                                         np.arange(g0 + 1024 + m * 128, g0 + 1024 + (m + 1) * 128)]) for m in range(8)])
    ab = np.arange(A_PROJ + 1536, A_PROJ + 1544)
    return np.concatenate([pa, pb, pg, ab])


def _colvec(v):
    v = np.asarray(v, np.float32).reshape(-1)
    return np.ascontiguousarray(v.reshape(-1, 128).T)


def kernel(**inp):
    stage = int(inp.pop("_stage", 99))
    f = lambda a: np.ascontiguousarray(np.asarray(a, dtype=np.float32))
    pa = _perm_a()
    cols = np.zeros((128, NCOL), np.float32)
    def putc(name, arr):
        cols[:, COLS[name]:COLS[name] + arr.shape[1]] = arr
    putc("ffn1_norm", _colvec(inp["ffn1_norm"][0])); putc("mix_norm", _colvec(inp["mix_norm"][0]))
    putc("ffn2_norm", _colvec(inp["ffn2_norm"][0])); putc("final_norm", _colvec(inp["final_norm"]))
    putc("mu", _colvec(f(inp["rwkv_mu"])[0][pa]))
    for nm, key in [("w0", "rwkv_w0"), ("a0", "rwkv_a0"), ("k_k", "rwkv_k_k"), ("k_a", "rwkv_k_a"),
                    ("r_k", "rwkv_r_k"), ("lnx_w", "rwkv_lnx_w"), ("lnx_b", "rwkv_lnx_b")]:
        putc(nm, _colvec(f(inp[key])[0]))
    cw = f(inp["gdn_conv_w"])[0]
    putc("conv_w", np.concatenate([_colvec(cw[i]) for i in range(4)], axis=1))
    putc("norm_w", _colvec(f(inp["gdn_norm_w"])[0]))
    putc("A_log", np.tile(f(inp["gdn_A_log"])[0][None, :], (128, 1)))
    putc("dt_bias", np.tile(f(inp["gdn_dt_bias"])[0][None, :], (128, 1)))
    consts = make_consts()
    w_in = np.ascontiguousarray(f(inp["w_in"])[0][:, _win_perm()])
    shared = {
        "w1g": f(inp["ffn1_w_gate"])[0], "w1u": f(inp["ffn1_w_up"])[0], "w1d": f(inp["ffn1_w_down"])[0],
        "w2g": f(inp["ffn2_w_gate"])[0], "w2u": f(inp["ffn2_w_up"])[0], "w2d": f(inp["ffn2_w_down"])[0],
        "w_in": w_in, "proj_a": f(inp["proj_a"])[0], "proj_b": f(inp["proj_b"])[0], "w_out": f(inp["w_out"])[0],
        "lw2": f(inp["rwkv_w2"])[0], "la2": f(inp["rwkv_a2"])[0], "lg2": f(inp["rwkv_g2"])[0],
        "cols": cols, "consts": consts, "consts2": make_consts2(),
    }
    xp = f(inp["x_prompt"]); xs = f(inp["x_sample"])
    srw = f(inp["state_rwkv"])[0]; ssh = f(inp["state_rwkv_shift"])[0]
    sdl = f(inp["state_delta"])[0]; scv = f(inp["state_conv"])[0]
    in_maps = []
    for c in range(NCORES):
        sl = slice(c * NSQ, (c + 1) * NSQ)
        m = dict(shared)
        m["x"] = np.ascontiguousarray(np.concatenate([xp[c], xs[sl].reshape(NSQ * LS, D)], axis=0))
        m["s_rwkv"] = np.ascontiguousarray(srw[sl].reshape(NSQ * 512, 64))
        m["s_shift"] = np.ascontiguousarray(ssh[sl][:, pa])
        m["s_delta"] = np.ascontiguousarray(sdl[sl].reshape(NSQ * 512, 128))
        m["s_conv"] = np.ascontiguousarray(scv[sl].reshape(NSQ * 3, 1536))
        in_maps.append(m)
    nc = build(stage)
    res = run_bass_kernel_spmd(nc, in_maps, core_ids=list(range(NCORES)))
    R = res.results
    inv = np.argsort(pa)
    y = np.stack([r["y"] for r in R])
    y_prompt = np.ascontiguousarray(y[:, :SEQ, :])
    y_sample = np.ascontiguousarray(y[:, SEQ:, :].reshape(NCORES * NSQ, LS, D))
    qb = _qkv_from_b()
    rwkv_p = np.stack([r["rwkv_p"].reshape(8, 64, 64) for r in R])[None]
    shift_p = np.stack([r["tokrows_p"][3, :A_PROJ][inv] for r in R])[None]
    delta_p = np.stack([r["delta_p"].reshape(4, 128, 128) for r in R])[None]
    conv_p = np.stack([r["tokrows_p"][1:4, A_PROJ:][:, qb] for r in R])[None]
    rwkv_s = np.concatenate([r["rwkv_s"].reshape(NSQ, 8, 64, 64) for r in R])[None]
    shift_s = np.concatenate([r["tokrows_s"].reshape(NSQ, LS, 3840)[:, 3, :A_PROJ][:, inv] for r in R])[None]
    delta_s = np.concatenate([r["delta_s"].reshape(NSQ, 4, 128, 128) for r in R])[None]
    conv_s = np.concatenate([r["tokrows_s"].reshape(NSQ, LS, 3840)[:, 1:4, A_PROJ:][:, :, qb] for r in R])[None]
    outs = (y_prompt, y_sample, rwkv_p, shift_p, delta_p, conv_p, rwkv_s, shift_s, delta_s, conv_s)
    return tuple(np.ascontiguousarray(o.astype(np.float32)) for o in outs)
```
